# Optimizing a Trainium2 kernel written in Bass

```python
import functools
import jax, jax.numpy as jnp
from jax import lax
import numpy as np

D_MODEL = 1024
BATCH = 32
SEQ = 256
DEPTH = 2
DEC_BATCH = 2
DEC_SEQ = 1024
PAST_LEN = 256

GRID_W = 64
DH = 64
H_A = 8
KV_A = 2
H_B = 8
H_C = 8
H_D = 8
KV_D = 2
D_MIX_EVEN = (H_A + H_B) * DH
D_MIX_ODD = (H_C + H_D) * DH
D_FF = 2816
ADA_CHUNKS = 9
QBLOCK = 128
MLSTM_CHUNK = 64
NA_KH = 8
NA_KW = 16
NA_SLAB = 2 * NA_KW
SWA_WIN = 128
ROPE_THETA = 10000.0
ROPE_FREQS = DH // 4
ATTN_SCALE = DH ** -0.5
NEG_INF = -1e30
EPS = 1e-6
EVEN_SIZES = (H_A * DH, KV_A * DH, KV_A * DH, H_B * DH, H_B * DH, H_B * DH, 4 * H_B, H_B * DH)
ODD_SIZES = (H_C * DH, H_C * DH, H_C * DH, H_D * DH, KV_D * DH, KV_D * DH)

kernel_name = "hybrid_diffusion_prefix_trunk_step"


def rms_norm(x, g):
    xf = x.astype(jnp.float32)
    y = xf * lax.rsqrt(jnp.mean(xf * xf, axis=-1, keepdims=True) + EPS)
    return (y * g.astype(jnp.float32)).astype(x.dtype)


def modulate(h, shift, scale):
    return h * (1 + scale) + shift


def adaln(cond, w, b):
    m = jax.nn.silu(cond) @ w + b
    m = m.reshape(-1, 1, ADA_CHUNKS, D_MODEL)
    return [m[:, :, i] for i in range(ADA_CHUNKS)]


def swiglu(h, w_in, w_out):
    g, u = jnp.split(h @ w_in, 2, axis=-1)
    return (jax.nn.silu(g) * u) @ w_out


def split_cols(p, sizes):
    return jnp.split(p, np.cumsum(sizes)[:-1].tolist(), axis=-1)


def axial_rope(L, dtype):
    t = jnp.arange(L)
    pos = jnp.stack([t // GRID_W, t % GRID_W], axis=-1).astype(jnp.float32)
    freqs = ROPE_THETA ** (-jnp.arange(ROPE_FREQS, dtype=jnp.float32) / ROPE_FREQS)
    ang = (pos[:, :, None] * freqs).reshape(L, 2 * ROPE_FREQS)
    return jnp.cos(ang)[:, None, :].astype(dtype), jnp.sin(ang)[:, None, :].astype(dtype)


def apply_rope(x, cos, sin):
    x1, x2 = jnp.split(x, 2, axis=-1)
    return jnp.concatenate([x1 * cos - x2 * sin, x2 * cos + x1 * sin], axis=-1)


def attend_dense(q, k, v, sink=None):
    B, L, H, dh = q.shape
    KV = k.shape[2]
    G = H // KV
    nb = L // QBLOCK
    qb = q.reshape(B, nb, QBLOCK, KV, G, dh).swapaxes(0, 1)

    def block(qblk):
        s = jnp.einsum('bqkgd,bskd->bkgqs', qblk, k).astype(jnp.float32) * ATTN_SCALE
        if sink is None:
            p = jax.nn.softmax(s, axis=-1)
        else:
            sk = jnp.broadcast_to(sink.astype(jnp.float32).reshape(KV, G, 1, 1), s.shape[:-1] + (1,))
            p = jax.nn.softmax(jnp.concatenate([sk, s], axis=-1), axis=-1)[..., 1:]
        return jnp.einsum('bkgqs,bskd->bqkgd', p.astype(v.dtype), v)

    o = lax.map(block, qb)
    return o.swapaxes(0, 1).reshape(B, L, H * dh)


def swa_latent(q, k, v, k_ctx, v_ctx, sink):
    B, L, H, dh = q.shape
    KV = k.shape[2]
    G = H // KV
    nb = L // QBLOCK
    span = QBLOCK + 2 * SWA_WIN
    pad = ((0, 0), (SWA_WIN, SWA_WIN), (0, 0), (0, 0))
    kidx = jnp.arange(nb)[:, None] * QBLOCK + jnp.arange(span)[None, :]
    kb = jnp.pad(k, pad)[:, kidx]
    vb = jnp.pad(v, pad)[:, kidx]
    qb = q.reshape(B, nb, QBLOCK, KV, G, dh)
    s_loc = jnp.einsum('bnqkgd,bnskd->bnkgqs', qb, kb).astype(jnp.float32) * ATTN_SCALE
    qpos = jnp.arange(L).reshape(nb, QBLOCK)
    kpos = kidx - SWA_WIN
    ok = ((kpos[:, None, :] >= 0) & (kpos[:, None, :] < L)
          & (jnp.abs(kpos[:, None, :] - qpos[:, :, None]) <= SWA_WIN))
    s_loc = jnp.where(ok[None, :, None, None], s_loc, NEG_INF)
    s_ctx = jnp.einsum('bnqkgd,bskd->bnkgqs', qb, k_ctx).astype(jnp.float32) * ATTN_SCALE
    sk = jnp.broadcast_to(sink.astype(jnp.float32).reshape(1, 1, KV, G, 1, 1), s_ctx.shape[:-1] + (1,))
    p = jax.nn.softmax(jnp.concatenate([sk, s_ctx, s_loc], axis=-1), axis=-1).astype(v.dtype)
    P = k_ctx.shape[1]
    o = (jnp.einsum('bnkgqs,bskd->bnqkgd', p[..., 1:1 + P], v_ctx)
         + jnp.einsum('bnkgqs,bnskd->bnqkgd', p[..., 1 + P:], vb))
    return o.reshape(B, L, H * dh)


def na_latent(q, k, v, k_ctx, v_ctx, rpb):
    B, L, H, dh = q.shape
    rows = L // GRID_W
    kh = min(NA_KH, rows)
    ncb = GRID_W // NA_KW
    r = jnp.arange(rows)
    row_idx = jnp.clip(r - kh // 2, 0, rows - kh)[:, None] + jnp.arange(kh)[None, :]
    qcol = jnp.arange(GRID_W).reshape(ncb, NA_KW)
    col_idx = (jnp.clip(jnp.arange(ncb) * NA_KW - NA_KW // 2, 0, GRID_W - NA_SLAB)[:, None]
               + jnp.arange(NA_SLAB)[None, :])
    win_start = jnp.clip(qcol - NA_KW // 2, 0, GRID_W - NA_KW)
    col_ok = ((col_idx[:, None, :] >= win_start[..., None])
              & (col_idx[:, None, :] < win_start[..., None] + NA_KW))
    mask = jnp.broadcast_to(col_ok[:, :, None, :], (ncb, NA_KW, kh, NA_SLAB)).reshape(ncb, NA_KW, kh * NA_SLAB)
    gi = row_idx[:, None, :, None]
    gj = col_idx[None, :, None, :]
    kg = k.reshape(B, rows, GRID_W, H, dh)[:, gi, gj].reshape(B, rows, ncb, kh * NA_SLAB, H, dh)
    vg = v.reshape(B, rows, GRID_W, H, dh)[:, gi, gj].reshape(B, rows, ncb, kh * NA_SLAB, H, dh)
    qg = q.reshape(B, rows, ncb, NA_KW, H, dh)
    s_win = jnp.einsum('brnqhd,brnshd->brnhqs', qg, kg).astype(jnp.float32) * ATTN_SCALE
    dy = row_idx - r[:, None] + (NA_KH - 1)
    dx = jnp.clip(col_idx[:, None, :] - qcol[..., None] + (NA_KW - 1), 0, 2 * NA_KW - 2)
    bias = rpb.astype(jnp.float32)[:, dy[:, None, None, :, None], dx[None, :, :, None, :]]
    bias = bias.transpose(1, 2, 0, 3, 4, 5).reshape(rows, ncb, H, NA_KW, kh * NA_SLAB)
    s_win = jnp.where(mask[None, None, :, None], s_win + bias[None], NEG_INF)
    s_ctx = jnp.einsum('brnqhd,bshd->brnhqs', qg, k_ctx).astype(jnp.float32) * ATTN_SCALE
    p = jax.nn.softmax(jnp.concatenate([s_ctx, s_win], axis=-1), axis=-1).astype(v.dtype)
    P = k_ctx.shape[1]
    o = (jnp.einsum('brnhqs,bshd->brnqhd', p[..., :P], v_ctx)
         + jnp.einsum('brnhqs,brnshd->brnqhd', p[..., P:], vg))
    return o.reshape(B, L, H * dh)


def mlstm_scan(q, k, v, i_pre, f_pre, C0, n0, m0):
    B, L, H, dh = q.shape
    nc = L // MLSTM_CHUNK
    f32 = jnp.float32

    def chunks(a):
        return a.astype(f32).reshape((B, nc, MLSTM_CHUNK) + a.shape[2:]).swapaxes(0, 1)

    xs = (chunks(q), chunks(k) * (dh ** -0.5), chunks(v), chunks(i_pre),
          chunks(jax.nn.log_sigmoid(f_pre.astype(f32))))
    tri = jnp.tril(jnp.ones((MLSTM_CHUNK, MLSTM_CHUNK), dtype=bool))

    def step(carry, xc):
        C, n, m = carry
        qc, kc, vc, li, lf = xc
        li = li.swapaxes(1, 2)
        b = jnp.cumsum(lf.swapaxes(1, 2), axis=-1)
        d = jnp.where(tri, b[..., :, None] - b[..., None, :] + li[..., None, :], -jnp.inf)
        inter = b + m[..., None]
        mt = jnp.maximum(inter, d.max(axis=-1))
        w = jnp.exp(d - mt[..., None]) * jnp.einsum('bthd,bshd->bhts', qc, kc)
        w_inter = jnp.exp(inter - mt)
        num = (jnp.einsum('bhts,bshd->bhtd', w, vc)
               + w_inter[..., None] * jnp.einsum('bthk,bhkv->bhtv', qc, C))
        den = w.sum(axis=-1) + w_inter * jnp.einsum('bthk,bhk->bht', qc, n)
        h = num / jnp.maximum(jnp.abs(den), jnp.exp(-mt))[..., None]
        b_last = b[..., -1]
        dec = b_last[..., None] - b + li
        m_new = jnp.maximum(b_last + m, dec.max(axis=-1))
        ws = jnp.exp(dec - m_new[..., None])
        wc = jnp.exp(b_last + m - m_new)
        C_new = wc[..., None, None] * C + jnp.einsum('bhs,bshk,bshv->bhkv', ws, kc, vc)
        n_new = wc[..., None] * n + jnp.einsum('bhs,bshk->bhk', ws, kc)
        return (C_new, n_new, m_new), h.swapaxes(1, 2)

    (C, n, m), hs = lax.scan(step, (C0.astype(f32), n0.astype(f32), m0.astype(f32)), xs)
    return hs.swapaxes(0, 1).reshape(B, L, H, dh), C, n, m


def mlstm_bidir(qb, kb, vb, gates, ob, head_g, C0, n0, m0):
    B, L = qb.shape[:2]
    flip = lambda a: a[:, ::-1]
    hf, Cf, nf, mf = mlstm_scan(qb, kb, vb, gates[:, :, 0], gates[:, :, 1], C0[:, 0], n0[:, 0], m0[:, 0])
    hb, Cb, nb_, mb = mlstm_scan(flip(qb), flip(kb), flip(vb), flip(gates[:, :, 2]), flip(gates[:, :, 3]),
                                 C0[:, 1], n0[:, 1], m0[:, 1])
    h = (hf + flip(hb)).astype(qb.dtype)
    y = jax.nn.sigmoid(ob) * rms_norm(h, head_g.reshape(H_B, DH)).reshape(B, L, H_B * DH)
    return y, jnp.stack([Cf, Cb], axis=1), jnp.stack([nf, nb_], axis=1), jnp.stack([mf, mb], axis=1)


def even_project(h, w_in, qk_g, gate_b):
    B, L, _ = h.shape
    qa, ka, va, qb, kb, vb, gates, ob = split_cols(h @ w_in, EVEN_SIZES)
    qa = rms_norm(qa.reshape(B, L, H_A, DH), qk_g[0])
    ka = rms_norm(ka.reshape(B, L, KV_A, DH), qk_g[1])
    va = va.reshape(B, L, KV_A, DH)
    heads = lambda a: a.reshape(B, L, H_B, DH)
    gates = (gates + gate_b).reshape(B, L, 4, H_B)
    return qa, ka, va, heads(qb), heads(kb), heads(vb), gates, ob


def even_mixer_context(h, params):
    w_in, w_out, qk_g, gate_b, head_g = params
    B = h.shape[0]
    qa, ka, va, qb, kb, vb, gates, ob = even_project(h, w_in, qk_g, gate_b)
    ya = attend_dense(qa, ka, va)
    C0 = jnp.zeros((B, 2, H_B, DH, DH), jnp.float32)
    n0 = jnp.zeros((B, 2, H_B, DH), jnp.float32)
    m0 = jnp.zeros((B, 2, H_B), jnp.float32)
    yb, C, n, m = mlstm_bidir(qb, kb, vb, gates, ob, head_g, C0, n0, m0)
    return jnp.concatenate([ya, yb], axis=-1) @ w_out, (ka, va, C, n, m)


def even_mixer_latent(h, params, cache):
    w_in, w_out, qk_g, gate_b, head_g = params
    k_ctx, v_ctx, C0, n0, m0 = cache
    L = h.shape[1]
    qa, ka, va, qb, kb, vb, gates, ob = even_project(h, w_in, qk_g, gate_b)
    cos, sin = axial_rope(L, qa.dtype)
    qa, ka = apply_rope(qa, cos, sin), apply_rope(ka, cos, sin)
    ya = attend_dense(qa, jnp.concatenate([k_ctx.astype(ka.dtype), ka], axis=1),
                      jnp.concatenate([v_ctx.astype(va.dtype), va], axis=1))
    yb = mlstm_bidir(qb, kb, vb, gates, ob, head_g, C0, n0, m0)[0]
    return jnp.concatenate([ya, yb], axis=-1) @ w_out, ()


def odd_project(h, w_in):
    B, L, _ = h.shape
    qc, kc, vc, qd, kd, vd = split_cols(h @ w_in, ODD_SIZES)
    c_heads = lambda a: a.reshape(B, L, H_C, DH)
    return (c_heads(qc), c_heads(kc), c_heads(vc), qd.reshape(B, L, H_D, DH),
            kd.reshape(B, L, KV_D, DH), vd.reshape(B, L, KV_D, DH))


def odd_mixer_context(h, params):
    w_in, w_out, rpb, sink = params
    qc, kc, vc, qd, kd, vd = odd_project(h, w_in)
    yc = attend_dense(qc, kc, vc)
    yd = attend_dense(qd, kd, vd, sink)
    return jnp.concatenate([yc, yd], axis=-1) @ w_out, (kc, vc, kd, vd)


def odd_mixer_latent(h, params, cache):
    w_in, w_out, rpb, sink = params
    kc_ctx, vc_ctx, kd_ctx, vd_ctx = cache
    L = h.shape[1]
    qc, kc, vc, qd, kd, vd = odd_project(h, w_in)
    yc = na_latent(qc, kc, vc, kc_ctx.astype(kc.dtype), vc_ctx.astype(vc.dtype), rpb)
    cos, sin = axial_rope(L, qd.dtype)
    yd = swa_latent(apply_rope(qd, cos, sin), apply_rope(kd, cos, sin), vd,
                    kd_ctx.astype(kd.dtype), vd_ctx.astype(vd.dtype), sink)
    return jnp.concatenate([yc, yd], axis=-1) @ w_out, ()


def trunk_layer(x, mod, norm_g, f1_in, f1_out, f2_in, f2_out, mixer):
    sh1, sc1, g1, sh2, sc2, g2, sh3, sc3, g3 = mod
    x = x + 0.5 * g1 * swiglu(modulate(rms_norm(x, norm_g[0]), sh1, sc1), f1_in, f1_out)
    mo, st = mixer(modulate(rms_norm(x, norm_g[1]), sh2, sc2))
    x = x + g2 * mo
    x = x + 0.5 * g3 * swiglu(modulate(rms_norm(x, norm_g[2]), sh3, sc3), f2_in, f2_out)
    return x, st


def setup_inputs(seed: int = 0) -> dict:
    key = jax.random.key(seed)
    ks = iter(list(jax.random.split(key, 80)))

    def rnd(shape, scale, offset=0.0):
        return offset + scale * jax.random.normal(next(ks), shape, jnp.float32)

    D = D_MODEL
    inp = {}
    inp['x_prompt'] = rnd((BATCH, SEQ, D), 1.0)
    inp['x_sample'] = rnd((DEC_BATCH, DEC_SEQ, D), 1.0)
    inp['cache_l0_attn_k'] = rnd((DEC_BATCH, PAST_LEN, KV_A, DH), 1.0)
    inp['cache_l0_attn_v'] = rnd((DEC_BATCH, PAST_LEN, KV_A, DH), 1.0)
    inp['state_l0_mlstm_C'] = rnd((DEC_BATCH, 2, H_B, DH, DH), 0.3)
    inp['state_l0_mlstm_n'] = rnd((DEC_BATCH, 2, H_B, DH), 0.3)
    inp['state_l0_mlstm_m'] = rnd((DEC_BATCH, 2, H_B), 1.0)
    inp['cache_l1_na_k'] = rnd((DEC_BATCH, PAST_LEN, H_C, DH), 1.0)
    inp['cache_l1_na_v'] = rnd((DEC_BATCH, PAST_LEN, H_C, DH), 1.0)
    inp['cache_l1_swa_k'] = rnd((DEC_BATCH, PAST_LEN, KV_D, DH), 1.0)
    inp['cache_l1_swa_v'] = rnd((DEC_BATCH, PAST_LEN, KV_D, DH), 1.0)
    inp['c'] = rnd((DEC_BATCH, D), 1.0)
    inp['c_ctx'] = rnd((D,), 1.0)
    inp['norm_final'] = rnd((D,), 0.05, 1.0)
    for l, (p_in, d_mix) in enumerate(((sum(EVEN_SIZES), D_MIX_EVEN), (sum(ODD_SIZES), D_MIX_ODD))):
        inp[f'ada_w_l{l}'] = rnd((D, ADA_CHUNKS * D), 0.5 * D ** -0.5)
        inp[f'ada_b_l{l}'] = rnd((ADA_CHUNKS * D,), 0.1)
        inp[f'norm_l{l}'] = rnd((3, D), 0.05, 1.0)
        inp[f'ffn1_in_l{l}'] = rnd((D, 2 * D_FF), D ** -0.5)
        inp[f'ffn1_out_l{l}'] = rnd((D_FF, D), D_FF ** -0.5)
        inp[f'ffn2_in_l{l}'] = rnd((D, 2 * D_FF), D ** -0.5)
        inp[f'ffn2_out_l{l}'] = rnd((D_FF, D), D_FF ** -0.5)
        inp[f'mix_in_l{l}'] = rnd((D, p_in), D ** -0.5)
        inp[f'mix_out_l{l}'] = rnd((d_mix, D), d_mix ** -0.5)
        if l == 0:
            inp['qk_norm_l0'] = rnd((2, DH), 0.05, 1.0)
            inp['gate_bias_l0'] = jnp.concatenate([rnd((H_B,), 0.1, -1.0), rnd((H_B,), 0.1, 3.0),
                                                   rnd((H_B,), 0.1, -1.0), rnd((H_B,), 0.1, 3.0)])
            inp['head_norm_l0'] = rnd((H_B * DH,), 0.05, 1.0)
        else:
            inp['rpb_l1'] = rnd((H_C, 2 * NA_KH - 1, 2 * NA_KW - 1), 0.1)
            inp['sink_l1'] = rnd((H_D,), 0.5)
    return inp


def reference(x_prompt, x_sample, cache_l0_attn_k, cache_l0_attn_v, state_l0_mlstm_C, state_l0_mlstm_n,
              state_l0_mlstm_m, cache_l1_na_k, cache_l1_na_v, cache_l1_swa_k, cache_l1_swa_v, c, c_ctx,
              norm_final,
              ada_w_l0, ada_b_l0, norm_l0, ffn1_in_l0, ffn1_out_l0, ffn2_in_l0, ffn2_out_l0,
              mix_in_l0, mix_out_l0, qk_norm_l0, gate_bias_l0, head_norm_l0,
              ada_w_l1, ada_b_l1, norm_l1, ffn1_in_l1, ffn1_out_l1, ffn2_in_l1, ffn2_out_l1,
              mix_in_l1, mix_out_l1, rpb_l1, sink_l1):
    common = ((ada_w_l0, ada_b_l0, norm_l0, ffn1_in_l0, ffn1_out_l0, ffn2_in_l0, ffn2_out_l0),
              (ada_w_l1, ada_b_l1, norm_l1, ffn1_in_l1, ffn1_out_l1, ffn2_in_l1, ffn2_out_l1))
    mixers = ((mix_in_l0, mix_out_l0, qk_norm_l0, gate_bias_l0, head_norm_l0),
              (mix_in_l1, mix_out_l1, rpb_l1, sink_l1))
    caches = ((cache_l0_attn_k, cache_l0_attn_v, state_l0_mlstm_C, state_l0_mlstm_n, state_l0_mlstm_m),
              (cache_l1_na_k, cache_l1_na_v, cache_l1_swa_k, cache_l1_swa_v))
    xp, xs = x_prompt, x_sample
    ctx_states = []
    for l in range(DEPTH):
        ada_w, ada_b, norm_g, f1_in, f1_out, f2_in, f2_out = common[l]
        even = l % 2 == 0
        ctx_mixer = functools.partial(even_mixer_context if even else odd_mixer_context, params=mixers[l])
        lat_mixer = functools.partial(even_mixer_latent if even else odd_mixer_latent,
                                      params=mixers[l], cache=caches[l])
        xp, st = trunk_layer(xp, adaln(c_ctx, ada_w, ada_b), norm_g, f1_in, f1_out, f2_in, f2_out, ctx_mixer)
        xs, _ = trunk_layer(xs, adaln(c, ada_w, ada_b), norm_g, f1_in, f1_out, f2_in, f2_out, lat_mixer)
        ctx_states.append(st)
    y_prompt = rms_norm(xp, norm_final)
    y_sample = rms_norm(xs, norm_final)
    (k0, v0, C0, n0, m0), (kc1, vc1, kd1, vd1) = ctx_states
    return (y_prompt, y_sample, k0, v0, C0, n0, m0, kc1, vc1, kd1, vd1)
```

```python
import contextlib
import numpy as np
import concourse.bass as bass
import concourse.mybir as mybir
from concourse.bass_utils import run_bass_kernel_spmd

F32 = mybir.dt.float32
BF16 = mybir.dt.bfloat16
AF = mybir.ActivationFunctionType
ALU = mybir.AluOpType
AX = mybir.AxisListType

ENGS = ('pe', 'act', 'dve', 'pool', 'sp')
NEG = -1.0e30
EPS = 1e-6


class Res:
    __slots__ = ('name', 'w', 'r', 'sem', 'dcount', 'excl')

    def __init__(self, name=''):
        self.name = name
        self.excl = False
        self.w = {}
        self.r = {}
        self.sem = None
        self.dcount = 0


class _Rec:
    def __init__(self):
        self.call = None

    def __getattr__(self, name):
        def f(*a, **k):
            self.call = (name, a, k)
            return self
        return f


class Ins:
    __slots__ = ('fn', 'waits', 'milestone', 'dma_res')

    def __init__(self, fn):
        rec = _Rec()
        fn(rec)
        name, a, k = rec.call
        self.fn = lambda eh: getattr(eh, name)(*a, **k)
        self.waits = []
        self.milestone = False
        self.dma_res = None


class Sched:
    def __init__(self, nc, stack):
        self.nc = nc
        self.stack = stack
        self.streams = {e: [] for e in ENGS}
        self.known = {e: {} for e in ENGS}
        self.dma_res = []
        self.semof = {}
        self.esem = {}
        for e in ('pe', 'act', 'dve', 'pool'):
            self.esem[e] = stack.enter_context(nc.semaphore('es_' + e))

    def _waits(self, ins, eng, deps):
        for key, val in deps.items():
            if self.known[eng].get(key, -1) >= val:
                continue
            self.known[eng][key] = val
            ins.waits.append((key, val))
            if key[0] == 'e':
                self.streams[key[1]][val].milestone = True

    def _deps(self, eng, reads, writes):
        deps = {}

        def add(d, raw):
            for k, v in d.items():
                if k[0] == 'e' and k[1] == eng and (eng == 'pe' or not raw):
                    continue
                if deps.get(k, -1) < v:
                    deps[k] = v
        for r in reads:
            add(r.w, True)
            if r.excl:
                add({k: v for k, v in r.r.items() if not (k[0] == 'e' and k[1] == eng)}, False)
        for r in writes:
            add(r.w, False)
            add(r.r, False)
        return deps

    def op(self, eng, fn, reads=(), writes=()):
        ins = Ins(fn)
        idx = len(self.streams[eng])
        self._waits(ins, eng, self._deps(eng, reads, writes))
        key = ('e', eng)
        for r in writes:
            r.w = {key: idx}
            r.r = {}
        for r in reads:
            if r not in writes:
                r.r[key] = idx
        self.streams[eng].append(ins)
        return ins

    def dma(self, eng, fn, sres, reads=(), writes=()):
        ins = Ins(fn)
        ins.dma_res = sres
        if sres.sem is None:
            sres.sem = self.stack.enter_context(self.nc.semaphore('ds_%d' % len(self.dma_res)))
            self.dma_res.append(sres)
            self.semof[id(sres)] = sres
        self._waits(ins, eng, self._deps(eng, reads, writes))
        sres.dcount += 1
        key = ('d', id(sres))
        val = 16 * sres.dcount
        for r in writes:
            r.w = {key: val}
            r.r = {}
        for r in reads:
            if r not in writes:
                r.r[key] = val
        self.streams[eng].append(ins)
        return ins

    def emit(self):
        nc = self.nc
        ordinal = {}
        for e in ('pe', 'act', 'dve', 'pool'):
            c = 0
            for i, ins in enumerate(self.streams[e]):
                if ins.milestone:
                    c += 1
                    ordinal[(e, i)] = c

        def run(eng, eh):
            for i, ins in enumerate(self.streams[eng]):
                for key, val in ins.waits:
                    if key[0] == 'e':
                        eh.wait_ge(self.esem[key[1]], ordinal[(key[1], val)])
                    else:
                        eh.wait_ge(self.semof[key[1]].sem, val)
                bi = ins.fn(eh)
                if ins.dma_res is not None:
                    bi.then_inc(ins.dma_res.sem, 16)
                elif ins.milestone:
                    bi.then_inc(self.esem[eng], 1)
            if eng == 'sp':
                for r in self.dma_res:
                    eh.wait_ge(r.sem, 16 * r.dcount)

        with nc.Block() as block:
            @block.tensor
            def _(e):
                run('pe', e)

            @block.scalar
            def _(e):
                run('act', e)

            @block.vector
            def _(e):
                run('dve', e)

            @block.gpsimd
            def _(e):
                run('pool', e)

            @block.sync
            def _(e):
                run('sp', e)


class Tl:
    __slots__ = ('a', 'r')

    def __init__(self, a, r):
        self.a = a
        self.r = r


D = 1024
DFF = 2816
NTOK = 2048
L0_COLS = 3328
L1_COLS = 2560


def _rope_tables():
    t = np.arange(1024)
    pos = np.stack([t // 64, t % 64], -1).astype(np.float32)
    freqs = (10000.0 ** (-np.arange(16, dtype=np.float32) / 16)).astype(np.float32)
    ang = (pos[:, :, None] * freqs).reshape(1024, 32).astype(np.float32)
    cos = np.cos(ang).astype(np.float32).T
    sin = np.sin(ang).astype(np.float32).T
    C = np.concatenate([cos, cos, cos, cos], 0)
    Sg = np.concatenate([-sin, sin, -sin, sin], 0)
    return np.ascontiguousarray(C), np.ascontiguousarray(Sg)


def _consts():
    c = {}
    C, Sg = _rope_tables()
    c['ropeC'] = C
    c['ropeS'] = Sg
    s = np.arange(128)[:, None]
    t = np.arange(128)[None, :]
    c['maskF'] = np.where(t >= s, 0.0, NEG).astype(np.float32)
    c['maskB'] = np.where(t <= s, 0.0, NEG).astype(np.float32)
    c['mask01F'] = (t >= s).astype(np.float32)
    c['mask01B'] = (t <= s).astype(np.float32)
    psw = np.zeros((128, 128), np.float32)
    for m in range(128):
        d = m % 64
        k = (m - d) + ((d + 32) % 64)
        psw[k, m] = 1.0
    c['psw'] = psw
    oh = np.zeros((64, 16), np.float32)
    for h in range(8):
        oh[h, h] = 1.0
        oh[32 + h, 8 + h] = 1.0
    c['oh'] = oh
    sc = np.arange(64)[:, None]
    qc = np.arange(64)[None, :]
    ws = np.clip(qc - 8, 0, 48)
    cm = np.where((sc >= ws) & (sc < ws + 16), 0.0, NEG).astype(np.float32)
    c['cmask'] = np.concatenate([cm, cm], 0)
    return c


def _na_rows():
    start = [min(max(r - 4, 0), 8) for r in range(16)]
    valid = [[start[r] <= s < start[r] + 8 for r in range(16)] for s in range(16)]
    return valid


def build_program(dbg=None):
    nc = bass.Bass("TRN2", target_bir_lowering=False)
    st = contextlib.ExitStack()
    with st:
        _build(nc, st, dbg)
    return nc


def _build(nc, st, dbg):
    S = Sched(nc, st)

    def din(name, shape):
        return nc.dram_tensor(name, list(shape), F32, kind="ExternalInput")

    def dout(name, shape):
        return nc.dram_tensor(name, list(shape), F32, kind="ExternalOutput")

    xin = din('xin', [NTOK, D])
    condT = din('condT', [128, 8, 2])
    gfin = din('gfin', [128, 8])
    Wd = []
    for l in range(2):
        w = {}
        w['ada_w'] = din('ada_w%d' % l, [D, 9 * D])
        w['ada_b'] = din('ada_b%d' % l, [128, 72])
        w['norm'] = din('norm%d' % l, [128, 3, 8])
        for f in (1, 2):
            w['f%din' % f] = din('f%din%d' % (f, l), [D, 2 * DFF])
            w['f%dout' % f] = din('f%dout%d' % (f, l), [DFF, D])
        w['mix_in'] = din('mixin%d' % l, [D, L0_COLS if l == 0 else L1_COLS])
        w['mix_out'] = din('mixout%d' % l, [D, D])
        Wd.append(w)
    qkg = din('qkg', [128, 2])
    gkbc = din('gkbc', [64])
    gbias = din('gbias', [64, 2])
    hgn = din('hgn', [512])
    kctx0 = din('kctx0', [256, 128])
    vctx0 = din('vctx0', [256, 128])
    C0 = din('C0', [2, 8, 64, 64])
    n0 = din('n0', [2, 8, 64])
    m0 = din('m0', [16])
    kcctx = din('kcctx', [256, 512])
    vcctx = din('vcctx', [256, 512])
    kdctx = din('kdctx', [256, 128])
    vdctx = din('vdctx', [256, 128])
    natb = din('natb', [8, 128, 15 * 64])
    sink = din('sink', [8])
    cd = {k: din('c_' + k, v.shape) for k, v in _consts().items()}

    o_yp = dout('o_yp', [1024, D])
    o_ys = dout('o_ys', [1024, D])
    o_k0 = dout('o_k0', [4, 256, 128])
    o_v0 = dout('o_v0', [4, 256, 128])
    o_C = dout('o_C', [4, 2, 8, 64, 64])
    o_n = dout('o_n', [4, 2, 8, 64])
    o_m = dout('o_m', [4, 2, 8])
    o_kc = dout('o_kc', [4, 256, 512])
    o_vc = dout('o_vc', [4, 256, 512])
    o_kd = dout('o_kd', [4, 256, 128])
    o_vd = dout('o_vd', [4, 256, 128])
    dbg_out = {}
    if dbg:
        for name, shape in dbg.items():
            dbg_out[name] = dout('dbg_' + name, shape)

    cnt = [0]

    def sb(shape, dt, name=None):
        cnt[0] += 1
        t = st.enter_context(nc.sbuf_tensor(name or ('t%d' % cnt[0]), list(shape), dt))
        return t

    def mk(shape, dt, name=None):
        t = sb(shape, dt, name)
        return Tl(t[:], Res(name or ''))

    banks = []
    for i in range(8):
        t = st.enter_context(nc.psum_tensor('bank%d' % i, [128, 512], F32))
        banks.append(Tl(t[:], Res('bank%d' % i)))
        banks[-1].r.excl = True
    rot = {'a': [0, 1], 'b': [2, 3], 'c': [4, 5], 'x': [6, 7], 's': [0, 1, 2, 3]}
    rotc = {k: 0 for k in rot}

    def ps(tag):
        i = rot[tag][rotc[tag] % len(rot[tag])]
        rotc[tag] += 1
        return banks[i]

    AR_BYTES = 94 * 1024
    arena = sb([128, AR_BYTES // 2], BF16, 'arena')
    ar = {'off': 0, 'live': [], 'inherit': {}}

    def ar_reset():
        toks = dict(ar['inherit'])
        for r in ar['live']:
            for d in (r.w, r.r):
                for k, v in d.items():
                    if toks.get(k, -1) < v:
                        toks[k] = v
        ar['inherit'] = toks
        ar['live'] = []
        ar['off'] = 0

    def ar_res(name=''):
        r = Res(name)
        r.w = dict(ar['inherit'])
        ar['live'].append(r)
        return r

    def carve(shape, dt, name=''):
        n = int(np.prod(shape[1:]))
        nb = n * (4 if dt == F32 else 2)
        nb = (nb + 31) // 32 * 32
        off = ar['off']
        assert off + nb <= AR_BYTES, ('arena overflow', name, off, nb)
        ar['off'] = off + nb
        a = arena[:, off // 2: off // 2 + (n * (2 if dt == F32 else 1))]
        if dt == F32:
            a = a.bitcast(F32)
        if shape[0] != 128:
            a = a[0:shape[0]]
        if len(shape) > 2:
            names = ' '.join('d%d' % i for i in range(1, len(shape)))
            kw = {'d%d' % i: shape[i] for i in range(1, len(shape) - 1)}
            a = a.rearrange('p (%s) -> p %s' % (names, names), **kw)
        return Tl(a, ar_res(name))

    xT = sb([128, 8, NTOK], F32, 'xT')
    xres = [[Res('x%d_%d' % (m, tg)) for tg in range(4)] for m in range(8)]

    def xap(m, tg):
        return xT[:, m, tg * 512:(tg + 1) * 512]

    NRING = 4
    ring_all = [mk([128, 4096], BF16, 'ring%d' % i) for i in range(NRING)]
    ring = list(ring_all)
    ringc = [0]

    def ring_next():
        s_ = ring[ringc[0] % len(ring)]
        ringc[0] += 1
        return s_

    def wslab_in(wd, c0, n):
        s = ring_next()
        dst = s.a[:, 0:8 * n].rearrange('p (k n) -> p k n', k=8)
        src = wd.ap()[:, c0:c0 + n].rearrange('(k p) n -> p k n', p=128)
        S.dma('pool', lambda e: e.dma_start(out=dst, in_=src), s.r, writes=[s.r])
        return Tl(dst, s.r)

    def wslab_out(wd, r0, nchunk):
        s = ring_next()
        dst = s.a[:, 0:nchunk * 1024].rearrange('p (c n) -> p c n', c=nchunk)
        src = wd.ap()[r0:r0 + 128 * nchunk, :].rearrange('(c p) n -> p c n', p=128)
        S.dma('pool', lambda e: e.dma_start(out=dst, in_=src), s.r, writes=[s.r])
        return Tl(dst, s.r)

    CR = Res('consts')

    def cload(dram, shape, dt=F32, src=None, name=None):
        t = sb(shape, dt, name)
        a = src if src is not None else dram.ap()
        S.dma('sp', lambda e: e.dma_start(out=t[:], in_=a), CR, writes=[CR])
        return t

    ident_f = sb([128, 128], F32, 'ident_f')
    ident_b = sb([128, 128], BF16, 'ident_b')
    ones_b = sb([128, 128], BF16, 'ones_b')
    bd_b = sb([128, 128], BF16, 'bd_b')
    ones_f = sb([64, 128], F32, 'ones_f')
    KR = Res('kconst')
    S.op('pool', lambda e: e.memset(ident_f[:], 0.0), writes=[KR])
    S.op('pool', lambda e: e.affine_select(out=ident_f[:], in_=ident_f[:], pattern=[[-1, 128]], compare_op=ALU.not_equal,
                                           fill=1.0, base=0, channel_multiplier=1), reads=[KR], writes=[KR])
    S.op('pool', lambda e: e.tensor_copy(ident_b[:], ident_f[:]), reads=[KR], writes=[KR])
    S.op('pool', lambda e: e.memset(ones_b[:], 1.0), writes=[KR])
    S.op('pool', lambda e: e.memset(ones_f[:], 1.0), writes=[KR])
    S.op('pool', lambda e: e.memset(bd_b[:], 0.0), writes=[KR])
    S.op('pool', lambda e: e.memset(bd_b[0:64, 0:64], 1.0), reads=[KR], writes=[KR])
    S.op('pool', lambda e: e.memset(bd_b[64:128, 64:128], 1.0), reads=[KR], writes=[KR])

    cond_t = cload(condT, [128, 8, 2])
    gfin_t = cload(gfin, [128, 8])
    adab_t = [cload(Wd[l]['ada_b'], [128, 72]) for l in range(2)]
    norm_t = [cload(Wd[l]['norm'], [128, 3, 8]) for l in range(2)]
    qkg_t = cload(qkg, [128, 2])
    gkbc_t = cload(gkbc, [128, 64], src=gkbc.ap().partition_broadcast(128))
    gbias_t = cload(gbias, [64, 2])
    hgn_t = cload(hgn, [128, 512], src=hgn.ap().partition_broadcast(128))
    m0bc_t = cload(m0, [128, 16], src=m0.ap().partition_broadcast(128))
    m0col_t = sb([64, 1], F32, 'm0col')
    S.dma('sp', lambda e: e.dma_start(out=m0col_t[0:8, :], in_=m0.ap()[0:8].rearrange('(p o) -> p o', o=1)), CR, writes=[CR])
    S.dma('sp', lambda e: e.dma_start(out=m0col_t[32:40, :], in_=m0.ap()[8:16].rearrange('(p o) -> p o', o=1)), CR, writes=[CR])
    sink_t = cload(sink, [128, 8], src=sink.ap().partition_broadcast(128))
    ropeC_t = sb([128, 1024], BF16, 'ropeC')
    ropeS_t = sb([128, 1024], BF16, 'ropeS')
    S.dma('pool', lambda e: e.dma_start(out=ropeC_t[:], in_=cd['ropeC'].ap()), CR, writes=[CR])
    S.dma('pool', lambda e: e.dma_start(out=ropeS_t[:], in_=cd['ropeS'].ap()), CR, writes=[CR])
    maskF_t = cload(cd['maskF'], [128, 128])
    maskB_t = cload(cd['maskB'], [128, 128])
    mask01F_t = cload(cd['mask01F'], [128, 128])
    mask01B_t = cload(cd['mask01B'], [128, 128])
    psw_f = cload(cd['psw'], [128, 128])
    oh_t = cload(cd['oh'], [64, 16])
    cmask_t = cload(cd['cmask'], [128, 64])
    psw_b = sb([128, 128], BF16, 'psw_b')
    S.op('pool', lambda e: e.tensor_copy(psw_b[:], psw_f[:]), reads=[CR], writes=[KR])
    esink_t = sb([128, 8], F32, 'esink')
    S.op('act', lambda e: e.activation(esink_t[:], sink_t[:], AF.Exp), reads=[CR], writes=[KR])
    ngbias_t = sb([64, 1], F32, 'ngbias')
    S.op('pool', lambda e: e.tensor_scalar(ngbias_t[:], gbias_t[:, 1:2], -1.0, None, ALU.mult), reads=[CR], writes=[KR])
    CK = [CR, KR]

    scond = sb([128, 8, 2], BF16, 'scond')
    S.op('act', lambda e: e.activation(scond[:], cond_t[:], AF.Silu), reads=[CR], writes=[KR])
    mod_t = [sb([128, 72, 2], F32, 'mod%d' % l) for l in range(2)]
    modT_R = [[Res('modT') for n in range(3)] for l in range(2)]
    modD_R = [[Res('modD') for n in range(3)] for l in range(2)]
    modA = [sb([128, 3, 8, 2], F32, 'modA%d' % l) for l in range(2)]
    modG = [sb([128, 3, 8, 2], F32, 'modG%d' % l) for l in range(2)]

    def adaln_slabs(l, s0, s1):
        for s in range(s0, s1):
            W = wslab_in(Wd[l]['ada_w'], 512 * s, 512)
            pb = ps('x')
            for c in range(4):
                for k in range(8):
                    S.op('pe', lambda e, c=c, k=k, W=W, pb=pb: e.matmul(pb.a[:, 2 * c:2 * c + 2], W.a[:, k, c * 128:(c + 1) * 128],
                                                                      scond[:, k, :], start=(k == 0), stop=(k == 7)),
                         reads=[W.r, KR], writes=[pb.r])
            dstv = mod_t[l][:, 4 * s:4 * s + 4, :]
            bv = adab_t[l][:, 4 * s:4 * s + 4].unsqueeze(2).to_broadcast([128, 4, 2])
            S.op('dve', lambda e, pb=pb, dstv=dstv, bv=bv: e.tensor_tensor(dstv, pb.a[:, 0:8].rearrange('p (c j) -> p c j', c=4), bv, ALU.add),
                 reads=[pb.r, CR], writes=[modT_R[l][s // 6]])

    def adaln_finish(l, ns=(0, 1, 2)):
        for n in ns:
            sc = mod_t[l][:, (3 * n + 1) * 8:(3 * n + 2) * 8, :]
            g = mod_t[l][:, (3 * n + 2) * 8:(3 * n + 3) * 8, :]
            gn = norm_t[l][:, n, :].unsqueeze(2).to_broadcast([128, 8, 2])
            S.op('dve', lambda e, n=n, sc=sc, gn=gn: e.scalar_tensor_tensor(modA[l][:, n, :, :], sc, 1.0, gn, ALU.add, ALU.mult),
                 reads=[modT_R[l][n], CR], writes=[modD_R[l][n]])
            fac = 1.0 if n == 1 else 0.5
            S.op('dve', lambda e, n=n, g=g, fac=fac: e.tensor_scalar(modG[l][:, n, :, :], g, fac, None, ALU.mult),
                 reads=[modT_R[l][n], modD_R[l][n]], writes=[modD_R[l][n]])

    def modB(l, n, m, cnd):
        return mod_t[l][:, (3 * n) * 8 + m, cnd:cnd + 1]

    def tg_cond(tg):
        return 0 if tg < 2 else 1

    def rms_stats(tg, tmp_sq, rs):
        pb = ps('x')
        for m in range(8):
            sq = tmp_sq[m % 2]
            S.op('act', lambda e, m=m, sq=sq: e.activation(sq.a, xap(m, tg), AF.Square), reads=[xres[m][tg]], writes=[sq.r])
            S.op('pe', lambda e, m=m, sq=sq, pb=pb: e.matmul(pb.a, ones_b[:], sq.a, start=(m == 0), stop=(m == 7)),
                 reads=[sq.r, KR], writes=[pb.r])
        S.op('act', lambda e, pb=pb: e.activation(rs.a, pb.a, AF.Sqrt, bias=EPS, scale=1.0 / D), reads=[pb.r], writes=[rs.r])
        S.op('dve', lambda e: e.reciprocal(rs.a, rs.a), reads=[rs.r], writes=[rs.r])

    def norm_mod(l, n, tg, hdst, hcol0, tmp_sq, rs, tmpf):
        cnd = tg_cond(tg)
        rms_stats(tg, tmp_sq, rs)
        for m in range(8):
            tf = tmpf[m % 2]
            S.op('dve', lambda e, m=m, tf=tf: e.scalar_tensor_tensor(tf.a, xap(m, tg), modA[l][:, n, m, cnd:cnd + 1], rs.a, ALU.mult, ALU.mult),
                 reads=[xres[m][tg], rs.r, modD_R[l][n]], writes=[tf.r])
            S.op('act', lambda e, m=m, tf=tf: e.activation(hdst.a[:, m, hcol0:hcol0 + 512], tf.a, AF.Identity, bias=modB(l, n, m, cnd), scale=1.0),
                 reads=[tf.r, modT_R[l][n]], writes=[hdst.r])

    def load_x():
        ar_reset()
        stg = [carve([128, 1024], F32, 'xstg%d' % i) for i in range(4)]
        for tg in range(4):
            for j in range(4):
                t = tg * 4 + j
                S.dma('sp', lambda e, j=j, t=t: e.dma_start(out=stg[j].a, in_=xin.ap()[t * 128:(t + 1) * 128, :]), stg[j].r, writes=[stg[j].r])
            for m in range(8):
                pb = ps('a' if m % 2 == 0 else 'b')
                for j in range(4):
                    S.op('pe', lambda e, j=j, m=m, pb=pb: e.transpose(pb.a[:, j * 128:(j + 1) * 128], stg[j].a[:, m * 128:(m + 1) * 128], ident_f[:]),
                         reads=[stg[j].r, KR], writes=[pb.r])
                if m % 2 == 0:
                    S.op('dve', lambda e, m=m, pb=pb: e.tensor_copy(xap(m, tg), pb.a), reads=[pb.r], writes=[xres[m][tg]])
                else:
                    S.op('act', lambda e, m=m, pb=pb: e.copy(xap(m, tg), pb.a), reads=[pb.r], writes=[xres[m][tg]])

    def ffn(l, which, between=None):
        n = 0 if which == 1 else 2
        ar_reset()
        h = carve([128, 8, NTOK], BF16, 'h')
        hres = [ar_res('h%d' % tg) for tg in range(4)]
        hid = carve([128, 4, NTOK], BF16, 'hid')
        hidres = [[ar_res('hid') for tg in range(4)] for c in range(4)]
        tmp_sq = [carve([128, 512], BF16, 'sq') for _ in range(2)]
        rs = carve([128, 512], F32, 'rs')
        tmpf = [carve([128, 512], F32, 'tmpf') for _ in range(2)]
        sgt = [carve([128, 512], F32, 'sg') for _ in range(2)]
        for tg in range(4):
            norm_mod(l, n, tg, Tl(h.a, hres[tg]), tg * 512, tmp_sq, rs, tmpf)
        win = Wd[l]['f%din' % which]
        wout = Wd[l]['f%dout' % which]
        for i in range(6):
            nch = 4 if i < 5 else 2
            G = wslab_in(win, 512 * i, 128 * nch)
            U = wslab_in(win, DFF + 512 * i, 128 * nch)
            O = wslab_out(wout, 512 * i, nch)
            for tg in range(4):
                for c in range(nch):
                    pg = ps('a')
                    pu = ps('b')
                    for k in range(8):
                        S.op('pe', lambda e, k=k, c=c, tg=tg, pg=pg, G=G: e.matmul(pg.a, G.a[:, k, c * 128:(c + 1) * 128], h.a[:, k, tg * 512:(tg + 1) * 512],
                                                                                   start=(k == 0), stop=(k == 7)),
                             reads=[G.r, hres[tg]], writes=[pg.r])
                    for k in range(8):
                        S.op('pe', lambda e, k=k, c=c, tg=tg, pu=pu, U=U: e.matmul(pu.a, U.a[:, k, c * 128:(c + 1) * 128], h.a[:, k, tg * 512:(tg + 1) * 512],
                                                                                   start=(k == 0), stop=(k == 7)),
                             reads=[U.r, hres[tg]], writes=[pu.r])
                    sg = sgt[c % 2]
                    S.op('act', lambda e, pg=pg, sg=sg: e.activation(sg.a, pg.a, AF.Silu), reads=[pg.r], writes=[sg.r])
                    S.op('dve', lambda e, c=c, tg=tg, pu=pu, sg=sg: e.tensor_tensor(hid.a[:, c, tg * 512:(tg + 1) * 512], sg.a, pu.a, ALU.mult),
                         reads=[pu.r, sg.r], writes=[hidres[c][tg]])
            for tg in range(4):
                cnd = tg_cond(tg)
                for m in range(8):
                    py = ps('c')
                    for c in range(nch):
                        S.op('pe', lambda e, c=c, m=m, tg=tg, py=py, O=O, nch=nch: e.matmul(py.a, O.a[:, c, m * 128:(m + 1) * 128], hid.a[:, c, tg * 512:(tg + 1) * 512],
                                                                                            start=(c == 0), stop=(c == nch - 1)),
                             reads=[O.r, hidres[c][tg]], writes=[py.r])
                    S.op('dve', lambda e, m=m, tg=tg, py=py, cnd=cnd: e.scalar_tensor_tensor(xap(m, tg), py.a, modG[l][:, n, m, cnd:cnd + 1], xap(m, tg), ALU.mult, ALU.add),
                         reads=[py.r, modD_R[l][n], xres[m][tg]], writes=[xres[m][tg]])
            if between is not None:
                between(i)

    def final_out():
        ar_reset()
        tmp_sq = [carve([128, 512], BF16, 'sq') for _ in range(2)]
        rs = carve([128, 512], F32, 'rs')
        yT = carve([128, 8, 512], F32, 'yT')
        ystg = [carve([128, 1024], F32, 'ystg') for _ in range(2)]
        for tg in range(4):
            rms_stats(tg, tmp_sq, rs)
            for m in range(8):
                S.op('dve', lambda e, m=m: e.scalar_tensor_tensor(yT.a[:, m, :], xap(m, tg), gfin_t[:, m:m + 1], rs.a, ALU.mult, ALU.mult),
                     reads=[xres[m][tg], rs.r, CR], writes=[yT.r])
            for j in range(4):
                t = tg * 4 + j
                ys = ystg[j % 2]
                for half in range(2):
                    pb = ps('a' if half == 0 else 'b')
                    for mm in range(4):
                        m = half * 4 + mm
                        S.op('pe', lambda e, mm=mm, m=m, j=j, pb=pb: e.transpose(pb.a[:, mm * 128:(mm + 1) * 128], yT.a[:, m, j * 128:(j + 1) * 128], ident_f[:]),
                             reads=[yT.r, KR], writes=[pb.r])
                    if half == 0:
                        S.op('act', lambda e, pb=pb, ys=ys: e.copy(ys.a[:, 0:512], pb.a), reads=[pb.r], writes=[ys.r])
                    else:
                        S.op('dve', lambda e, pb=pb, ys=ys: e.tensor_copy(ys.a[:, 512:1024], pb.a), reads=[pb.r], writes=[ys.r])
                dst = (o_yp if t < 8 else o_ys).ap()[(t % 8) * 128:(t % 8 + 1) * 128, :]
                S.dma('sp', lambda e, ys=ys, dst=dst: e.dma_start(out=dst, in_=ys.a), ys.r, reads=[ys.r])

    def proj_fm(W, c0, M, hg, ncols, post):
        for b in range(ncols // 512):
            pb = ps('a')
            for k in range(8):
                S.op('pe', lambda e, k=k, b=b, pb=pb: e.matmul(pb.a[0:M, :], W.a[:, k, c0:c0 + M], hg.a[:, k, b * 512:(b + 1) * 512],
                                                               start=(k == 0), stop=(k == 7)),
                     reads=[W.r, hg.r], writes=[pb.r])
            post(b, pb)

    def proj_tm(W, c0, n, hg, ntiles, post):
        for t in range(ntiles):
            pb = ps('b')
            for k in range(8):
                S.op('pe', lambda e, k=k, t=t, pb=pb: e.matmul(pb.a[:, 0:n], hg.a[:, k, t * 128:(t + 1) * 128], W.a[:, k, c0:c0 + n],
                                                               start=(k == 0), stop=(k == 7)),
                     reads=[W.r, hg.r], writes=[pb.r])
            post(t, pb)

    class Work:
        pass

    LA = 3

    def _emit_pv(wk, blk, kd, vq, c0, PT, b0, last, qts, finalize):
        if blk['po'] is None:
            blk['po'] = ps('c')
        po = blk['po']
        ns = kd['ns']
        pb = kd.get('pbase', 0)
        for i in vq:
            oc = (i - b0) * 128
            first = blk['first']
            S.op('pe', lambda e: e.matmul(po.a[:, oc:oc + 65], PT.a[pb:pb + ns, i * 128 - c0:i * 128 - c0 + 128], kd['vaug'],
                                          start=first, stop=True, skip_group_check=True),
                 reads=[PT.r] + list(kd['res']), writes=[po.r])
            blk['first'] = False
        if last:
            if hasattr(finalize, 'blk'):
                finalize.blk(qts, po.a.rearrange('p (q c) -> p q c', q=4), po.r)
            else:
                for i in qts:
                    oc = (i - b0) * 128
                    finalize(i, po.a[:, oc:oc + 65], po.r)

    def attn_flush(wk):
        while wk.pipe:
            wk.pipe.pop(0)()

    def attn_job(wk, nq, qT, qres, ktiles, valid, prob, finalize, after_block=None):
        import functools
        nqt = nq // 128
        for b0 in range(0, nqt, 4):
            qts = list(range(b0, min(b0 + 4, nqt)))
            blk = {'po': None, 'first': True}
            steps = []
            for kt, kd in enumerate(ktiles):
                vq = [i for i in qts if valid(kt, i)]
                if vq:
                    steps.append((kt, kd, vq))
            for si, (kt, kd, vq) in enumerate(steps):
                lo, hi = min(vq), max(vq) + 1
                c0, n = lo * 128, (hi - lo) * 128
                ns = kd['ns']
                PT = wk.PT[wk.ptc % len(wk.PT)]
                wk.ptc += 1
                if kd.get('noscore'):
                    prob(kt, kd, c0, n, None, PT)
                else:
                    pscore = ps('s')
                    S.op('pe', lambda e: e.matmul(pscore.a[0:ns, 0:n], kd['kT'], qT(c0, n), start=True, stop=True),
                         reads=list(kd['res']) + list(qres), writes=[pscore.r])
                    prob(kt, kd, c0, n, pscore, PT)
                wk.pipe.append(functools.partial(_emit_pv, wk, blk, kd, vq, c0, PT, b0, si == len(steps) - 1, qts, finalize))
                while len(wk.pipe) > LA:
                    wk.pipe.pop(0)()
            if after_block is not None:
                after_block(b0 * 128)

    def prob_exp(kt, kd, c0, n, pscore, PT):
        ns = kd['ns']
        S.op('act', lambda e: e.activation(PT.a[0:ns, 0:n], pscore.a[0:ns, 0:n], AF.Exp, scale=0.125), reads=[pscore.r], writes=[PT.r])

    def mixer(l, grp):
        ar_reset()
        sample = grp == 2
        NT = 1024 if sample else 512
        tgs = [2, 3] if sample else [grp]
        ntile = NT // 128
        nseq = 1 if sample else 2
        L = 1024 if sample else 256
        tps = L // 128
        cnd = 1 if sample else 0
        tok0 = 1024 if sample else grp * 512
        hg = carve([128, 8, NT], BF16, 'hg')
        tmp_sq = [carve([128, 512], BF16, 'sq') for _ in range(2)]
        rs = carve([128, 512], F32, 'rs')
        tmpf = [carve([128, 512], F32, 'tmpf') for _ in range(2)]
        for i, tg in enumerate(tgs):
            norm_mod(l, 1, tg, hg, i * 512, tmp_sq, rs, tmpf)
        oT = Tl(hg.a, hg.r)
        wk = Work()
        wk.PT = [carve([128, 512], BF16, 'PT') for _ in range(4)]
        wk.ptc = 0
        wk.pipe = []
        opair = carve([128, ntile, 128], BF16, 'opair')
        sm = [carve([128, 8], F32, 'sm') for _ in range(8)]
        smc = [0]

        def smt():
            smc[0] += 1
            return sm[smc[0] % 8]
        stg_f = [carve([128, 528], F32, 'stgf') for _ in range(2)]
        stc = [0]

        def stg():
            stc[0] += 1
            return stg_f[stc[0] % 2]

        def seq_tile(s, j):
            return s * tps + j

        def fin_softmax(e_h, extra_den=None):
            ecol = e_h * 64

            class F:
                tile0 = 0

                def bind(self, tile0):
                    self.tile0 = tile0
                    return self

                def blk(self, qts, pv, por):
                    i0, nq = qts[0], len(qts)
                    t = smt()
                    den = pv[:, 0:nq, 64]
                    if extra_den is None:
                        S.op('dve', lambda e: e.reciprocal(t.a[:, 0:nq], den), reads=[por], writes=[t.r])
                    else:
                        S.op('dve', lambda e: e.tensor_scalar(t.a[:, 0:nq], den, extra_den, None, ALU.add), reads=[por, KR], writes=[t.r])
                        S.op('dve', lambda e: e.reciprocal(t.a[:, 0:nq], t.a[:, 0:nq]), reads=[t.r], writes=[t.r])
                    S.op('dve', lambda e: e.tensor_tensor(opair.a[:, self.tile0 + i0:self.tile0 + i0 + nq, ecol:ecol + 64], pv[:, 0:nq, 0:64],
                                                          t.a[:, 0:nq].unsqueeze(2).to_broadcast([128, nq, 64]), ALU.mult),
                         reads=[por, t.r], writes=[opair.r])
            return F()

        def flush_pair(chunk):
            attn_flush(wk)
            for b in range(0, ntile, 4):
                pb = ps('x')
                pbv = pb.a.bitcast(BF16)
                nn = min(4, ntile - b)
                for j in range(nn):
                    S.op('pe', lambda e, j=j, b=b, pbv=pbv: e.transpose(pbv[:, j * 128:(j + 1) * 128], opair.a[:, b + j, :], ident_b[:]),
                         reads=[opair.r, KR], writes=[pb.r])
                S.op('act', lambda e, b=b, nn=nn, pbv=pbv: e.copy(oT.a[:, chunk, b * 128:(b + nn) * 128], pbv[:, 0:nn * 128]), reads=[pb.r], writes=[oT.r])

        def qknorm_fm(pb, gcol, dst_ap, dst_res, rope_cols=None):
            sq = tmp_sq[0]
            S.op('act', lambda e: e.activation(sq.a, pb.a, AF.Square), reads=[pb.r], writes=[sq.r])
            p2 = ps('x')
            S.op('pe', lambda e: e.matmul(p2.a, bd_b[:], sq.a, start=True, stop=True), reads=[sq.r, KR], writes=[p2.r])
            S.op('act', lambda e: e.activation(rs.a, p2.a, AF.Ln, bias=EPS, scale=1.0 / 64), reads=[p2.r], writes=[rs.r])
            S.op('act', lambda e: e.activation(rs.a, rs.a, AF.Exp, scale=-0.5), reads=[rs.r], writes=[rs.r])
            if rope_cols is None:
                S.op('dve', lambda e: e.scalar_tensor_tensor(dst_ap, pb.a, qkg_t[:, gcol:gcol + 1], rs.a, ALU.mult, ALU.mult),
                     reads=[pb.r, rs.r, CR], writes=[dst_res])
            else:
                qn = tmp_sq[1]
                S.op('dve', lambda e: e.scalar_tensor_tensor(qn.a, pb.a, qkg_t[:, gcol:gcol + 1], rs.a, ALU.mult, ALU.mult),
                     reads=[pb.r, rs.r, CR], writes=[qn.r])
                rope_fm(qn.a, qn.r, dst_ap, dst_res, rope_cols)

        def rope_fm(src_ap, src_res, dst_ap, dst_res, c0):
            p3 = ps('x')
            S.op('pe', lambda e: e.matmul(p3.a, psw_b[:], src_ap, start=True, stop=True), reads=[src_res, KR], writes=[p3.r])
            t1, t2 = tmpf[0], tmpf[1]
            S.op('dve', lambda e: e.tensor_tensor(t1.a, p3.a, ropeS_t[:, c0:c0 + 512], ALU.mult), reads=[p3.r, CR], writes=[t1.r])
            S.op('pool', lambda e: e.tensor_tensor(t2.a, src_ap, ropeC_t[:, c0:c0 + 512], ALU.mult), reads=[src_res, CR], writes=[t2.r])
            S.op('dve', lambda e: e.tensor_tensor(dst_ap, t1.a, t2.a, ALU.add), reads=[t1.r, t2.r], writes=[dst_res])

        def load_ctx_kT(dram, ncol_pairs, dst, dup):
            for j in range(2):
                s_ = stg()
                if dup:
                    srcv = bass.AP(dram, j * 128 * 128, [[128, 128], [64, 2], [0, 2], [1, 64]])
                    S.dma('sp', lambda e, s_=s_, srcv=srcv: e.dma_start(out=s_.a[:, 0:256].rearrange('p (a b c) -> p a b c', a=2, b=2), in_=srcv), s_.r, writes=[s_.r])
                else:
                    for q_ in range(ncol_pairs):
                        S.dma('sp', lambda e, s_=s_, j=j, q_=q_: e.dma_start(out=s_.a[:, q_ * 128:(q_ + 1) * 128], in_=dram.ap()[j * 128:(j + 1) * 128, q_ * 128:(q_ + 1) * 128]), s_.r, writes=[s_.r])
                pb = ps('x')
                for c in range(ncol_pairs):
                    S.op('pe', lambda e, c=c, s_=s_, pb=pb: e.transpose(pb.a[:, c * 128:(c + 1) * 128], s_.a[:, c * 128:(c + 1) * 128], ident_f[:]),
                         reads=[s_.r, KR], writes=[pb.r])
                for c in range(ncol_pairs):
                    S.op('act', lambda e, c=c, j=j, pb=pb: e.copy(dst.a[:, c, j * 128:(j + 1) * 128], pb.a[:, c * 128:(c + 1) * 128]), reads=[pb.r], writes=[dst.r])

        def load_ctx_v(dram, nh, dst, t0):
            for j in range(2):
                s_ = stg()
                for q_ in range(0, 64 * nh, 128):
                    S.dma('sp', lambda e, s_=s_, j=j, q_=q_: e.dma_start(out=s_.a[:, q_:q_ + 128], in_=dram.ap()[j * 128:(j + 1) * 128, q_:q_ + 128]), s_.r, writes=[s_.r])
                S.op('act', lambda e, s_=s_, j=j: e.copy(dst.a[:, t0 + j, :, 0:64], s_.a[:, 0:64 * nh].rearrange('p (h d) -> p h d', h=nh)), reads=[s_.r], writes=[dst.r])

        def out_tm(pb, n, dram, seq, j, cast_dst=None):
            import os as _os2
            if ('nodma%d' % n) in _os2.environ.get('KSUB', '') and l == 1:
                return
            s_ = stg()
            if True:
                S.op('dve', lambda e: e.tensor_copy(s_.a[:, 0:n], pb.a[:, 0:n]), reads=[pb.r], writes=[s_.r])
            else:
                S.op('act', lambda e: e.copy(s_.a[:, 0:n], pb.a[:, 0:n]), reads=[pb.r], writes=[s_.r])
            import os as _os3
            oq = _os3.environ.get('KOUTQ', 'pool')
            if ('nostore%d' % n) in _os2.environ.get('KSUB', '') and l == 1:
                return
            S.dma(oq, lambda e: e.dma_start(out=dram.ap()[seq, j * 128:(j + 1) * 128, :], in_=s_.a[:, 0:n]), s_.r, reads=[s_.r])

        mi = Wd[l]['mix_in']
        if l == 0:
            nkt = (2 if sample else 0) + ntile
            QA = carve([128, 4, NT], BF16, 'QA')
            KA = carve([128, 2, 256 + NT if sample else NT], BF16, 'KA')
            VA = carve([128, nkt, 2, 65], BF16, 'VA')
            if sample:
                rsv = [ring.pop(), ring.pop()]
                QB = Tl(rsv[0].a.rearrange('p (c n) -> p c n', c=4), rsv[0].r)
                KB = Tl(rsv[1].a.rearrange('p (c n) -> p c n', c=4), rsv[1].r)
            else:
                QB = carve([128, 4, NT], BF16, 'QB')
                KB = carve([128, 4, NT], BF16, 'KB')
            VB = carve([128, ntile, 8, 65], BF16, 'VB')
            OB = carve([128, ntile, 512], BF16, 'OB')
            KBT = None if sample else carve([128, ntile, 512], BF16, 'KBT')
            LI = carve([64, NT], F32, 'LI')
            LF = carve([64, NT], F32, 'LF')
            BT = carve([64, NT], F32, 'BT')
            TOK = carve([128, ntile, 3, 16], F32, 'TOK')
            HF = carve([128, tps, 64], F32, 'HF')
            NBC = [carve([128, 512], F32, 'NBC') for _ in range(2)]
            EE01 = carve([128, 1024], F32, 'EE01')
            EE = [Tl(EE01.a[:, i * 512:(i + 1) * 512], ar_res('EE%d' % i)) for i in range(2)]
            ONES = Tl(EE01.a[0:64, 0:NT], EE01.r)
            EEall = [EE01.r, EE[0].r, EE[1].r]
            RM = Tl(tmpf[1].a[0:64], tmpf[1].r)
            koff = 256 if sample else 0
            S.op('pool', lambda e: e.memset(VA.a, 1.0), writes=[VA.r])
            S.op('pool', lambda e: e.memset(VB.a, 1.0), writes=[VB.r])
            S.op('pool', lambda e: e.memset(LI.a, 0.0), writes=[LI.r])
            S.op('pool', lambda e: e.memset(LF.a, 0.0), writes=[LF.r])
            S.op('pool', lambda e: e.memset(BT.a, 0.0), writes=[BT.r])
            S.op('pool', lambda e: e.memset(ONES.a, 1.0), writes=EEall)
            if sample:
                load_ctx_kT(kctx0, 2, KA, True)
                load_ctx_v(vctx0, 2, VA, 0)
                VV = carve([128, 2, 4, 65], BF16, 'VV')
                C0v = C0.ap().rearrange('a (h t) k v -> a t k h v', t=2)
                n0v = n0.ap().rearrange('a (h t) k -> a t k h', t=2)
                for half in range(2):
                    s_ = stg()
                    for dr in range(2):
                        S.dma('sp', lambda e, s_=s_, dr=dr, half=half: e.dma_start(
                            out=s_.a[half * 64:half * 64 + 64, dr * 256: dr * 256 + 256].rearrange('p (h v) -> p h v', h=4),
                            in_=C0v[dr, half]), s_.r, writes=[s_.r])
                        S.dma('sp', lambda e, s_=s_, dr=dr, half=half: e.dma_start(
                            out=s_.a[half * 64:half * 64 + 64, 512 + dr * 4:512 + dr * 4 + 4],
                            in_=n0v[dr, half], allow_slow_non_contiguous=True), s_.r, writes=[s_.r])
                    for dr in range(2):
                        S.op('act', lambda e, s_=s_, dr=dr, half=half: e.copy(
                            VV.a[half * 64:half * 64 + 64, dr, :, 0:64],
                            s_.a[half * 64:half * 64 + 64, dr * 256:dr * 256 + 256].rearrange('p (h v) -> p h v', h=4)),
                            reads=[s_.r], writes=[VV.r])
                        S.op('act', lambda e, s_=s_, dr=dr, half=half: e.copy(
                            VV.a[half * 64:half * 64 + 64, dr, :, 64:65],
                            s_.a[half * 64:half * 64 + 64, 512 + dr * 4:512 + dr * 4 + 4].unsqueeze(2)),
                            reads=[s_.r], writes=[VV.r])
            W = wslab_in(mi, 0, 512)
            for c in range(4):
                def post(b, pb, c=c):
                    qknorm_fm(pb, 0, QA.a[:, c, b * 512:(b + 1) * 512], QA.r, rope_cols=(b * 512 if sample else None))
                proj_fm(W, c * 128, 128, hg, NT, post)
            W = wslab_in(mi, 512, 512)
            for c in range(2):
                def post(b, pb, c=c):
                    qknorm_fm(pb, 1, KA.a[:, c, koff + b * 512:koff + (b + 1) * 512], KA.r, rope_cols=(b * 512 if sample else None))
                proj_fm(W, c * 128, 128, hg, NT, post)

            def post_gi(b, pb):
                S.op('act', lambda e: e.activation(LI.a[0:40, b * 512:(b + 1) * 512], pb.a[0:40, :], AF.Identity, bias=gbias_t[0:40, 0:1], scale=1.0),
                     reads=[pb.r, CR], writes=[LI.r])
            proj_fm(W, 256, 64, hg, NT, post_gi)

            def post_gf(b, pb):
                t1 = tmpf[0]
                S.op('act', lambda e: e.activation(t1.a[0:40, :], pb.a[0:40, :], AF.Exp, bias=ngbias_t[0:40, 0:1], scale=-1.0), reads=[pb.r, KR], writes=[t1.r])
                S.op('act', lambda e: e.activation(t1.a[0:40, :], t1.a[0:40, :], AF.Ln, bias=1.0, scale=1.0), reads=[t1.r], writes=[t1.r])
                S.op('dve', lambda e: e.tensor_scalar(LF.a[0:40, b * 512:(b + 1) * 512], t1.a[0:40, :], -1.0, None, ALU.mult), reads=[t1.r], writes=[LF.r])
            proj_fm(W, 320, 64, hg, NT, post_gf)

            def post_va(t, pb):
                kt = (2 if sample else 0) + t
                S.op('act', lambda e: e.copy(VA.a[:, kt, :, 0:64], pb.a[:, 0:128].rearrange('p (h d) -> p h d', h=2)), reads=[pb.r], writes=[VA.r])
                if not sample:
                    out_tm(pb, 128, o_v0, grp * 2 + t // tps, t % tps)
            proj_tm(W, 384, 128, hg, ntile, post_va)
            W = wslab_in(mi, 1024, 512)
            for c in range(4):
                def post(b, pb, c=c):
                    S.op('act', lambda e: e.copy(QB.a[:, c, b * 512:(b + 1) * 512], pb.a), reads=[pb.r], writes=[QB.r])
                proj_fm(W, c * 128, 128, hg, NT, post)
            W = wslab_in(mi, 1536, 512)
            for c in range(4):
                def post(b, pb, c=c):
                    S.op('dve', lambda e: e.tensor_copy(KB.a[:, c, b * 512:(b + 1) * 512], pb.a), reads=[pb.r], writes=[KB.r])
                proj_fm(W, c * 128, 128, hg, NT, post)
            if not sample:
                def post(t, pb):
                    S.op('act', lambda e: e.copy(KBT.a[:, t, :], pb.a), reads=[pb.r], writes=[KBT.r])
                proj_tm(W, 0, 512, hg, ntile, post)
            W = wslab_in(mi, 2048, 512)
            def post(t, pb):
                S.op('dve', lambda e: e.tensor_copy(VB.a[:, t, :, 0:64], pb.a.rearrange('p (h d) -> p h d', h=8)), reads=[pb.r], writes=[VB.r])
            proj_tm(W, 0, 512, hg, ntile, post)
            W = wslab_in(mi, 2560, 512)
            def post(t, pb):
                S.op('act', lambda e: e.activation(OB.a[:, t, :], pb.a, AF.Sigmoid), reads=[pb.r], writes=[OB.r])
            proj_tm(W, 0, 512, hg, ntile, post)
            if not sample:
                W = wslab_in(mi, 3072, 128)
                def post(t, pb):
                    t_ = smt()
                    s_ = stg()
                    for hh in range(2):
                        S.op('act', lambda e, hh=hh: e.activation(s_.a[:, 256 + hh * 64:256 + hh * 64 + 64], pb.a[:, hh * 64:(hh + 1) * 64], AF.Square,
                                                                  accum_out=t_.a[:, hh:hh + 1]), reads=[pb.r], writes=[s_.r, t_.r])
                    S.op('act', lambda e: e.activation(t_.a[:, 0:2], t_.a[:, 0:2], AF.Sqrt, bias=EPS, scale=1.0 / 64), reads=[t_.r], writes=[t_.r])
                    S.op('dve', lambda e: e.reciprocal(t_.a[:, 0:2], t_.a[:, 0:2]), reads=[t_.r], writes=[t_.r])
                    for hh in range(2):
                        S.op('dve', lambda e, hh=hh: e.scalar_tensor_tensor(s_.a[:, hh * 64:(hh + 1) * 64], pb.a[:, hh * 64:(hh + 1) * 64], t_.a[:, hh:hh + 1], gkbc_t[:],
                                                                            ALU.mult, ALU.mult), reads=[pb.r, t_.r, CR], writes=[s_.r])
                    S.dma('sp', lambda e: e.dma_start(out=o_k0.ap()[grp * 2 + t // tps, (t % tps) * 128:(t % tps + 1) * 128, :], in_=s_.a[:, 0:128]), s_.r, reads=[s_.r])
                proj_tm(W, 0, 128, hg, ntile, post)

            for c in range(4):
                for s in range(nseq):
                    q0 = s * L
                    kv = c // 2
                    for e_ in range(2):
                        pbs = 64 * e_
                        kts = []
                        if sample:
                            for j in range(2):
                                kts.append(dict(kT=KA.a[pbs:pbs + 64, kv, j * 128:(j + 1) * 128], ns=128, vaug=VA.a[:, j, kv, :], res=[KA.r, VA.r]))
                        for j in range(tps):
                            kts.append(dict(kT=KA.a[pbs:pbs + 64, kv, koff + q0 + j * 128:koff + q0 + (j + 1) * 128], ns=128,
                                            vaug=VA.a[:, (2 if sample else 0) + seq_tile(s, j), kv, :], res=[KA.r, VA.r]))
                        fs = fin_softmax(e_)
                        attn_job(wk, L, lambda c0, n, c=c, pbs=pbs, q0=q0: QA.a[pbs:pbs + 64, c, q0 + c0:q0 + c0 + n], [QA.r], kts,
                                 lambda kt, i: True, prob_exp,
                                 fs.bind(s * tps))
                    if s == nseq - 1:
                        flush_pair(c)

            def revap(a, c0, n):
                return bass.AP(a.tensor, a[:, c0 + n - 1:c0 + n].offset, [list(a.ap[0]), [-1, n]])
            for s in range(nseq):
                c0 = s * L
                S.op('dve', lambda e, c0=c0: e.tensor_tensor_scan(BT.a[0:8, c0:c0 + L], ONES.a[0:8, c0:c0 + L], LF.a[0:8, c0:c0 + L], 0.0, ALU.mult, ALU.add),
                     reads=[LF.r] + EEall, writes=[BT.r])
                S.op('dve', lambda e, c0=c0: e.tensor_tensor_scan(revap(BT.a[32:40], c0, L), ONES.a[32:40, c0:c0 + L], revap(LF.a[32:40], c0, L), 0.0, ALU.mult, ALU.add),
                     reads=[LF.r] + EEall, writes=[BT.r])
            S.op('dve', lambda e: e.tensor_tensor(LI.a[0:40, :], LI.a[0:40, :], BT.a[0:40, :], ALU.subtract), reads=[LI.r, BT.r], writes=[LI.r])
            for s in range(nseq):
                c0 = s * L
                ini_f = m0col_t[0:8, :] if sample else 0.0
                ini_b = m0col_t[32:40, :] if sample else 0.0
                S.op('dve', lambda e, c0=c0, ini_f=ini_f: e.tensor_tensor_scan(LF.a[0:8, c0:c0 + L], ONES.a[0:8, c0:c0 + L], LI.a[0:8, c0:c0 + L], ini_f, ALU.mult, ALU.max),
                     reads=[LI.r, CR] + EEall, writes=[LF.r])
                S.op('dve', lambda e, c0=c0, ini_b=ini_b: e.tensor_tensor_scan(revap(LF.a[32:40], c0, L), ONES.a[32:40, c0:c0 + L], revap(LI.a[32:40], c0, L), ini_b, ALU.mult, ALU.max),
                     reads=[LI.r, CR] + EEall, writes=[LF.r])
            S.op('dve', lambda e: e.tensor_scalar(LF.a[0:40, :], LF.a[0:40, :], -1.0, None, ALU.mult), reads=[LF.r], writes=[LF.r])
            S.op('dve', lambda e: e.tensor_tensor(BT.a[0:40, :], LF.a[0:40, :], BT.a[0:40, :], ALU.subtract), reads=[LF.r, BT.r], writes=[BT.r])
            if not sample:
                for s in range(nseq):
                    c0 = s * L
                    sg_ = grp * 2 + s
                    t_ = smt()
                    S.op('dve', lambda e, c0=c0, t_=t_: e.tensor_scalar(t_.a[0:8, 0:1], BT.a[0:8, c0 + L - 1:c0 + L], -1.0, None, ALU.mult), reads=[BT.r], writes=[t_.r])
                    S.op('dve', lambda e, c0=c0, t_=t_: e.tensor_scalar(t_.a[32:40, 0:1], BT.a[32:40, c0:c0 + 1], -1.0, None, ALU.mult), reads=[BT.r], writes=[t_.r])
                    S.dma('sp', lambda e, t_=t_, sg_=sg_: e.dma_start(out=o_m.ap()[sg_, 0, :].rearrange('(p o) -> p o', o=1), in_=t_.a[0:8, 0:1], allow_slow_non_contiguous=True), t_.r, reads=[t_.r])
                    S.dma('sp', lambda e, t_=t_, sg_=sg_: e.dma_start(out=o_m.ap()[sg_, 1, :].rearrange('(p o) -> p o', o=1), in_=t_.a[32:40, 0:1], allow_slow_non_contiguous=True), t_.r, reads=[t_.r])
            S.op('act', lambda e: e.activation(BT.a[0:40, :], BT.a[0:40, :], AF.Exp), reads=[BT.r], writes=[BT.r])
            WF = None
            if not sample:
                WF = carve([64, NT], F32, 'WF')
                S.op('pool', lambda e: e.memset(WF.a, 0.0), writes=[WF.r])
                for s in range(nseq):
                    c0 = s * L
                    S.op('act', lambda e, c0=c0: e.activation(WF.a[0:8, c0:c0 + L], LI.a[0:8, c0:c0 + L], AF.Exp, bias=LF.a[0:8, c0 + L - 1:c0 + L], scale=1.0),
                         reads=[LI.r, LF.r], writes=[WF.r])
                    S.op('act', lambda e, c0=c0: e.activation(WF.a[32:40, c0:c0 + L], LI.a[32:40, c0:c0 + L], AF.Exp, bias=LF.a[32:40, c0:c0 + 1], scale=1.0),
                         reads=[LI.r, LF.r], writes=[WF.r])
            for t in range(ntile):
                pb = ps('x')
                srcs = [LI, BT] + ([WF] if WF is not None else [])
                for qi, src in enumerate(srcs):
                    S.op('pe', lambda e, qi=qi, src=src, t=t, pb=pb: e.transpose(pb.a[:, qi * 64:qi * 64 + 40], src.a[0:40, t * 128:(t + 1) * 128], ident_f[0:40, 0:40]),
                         reads=[src.r, KR], writes=[pb.r])
                nq_ = len(srcs)
                S.op('dve', lambda e, t=t, pb=pb, nq_=nq_: e.tensor_copy(TOK.a[:, t, 0:nq_, :].rearrange('p q (a h) -> p q a h', a=2),
                                                                        pb.a[:, 0:nq_ * 64].rearrange('p (q a h) -> p q a h', q=nq_, a=2)[:, :, :, 0:8]),
                     reads=[pb.r], writes=[TOK.r])

            S.op('dve', lambda e: e.tensor_scalar(TOK.a[:, :, 0, :], TOK.a[:, :, 0, :], float(np.log(0.125)), None, ALU.add), reads=[TOK.r], writes=[TOK.r])
            def _mk_mjob(c, s, e_, dr, jidx):
                    q0 = s * L
                    hd_ = 2 * c + e_
                    pbs = 64 * e_
                    hd = dr * 8 + hd_
                    row = dr * 32 + hd_
                    mask_t = maskF_t if dr == 0 else maskB_t
                    nbcs = {}

                    def pro(b0):
                        nb = min(512, L - b0)
                        S.op('dve', lambda e, b0=b0, nb=nb, hd=hd: e.tensor_scalar(RM.a[0:40, 0:nb], LF.a[0:40, q0 + b0:q0 + b0 + nb], oh_t[0:40, hd:hd + 1], None, ALU.mult),
                             reads=[LF.r, CR], writes=[RM.r])
                        pbc = ps('x')
                        S.op('pe', lambda e, nb=nb, pbc=pbc: e.matmul(pbc.a[:, 0:nb], ones_f[0:40, :], RM.a[0:40, 0:nb], start=True, stop=True),
                             reads=[RM.r, KR], writes=[pbc.r])
                        nbt = NBC[(b0 // 512) % 2] if sample else NBC[jidx % 2]
                        S.op('act', lambda e, nb=nb, pbc=pbc, nbt=nbt: e.copy(nbt.a[:, 0:nb], pbc.a[:, 0:nb]), reads=[pbc.r], writes=[nbt.r])
                        nbcs[b0] = nbt
                    kts = []
                    if sample:
                        kts.append(dict(noscore=True, virt=True, ns=64, pbase=pbs, vaug=VV.a[pbs:pbs + 64, dr, hd_ // 2, :], res=[VV.r]))
                    for j in range(tps):
                        kts.append(dict(kT=KB.a[pbs:pbs + 64, c, q0 + j * 128:q0 + (j + 1) * 128], ns=128, j=j,
                                        vaug=VB.a[:, seq_tile(s, j), hd_, :], res=[KB.r, VB.r]))

                    def valid(kt, i, dr=dr):
                        if sample:
                            if kt == 0:
                                return True
                            kt -= 1
                        return i >= kt if dr == 0 else i <= kt

                    def prob(kt, kd, c0, n, pscore, PT, dr=dr, hd=hd, pbs=pbs, c=c, q0=q0, nbcs=nbcs, mask_t=mask_t, s=s):
                        b0 = (c0 // 512) * 512
                        nbt = nbcs[b0]
                        lc = c0 - b0
                        ee = EE[wk.ptc % 2]
                        if kd.get('virt'):
                            S.op('act', lambda e: e.activation(ee.a[pbs:pbs + 64, 0:n], nbt.a[pbs:pbs + 64, lc:lc + n], AF.Exp, bias=m0bc_t[pbs:pbs + 64, hd:hd + 1], scale=1.0),
                                 reads=[nbt.r, CR], writes=[ee.r])
                            S.op('dve', lambda e: e.tensor_tensor(PT.a[pbs:pbs + 64, 0:n], ee.a[pbs:pbs + 64, 0:n], QB.a[pbs:pbs + 64, c, q0 + c0:q0 + c0 + n], ALU.mult),
                                 reads=[ee.r, QB.r], writes=[PT.r])
                            return
                        j = kd['j']
                        tl = seq_tile(s, j)
                        abias = TOK.a[:, tl, 0, hd:hd + 1]
                        dc = j * 128 - c0
                        m01 = mask01F_t if dr == 0 else mask01B_t
                        S.op('act', lambda e: e.activation(ee.a[:, 0:n], nbt.a[:, lc:lc + n], AF.Exp, bias=abias, scale=1.0), reads=[nbt.r, TOK.r], writes=[ee.r])
                        S.op('dve', lambda e: e.scalar_tensor_tensor(PT.a[:, 0:n], ee.a[:, 0:n], 0.125, pscore.a[:, 0:n], ALU.min, ALU.mult),
                             reads=[pscore.r, ee.r], writes=[PT.r])
                        if 0 <= dc < n:
                            S.op('dve', lambda e: e.tensor_tensor(PT.a[:, dc:dc + 128], PT.a[:, dc:dc + 128], m01[:], ALU.mult), reads=[PT.r, CR], writes=[PT.r])

                    def fin(i, po, por):
                        raise AssertionError('block finalize only')

                    def fin_blk(qts, pv, por, dr=dr, hd=hd, s=s, e_=e_, hd_=hd_):
                        i0, nq = qts[0], len(qts)
                        tl0 = seq_tile(s, i0)
                        t_ = smt()
                        den = pv[:, 0:nq, 64]
                        num = pv[:, 0:nq, 0:64]
                        S.op('dve', lambda e: e.tensor_tensor(t_.a[:, 0:nq], den, TOK.a[:, tl0:tl0 + nq, 1, hd], ALU.max), reads=[por, TOK.r], writes=[t_.r])
                        S.op('dve', lambda e: e.scalar_tensor_tensor(t_.a[:, 0:nq], den, -1.0, t_.a[:, 0:nq], ALU.mult, ALU.max), reads=[por, t_.r], writes=[t_.r])
                        S.op('dve', lambda e: e.reciprocal(t_.a[:, 0:nq], t_.a[:, 0:nq]), reads=[t_.r], writes=[t_.r])
                        rb = t_.a[:, 0:nq].unsqueeze(2).to_broadcast([128, nq, 64])
                        HFv = HF.a[:, i0:i0 + nq, :]
                        if dr == 0:
                            S.op('dve', lambda e: e.tensor_tensor(HFv, num, rb, ALU.mult), reads=[por, t_.r], writes=[HF.r])
                            return
                        sb_ = stg()
                        tv = sb_.a[:, 0:nq * 64].rearrange('p (t d) -> p t d', t=nq)
                        S.op('dve', lambda e: e.tensor_tensor(tv, num, rb, ALU.mult), reads=[por, t_.r], writes=[sb_.r])
                        S.op('dve', lambda e: e.tensor_tensor(HFv, HFv, tv, ALU.add), reads=[sb_.r, HF.r], writes=[HF.r])
                        if qts[-1] != tps - 1:
                            return
                        s_ = stg()
                        t8 = smt()
                        sv = s_.a[:, 0:tps * 64].rearrange('p (t d) -> p t d', t=tps)
                        S.op('dve', lambda e: e.tensor_tensor(sv, HF.a, HF.a, ALU.mult), reads=[HF.r], writes=[s_.r])
                        S.op('dve', lambda e: e.tensor_reduce(t8.a[:, 0:tps], sv, AX.X, ALU.add), reads=[s_.r], writes=[t8.r])
                        S.op('act', lambda e: e.activation(t8.a[:, 0:tps], t8.a[:, 0:tps], AF.Ln, bias=EPS, scale=1.0 / 64), reads=[t8.r], writes=[t8.r])
                        S.op('act', lambda e: e.activation(t8.a[:, 0:tps], t8.a[:, 0:tps], AF.Exp, scale=-0.5), reads=[t8.r], writes=[t8.r])
                        S.op('dve', lambda e: e.tensor_tensor(sv, HF.a, t8.a[:, 0:tps].unsqueeze(2).to_broadcast([128, tps, 64]), ALU.mult),
                             reads=[HF.r, t8.r], writes=[s_.r])
                        S.op('pool', lambda e: e.tensor_tensor(sv, sv, hgn_t[:, hd_ * 64:(hd_ + 1) * 64].unsqueeze(1).to_broadcast([128, tps, 64]), ALU.mult),
                             reads=[s_.r, CR], writes=[s_.r])
                        S.op('pool', lambda e: e.tensor_tensor(opair.a[:, s * tps:(s + 1) * tps, e_ * 64:(e_ + 1) * 64], sv,
                                                               OB.a[:, s * tps:(s + 1) * tps, hd_ * 64:(hd_ + 1) * 64], ALU.mult),
                             reads=[s_.r, OB.r], writes=[opair.r])


                    fin.blk = fin_blk

                    def run(after_block):
                        attn_job(wk, L, lambda c0, n: QB.a[pbs:pbs + 64, c, q0 + c0:q0 + c0 + n], [QB.r], kts, valid, prob, fin, after_block=after_block)
                    return dict(pro=pro, run=run)

            mjobs = []
            for c in range(4):
                for s in range(nseq):
                    for e_ in range(2):
                        for dr in range(2):
                            jb = _mk_mjob(c, s, e_, dr, len(mjobs))
                            jb['flush'] = (4 + c) if (s == nseq - 1 and e_ == 1 and dr == 1) else None
                            mjobs.append(jb)
            blocks0 = list(range(0, L, 512))
            for b0 in blocks0:
                mjobs[0]['pro'](b0)
            for k, jb in enumerate(mjobs):
                nxt = mjobs[k + 1] if k + 1 < len(mjobs) else None
                if sample:
                    jb['run'](lambda b0, nxt=nxt: nxt['pro'](b0) if nxt is not None else None)
                else:
                    if nxt is not None:
                        nxt['pro'](0)
                    jb['run'](None)
                if jb['flush'] is not None:
                    flush_pair(jb['flush'])

            if not sample:
                WVt = [carve([128, 8, 65], BF16, 'WV%d' % j) for j in range(tps)]
                for s in range(nseq):
                    sg_ = grp * 2 + s
                    for dr in range(2):
                        pcs = [ps('a'), ps('b')]
                        WVs = []
                        for j in range(tps):
                            tl = seq_tile(s, j)
                            wv = WVt[j]
                            wf = TOK.a[:, tl, 2, dr * 8:dr * 8 + 8].unsqueeze(2).to_broadcast([128, 8, 65])
                            S.op('dve', lambda e, wv=wv, tl=tl, wf=wf: e.tensor_tensor(wv.a, VB.a[:, tl, :, :], wf, ALU.mult), reads=[VB.r, TOK.r], writes=[wv.r])
                            WVs.append(wv)
                        for hh in range(8):
                            pc = pcs[hh // 4]
                            oc = (hh % 4) * 128
                            for j in range(tps):
                                tl = seq_tile(s, j)
                                S.op('pe', lambda e, hh=hh, j=j, tl=tl, pc=pc, oc=oc: e.matmul(pc.a[0:64, oc:oc + 65], KBT.a[:, tl, hh * 64:(hh + 1) * 64], WVs[j].a[:, hh, :],
                                                                                               start=(j == 0), stop=(j == tps - 1), skip_group_check=True),
                                     reads=[KBT.r, WVs[j].r], writes=[pc.r])
                        s_ = stg()
                        for half in range(2):
                            S.op('act', lambda e, half=half, s_=s_: e.activation(s_.a[0:64, half * 260:half * 260 + 260].rearrange('p (h v) -> p h v', h=4),
                                                                                 pcs[half].a[0:64, :].rearrange('p (h v) -> p h v', h=4)[:, :, 0:65], AF.Copy, scale=0.125),
                                 reads=[pcs[half].r], writes=[s_.r])
                        sv = s_.a[0:64, 0:520].rearrange('p (h v) -> p h v', h=8)
                        S.dma('sp', lambda e, sv=sv, sg_=sg_, dr=dr, s_=s_: e.dma_start(out=o_C.ap()[sg_, dr].rearrange('h k v -> k h v'), in_=sv[:, :, 0:64]), s_.r, reads=[s_.r])
                        S.dma('sp', lambda e, sv=sv, sg_=sg_, dr=dr, s_=s_: e.dma_start(out=o_n.ap()[sg_, dr].rearrange('h k -> k h'), in_=sv[:, :, 64], allow_slow_non_contiguous=True), s_.r, reads=[s_.r])
            if sample:
                ring.extend(rsv)
        else:
            nkt = (2 if sample else 0) + ntile
            koff = 256 if sample else 0
            QC = carve([128, 4, NT], BF16, 'QC')
            KC = carve([128, 4, koff + NT], BF16, 'KC')
            VC = carve([128, nkt, 8, 65], BF16, 'VC')
            QD = carve([128, 4, NT], BF16, 'QD')
            KD = carve([128, 2, koff + NT], BF16, 'KD')
            VD = carve([128, nkt, 2, 65], BF16, 'VD')
            S.op('pool', lambda e: e.memset(VC.a, 1.0), writes=[VC.r])
            S.op('pool', lambda e: e.memset(VD.a, 1.0), writes=[VD.r])
            if sample:
                load_ctx_kT(kcctx, 4, KC, False)
                load_ctx_v(vcctx, 8, VC, 0)
                load_ctx_kT(kdctx, 2, KD, True)
                load_ctx_v(vdctx, 2, VD, 0)
            W = wslab_in(mi, 0, 512)
            for c in range(4):
                def post(b, pb, c=c):
                    S.op('act', lambda e: e.copy(QC.a[:, c, b * 512:(b + 1) * 512], pb.a), reads=[pb.r], writes=[QC.r])
                proj_fm(W, c * 128, 128, hg, NT, post)
            W = wslab_in(mi, 512, 512)
            for c in range(4):
                def post(b, pb, c=c):
                    S.op('dve', lambda e: e.tensor_copy(KC.a[:, c, koff + b * 512:koff + (b + 1) * 512], pb.a), reads=[pb.r], writes=[KC.r])
                proj_fm(W, c * 128, 128, hg, NT, post)
            if not sample:
                def post(t, pb):
                    out_tm(pb, 512, o_kc, grp * 2 + t // tps, t % tps)
                proj_tm(W, 0, 512, hg, ntile, post)
            W = wslab_in(mi, 1024, 512)
            def post(t, pb):
                kt = (2 if sample else 0) + t
                S.op('dve', lambda e: e.tensor_copy(VC.a[:, kt, :, 0:64], pb.a.rearrange('p (h d) -> p h d', h=8)), reads=[pb.r], writes=[VC.r])
                if not sample:
                    out_tm(pb, 512, o_vc, grp * 2 + t // tps, t % tps)
            proj_tm(W, 0, 512, hg, ntile, post)
            W = wslab_in(mi, 1536, 512)
            for c in range(4):
                def post(b, pb, c=c):
                    if sample:
                        qn = tmp_sq[1]
                        S.op('act', lambda e: e.copy(qn.a, pb.a), reads=[pb.r], writes=[qn.r])
                        rope_fm(qn.a, qn.r, QD.a[:, c, b * 512:(b + 1) * 512], QD.r, b * 512)
                    else:
                        S.op('act', lambda e: e.copy(QD.a[:, c, b * 512:(b + 1) * 512], pb.a), reads=[pb.r], writes=[QD.r])
                proj_fm(W, c * 128, 128, hg, NT, post)
            W = wslab_in(mi, 2048, 512)
            for c in range(2):
                def post(b, pb, c=c):
                    if sample:
                        qn = tmp_sq[1]
                        S.op('act', lambda e: e.copy(qn.a, pb.a), reads=[pb.r], writes=[qn.r])
                        rope_fm(qn.a, qn.r, KD.a[:, c, koff + b * 512:koff + (b + 1) * 512], KD.r, b * 512)
                    else:
                        S.op('act', lambda e: e.copy(KD.a[:, c, b * 512:(b + 1) * 512], pb.a), reads=[pb.r], writes=[KD.r])
                proj_fm(W, c * 128, 128, hg, NT, post)
            if not sample:
                def post(t, pb):
                    out_tm(pb, 128, o_kd, grp * 2 + t // tps, t % tps)
                proj_tm(W, 256, 128, hg, ntile, post)
            def post(t, pb):
                kt = (2 if sample else 0) + t
                S.op('dve', lambda e: e.tensor_copy(VD.a[:, kt, :, 0:64], pb.a[:, 0:128].rearrange('p (h d) -> p h d', h=2)), reads=[pb.r], writes=[VD.r])
                if not sample:
                    out_tm(pb, 128, o_vd, grp * 2 + t // tps, t % tps)
            proj_tm(W, 384, 128, hg, ntile, post)

            import os as _os
            ksub = _os.environ.get('KSUB', '')
            if not sample:
                for c in range(4 if 'nomha' not in ksub else 0):
                    for s in range(nseq):
                        q0 = s * L
                        for e_ in range(2):
                            pbs = 64 * e_
                            hh = 2 * c + e_
                            kts = [dict(kT=KC.a[pbs:pbs + 64, c, q0 + j * 128:q0 + (j + 1) * 128], ns=128, vaug=VC.a[:, seq_tile(s, j), hh, :], res=[KC.r, VC.r])
                                   for j in range(tps)]
                            fs = fin_softmax(e_)
                            attn_job(wk, L, lambda c0, n, c=c, pbs=pbs, q0=q0: QC.a[pbs:pbs + 64, c, q0 + c0:q0 + c0 + n], [QC.r], kts,
                                     lambda kt, i: True, prob_exp, fs.bind(s * tps))
                        if s == nseq - 1:
                            flush_pair(c)
                for c in range(4 if 'nogqa' not in ksub else 0):
                    for s in range(nseq):
                        q0 = s * L
                        kv = c // 2
                        for e_ in range(2):
                            pbs = 64 * e_
                            hh = 2 * c + e_
                            kts = [dict(kT=KD.a[pbs:pbs + 64, kv, q0 + j * 128:q0 + (j + 1) * 128], ns=128, vaug=VD.a[:, seq_tile(s, j), kv, :], res=[KD.r, VD.r])
                                   for j in range(tps)]
                            fs = fin_softmax(e_, extra_den=esink_t[:, hh:hh + 1])
                            attn_job(wk, L, lambda c0, n, c=c, pbs=pbs, q0=q0: QD.a[pbs:pbs + 64, c, q0 + c0:q0 + c0 + n], [QD.r], kts,
                                     lambda kt, i: True, prob_exp, fs.bind(s * tps))
                        if s == nseq - 1:
                            flush_pair(4 + c)
            else:
                navalid = _na_rows()
                TB = [carve([128, 15, 64], F32, 'TB') for _ in range(2)]
                ARG = [carve([128, 512], F32, 'ARG') for _ in range(2)]
                for c in range(4 if 'nona' not in ksub else 0):
                    for e_ in range(2):
                        pbs = 64 * e_
                        hh = 2 * c + e_
                        tb = TB[hh % 2]
                        S.dma('sp', lambda e, tb=tb, hh=hh: e.dma_start(out=tb.a, in_=natb.ap()[hh].rearrange('p (a b) -> p a b', a=15)), tb.r, writes=[tb.r])
                        S.op('pool', lambda e, tb=tb: e.tensor_tensor(tb.a, tb.a, cmask_t[:].unsqueeze(1).to_broadcast([128, 15, 64]), ALU.add), reads=[tb.r, CR], writes=[tb.r])
                        kts = []
                        for j in range(2):
                            kts.append(dict(kT=KC.a[pbs:pbs + 64, c, j * 128:(j + 1) * 128], ns=128, vaug=VC.a[:, j, hh, :], res=[KC.r, VC.r], ctx=True))
                        for j in range(8):
                            kts.append(dict(kT=KC.a[pbs:pbs + 64, c, 256 + j * 128:256 + (j + 1) * 128], ns=128, vaug=VC.a[:, 2 + j, hh, :], res=[KC.r, VC.r], j=j))

                        def valid(kt, i):
                            if kt < 2:
                                return True
                            j = kt - 2
                            return any(navalid[2 * j + a][2 * i + b] for a in range(2) for b in range(2))

                        def prob(kt, kd, c0, n, pscore, PT, tb=tb):
                            if kd.get('ctx'):
                                return prob_exp(kt, kd, c0, n, pscore, PT)
                            j = kd['j']
                            S.op('pool', lambda e: e.memset(PT.a[:, 0:n], 0.0), writes=[PT.r])
                            arg = ARG[wk.ptc % 2]
                            r0 = c0 // 64
                            nr = n // 64
                            for a in range(2):
                                srow = 2 * j + a
                                rows = [r for r in range(r0, r0 + nr) if navalid[srow][r]]
                                if not rows:
                                    continue
                                rl, rh = min(rows), max(rows) + 1
                                cl, cn = (rl - r0) * 64, (rh - rl) * 64
                                dy0 = rl - srow + 7
                                pa = a * 64
                                S.op('dve', lambda e, pa=pa, cl=cl, cn=cn, dy0=dy0, rl=rl, rh=rh: e.scalar_tensor_tensor(
                                    arg.a[pa:pa + 64, cl:cl + cn], pscore.a[pa:pa + 64, cl:cl + cn], 0.125,
                                    tb.a[pa:pa + 64, dy0:dy0 + (rh - rl), :].rearrange('p a b -> p (a b)'), ALU.mult, ALU.add),
                                    reads=[pscore.r, tb.r], writes=[arg.r])
                                S.op('act', lambda e, pa=pa, cl=cl, cn=cn: e.activation(PT.a[pa:pa + 64, cl:cl + cn], arg.a[pa:pa + 64, cl:cl + cn], AF.Exp),
                                     reads=[arg.r], writes=[PT.r])
                        fs = fin_softmax(e_)
                        attn_job(wk, L, lambda c0, n, c=c, pbs=pbs: QC.a[pbs:pbs + 64, c, c0:c0 + n], [QC.r], kts, valid, prob,
                                 fs.bind(0))
                    flush_pair(c)
                for c in range(4 if 'noswa' not in ksub else 0):
                    kv = c // 2
                    for e_ in range(2):
                        pbs = 64 * e_
                        hh = 2 * c + e_
                        kts = []
                        for j in range(2):
                            kts.append(dict(kT=KD.a[pbs:pbs + 64, kv, j * 128:(j + 1) * 128], ns=128, vaug=VD.a[:, j, kv, :], res=[KD.r, VD.r], ctx=True))
                        for j in range(8):
                            kts.append(dict(kT=KD.a[pbs:pbs + 64, kv, 256 + j * 128:256 + (j + 1) * 128], ns=128, vaug=VD.a[:, 2 + j, kv, :], res=[KD.r, VD.r], j=j))

                        def valid(kt, i):
                            return True if kt < 2 else abs(i - (kt - 2)) <= 1

                        def prob(kt, kd, c0, n, pscore, PT):
                            if kd.get('ctx'):
                                return prob_exp(kt, kd, c0, n, pscore, PT)
                            j = kd['j']
                            arg = ARG[wk.ptc % 2]
                            for i in range(c0 // 128, (c0 + n) // 128):
                                lc = i * 128 - c0
                                if i == j:
                                    S.op('act', lambda e, lc=lc: e.activation(PT.a[:, lc:lc + 128], pscore.a[:, lc:lc + 128], AF.Exp, scale=0.125), reads=[pscore.r], writes=[PT.r])
                                else:
                                    mk_ = maskF_t if i == j - 1 else maskB_t
                                    S.op('dve', lambda e, lc=lc, mk_=mk_: e.scalar_tensor_tensor(arg.a[:, lc:lc + 128], pscore.a[:, lc:lc + 128], 0.125, mk_[:], ALU.mult, ALU.add),
                                         reads=[pscore.r, CR], writes=[arg.r])
                                    S.op('act', lambda e, lc=lc: e.activation(PT.a[:, lc:lc + 128], arg.a[:, lc:lc + 128], AF.Exp), reads=[arg.r], writes=[PT.r])
                        fs = fin_softmax(e_, extra_den=esink_t[:, hh:hh + 1])
                        attn_job(wk, L, lambda c0, n, c=c, pbs=pbs: QD.a[pbs:pbs + 64, c, c0:c0 + n], [QD.r], kts, valid, prob,
                                 fs.bind(0))
                    flush_pair(4 + c)

        mo = Wd[l]['mix_out']
        O1 = wslab_out(mo, 0, 4)
        O2 = wslab_out(mo, 512, 4)
        for i, tg in enumerate(tgs):
            for m in range(8):
                py = ps('a')
                for cc in range(8):
                    Ow = O1 if cc < 4 else O2
                    S.op('pe', lambda e, cc=cc, m=m, i=i, py=py, Ow=Ow: e.matmul(py.a, Ow.a[:, cc % 4, m * 128:(m + 1) * 128], oT.a[:, cc, i * 512:(i + 1) * 512],
                                                                                start=(cc == 0), stop=(cc == 7)),
                         reads=[Ow.r, oT.r], writes=[py.r])
                S.op('dve', lambda e, m=m, tg=tg, py=py: e.scalar_tensor_tensor(xap(m, tg), py.a, modG[l][:, 1, m, cnd:cnd + 1], xap(m, tg), ALU.mult, ALU.add),
                     reads=[py.r, modD_R[l][1], xres[m][tg]], writes=[xres[m][tg]])

    import os
    parts = os.environ.get('KPARTS', 'all')

    def on(p):
        return parts == 'all' or p in parts.split(',')
    adaln_slabs(0, 0, 6)
    load_x()
    adaln_finish(0, (0,))
    for l in range(2):
        if l == 0:
            if on('f01'):
                ffn(0, 1, between=lambda i: adaln_slabs(0, 6 + 2 * i, 8 + 2 * i))
            else:
                adaln_slabs(0, 6, 18)
            adaln_finish(0, (1, 2))
        else:
            if on('f11'):
                ffn(1, 1)
        for grp in range(3):
            if on('m%d%d' % (l, grp)):
                mixer(l, grp)
        if l == 0:
            if on('f02'):
                ffn(0, 2, between=lambda i: adaln_slabs(1, 3 * i, 3 * i + 3))
            else:
                adaln_slabs(1, 0, 18)
            adaln_finish(1)
        else:
            if on('f12'):
                ffn(1, 2)
    final_out()
    S.emit()


_PROG = {}


def _prep_weights(inp):
    sh = {}
    for l in range(2):
        sh['ada_w%d' % l] = np.ascontiguousarray(inp['ada_w_l%d' % l], dtype=np.float32)
        sh['ada_b%d' % l] = np.ascontiguousarray(inp['ada_b_l%d' % l].reshape(72, 128).T, dtype=np.float32)
        sh['norm%d' % l] = np.ascontiguousarray(inp['norm_l%d' % l].reshape(3, 8, 128).transpose(2, 0, 1), dtype=np.float32)
        for f in (1, 2):
            sh['f%din%d' % (f, l)] = np.ascontiguousarray(inp['ffn%d_in_l%d' % (f, l)], dtype=np.float32)
            sh['f%dout%d' % (f, l)] = np.ascontiguousarray(inp['ffn%d_out_l%d' % (f, l)], dtype=np.float32)
        sh['mixout%d' % l] = np.ascontiguousarray(inp['mix_out_l%d' % l], dtype=np.float32)
    w = np.asarray(inp['mix_in_l0'], dtype=np.float32)
    qa, ka, va = w[:, 0:512], w[:, 512:640], w[:, 640:768]
    qb, kb, vb = w[:, 768:1280], w[:, 1280:1792], w[:, 1792:2304]
    gt, ob = w[:, 2304:2336], w[:, 2336:2848]
    z = np.zeros((D, 24), np.float32)
    g1 = np.concatenate([gt[:, 0:8], z, gt[:, 16:24], z], 1)
    g2 = np.concatenate([gt[:, 8:16], z, gt[:, 24:32], z], 1)
    kadup = np.concatenate([ka[:, 0:64], ka[:, 0:64], ka[:, 64:128], ka[:, 64:128]], 1)
    m0_ = np.concatenate([qa, kadup, g1, g2, va, qb, kb, vb, ob, ka, np.zeros((D, L0_COLS - 3200), np.float32)], 1)
    assert m0_.shape[1] == L0_COLS
    sh['mixin0'] = np.ascontiguousarray(m0_)
    w = np.asarray(inp['mix_in_l1'], dtype=np.float32)
    qc, kc, vc, qd, kd, vd = w[:, 0:512], w[:, 512:1024], w[:, 1024:1536], w[:, 1536:2048], w[:, 2048:2176], w[:, 2176:2304]
    kddup = np.concatenate([kd[:, 0:64], kd[:, 0:64], kd[:, 64:128], kd[:, 64:128]], 1)
    m1_ = np.concatenate([qc, kc, vc, qd, kddup, kd, vd], 1)
    assert m1_.shape[1] == L1_COLS
    sh['mixin1'] = np.ascontiguousarray(m1_)
    sh['gfin'] = np.ascontiguousarray(np.asarray(inp['norm_final'], np.float32).reshape(8, 128).T)
    qk = np.asarray(inp['qk_norm_l0'], np.float32)
    sh['qkg'] = np.ascontiguousarray(np.stack([np.tile(qk[0], 2), np.tile(qk[1], 2)], 1))
    sh['gkbc'] = np.ascontiguousarray(qk[1])
    gb = np.asarray(inp['gate_bias_l0'], np.float32)
    gbt = np.zeros((64, 2), np.float32)
    gbt[0:8, 0] = gb[0:8]
    gbt[32:40, 0] = gb[16:24]
    gbt[0:8, 1] = gb[8:16]
    gbt[32:40, 1] = gb[24:32]
    sh['gbias'] = gbt
    sh['hgn'] = np.ascontiguousarray(inp['head_norm_l0'], dtype=np.float32)
    sh['sink'] = np.ascontiguousarray(inp['sink_l1'], dtype=np.float32)
    rpb = np.asarray(inp['rpb_l1'], np.float32)
    sc = np.arange(64)[:, None]
    qc_ = np.arange(64)[None, :]
    dx = np.clip(sc - qc_ + 15, 0, 30)
    tb = np.zeros((8, 128, 15, 64), np.float32)
    for dyi in range(15):
        blk = rpb[:, 14 - dyi, :][:, dx]
        tb[:, 0:64, dyi, :] = blk
        tb[:, 64:128, dyi, :] = blk
    sh['natb'] = np.ascontiguousarray(tb.reshape(8, 128, 15 * 64))
    for k, v in _consts().items():
        sh['c_' + k] = v
    return sh


def kernel(**inp):
    inp = {k: np.asarray(v) for k, v in inp.items()}
    dbg = inp.pop('_dbg', None)
    key = 'main'
    if key not in _PROG:
        _PROG[key] = build_program(None)
    nc = _PROG[key]
    sh = _prep_weights(inp)
    in_maps = []
    for i in range(8):
        b = i // 4
        m = dict(sh)
        xp = inp['x_prompt'][4 * i:4 * i + 4].reshape(1024, D)
        xs = inp['x_sample'][b]
        m['xin'] = np.ascontiguousarray(np.concatenate([xp, xs], 0), dtype=np.float32)
        cond = np.stack([inp['c_ctx'], inp['c'][b]], 0).astype(np.float32)
        m['condT'] = np.ascontiguousarray(cond.reshape(2, 8, 128).transpose(2, 1, 0))
        m['kctx0'] = np.ascontiguousarray(inp['cache_l0_attn_k'][b].reshape(256, 128), dtype=np.float32)
        m['vctx0'] = np.ascontiguousarray(inp['cache_l0_attn_v'][b].reshape(256, 128), dtype=np.float32)
        m['C0'] = np.ascontiguousarray(inp['state_l0_mlstm_C'][b], dtype=np.float32)
        m['n0'] = np.ascontiguousarray(inp['state_l0_mlstm_n'][b], dtype=np.float32)
        m['m0'] = np.ascontiguousarray(inp['state_l0_mlstm_m'][b].reshape(16), dtype=np.float32)
        m['kcctx'] = np.ascontiguousarray(inp['cache_l1_na_k'][b].reshape(256, 512), dtype=np.float32)
        m['vcctx'] = np.ascontiguousarray(inp['cache_l1_na_v'][b].reshape(256, 512), dtype=np.float32)
        m['kdctx'] = np.ascontiguousarray(inp['cache_l1_swa_k'][b].reshape(256, 128), dtype=np.float32)
        m['vdctx'] = np.ascontiguousarray(inp['cache_l1_swa_v'][b].reshape(256, 128), dtype=np.float32)
        in_maps.append(m)
    import os
    ncore = int(os.environ.get('KCORES', '8'))
    res = run_bass_kernel_spmd(nc, in_maps[:ncore], core_ids=list(range(ncore)))
    R = list(res.results)
    while len(R) < 8:
        R.append(R[0])
    cat = lambda k: np.concatenate([np.asarray(R[i][k]) for i in range(8)], 0)
    y_prompt = cat('o_yp').reshape(32, 256, D)
    y_sample = np.stack([np.asarray(R[0]['o_ys']), np.asarray(R[4]['o_ys'])], 0)
    k0 = cat('o_k0').reshape(32, 256, 2, 64)
    v0 = cat('o_v0').reshape(32, 256, 2, 64)
    Cst = cat('o_C')
    nst = cat('o_n')
    mst = cat('o_m')
    kc1 = cat('o_kc').reshape(32, 256, 8, 64)
    vc1 = cat('o_vc').reshape(32, 256, 8, 64)
    kd1 = cat('o_kd').reshape(32, 256, 2, 64)
    vd1 = cat('o_vd').reshape(32, 256, 2, 64)
    outs = (y_prompt, y_sample, k0, v0, Cst, nst, mst, kc1, vc1, kd1, vd1)
    return tuple(np.ascontiguousarray(o, dtype=np.float32) for o in outs)
```

```python
import contextlib
import numpy as np
import concourse.bass as bass
import concourse.mybir as mybir
from concourse.bass_utils import run_bass_kernel_spmd

F32 = mybir.dt.float32
BF16 = mybir.dt.bfloat16
AF = mybir.ActivationFunctionType
ALU = mybir.AluOpType
AX = mybir.AxisListType

ENGS = ('pe', 'act', 'dve', 'pool', 'sp')
NEG = -1.0e30
EPS = 1e-6


class Res:
    __slots__ = ('name', 'w', 'r', 'sem', 'dcount', 'excl')

    def __init__(self, name=''):
        self.name = name
        self.excl = False
        self.w = {}
        self.r = {}
        self.sem = None
        self.dcount = 0


class _Rec:
    def __init__(self):
        self.call = None

    def __getattr__(self, name):
        def f(*a, **k):
            self.call = (name, a, k)
            return self
        return f


class Ins:
    __slots__ = ('fn', 'waits', 'milestone', 'dma_res')

    def __init__(self, fn):
        rec = _Rec()
        fn(rec)
        name, a, k = rec.call
        self.fn = lambda eh: getattr(eh, name)(*a, **k)
        self.waits = []
        self.milestone = False
        self.dma_res = None


class Sched:
    def __init__(self, nc, stack):
        self.nc = nc
        self.stack = stack
        self.streams = {e: [] for e in ENGS}
        self.known = {e: {} for e in ENGS}
        self.dma_res = []
        self.semof = {}
        self.esem = {}
        for e in ('pe', 'act', 'dve', 'pool'):
            self.esem[e] = stack.enter_context(nc.semaphore('es_' + e))

    def _waits(self, ins, eng, deps):
        for key, val in deps.items():
            if self.known[eng].get(key, -1) >= val:
                continue
            self.known[eng][key] = val
            ins.waits.append((key, val))
            if key[0] == 'e':
                self.streams[key[1]][val].milestone = True

    def _deps(self, eng, reads, writes):
        deps = {}

        def add(d, raw):
            for k, v in d.items():
                if k[0] == 'e' and k[1] == eng and (eng == 'pe' or not raw):
                    continue
                if deps.get(k, -1) < v:
                    deps[k] = v
        for r in reads:
            add(r.w, True)
            if r.excl:
                add({k: v for k, v in r.r.items() if not (k[0] == 'e' and k[1] == eng)}, False)
        for r in writes:
            add(r.w, False)
            add(r.r, False)
        return deps

    def op(self, eng, fn, reads=(), writes=()):
        ins = Ins(fn)
        idx = len(self.streams[eng])
        self._waits(ins, eng, self._deps(eng, reads, writes))
        key = ('e', eng)
        for r in writes:
            r.w = {key: idx}
            r.r = {}
        for r in reads:
            if r not in writes:
                r.r[key] = idx
        self.streams[eng].append(ins)
        return ins

    def dma(self, eng, fn, sres, reads=(), writes=()):
        ins = Ins(fn)
        ins.dma_res = sres
        if sres.sem is None:
            sres.sem = self.stack.enter_context(self.nc.semaphore('ds_%d' % len(self.dma_res)))
            self.dma_res.append(sres)
            self.semof[id(sres)] = sres
        self._waits(ins, eng, self._deps(eng, reads, writes))
        sres.dcount += 1
        key = ('d', id(sres))
        val = 16 * sres.dcount
        for r in writes:
            r.w = {key: val}
            r.r = {}
        for r in reads:
            if r not in writes:
                r.r[key] = val
        self.streams[eng].append(ins)
        return ins

    def emit(self):
        nc = self.nc
        ordinal = {}
        for e in ('pe', 'act', 'dve', 'pool'):
            c = 0
            for i, ins in enumerate(self.streams[e]):
                if ins.milestone:
                    c += 1
                    ordinal[(e, i)] = c

        def run(eng, eh):
            for i, ins in enumerate(self.streams[eng]):
                for key, val in ins.waits:
                    if key[0] == 'e':
                        eh.wait_ge(self.esem[key[1]], ordinal[(key[1], val)])
                    else:
                        eh.wait_ge(self.semof[key[1]].sem, val)
                bi = ins.fn(eh)
                if ins.dma_res is not None:
                    bi.then_inc(ins.dma_res.sem, 16)
                elif ins.milestone:
                    bi.then_inc(self.esem[eng], 1)
            if eng == 'sp':
                for r in self.dma_res:
                    eh.wait_ge(r.sem, 16 * r.dcount)

        with nc.Block() as block:
            @block.tensor
            def _(e):
                run('pe', e)

            @block.scalar
            def _(e):
                run('act', e)

            @block.vector
            def _(e):
                run('dve', e)

            @block.gpsimd
            def _(e):
                run('pool', e)

            @block.sync
            def _(e):
                run('sp', e)


class Tl:
    __slots__ = ('a', 'r')

    def __init__(self, a, r):
        self.a = a
        self.r = r


D = 1024
DFF = 2816
NTOK = 2048
L0_COLS = 3328
L1_COLS = 2560


def _rope_tables():
    t = np.arange(1024)
    pos = np.stack([t // 64, t % 64], -1).astype(np.float32)
    freqs = (10000.0 ** (-np.arange(16, dtype=np.float32) / 16)).astype(np.float32)
    ang = (pos[:, :, None] * freqs).reshape(1024, 32).astype(np.float32)
    cos = np.cos(ang).astype(np.float32).T
    sin = np.sin(ang).astype(np.float32).T
    C = np.concatenate([cos, cos, cos, cos], 0)
    Sg = np.concatenate([-sin, sin, -sin, sin], 0)
    return np.ascontiguousarray(C), np.ascontiguousarray(Sg)


def _consts():
    c = {}
    C, Sg = _rope_tables()
    c['ropeC'] = C
    c['ropeS'] = Sg
    s = np.arange(128)[:, None]
    t = np.arange(128)[None, :]
    c['maskF'] = np.where(t >= s, 0.0, NEG).astype(np.float32)
    c['maskB'] = np.where(t <= s, 0.0, NEG).astype(np.float32)
    c['mask01F'] = (t >= s).astype(np.float32)
    c['mask01B'] = (t <= s).astype(np.float32)
    psw = np.zeros((128, 128), np.float32)
    for m in range(128):
        d = m % 64
        k = (m - d) + ((d + 32) % 64)
        psw[k, m] = 1.0
    c['psw'] = psw
    oh = np.zeros((64, 16), np.float32)
    for h in range(8):
        oh[h, h] = 1.0
        oh[32 + h, 8 + h] = 1.0
    c['oh'] = oh
    sc = np.arange(64)[:, None]
    qc = np.arange(64)[None, :]
    ws = np.clip(qc - 8, 0, 48)
    cm = np.where((sc >= ws) & (sc < ws + 16), 0.0, NEG).astype(np.float32)
    c['cmask'] = np.concatenate([cm, cm], 0)
    return c


def _na_rows():
    start = [min(max(r - 4, 0), 8) for r in range(16)]
    valid = [[start[r] <= s < start[r] + 8 for r in range(16)] for s in range(16)]
    return valid


def build_program(dbg=None):
    nc = bass.Bass("TRN2", target_bir_lowering=False)
    st = contextlib.ExitStack()
    with st:
        _build(nc, st, dbg)
    return nc


def _build(nc, st, dbg):
    S = Sched(nc, st)

    def din(name, shape):
        return nc.dram_tensor(name, list(shape), F32, kind="ExternalInput")

    def dout(name, shape):
        return nc.dram_tensor(name, list(shape), F32, kind="ExternalOutput")

    xin = din('xin', [NTOK, D])
    condT = din('condT', [128, 8, 2])
    gfin = din('gfin', [128, 8])
    Wd = []
    for l in range(2):
        w = {}
        w['ada_w'] = din('ada_w%d' % l, [D, 9 * D])
        w['ada_b'] = din('ada_b%d' % l, [128, 72])
        w['norm'] = din('norm%d' % l, [128, 3, 8])
        for f in (1, 2):
            w['f%din' % f] = din('f%din%d' % (f, l), [D, 2 * DFF])
            w['f%dout' % f] = din('f%dout%d' % (f, l), [DFF, D])
        w['mix_in'] = din('mixin%d' % l, [D, L0_COLS if l == 0 else L1_COLS])
        w['mix_out'] = din('mixout%d' % l, [D, D])
        Wd.append(w)
    qkg = din('qkg', [128, 2])
    gkbc = din('gkbc', [64])
    gbias = din('gbias', [64, 2])
    hgn = din('hgn', [512])
    kctx0 = din('kctx0', [256, 128])
    vctx0 = din('vctx0', [256, 128])
    C0 = din('C0', [2, 8, 64, 64])
    n0 = din('n0', [2, 8, 64])
    m0 = din('m0', [16])
    kcctx = din('kcctx', [256, 512])
    vcctx = din('vcctx', [256, 512])
    kdctx = din('kdctx', [256, 128])
    vdctx = din('vdctx', [256, 128])
    natb = din('natb', [8, 128, 15 * 64])
    sink = din('sink', [8])
    cd = {k: din('c_' + k, v.shape) for k, v in _consts().items()}

    o_yp = dout('o_yp', [1024, D])
    o_ys = dout('o_ys', [1024, D])
    o_k0 = dout('o_k0', [4, 256, 128])
    o_v0 = dout('o_v0', [4, 256, 128])
    o_C = dout('o_C', [4, 2, 8, 64, 64])
    o_n = dout('o_n', [4, 2, 8, 64])
    o_m = dout('o_m', [4, 2, 8])
    o_kc = dout('o_kc', [4, 256, 512])
    o_vc = dout('o_vc', [4, 256, 512])
    o_kd = dout('o_kd', [4, 256, 128])
    o_vd = dout('o_vd', [4, 256, 128])
    dbg_out = {}
    if dbg:
        for name, shape in dbg.items():
            dbg_out[name] = dout('dbg_' + name, shape)

    cnt = [0]

    def sb(shape, dt, name=None):
        cnt[0] += 1
        t = st.enter_context(nc.sbuf_tensor(name or ('t%d' % cnt[0]), list(shape), dt))
        return t

    def mk(shape, dt, name=None):
        t = sb(shape, dt, name)
        return Tl(t[:], Res(name or ''))

    banks = []
    for i in range(8):
        t = st.enter_context(nc.psum_tensor('bank%d' % i, [128, 512], F32))
        banks.append(Tl(t[:], Res('bank%d' % i)))
        banks[-1].r.excl = True
    rot = {'a': [0, 1], 'b': [2, 3], 'c': [4, 5], 'x': [6, 7], 's': [0, 1, 2, 3]}
    rotc = {k: 0 for k in rot}

    def ps(tag):
        i = rot[tag][rotc[tag] % len(rot[tag])]
        rotc[tag] += 1
        return banks[i]

    AR_BYTES = 94 * 1024
    arena = sb([128, AR_BYTES // 2], BF16, 'arena')
    ar = {'off': 0, 'live': [], 'inherit': {}}

    def ar_reset():
        toks = dict(ar['inherit'])
        for r in ar['live']:
            for d in (r.w, r.r):
                for k, v in d.items():
                    if toks.get(k, -1) < v:
                        toks[k] = v
        ar['inherit'] = toks
        ar['live'] = []
        ar['off'] = 0

    def ar_res(name=''):
        r = Res(name)
        r.w = dict(ar['inherit'])
        ar['live'].append(r)
        return r

    def carve(shape, dt, name=''):
        n = int(np.prod(shape[1:]))
        nb = n * (4 if dt == F32 else 2)
        nb = (nb + 31) // 32 * 32
        off = ar['off']
        assert off + nb <= AR_BYTES, ('arena overflow', name, off, nb)
        ar['off'] = off + nb
        a = arena[:, off // 2: off // 2 + (n * (2 if dt == F32 else 1))]
        if dt == F32:
            a = a.bitcast(F32)
        if shape[0] != 128:
            a = a[0:shape[0]]
        if len(shape) > 2:
            names = ' '.join('d%d' % i for i in range(1, len(shape)))
            kw = {'d%d' % i: shape[i] for i in range(1, len(shape) - 1)}
            a = a.rearrange('p (%s) -> p %s' % (names, names), **kw)
        return Tl(a, ar_res(name))

    xT = sb([128, 8, NTOK], F32, 'xT')
    xres = [[Res('x%d_%d' % (m, tg)) for tg in range(4)] for m in range(8)]

    def xap(m, tg):
        return xT[:, m, tg * 512:(tg + 1) * 512]

    NRING = 4
    ring_all = [mk([128, 4096], BF16, 'ring%d' % i) for i in range(NRING)]
    ring = list(ring_all)
    ringc = [0]

    def ring_next():
        s_ = ring[ringc[0] % len(ring)]
        ringc[0] += 1
        return s_

    def wslab_in(wd, c0, n):
        s = ring_next()
        dst = s.a[:, 0:8 * n].rearrange('p (k n) -> p k n', k=8)
        src = wd.ap()[:, c0:c0 + n].rearrange('(k p) n -> p k n', p=128)
        S.dma('pool', lambda e: e.dma_start(out=dst, in_=src), s.r, writes=[s.r])
        return Tl(dst, s.r)

    def wslab_out(wd, r0, nchunk):
        s = ring_next()
        dst = s.a[:, 0:nchunk * 1024].rearrange('p (c n) -> p c n', c=nchunk)
        src = wd.ap()[r0:r0 + 128 * nchunk, :].rearrange('(c p) n -> p c n', p=128)
        S.dma('pool', lambda e: e.dma_start(out=dst, in_=src), s.r, writes=[s.r])
        return Tl(dst, s.r)

    CR = Res('consts')

    def cload(dram, shape, dt=F32, src=None, name=None):
        t = sb(shape, dt, name)
        a = src if src is not None else dram.ap()
        S.dma('sp', lambda e: e.dma_start(out=t[:], in_=a), CR, writes=[CR])
        return t

    ident_f = sb([128, 128], F32, 'ident_f')
    ident_b = sb([128, 128], BF16, 'ident_b')
    ones_b = sb([128, 128], BF16, 'ones_b')
    bd_b = sb([128, 128], BF16, 'bd_b')
    ones_f = sb([64, 128], F32, 'ones_f')
    KR = Res('kconst')
    S.op('pool', lambda e: e.memset(ident_f[:], 0.0), writes=[KR])
    S.op('pool', lambda e: e.affine_select(out=ident_f[:], in_=ident_f[:], pattern=[[-1, 128]], compare_op=ALU.not_equal,
                                           fill=1.0, base=0, channel_multiplier=1), reads=[KR], writes=[KR])
    S.op('pool', lambda e: e.tensor_copy(ident_b[:], ident_f[:]), reads=[KR], writes=[KR])
    S.op('pool', lambda e: e.memset(ones_b[:], 1.0), writes=[KR])
    S.op('pool', lambda e: e.memset(ones_f[:], 1.0), writes=[KR])
    S.op('pool', lambda e: e.memset(bd_b[:], 0.0), writes=[KR])
    S.op('pool', lambda e: e.memset(bd_b[0:64, 0:64], 1.0), reads=[KR], writes=[KR])
    S.op('pool', lambda e: e.memset(bd_b[64:128, 64:128], 1.0), reads=[KR], writes=[KR])

    cond_t = cload(condT, [128, 8, 2])
    gfin_t = cload(gfin, [128, 8])
    adab_t = [cload(Wd[l]['ada_b'], [128, 72]) for l in range(2)]
    norm_t = [cload(Wd[l]['norm'], [128, 3, 8]) for l in range(2)]
    qkg_t = cload(qkg, [128, 2])
    gkbc_t = cload(gkbc, [128, 64], src=gkbc.ap().partition_broadcast(128))
    gbias_t = cload(gbias, [64, 2])
    hgn_t = cload(hgn, [128, 512], src=hgn.ap().partition_broadcast(128))
    m0bc_t = cload(m0, [128, 16], src=m0.ap().partition_broadcast(128))
    m0col_t = sb([64, 1], F32, 'm0col')
    S.dma('sp', lambda e: e.dma_start(out=m0col_t[0:8, :], in_=m0.ap()[0:8].rearrange('(p o) -> p o', o=1)), CR, writes=[CR])
    S.dma('sp', lambda e: e.dma_start(out=m0col_t[32:40, :], in_=m0.ap()[8:16].rearrange('(p o) -> p o', o=1)), CR, writes=[CR])
    sink_t = cload(sink, [128, 8], src=sink.ap().partition_broadcast(128))
    ropeC_t = sb([128, 1024], BF16, 'ropeC')
    ropeS_t = sb([128, 1024], BF16, 'ropeS')
    S.dma('pool', lambda e: e.dma_start(out=ropeC_t[:], in_=cd['ropeC'].ap()), CR, writes=[CR])
    S.dma('pool', lambda e: e.dma_start(out=ropeS_t[:], in_=cd['ropeS'].ap()), CR, writes=[CR])
    maskF_t = cload(cd['maskF'], [128, 128])
    maskB_t = cload(cd['maskB'], [128, 128])
    mask01F_t = cload(cd['mask01F'], [128, 128])
    mask01B_t = cload(cd['mask01B'], [128, 128])
    psw_f = cload(cd['psw'], [128, 128])
    oh_t = cload(cd['oh'], [64, 16])
    cmask_t = cload(cd['cmask'], [128, 64])
    psw_b = sb([128, 128], BF16, 'psw_b')
    S.op('pool', lambda e: e.tensor_copy(psw_b[:], psw_f[:]), reads=[CR], writes=[KR])
    esink_t = sb([128, 8], F32, 'esink')
    S.op('act', lambda e: e.activation(esink_t[:], sink_t[:], AF.Exp), reads=[CR], writes=[KR])
    ngbias_t = sb([64, 1], F32, 'ngbias')
    S.op('pool', lambda e: e.tensor_scalar(ngbias_t[:], gbias_t[:, 1:2], -1.0, None, ALU.mult), reads=[CR], writes=[KR])
    CK = [CR, KR]

    scond = sb([128, 8, 2], BF16, 'scond')
    S.op('act', lambda e: e.activation(scond[:], cond_t[:], AF.Silu), reads=[CR], writes=[KR])
    mod_t = [sb([128, 72, 2], F32, 'mod%d' % l) for l in range(2)]
    modT_R = [[Res('modT') for n in range(3)] for l in range(2)]
    modD_R = [[Res('modD') for n in range(3)] for l in range(2)]
    modA = [sb([128, 3, 8, 2], F32, 'modA%d' % l) for l in range(2)]
    modG = [sb([128, 3, 8, 2], F32, 'modG%d' % l) for l in range(2)]

    def adaln_slabs(l, s0, s1):
        for s in range(s0, s1):
            W = wslab_in(Wd[l]['ada_w'], 512 * s, 512)
            pb = ps('x')
            for c in range(4):
                for k in range(8):
                    S.op('pe', lambda e, c=c, k=k, W=W, pb=pb: e.matmul(pb.a[:, 2 * c:2 * c + 2], W.a[:, k, c * 128:(c + 1) * 128],
                                                                      scond[:, k, :], start=(k == 0), stop=(k == 7)),
                         reads=[W.r, KR], writes=[pb.r])
            dstv = mod_t[l][:, 4 * s:4 * s + 4, :]
            bv = adab_t[l][:, 4 * s:4 * s + 4].unsqueeze(2).to_broadcast([128, 4, 2])
            S.op('dve', lambda e, pb=pb, dstv=dstv, bv=bv: e.tensor_tensor(dstv, pb.a[:, 0:8].rearrange('p (c j) -> p c j', c=4), bv, ALU.add),
                 reads=[pb.r, CR], writes=[modT_R[l][s // 6]])

    def adaln_finish(l, ns=(0, 1, 2)):
        for n in ns:
            sc = mod_t[l][:, (3 * n + 1) * 8:(3 * n + 2) * 8, :]
            g = mod_t[l][:, (3 * n + 2) * 8:(3 * n + 3) * 8, :]
            gn = norm_t[l][:, n, :].unsqueeze(2).to_broadcast([128, 8, 2])
            S.op('dve', lambda e, n=n, sc=sc, gn=gn: e.scalar_tensor_tensor(modA[l][:, n, :, :], sc, 1.0, gn, ALU.add, ALU.mult),
                 reads=[modT_R[l][n], CR], writes=[modD_R[l][n]])
            fac = 1.0 if n == 1 else 0.5
            S.op('dve', lambda e, n=n, g=g, fac=fac: e.tensor_scalar(modG[l][:, n, :, :], g, fac, None, ALU.mult),
                 reads=[modT_R[l][n], modD_R[l][n]], writes=[modD_R[l][n]])

    def modB(l, n, m, cnd):
        return mod_t[l][:, (3 * n) * 8 + m, cnd:cnd + 1]

    def tg_cond(tg):
        return 0 if tg < 2 else 1

    def rms_stats(tg, tmp_sq, rs):
        pb = ps('x')
        for m in range(8):
            sq = tmp_sq[m % 2]
            S.op('act', lambda e, m=m, sq=sq: e.activation(sq.a, xap(m, tg), AF.Square), reads=[xres[m][tg]], writes=[sq.r])
            S.op('pe', lambda e, m=m, sq=sq, pb=pb: e.matmul(pb.a, ones_b[:], sq.a, start=(m == 0), stop=(m == 7)),
                 reads=[sq.r, KR], writes=[pb.r])
        S.op('act', lambda e, pb=pb: e.activation(rs.a, pb.a, AF.Sqrt, bias=EPS, scale=1.0 / D), reads=[pb.r], writes=[rs.r])
        S.op('dve', lambda e: e.reciprocal(rs.a, rs.a), reads=[rs.r], writes=[rs.r])

    def norm_mod(l, n, tg, hdst, hcol0, tmp_sq, rs, tmpf):
        cnd = tg_cond(tg)
        rms_stats(tg, tmp_sq, rs)
        for m in range(8):
            tf = tmpf[m % 2]
            S.op('dve', lambda e, m=m, tf=tf: e.scalar_tensor_tensor(tf.a, xap(m, tg), modA[l][:, n, m, cnd:cnd + 1], rs.a, ALU.mult, ALU.mult),
                 reads=[xres[m][tg], rs.r, modD_R[l][n]], writes=[tf.r])
            S.op('act', lambda e, m=m, tf=tf: e.activation(hdst.a[:, m, hcol0:hcol0 + 512], tf.a, AF.Identity, bias=modB(l, n, m, cnd), scale=1.0),
                 reads=[tf.r, modT_R[l][n]], writes=[hdst.r])

    def load_x():
        ar_reset()
        stg = [carve([128, 1024], F32, 'xstg%d' % i) for i in range(4)]
        for tg in range(4):
            for j in range(4):
                t = tg * 4 + j
                S.dma('sp', lambda e, j=j, t=t: e.dma_start(out=stg[j].a, in_=xin.ap()[t * 128:(t + 1) * 128, :]), stg[j].r, writes=[stg[j].r])
            for m in range(8):
                pb = ps('a' if m % 2 == 0 else 'b')
                for j in range(4):
                    S.op('pe', lambda e, j=j, m=m, pb=pb: e.transpose(pb.a[:, j * 128:(j + 1) * 128], stg[j].a[:, m * 128:(m + 1) * 128], ident_f[:]),
                         reads=[stg[j].r, KR], writes=[pb.r])
                if m % 2 == 0:
                    S.op('dve', lambda e, m=m, pb=pb: e.tensor_copy(xap(m, tg), pb.a), reads=[pb.r], writes=[xres[m][tg]])
                else:
                    S.op('act', lambda e, m=m, pb=pb: e.copy(xap(m, tg), pb.a), reads=[pb.r], writes=[xres[m][tg]])

    def ffn(l, which, between=None):
        n = 0 if which == 1 else 2
        ar_reset()
        h = carve([128, 8, NTOK], BF16, 'h')
        hres = [ar_res('h%d' % tg) for tg in range(4)]
        hid = carve([128, 4, NTOK], BF16, 'hid')
        hidres = [[ar_res('hid') for tg in range(4)] for c in range(4)]
        tmp_sq = [carve([128, 512], BF16, 'sq') for _ in range(2)]
        rs = carve([128, 512], F32, 'rs')
        tmpf = [carve([128, 512], F32, 'tmpf') for _ in range(2)]
        sgt = [carve([128, 512], F32, 'sg') for _ in range(2)]
        for tg in range(4):
            norm_mod(l, n, tg, Tl(h.a, hres[tg]), tg * 512, tmp_sq, rs, tmpf)
        win = Wd[l]['f%din' % which]
        wout = Wd[l]['f%dout' % which]
        for i in range(6):
            nch = 4 if i < 5 else 2
            G = wslab_in(win, 512 * i, 128 * nch)
            U = wslab_in(win, DFF + 512 * i, 128 * nch)
            O = wslab_out(wout, 512 * i, nch)
            for tg in range(4):
                for c in range(nch):
                    pg = ps('a')
                    pu = ps('b')
                    for k in range(8):
                        S.op('pe', lambda e, k=k, c=c, tg=tg, pg=pg, G=G: e.matmul(pg.a, G.a[:, k, c * 128:(c + 1) * 128], h.a[:, k, tg * 512:(tg + 1) * 512],
                                                                                   start=(k == 0), stop=(k == 7)),
                             reads=[G.r, hres[tg]], writes=[pg.r])
                    for k in range(8):
                        S.op('pe', lambda e, k=k, c=c, tg=tg, pu=pu, U=U: e.matmul(pu.a, U.a[:, k, c * 128:(c + 1) * 128], h.a[:, k, tg * 512:(tg + 1) * 512],
                                                                                   start=(k == 0), stop=(k == 7)),
                             reads=[U.r, hres[tg]], writes=[pu.r])
                    sg = sgt[c % 2]
                    S.op('act', lambda e, pg=pg, sg=sg: e.activation(sg.a, pg.a, AF.Silu), reads=[pg.r], writes=[sg.r])
                    S.op('dve', lambda e, c=c, tg=tg, pu=pu, sg=sg: e.tensor_tensor(hid.a[:, c, tg * 512:(tg + 1) * 512], sg.a, pu.a, ALU.mult),
                         reads=[pu.r, sg.r], writes=[hidres[c][tg]])
            for tg in range(4):
                cnd = tg_cond(tg)
                for m in range(8):
                    py = ps('c')
                    for c in range(nch):
                        S.op('pe', lambda e, c=c, m=m, tg=tg, py=py, O=O, nch=nch: e.matmul(py.a, O.a[:, c, m * 128:(m + 1) * 128], hid.a[:, c, tg * 512:(tg + 1) * 512],
                                                                                            start=(c == 0), stop=(c == nch - 1)),
                             reads=[O.r, hidres[c][tg]], writes=[py.r])
                    S.op('dve', lambda e, m=m, tg=tg, py=py, cnd=cnd: e.scalar_tensor_tensor(xap(m, tg), py.a, modG[l][:, n, m, cnd:cnd + 1], xap(m, tg), ALU.mult, ALU.add),
                         reads=[py.r, modD_R[l][n], xres[m][tg]], writes=[xres[m][tg]])
            if between is not None:
                between(i)

    def final_out():
        ar_reset()
        tmp_sq = [carve([128, 512], BF16, 'sq') for _ in range(2)]
        rs = carve([128, 512], F32, 'rs')
        yT = carve([128, 8, 512], F32, 'yT')
        ystg = [carve([128, 1024], F32, 'ystg') for _ in range(2)]
        for tg in range(4):
            rms_stats(tg, tmp_sq, rs)
            for m in range(8):
                S.op('dve', lambda e, m=m: e.scalar_tensor_tensor(yT.a[:, m, :], xap(m, tg), gfin_t[:, m:m + 1], rs.a, ALU.mult, ALU.mult),
                     reads=[xres[m][tg], rs.r, CR], writes=[yT.r])
            for j in range(4):
                t = tg * 4 + j
                ys = ystg[j % 2]
                for half in range(2):
                    pb = ps('a' if half == 0 else 'b')
                    for mm in range(4):
                        m = half * 4 + mm
                        S.op('pe', lambda e, mm=mm, m=m, j=j, pb=pb: e.transpose(pb.a[:, mm * 128:(mm + 1) * 128], yT.a[:, m, j * 128:(j + 1) * 128], ident_f[:]),
                             reads=[yT.r, KR], writes=[pb.r])
                    if half == 0:
                        S.op('act', lambda e, pb=pb, ys=ys: e.copy(ys.a[:, 0:512], pb.a), reads=[pb.r], writes=[ys.r])
                    else:
                        S.op('dve', lambda e, pb=pb, ys=ys: e.tensor_copy(ys.a[:, 512:1024], pb.a), reads=[pb.r], writes=[ys.r])
                dst = (o_yp if t < 8 else o_ys).ap()[(t % 8) * 128:(t % 8 + 1) * 128, :]
                S.dma('sp', lambda e, ys=ys, dst=dst: e.dma_start(out=dst, in_=ys.a), ys.r, reads=[ys.r])

    def proj_fm(W, c0, M, hg, ncols, post):
        for b in range(ncols // 512):
            pb = ps('a')
            for k in range(8):
                S.op('pe', lambda e, k=k, b=b, pb=pb: e.matmul(pb.a[0:M, :], W.a[:, k, c0:c0 + M], hg.a[:, k, b * 512:(b + 1) * 512],
                                                               start=(k == 0), stop=(k == 7)),
                     reads=[W.r, hg.r], writes=[pb.r])
            post(b, pb)

    def proj_tm(W, c0, n, hg, ntiles, post):
        for t in range(ntiles):
            pb = ps('b')
            for k in range(8):
                S.op('pe', lambda e, k=k, t=t, pb=pb: e.matmul(pb.a[:, 0:n], hg.a[:, k, t * 128:(t + 1) * 128], W.a[:, k, c0:c0 + n],
                                                               start=(k == 0), stop=(k == 7)),
                     reads=[W.r, hg.r], writes=[pb.r])
            post(t, pb)

    class Work:
        pass

    LA = 3

    def _emit_pv(wk, blk, kd, vq, c0, PT, b0, last, qts, finalize):
        if blk['po'] is None:
            blk['po'] = ps('c')
        po = blk['po']
        ns = kd['ns']
        pb = kd.get('pbase', 0)
        for i in vq:
            oc = (i - b0) * 128
            first = blk['first']
            S.op('pe', lambda e: e.matmul(po.a[:, oc:oc + 65], PT.a[pb:pb + ns, i * 128 - c0:i * 128 - c0 + 128], kd['vaug'],
                                          start=first, stop=True, skip_group_check=True),
                 reads=[PT.r] + list(kd['res']), writes=[po.r])
            blk['first'] = False
        if last:
            if hasattr(finalize, 'blk'):
                finalize.blk(qts, po.a.rearrange('p (q c) -> p q c', q=4), po.r)
            else:
                for i in qts:
                    oc = (i - b0) * 128
                    finalize(i, po.a[:, oc:oc + 65], po.r)

    def attn_flush(wk):
        while wk.pipe:
            wk.pipe.pop(0)()

    def attn_job(wk, nq, qT, qres, ktiles, valid, prob, finalize, after_block=None):
        import functools
        nqt = nq // 128
        for b0 in range(0, nqt, 4):
            qts = list(range(b0, min(b0 + 4, nqt)))
            blk = {'po': None, 'first': True}
            steps = []
            for kt, kd in enumerate(ktiles):
                vq = [i for i in qts if valid(kt, i)]
                if vq:
                    steps.append((kt, kd, vq))
            for si, (kt, kd, vq) in enumerate(steps):
                lo, hi = min(vq), max(vq) + 1
                c0, n = lo * 128, (hi - lo) * 128
                ns = kd['ns']
                PT = wk.PT[wk.ptc % len(wk.PT)]
                wk.ptc += 1
                if kd.get('noscore'):
                    prob(kt, kd, c0, n, None, PT)
                else:
                    pscore = ps('s')
                    S.op('pe', lambda e: e.matmul(pscore.a[0:ns, 0:n], kd['kT'], qT(c0, n), start=True, stop=True),
                         reads=list(kd['res']) + list(qres), writes=[pscore.r])
                    prob(kt, kd, c0, n, pscore, PT)
                wk.pipe.append(functools.partial(_emit_pv, wk, blk, kd, vq, c0, PT, b0, si == len(steps) - 1, qts, finalize))
                while len(wk.pipe) > wk.LA:
                    wk.pipe.pop(0)()
            if after_block is not None:
                after_block(b0 * 128)

    def prob_exp(kt, kd, c0, n, pscore, PT):
        ns = kd['ns']
        S.op('act', lambda e: e.activation(PT.a[0:ns, 0:n], pscore.a[0:ns, 0:n], AF.Exp, scale=0.125), reads=[pscore.r], writes=[PT.r])

    def mixer(l, grp):
        ar_reset()
        sample = grp == 2
        NT = 1024 if sample else 512
        tgs = [2, 3] if sample else [grp]
        ntile = NT // 128
        nseq = 1 if sample else 2
        L = 1024 if sample else 256
        tps = L // 128
        cnd = 1 if sample else 0
        tok0 = 1024 if sample else grp * 512
        hg = carve([128, 8, NT], BF16, 'hg')
        tmp_sq = [carve([128, 512], BF16, 'sq') for _ in range(2)]
        rs = carve([128, 512], F32, 'rs')
        tmpf = [carve([128, 512], F32, 'tmpf') for _ in range(2)]
        for i, tg in enumerate(tgs):
            norm_mod(l, 1, tg, hg, i * 512, tmp_sq, rs, tmpf)
        oT = Tl(hg.a, hg.r)
        wk = Work()
        wk.LA = 3 if sample else 6
        wk.PT = [carve([128, 512], BF16, 'PT') for _ in range(4 if sample else 8)]
        wk.ptc = 0
        wk.pipe = []
        opair = carve([128, ntile, 128], BF16, 'opair')
        sm = [carve([128, 8], F32, 'sm') for _ in range(8)]
        smc = [0]

        def smt():
            smc[0] += 1
            return sm[smc[0] % 8]
        stg_f = [carve([128, 528], F32, 'stgf') for _ in range(2)]
        stc = [0]

        def stg():
            stc[0] += 1
            return stg_f[stc[0] % 2]

        def seq_tile(s, j):
            return s * tps + j

        def fin_softmax(e_h, extra_den=None):
            ecol = e_h * 64

            class F:
                tile0 = 0

                def bind(self, tile0):
                    self.tile0 = tile0
                    return self

                def blk(self, qts, pv, por):
                    i0, nq = qts[0], len(qts)
                    t = smt()
                    den = pv[:, 0:nq, 64]
                    if extra_den is None:
                        S.op('dve', lambda e: e.reciprocal(t.a[:, 0:nq], den), reads=[por], writes=[t.r])
                    else:
                        S.op('dve', lambda e: e.tensor_scalar(t.a[:, 0:nq], den, extra_den, None, ALU.add), reads=[por, KR], writes=[t.r])
                        S.op('dve', lambda e: e.reciprocal(t.a[:, 0:nq], t.a[:, 0:nq]), reads=[t.r], writes=[t.r])
                    S.op('dve', lambda e: e.tensor_tensor(opair.a[:, self.tile0 + i0:self.tile0 + i0 + nq, ecol:ecol + 64], pv[:, 0:nq, 0:64],
                                                          t.a[:, 0:nq].unsqueeze(2).to_broadcast([128, nq, 64]), ALU.mult),
                         reads=[por, t.r], writes=[opair.r])
            return F()

        def flush_pair(chunk):
            attn_flush(wk)
            for b in range(0, ntile, 4):
                pb = ps('x')
                pbv = pb.a.bitcast(BF16)
                nn = min(4, ntile - b)
                for j in range(nn):
                    S.op('pe', lambda e, j=j, b=b, pbv=pbv: e.transpose(pbv[:, j * 128:(j + 1) * 128], opair.a[:, b + j, :], ident_b[:]),
                         reads=[opair.r, KR], writes=[pb.r])
                S.op('act', lambda e, b=b, nn=nn, pbv=pbv: e.copy(oT.a[:, chunk, b * 128:(b + nn) * 128], pbv[:, 0:nn * 128]), reads=[pb.r], writes=[oT.r])

        def qknorm_fm(pb, gcol, dst_ap, dst_res, rope_cols=None):
            sq = tmp_sq[0]
            S.op('act', lambda e: e.activation(sq.a, pb.a, AF.Square), reads=[pb.r], writes=[sq.r])
            p2 = ps('x')
            S.op('pe', lambda e: e.matmul(p2.a, bd_b[:], sq.a, start=True, stop=True), reads=[sq.r, KR], writes=[p2.r])
            S.op('act', lambda e: e.activation(rs.a, p2.a, AF.Ln, bias=EPS, scale=1.0 / 64), reads=[p2.r], writes=[rs.r])
            S.op('act', lambda e: e.activation(rs.a, rs.a, AF.Exp, scale=-0.5), reads=[rs.r], writes=[rs.r])
            if rope_cols is None:
                S.op('dve', lambda e: e.scalar_tensor_tensor(dst_ap, pb.a, qkg_t[:, gcol:gcol + 1], rs.a, ALU.mult, ALU.mult),
                     reads=[pb.r, rs.r, CR], writes=[dst_res])
            else:
                qn = tmp_sq[1]
                S.op('dve', lambda e: e.scalar_tensor_tensor(qn.a, pb.a, qkg_t[:, gcol:gcol + 1], rs.a, ALU.mult, ALU.mult),
                     reads=[pb.r, rs.r, CR], writes=[qn.r])
                rope_fm(qn.a, qn.r, dst_ap, dst_res, rope_cols)

        def rope_fm(src_ap, src_res, dst_ap, dst_res, c0):
            p3 = ps('x')
            S.op('pe', lambda e: e.matmul(p3.a, psw_b[:], src_ap, start=True, stop=True), reads=[src_res, KR], writes=[p3.r])
            t1, t2 = tmpf[0], tmpf[1]
            S.op('dve', lambda e: e.tensor_tensor(t1.a, p3.a, ropeS_t[:, c0:c0 + 512], ALU.mult), reads=[p3.r, CR], writes=[t1.r])
            S.op('pool', lambda e: e.tensor_tensor(t2.a, src_ap, ropeC_t[:, c0:c0 + 512], ALU.mult), reads=[src_res, CR], writes=[t2.r])
            S.op('dve', lambda e: e.tensor_tensor(dst_ap, t1.a, t2.a, ALU.add), reads=[t1.r, t2.r], writes=[dst_res])

        def load_ctx_kT(dram, ncol_pairs, dst, dup):
            for j in range(2):
                s_ = stg()
                if dup:
                    srcv = bass.AP(dram, j * 128 * 128, [[128, 128], [64, 2], [0, 2], [1, 64]])
                    S.dma('sp', lambda e, s_=s_, srcv=srcv: e.dma_start(out=s_.a[:, 0:256].rearrange('p (a b c) -> p a b c', a=2, b=2), in_=srcv), s_.r, writes=[s_.r])
                else:
                    for q_ in range(ncol_pairs):
                        S.dma('sp', lambda e, s_=s_, j=j, q_=q_: e.dma_start(out=s_.a[:, q_ * 128:(q_ + 1) * 128], in_=dram.ap()[j * 128:(j + 1) * 128, q_ * 128:(q_ + 1) * 128]), s_.r, writes=[s_.r])
                pb = ps('x')
                for c in range(ncol_pairs):
                    S.op('pe', lambda e, c=c, s_=s_, pb=pb: e.transpose(pb.a[:, c * 128:(c + 1) * 128], s_.a[:, c * 128:(c + 1) * 128], ident_f[:]),
                         reads=[s_.r, KR], writes=[pb.r])
                for c in range(ncol_pairs):
                    S.op('act', lambda e, c=c, j=j, pb=pb: e.copy(dst.a[:, c, j * 128:(j + 1) * 128], pb.a[:, c * 128:(c + 1) * 128]), reads=[pb.r], writes=[dst.r])

        def load_ctx_v(dram, nh, dst, t0):
            for j in range(2):
                s_ = stg()
                for q_ in range(0, 64 * nh, 128):
                    S.dma('sp', lambda e, s_=s_, j=j, q_=q_: e.dma_start(out=s_.a[:, q_:q_ + 128], in_=dram.ap()[j * 128:(j + 1) * 128, q_:q_ + 128]), s_.r, writes=[s_.r])
                S.op('act', lambda e, s_=s_, j=j: e.copy(dst.a[:, t0 + j, :, 0:64], s_.a[:, 0:64 * nh].rearrange('p (h d) -> p h d', h=nh)), reads=[s_.r], writes=[dst.r])

        def out_tm(pb, n, dram, seq, j, cast_dst=None):
            import os as _os2
            if ('nodma%d' % n) in _os2.environ.get('KSUB', '') and l == 1:
                return
            s_ = stg()
            if True:
                S.op('dve', lambda e: e.tensor_copy(s_.a[:, 0:n], pb.a[:, 0:n]), reads=[pb.r], writes=[s_.r])
            else:
                S.op('act', lambda e: e.copy(s_.a[:, 0:n], pb.a[:, 0:n]), reads=[pb.r], writes=[s_.r])
            import os as _os3
            oq = _os3.environ.get('KOUTQ', 'pool')
            if ('nostore%d' % n) in _os2.environ.get('KSUB', '') and l == 1:
                return
            S.dma(oq, lambda e: e.dma_start(out=dram.ap()[seq, j * 128:(j + 1) * 128, :], in_=s_.a[:, 0:n]), s_.r, reads=[s_.r])

        mi = Wd[l]['mix_in']
        if l == 0:
            nkt = (2 if sample else 0) + ntile
            QA = carve([128, 4, NT], BF16, 'QA')
            KA = carve([128, 2, 256 + NT if sample else NT], BF16, 'KA')
            VA = carve([128, nkt, 2, 65], BF16, 'VA')
            if sample:
                rsv = [ring.pop(), ring.pop()]
                QB = Tl(rsv[0].a.rearrange('p (c n) -> p c n', c=4), rsv[0].r)
                KB = Tl(rsv[1].a.rearrange('p (c n) -> p c n', c=4), rsv[1].r)
            else:
                QB = carve([128, 4, NT], BF16, 'QB')
                KB = carve([128, 4, NT], BF16, 'KB')
            VB = carve([128, ntile, 8, 65], BF16, 'VB')
            OB = carve([128, ntile, 512], BF16, 'OB')
            KBT = None if sample else carve([128, ntile, 512], BF16, 'KBT')
            LI = carve([64, NT], F32, 'LI')
            LF = carve([64, NT], F32, 'LF')
            BT = carve([64, NT], F32, 'BT')
            TOK = carve([128, ntile, 3, 16], F32, 'TOK')
            HF = carve([128, tps, 64], F32, 'HF')
            NBC = [carve([128, 512], F32, 'NBC') for _ in range(2)]
            EE01 = carve([128, 1024], F32, 'EE01')
            EE = [Tl(EE01.a[:, i * 512:(i + 1) * 512], ar_res('EE%d' % i)) for i in range(2)]
            ONES = Tl(EE01.a[0:64, 0:NT], EE01.r)
            EEall = [EE01.r, EE[0].r, EE[1].r]
            RM = Tl(tmpf[1].a[0:64], tmpf[1].r)
            koff = 256 if sample else 0
            S.op('pool', lambda e: e.memset(VA.a, 1.0), writes=[VA.r])
            S.op('pool', lambda e: e.memset(VB.a, 1.0), writes=[VB.r])
            S.op('pool', lambda e: e.memset(LI.a, 0.0), writes=[LI.r])
            S.op('pool', lambda e: e.memset(LF.a, 0.0), writes=[LF.r])
            S.op('pool', lambda e: e.memset(BT.a, 0.0), writes=[BT.r])
            S.op('pool', lambda e: e.memset(ONES.a, 1.0), writes=EEall)
            if sample:
                load_ctx_kT(kctx0, 2, KA, True)
                load_ctx_v(vctx0, 2, VA, 0)
                VV = carve([128, 2, 4, 65], BF16, 'VV')
                C0v = C0.ap().rearrange('a (h t) k v -> a t k h v', t=2)
                n0v = n0.ap().rearrange('a (h t) k -> a t k h', t=2)
                for half in range(2):
                    s_ = stg()
                    for dr in range(2):
                        S.dma('sp', lambda e, s_=s_, dr=dr, half=half: e.dma_start(
                            out=s_.a[half * 64:half * 64 + 64, dr * 256: dr * 256 + 256].rearrange('p (h v) -> p h v', h=4),
                            in_=C0v[dr, half]), s_.r, writes=[s_.r])
                        S.dma('sp', lambda e, s_=s_, dr=dr, half=half: e.dma_start(
                            out=s_.a[half * 64:half * 64 + 64, 512 + dr * 4:512 + dr * 4 + 4],
                            in_=n0v[dr, half], allow_slow_non_contiguous=True), s_.r, writes=[s_.r])
                    for dr in range(2):
                        S.op('act', lambda e, s_=s_, dr=dr, half=half: e.copy(
                            VV.a[half * 64:half * 64 + 64, dr, :, 0:64],
                            s_.a[half * 64:half * 64 + 64, dr * 256:dr * 256 + 256].rearrange('p (h v) -> p h v', h=4)),
                            reads=[s_.r], writes=[VV.r])
                        S.op('act', lambda e, s_=s_, dr=dr, half=half: e.copy(
                            VV.a[half * 64:half * 64 + 64, dr, :, 64:65],
                            s_.a[half * 64:half * 64 + 64, 512 + dr * 4:512 + dr * 4 + 4].unsqueeze(2)),
                            reads=[s_.r], writes=[VV.r])
            W = wslab_in(mi, 0, 512)
            for c in range(4):
                def post(b, pb, c=c):
                    qknorm_fm(pb, 0, QA.a[:, c, b * 512:(b + 1) * 512], QA.r, rope_cols=(b * 512 if sample else None))
                proj_fm(W, c * 128, 128, hg, NT, post)
            W = wslab_in(mi, 512, 512)
            for c in range(2):
                def post(b, pb, c=c):
                    qknorm_fm(pb, 1, KA.a[:, c, koff + b * 512:koff + (b + 1) * 512], KA.r, rope_cols=(b * 512 if sample else None))
                proj_fm(W, c * 128, 128, hg, NT, post)

            def post_gi(b, pb):
                S.op('act', lambda e: e.activation(LI.a[0:40, b * 512:(b + 1) * 512], pb.a[0:40, :], AF.Identity, bias=gbias_t[0:40, 0:1], scale=1.0),
                     reads=[pb.r, CR], writes=[LI.r])
            proj_fm(W, 256, 64, hg, NT, post_gi)

            def post_gf(b, pb):
                t1 = tmpf[0]
                S.op('act', lambda e: e.activation(t1.a[0:40, :], pb.a[0:40, :], AF.Exp, bias=ngbias_t[0:40, 0:1], scale=-1.0), reads=[pb.r, KR], writes=[t1.r])
                S.op('act', lambda e: e.activation(t1.a[0:40, :], t1.a[0:40, :], AF.Ln, bias=1.0, scale=1.0), reads=[t1.r], writes=[t1.r])
                S.op('dve', lambda e: e.tensor_scalar(LF.a[0:40, b * 512:(b + 1) * 512], t1.a[0:40, :], -1.0, None, ALU.mult), reads=[t1.r], writes=[LF.r])
            proj_fm(W, 320, 64, hg, NT, post_gf)

            def post_va(t, pb):
                kt = (2 if sample else 0) + t
                S.op('act', lambda e: e.copy(VA.a[:, kt, :, 0:64], pb.a[:, 0:128].rearrange('p (h d) -> p h d', h=2)), reads=[pb.r], writes=[VA.r])
                if not sample:
                    out_tm(pb, 128, o_v0, grp * 2 + t // tps, t % tps)
            proj_tm(W, 384, 128, hg, ntile, post_va)
            W = wslab_in(mi, 1024, 512)
            for c in range(4):
                def post(b, pb, c=c):
                    S.op('act', lambda e: e.copy(QB.a[:, c, b * 512:(b + 1) * 512], pb.a), reads=[pb.r], writes=[QB.r])
                proj_fm(W, c * 128, 128, hg, NT, post)
            W = wslab_in(mi, 1536, 512)
            for c in range(4):
                def post(b, pb, c=c):
                    S.op('dve', lambda e: e.tensor_copy(KB.a[:, c, b * 512:(b + 1) * 512], pb.a), reads=[pb.r], writes=[KB.r])
                proj_fm(W, c * 128, 128, hg, NT, post)
            if not sample:
                def post(t, pb):
                    S.op('act', lambda e: e.copy(KBT.a[:, t, :], pb.a), reads=[pb.r], writes=[KBT.r])
                proj_tm(W, 0, 512, hg, ntile, post)
            W = wslab_in(mi, 2048, 512)
            def post(t, pb):
                S.op('dve', lambda e: e.tensor_copy(VB.a[:, t, :, 0:64], pb.a.rearrange('p (h d) -> p h d', h=8)), reads=[pb.r], writes=[VB.r])
            proj_tm(W, 0, 512, hg, ntile, post)
            W = wslab_in(mi, 2560, 512)
            def post(t, pb):
                S.op('act', lambda e: e.activation(OB.a[:, t, :], pb.a, AF.Sigmoid), reads=[pb.r], writes=[OB.r])
            proj_tm(W, 0, 512, hg, ntile, post)
            if not sample:
                W = wslab_in(mi, 3072, 128)
                def post(t, pb):
                    t_ = smt()
                    s_ = stg()
                    for hh in range(2):
                        S.op('act', lambda e, hh=hh: e.activation(s_.a[:, 256 + hh * 64:256 + hh * 64 + 64], pb.a[:, hh * 64:(hh + 1) * 64], AF.Square,
                                                                  accum_out=t_.a[:, hh:hh + 1]), reads=[pb.r], writes=[s_.r, t_.r])
                    S.op('act', lambda e: e.activation(t_.a[:, 0:2], t_.a[:, 0:2], AF.Sqrt, bias=EPS, scale=1.0 / 64), reads=[t_.r], writes=[t_.r])
                    S.op('dve', lambda e: e.reciprocal(t_.a[:, 0:2], t_.a[:, 0:2]), reads=[t_.r], writes=[t_.r])
                    for hh in range(2):
                        S.op('dve', lambda e, hh=hh: e.scalar_tensor_tensor(s_.a[:, hh * 64:(hh + 1) * 64], pb.a[:, hh * 64:(hh + 1) * 64], t_.a[:, hh:hh + 1], gkbc_t[:],
                                                                            ALU.mult, ALU.mult), reads=[pb.r, t_.r, CR], writes=[s_.r])
                    S.dma('sp', lambda e: e.dma_start(out=o_k0.ap()[grp * 2 + t // tps, (t % tps) * 128:(t % tps + 1) * 128, :], in_=s_.a[:, 0:128]), s_.r, reads=[s_.r])
                proj_tm(W, 0, 128, hg, ntile, post)

            for c in range(4):
                for s in range(nseq):
                    q0 = s * L
                    kv = c // 2
                    for e_ in range(2):
                        pbs = 64 * e_
                        kts = []
                        if sample:
                            for j in range(2):
                                kts.append(dict(kT=KA.a[pbs:pbs + 64, kv, j * 128:(j + 1) * 128], ns=128, vaug=VA.a[:, j, kv, :], res=[KA.r, VA.r]))
                        for j in range(tps):
                            kts.append(dict(kT=KA.a[pbs:pbs + 64, kv, koff + q0 + j * 128:koff + q0 + (j + 1) * 128], ns=128,
                                            vaug=VA.a[:, (2 if sample else 0) + seq_tile(s, j), kv, :], res=[KA.r, VA.r]))
                        fs = fin_softmax(e_)
                        attn_job(wk, L, lambda c0, n, c=c, pbs=pbs, q0=q0: QA.a[pbs:pbs + 64, c, q0 + c0:q0 + c0 + n], [QA.r], kts,
                                 lambda kt, i: True, prob_exp,
                                 fs.bind(s * tps))
                    if s == nseq - 1:
                        flush_pair(c)

            def revap(a, c0, n):
                return bass.AP(a.tensor, a[:, c0 + n - 1:c0 + n].offset, [list(a.ap[0]), [-1, n]])
            for s in range(nseq):
                c0 = s * L
                S.op('dve', lambda e, c0=c0: e.tensor_tensor_scan(BT.a[0:8, c0:c0 + L], ONES.a[0:8, c0:c0 + L], LF.a[0:8, c0:c0 + L], 0.0, ALU.mult, ALU.add),
                     reads=[LF.r] + EEall, writes=[BT.r])
                S.op('dve', lambda e, c0=c0: e.tensor_tensor_scan(revap(BT.a[32:40], c0, L), ONES.a[32:40, c0:c0 + L], revap(LF.a[32:40], c0, L), 0.0, ALU.mult, ALU.add),
                     reads=[LF.r] + EEall, writes=[BT.r])
            S.op('dve', lambda e: e.tensor_tensor(LI.a[0:40, :], LI.a[0:40, :], BT.a[0:40, :], ALU.subtract), reads=[LI.r, BT.r], writes=[LI.r])
            for s in range(nseq):
                c0 = s * L
                ini_f = m0col_t[0:8, :] if sample else 0.0
                ini_b = m0col_t[32:40, :] if sample else 0.0
                S.op('dve', lambda e, c0=c0, ini_f=ini_f: e.tensor_tensor_scan(LF.a[0:8, c0:c0 + L], ONES.a[0:8, c0:c0 + L], LI.a[0:8, c0:c0 + L], ini_f, ALU.mult, ALU.max),
                     reads=[LI.r, CR] + EEall, writes=[LF.r])
                S.op('dve', lambda e, c0=c0, ini_b=ini_b: e.tensor_tensor_scan(revap(LF.a[32:40], c0, L), ONES.a[32:40, c0:c0 + L], revap(LI.a[32:40], c0, L), ini_b, ALU.mult, ALU.max),
                     reads=[LI.r, CR] + EEall, writes=[LF.r])
            S.op('dve', lambda e: e.tensor_scalar(LF.a[0:40, :], LF.a[0:40, :], -1.0, None, ALU.mult), reads=[LF.r], writes=[LF.r])
            S.op('dve', lambda e: e.tensor_tensor(BT.a[0:40, :], LF.a[0:40, :], BT.a[0:40, :], ALU.subtract), reads=[LF.r, BT.r], writes=[BT.r])
            if not sample:
                for s in range(nseq):
                    c0 = s * L
                    sg_ = grp * 2 + s
                    t_ = smt()
                    S.op('dve', lambda e, c0=c0, t_=t_: e.tensor_scalar(t_.a[0:8, 0:1], BT.a[0:8, c0 + L - 1:c0 + L], -1.0, None, ALU.mult), reads=[BT.r], writes=[t_.r])
                    S.op('dve', lambda e, c0=c0, t_=t_: e.tensor_scalar(t_.a[32:40, 0:1], BT.a[32:40, c0:c0 + 1], -1.0, None, ALU.mult), reads=[BT.r], writes=[t_.r])
                    S.dma('sp', lambda e, t_=t_, sg_=sg_: e.dma_start(out=o_m.ap()[sg_, 0, :].rearrange('(p o) -> p o', o=1), in_=t_.a[0:8, 0:1], allow_slow_non_contiguous=True), t_.r, reads=[t_.r])
                    S.dma('sp', lambda e, t_=t_, sg_=sg_: e.dma_start(out=o_m.ap()[sg_, 1, :].rearrange('(p o) -> p o', o=1), in_=t_.a[32:40, 0:1], allow_slow_non_contiguous=True), t_.r, reads=[t_.r])
            S.op('act', lambda e: e.activation(BT.a[0:40, :], BT.a[0:40, :], AF.Exp), reads=[BT.r], writes=[BT.r])
            WF = None
            if not sample:
                WF = carve([64, NT], F32, 'WF')
                S.op('pool', lambda e: e.memset(WF.a, 0.0), writes=[WF.r])
                for s in range(nseq):
                    c0 = s * L
                    S.op('act', lambda e, c0=c0: e.activation(WF.a[0:8, c0:c0 + L], LI.a[0:8, c0:c0 + L], AF.Exp, bias=LF.a[0:8, c0 + L - 1:c0 + L], scale=1.0),
                         reads=[LI.r, LF.r], writes=[WF.r])
                    S.op('act', lambda e, c0=c0: e.activation(WF.a[32:40, c0:c0 + L], LI.a[32:40, c0:c0 + L], AF.Exp, bias=LF.a[32:40, c0:c0 + 1], scale=1.0),
                         reads=[LI.r, LF.r], writes=[WF.r])
            for t in range(ntile):
                pb = ps('x')
                srcs = [LI, BT] + ([WF] if WF is not None else [])
                for qi, src in enumerate(srcs):
                    S.op('pe', lambda e, qi=qi, src=src, t=t, pb=pb: e.transpose(pb.a[:, qi * 64:qi * 64 + 40], src.a[0:40, t * 128:(t + 1) * 128], ident_f[0:40, 0:40]),
                         reads=[src.r, KR], writes=[pb.r])
                nq_ = len(srcs)
                S.op('dve', lambda e, t=t, pb=pb, nq_=nq_: e.tensor_copy(TOK.a[:, t, 0:nq_, :].rearrange('p q (a h) -> p q a h', a=2),
                                                                        pb.a[:, 0:nq_ * 64].rearrange('p (q a h) -> p q a h', q=nq_, a=2)[:, :, :, 0:8]),
                     reads=[pb.r], writes=[TOK.r])

            S.op('dve', lambda e: e.tensor_scalar(TOK.a[:, :, 0, :], TOK.a[:, :, 0, :], float(np.log(0.125)), None, ALU.add), reads=[TOK.r], writes=[TOK.r])
            def _mk_mjob(c, s, e_, dr, jidx):
                    q0 = s * L
                    hd_ = 2 * c + e_
                    pbs = 64 * e_
                    hd = dr * 8 + hd_
                    row = dr * 32 + hd_
                    mask_t = maskF_t if dr == 0 else maskB_t
                    nbcs = {}

                    def pro(b0):
                        nb = min(512, L - b0)
                        S.op('dve', lambda e, b0=b0, nb=nb, hd=hd: e.tensor_scalar(RM.a[0:40, 0:nb], LF.a[0:40, q0 + b0:q0 + b0 + nb], oh_t[0:40, hd:hd + 1], None, ALU.mult),
                             reads=[LF.r, CR], writes=[RM.r])
                        pbc = ps('x')
                        S.op('pe', lambda e, nb=nb, pbc=pbc: e.matmul(pbc.a[:, 0:nb], ones_f[0:40, :], RM.a[0:40, 0:nb], start=True, stop=True),
                             reads=[RM.r, KR], writes=[pbc.r])
                        nbt = NBC[(b0 // 512) % 2] if sample else NBC[jidx % 2]
                        S.op('act', lambda e, nb=nb, pbc=pbc, nbt=nbt: e.copy(nbt.a[:, 0:nb], pbc.a[:, 0:nb]), reads=[pbc.r], writes=[nbt.r])
                        nbcs[b0] = nbt
                    kts = []
                    if sample:
                        kts.append(dict(noscore=True, virt=True, ns=64, pbase=pbs, vaug=VV.a[pbs:pbs + 64, dr, hd_ // 2, :], res=[VV.r]))
                    for j in range(tps):
                        kts.append(dict(kT=KB.a[pbs:pbs + 64, c, q0 + j * 128:q0 + (j + 1) * 128], ns=128, j=j,
                                        vaug=VB.a[:, seq_tile(s, j), hd_, :], res=[KB.r, VB.r]))

                    def valid(kt, i, dr=dr):
                        if sample:
                            if kt == 0:
                                return True
                            kt -= 1
                        return i >= kt if dr == 0 else i <= kt

                    def prob(kt, kd, c0, n, pscore, PT, dr=dr, hd=hd, pbs=pbs, c=c, q0=q0, nbcs=nbcs, mask_t=mask_t, s=s):
                        b0 = (c0 // 512) * 512
                        nbt = nbcs[b0]
                        lc = c0 - b0
                        ee = EE[wk.ptc % 2]
                        if kd.get('virt'):
                            S.op('act', lambda e: e.activation(ee.a[pbs:pbs + 64, 0:n], nbt.a[pbs:pbs + 64, lc:lc + n], AF.Exp, bias=m0bc_t[pbs:pbs + 64, hd:hd + 1], scale=1.0),
                                 reads=[nbt.r, CR], writes=[ee.r])
                            S.op('dve', lambda e: e.tensor_tensor(PT.a[pbs:pbs + 64, 0:n], ee.a[pbs:pbs + 64, 0:n], QB.a[pbs:pbs + 64, c, q0 + c0:q0 + c0 + n], ALU.mult),
                                 reads=[ee.r, QB.r], writes=[PT.r])
                            return
                        j = kd['j']
                        tl = seq_tile(s, j)
                        abias = TOK.a[:, tl, 0, hd:hd + 1]
                        dc = j * 128 - c0
                        m01 = mask01F_t if dr == 0 else mask01B_t
                        S.op('act', lambda e: e.activation(ee.a[:, 0:n], nbt.a[:, lc:lc + n], AF.Exp, bias=abias, scale=1.0), reads=[nbt.r, TOK.r], writes=[ee.r])
                        S.op('dve', lambda e: e.scalar_tensor_tensor(PT.a[:, 0:n], ee.a[:, 0:n], 0.125, pscore.a[:, 0:n], ALU.min, ALU.mult),
                             reads=[pscore.r, ee.r], writes=[PT.r])
                        if 0 <= dc < n:
                            S.op('dve', lambda e: e.tensor_tensor(PT.a[:, dc:dc + 128], PT.a[:, dc:dc + 128], m01[:], ALU.mult), reads=[PT.r, CR], writes=[PT.r])

                    def fin(i, po, por):
                        raise AssertionError('block finalize only')

                    def fin_blk(qts, pv, por, dr=dr, hd=hd, s=s, e_=e_, hd_=hd_):
                        i0, nq = qts[0], len(qts)
                        tl0 = seq_tile(s, i0)
                        t_ = smt()
                        den = pv[:, 0:nq, 64]
                        num = pv[:, 0:nq, 0:64]
                        S.op('dve', lambda e: e.tensor_tensor(t_.a[:, 0:nq], den, TOK.a[:, tl0:tl0 + nq, 1, hd], ALU.max), reads=[por, TOK.r], writes=[t_.r])
                        S.op('dve', lambda e: e.scalar_tensor_tensor(t_.a[:, 0:nq], den, -1.0, t_.a[:, 0:nq], ALU.mult, ALU.max), reads=[por, t_.r], writes=[t_.r])
                        S.op('dve', lambda e: e.reciprocal(t_.a[:, 0:nq], t_.a[:, 0:nq]), reads=[t_.r], writes=[t_.r])
                        rb = t_.a[:, 0:nq].unsqueeze(2).to_broadcast([128, nq, 64])
                        HFv = HF.a[:, i0:i0 + nq, :]
                        if dr == 0:
                            S.op('dve', lambda e: e.tensor_tensor(HFv, num, rb, ALU.mult), reads=[por, t_.r], writes=[HF.r])
                            return
                        sb_ = stg()
                        tv = sb_.a[:, 0:nq * 64].rearrange('p (t d) -> p t d', t=nq)
                        S.op('dve', lambda e: e.tensor_tensor(tv, num, rb, ALU.mult), reads=[por, t_.r], writes=[sb_.r])
                        S.op('dve', lambda e: e.tensor_tensor(HFv, HFv, tv, ALU.add), reads=[sb_.r, HF.r], writes=[HF.r])
                        if qts[-1] != tps - 1:
                            return
                        s_ = stg()
                        t8 = smt()
                        sv = s_.a[:, 0:tps * 64].rearrange('p (t d) -> p t d', t=tps)
                        S.op('dve', lambda e: e.tensor_tensor(sv, HF.a, HF.a, ALU.mult), reads=[HF.r], writes=[s_.r])
                        S.op('dve', lambda e: e.tensor_reduce(t8.a[:, 0:tps], sv, AX.X, ALU.add), reads=[s_.r], writes=[t8.r])
                        S.op('act', lambda e: e.activation(t8.a[:, 0:tps], t8.a[:, 0:tps], AF.Ln, bias=EPS, scale=1.0 / 64), reads=[t8.r], writes=[t8.r])
                        S.op('act', lambda e: e.activation(t8.a[:, 0:tps], t8.a[:, 0:tps], AF.Exp, scale=-0.5), reads=[t8.r], writes=[t8.r])
                        S.op('dve', lambda e: e.tensor_tensor(sv, HF.a, t8.a[:, 0:tps].unsqueeze(2).to_broadcast([128, tps, 64]), ALU.mult),
                             reads=[HF.r, t8.r], writes=[s_.r])
                        S.op('pool', lambda e: e.tensor_tensor(sv, sv, hgn_t[:, hd_ * 64:(hd_ + 1) * 64].unsqueeze(1).to_broadcast([128, tps, 64]), ALU.mult),
                             reads=[s_.r, CR], writes=[s_.r])
                        S.op('pool', lambda e: e.tensor_tensor(opair.a[:, s * tps:(s + 1) * tps, e_ * 64:(e_ + 1) * 64], sv,
                                                               OB.a[:, s * tps:(s + 1) * tps, hd_ * 64:(hd_ + 1) * 64], ALU.mult),
                             reads=[s_.r, OB.r], writes=[opair.r])


                    fin.blk = fin_blk

                    def run(after_block):
                        attn_job(wk, L, lambda c0, n: QB.a[pbs:pbs + 64, c, q0 + c0:q0 + c0 + n], [QB.r], kts, valid, prob, fin, after_block=after_block)
                    return dict(pro=pro, run=run)

            mjobs = []
            for c in range(4):
                for s in range(nseq):
                    for e_ in range(2):
                        for dr in range(2):
                            jb = _mk_mjob(c, s, e_, dr, len(mjobs))
                            jb['flush'] = (4 + c) if (s == nseq - 1 and e_ == 1 and dr == 1) else None
                            mjobs.append(jb)
            blocks0 = list(range(0, L, 512))
            for b0 in blocks0:
                mjobs[0]['pro'](b0)
            for k, jb in enumerate(mjobs):
                nxt = mjobs[k + 1] if k + 1 < len(mjobs) else None
                if sample:
                    jb['run'](lambda b0, nxt=nxt: nxt['pro'](b0) if nxt is not None else None)
                else:
                    if nxt is not None:
                        nxt['pro'](0)
                    jb['run'](None)
                if jb['flush'] is not None:
                    flush_pair(jb['flush'])

            if not sample:
                WVt = [carve([128, 8, 65], BF16, 'WV%d' % j) for j in range(tps)]
                for s in range(nseq):
                    sg_ = grp * 2 + s
                    for dr in range(2):
                        pcs = [ps('a'), ps('b')]
                        WVs = []
                        for j in range(tps):
                            tl = seq_tile(s, j)
                            wv = WVt[j]
                            wf = TOK.a[:, tl, 2, dr * 8:dr * 8 + 8].unsqueeze(2).to_broadcast([128, 8, 65])
                            S.op('dve', lambda e, wv=wv, tl=tl, wf=wf: e.tensor_tensor(wv.a, VB.a[:, tl, :, :], wf, ALU.mult), reads=[VB.r, TOK.r], writes=[wv.r])
                            WVs.append(wv)
                        for hh in range(8):
                            pc = pcs[hh // 4]
                            oc = (hh % 4) * 128
                            for j in range(tps):
                                tl = seq_tile(s, j)
                                S.op('pe', lambda e, hh=hh, j=j, tl=tl, pc=pc, oc=oc: e.matmul(pc.a[0:64, oc:oc + 65], KBT.a[:, tl, hh * 64:(hh + 1) * 64], WVs[j].a[:, hh, :],
                                                                                               start=(j == 0), stop=(j == tps - 1), skip_group_check=True),
                                     reads=[KBT.r, WVs[j].r], writes=[pc.r])
                        s_ = stg()
                        for half in range(2):
                            S.op('act', lambda e, half=half, s_=s_: e.activation(s_.a[0:64, half * 260:half * 260 + 260].rearrange('p (h v) -> p h v', h=4),
                                                                                 pcs[half].a[0:64, :].rearrange('p (h v) -> p h v', h=4)[:, :, 0:65], AF.Copy, scale=0.125),
                                 reads=[pcs[half].r], writes=[s_.r])
                        sv = s_.a[0:64, 0:520].rearrange('p (h v) -> p h v', h=8)
                        S.dma('sp', lambda e, sv=sv, sg_=sg_, dr=dr, s_=s_: e.dma_start(out=o_C.ap()[sg_, dr].rearrange('h k v -> k h v'), in_=sv[:, :, 0:64]), s_.r, reads=[s_.r])
                        S.dma('sp', lambda e, sv=sv, sg_=sg_, dr=dr, s_=s_: e.dma_start(out=o_n.ap()[sg_, dr].rearrange('h k -> k h'), in_=sv[:, :, 64], allow_slow_non_contiguous=True), s_.r, reads=[s_.r])
            if sample:
                ring.extend(rsv)
        else:
            nkt = (2 if sample else 0) + ntile
            koff = 256 if sample else 0
            QC = carve([128, 4, NT], BF16, 'QC')
            KC = carve([128, 4, koff + NT], BF16, 'KC')
            VC = carve([128, nkt, 8, 65], BF16, 'VC')
            QD = carve([128, 4, NT], BF16, 'QD')
            KD = carve([128, 2, koff + NT], BF16, 'KD')
            VD = carve([128, nkt, 2, 65], BF16, 'VD')
            S.op('pool', lambda e: e.memset(VC.a, 1.0), writes=[VC.r])
            S.op('pool', lambda e: e.memset(VD.a, 1.0), writes=[VD.r])
            if sample:
                load_ctx_kT(kcctx, 4, KC, False)
                load_ctx_v(vcctx, 8, VC, 0)
                load_ctx_kT(kdctx, 2, KD, True)
                load_ctx_v(vdctx, 2, VD, 0)
            W = wslab_in(mi, 0, 512)
            for c in range(4):
                def post(b, pb, c=c):
                    S.op('act', lambda e: e.copy(QC.a[:, c, b * 512:(b + 1) * 512], pb.a), reads=[pb.r], writes=[QC.r])
                proj_fm(W, c * 128, 128, hg, NT, post)
            W = wslab_in(mi, 512, 512)
            for c in range(4):
                def post(b, pb, c=c):
                    S.op('dve', lambda e: e.tensor_copy(KC.a[:, c, koff + b * 512:koff + (b + 1) * 512], pb.a), reads=[pb.r], writes=[KC.r])
                proj_fm(W, c * 128, 128, hg, NT, post)
            if not sample:
                def post(t, pb):
                    out_tm(pb, 512, o_kc, grp * 2 + t // tps, t % tps)
                proj_tm(W, 0, 512, hg, ntile, post)
            W = wslab_in(mi, 1024, 512)
            def post(t, pb):
                kt = (2 if sample else 0) + t
                S.op('dve', lambda e: e.tensor_copy(VC.a[:, kt, :, 0:64], pb.a.rearrange('p (h d) -> p h d', h=8)), reads=[pb.r], writes=[VC.r])
                if not sample:
                    out_tm(pb, 512, o_vc, grp * 2 + t // tps, t % tps)
            proj_tm(W, 0, 512, hg, ntile, post)
            W = wslab_in(mi, 1536, 512)
            for c in range(4):
                def post(b, pb, c=c):
                    if sample:
                        qn = tmp_sq[1]
                        S.op('act', lambda e: e.copy(qn.a, pb.a), reads=[pb.r], writes=[qn.r])
                        rope_fm(qn.a, qn.r, QD.a[:, c, b * 512:(b + 1) * 512], QD.r, b * 512)
                    else:
                        S.op('act', lambda e: e.copy(QD.a[:, c, b * 512:(b + 1) * 512], pb.a), reads=[pb.r], writes=[QD.r])
                proj_fm(W, c * 128, 128, hg, NT, post)
            W = wslab_in(mi, 2048, 512)
            for c in range(2):
                def post(b, pb, c=c):
                    if sample:
                        qn = tmp_sq[1]
                        S.op('act', lambda e: e.copy(qn.a, pb.a), reads=[pb.r], writes=[qn.r])
                        rope_fm(qn.a, qn.r, KD.a[:, c, koff + b * 512:koff + (b + 1) * 512], KD.r, b * 512)
                    else:
                        S.op('act', lambda e: e.copy(KD.a[:, c, b * 512:(b + 1) * 512], pb.a), reads=[pb.r], writes=[KD.r])
                proj_fm(W, c * 128, 128, hg, NT, post)
            if not sample:
                def post(t, pb):
                    out_tm(pb, 128, o_kd, grp * 2 + t // tps, t % tps)
                proj_tm(W, 256, 128, hg, ntile, post)
            def post(t, pb):
                kt = (2 if sample else 0) + t
                S.op('dve', lambda e: e.tensor_copy(VD.a[:, kt, :, 0:64], pb.a[:, 0:128].rearrange('p (h d) -> p h d', h=2)), reads=[pb.r], writes=[VD.r])
                if not sample:
                    out_tm(pb, 128, o_vd, grp * 2 + t // tps, t % tps)
            proj_tm(W, 384, 128, hg, ntile, post)

            import os as _os
            ksub = _os.environ.get('KSUB', '')
            if not sample:
                for c in range(4 if 'nomha' not in ksub else 0):
                    for s in range(nseq):
                        q0 = s * L
                        for e_ in range(2):
                            pbs = 64 * e_
                            hh = 2 * c + e_
                            kts = [dict(kT=KC.a[pbs:pbs + 64, c, q0 + j * 128:q0 + (j + 1) * 128], ns=128, vaug=VC.a[:, seq_tile(s, j), hh, :], res=[KC.r, VC.r])
                                   for j in range(tps)]
                            fs = fin_softmax(e_)
                            attn_job(wk, L, lambda c0, n, c=c, pbs=pbs, q0=q0: QC.a[pbs:pbs + 64, c, q0 + c0:q0 + c0 + n], [QC.r], kts,
                                     lambda kt, i: True, prob_exp, fs.bind(s * tps))
                        if s == nseq - 1:
                            flush_pair(c)
                for c in range(4 if 'nogqa' not in ksub else 0):
                    for s in range(nseq):
                        q0 = s * L
                        kv = c // 2
                        for e_ in range(2):
                            pbs = 64 * e_
                            hh = 2 * c + e_
                            kts = [dict(kT=KD.a[pbs:pbs + 64, kv, q0 + j * 128:q0 + (j + 1) * 128], ns=128, vaug=VD.a[:, seq_tile(s, j), kv, :], res=[KD.r, VD.r])
                                   for j in range(tps)]
                            fs = fin_softmax(e_, extra_den=esink_t[:, hh:hh + 1])
                            attn_job(wk, L, lambda c0, n, c=c, pbs=pbs, q0=q0: QD.a[pbs:pbs + 64, c, q0 + c0:q0 + c0 + n], [QD.r], kts,
                                     lambda kt, i: True, prob_exp, fs.bind(s * tps))
                        if s == nseq - 1:
                            flush_pair(4 + c)
            else:
                navalid = _na_rows()
                TB = [carve([128, 15, 64], F32, 'TB') for _ in range(2)]
                ARG = [carve([128, 512], F32, 'ARG') for _ in range(2)]
                for c in range(4 if 'nona' not in ksub else 0):
                    for e_ in range(2):
                        pbs = 64 * e_
                        hh = 2 * c + e_
                        tb = TB[hh % 2]
                        S.dma('sp', lambda e, tb=tb, hh=hh: e.dma_start(out=tb.a, in_=natb.ap()[hh].rearrange('p (a b) -> p a b', a=15)), tb.r, writes=[tb.r])
                        S.op('pool', lambda e, tb=tb: e.tensor_tensor(tb.a, tb.a, cmask_t[:].unsqueeze(1).to_broadcast([128, 15, 64]), ALU.add), reads=[tb.r, CR], writes=[tb.r])
                        kts = []
                        for j in range(2):
                            kts.append(dict(kT=KC.a[pbs:pbs + 64, c, j * 128:(j + 1) * 128], ns=128, vaug=VC.a[:, j, hh, :], res=[KC.r, VC.r], ctx=True))
                        for j in range(8):
                            kts.append(dict(kT=KC.a[pbs:pbs + 64, c, 256 + j * 128:256 + (j + 1) * 128], ns=128, vaug=VC.a[:, 2 + j, hh, :], res=[KC.r, VC.r], j=j))

                        def valid(kt, i):
                            if kt < 2:
                                return True
                            j = kt - 2
                            return any(navalid[2 * j + a][2 * i + b] for a in range(2) for b in range(2))

                        def prob(kt, kd, c0, n, pscore, PT, tb=tb):
                            if kd.get('ctx'):
                                return prob_exp(kt, kd, c0, n, pscore, PT)
                            j = kd['j']
                            S.op('pool', lambda e: e.memset(PT.a[:, 0:n], 0.0), writes=[PT.r])
                            arg = ARG[wk.ptc % 2]
                            r0 = c0 // 64
                            nr = n // 64
                            for a in range(2):
                                srow = 2 * j + a
                                rows = [r for r in range(r0, r0 + nr) if navalid[srow][r]]
                                if not rows:
                                    continue
                                rl, rh = min(rows), max(rows) + 1
                                cl, cn = (rl - r0) * 64, (rh - rl) * 64
                                dy0 = rl - srow + 7
                                pa = a * 64
                                S.op('dve', lambda e, pa=pa, cl=cl, cn=cn, dy0=dy0, rl=rl, rh=rh: e.scalar_tensor_tensor(
                                    arg.a[pa:pa + 64, cl:cl + cn], pscore.a[pa:pa + 64, cl:cl + cn], 0.125,
                                    tb.a[pa:pa + 64, dy0:dy0 + (rh - rl), :].rearrange('p a b -> p (a b)'), ALU.mult, ALU.add),
                                    reads=[pscore.r, tb.r], writes=[arg.r])
                                S.op('act', lambda e, pa=pa, cl=cl, cn=cn: e.activation(PT.a[pa:pa + 64, cl:cl + cn], arg.a[pa:pa + 64, cl:cl + cn], AF.Exp),
                                     reads=[arg.r], writes=[PT.r])
                        fs = fin_softmax(e_)
                        attn_job(wk, L, lambda c0, n, c=c, pbs=pbs: QC.a[pbs:pbs + 64, c, c0:c0 + n], [QC.r], kts, valid, prob,
                                 fs.bind(0))
                    flush_pair(c)
                for c in range(4 if 'noswa' not in ksub else 0):
                    kv = c // 2
                    for e_ in range(2):
                        pbs = 64 * e_
                        hh = 2 * c + e_
                        kts = []
                        for j in range(2):
                            kts.append(dict(kT=KD.a[pbs:pbs + 64, kv, j * 128:(j + 1) * 128], ns=128, vaug=VD.a[:, j, kv, :], res=[KD.r, VD.r], ctx=True))
                        for j in range(8):
                            kts.append(dict(kT=KD.a[pbs:pbs + 64, kv, 256 + j * 128:256 + (j + 1) * 128], ns=128, vaug=VD.a[:, 2 + j, kv, :], res=[KD.r, VD.r], j=j))

                        def valid(kt, i):
                            return True if kt < 2 else abs(i - (kt - 2)) <= 1

                        def prob(kt, kd, c0, n, pscore, PT):
                            if kd.get('ctx'):
                                return prob_exp(kt, kd, c0, n, pscore, PT)
                            j = kd['j']
                            arg = ARG[wk.ptc % 2]
                            for i in range(c0 // 128, (c0 + n) // 128):
                                lc = i * 128 - c0
                                if i == j:
                                    S.op('act', lambda e, lc=lc: e.activation(PT.a[:, lc:lc + 128], pscore.a[:, lc:lc + 128], AF.Exp, scale=0.125), reads=[pscore.r], writes=[PT.r])
                                else:
                                    mk_ = maskF_t if i == j - 1 else maskB_t
                                    S.op('dve', lambda e, lc=lc, mk_=mk_: e.scalar_tensor_tensor(arg.a[:, lc:lc + 128], pscore.a[:, lc:lc + 128], 0.125, mk_[:], ALU.mult, ALU.add),
                                         reads=[pscore.r, CR], writes=[arg.r])
                                    S.op('act', lambda e, lc=lc: e.activation(PT.a[:, lc:lc + 128], arg.a[:, lc:lc + 128], AF.Exp), reads=[arg.r], writes=[PT.r])
                        fs = fin_softmax(e_, extra_den=esink_t[:, hh:hh + 1])
                        attn_job(wk, L, lambda c0, n, c=c, pbs=pbs: QD.a[pbs:pbs + 64, c, c0:c0 + n], [QD.r], kts, valid, prob,
                                 fs.bind(0))
                    flush_pair(4 + c)

        mo = Wd[l]['mix_out']
        O1 = wslab_out(mo, 0, 4)
        O2 = wslab_out(mo, 512, 4)
        for i, tg in enumerate(tgs):
            for m in range(8):
                py = ps('a')
                for cc in range(8):
                    Ow = O1 if cc < 4 else O2
                    S.op('pe', lambda e, cc=cc, m=m, i=i, py=py, Ow=Ow: e.matmul(py.a, Ow.a[:, cc % 4, m * 128:(m + 1) * 128], oT.a[:, cc, i * 512:(i + 1) * 512],
                                                                                start=(cc == 0), stop=(cc == 7)),
                         reads=[Ow.r, oT.r], writes=[py.r])
                S.op('dve', lambda e, m=m, tg=tg, py=py: e.scalar_tensor_tensor(xap(m, tg), py.a, modG[l][:, 1, m, cnd:cnd + 1], xap(m, tg), ALU.mult, ALU.add),
                     reads=[py.r, modD_R[l][1], xres[m][tg]], writes=[xres[m][tg]])

    import os
    parts = os.environ.get('KPARTS', 'all')

    def on(p):
        return parts == 'all' or p in parts.split(',')
    load_x()
    adaln_slabs(0, 0, 6)
    adaln_finish(0, (0,))
    for l in range(2):
        if l == 0:
            if on('f01'):
                ffn(0, 1, between=lambda i: adaln_slabs(0, 6 + 2 * i, 8 + 2 * i))
            else:
                adaln_slabs(0, 6, 18)
            adaln_finish(0, (1, 2))
        else:
            if on('f11'):
                ffn(1, 1)
        for grp in range(3):
            if on('m%d%d' % (l, grp)):
                mixer(l, grp)
        if l == 0:
            if on('f02'):
                ffn(0, 2, between=lambda i: adaln_slabs(1, 3 * i, 3 * i + 3))
            else:
                adaln_slabs(1, 0, 18)
            adaln_finish(1)
        else:
            if on('f12'):
                ffn(1, 2)
    final_out()
    S.emit()


_PROG = {}


def _prep_weights(inp):
    sh = {}
    for l in range(2):
        sh['ada_w%d' % l] = np.ascontiguousarray(inp['ada_w_l%d' % l], dtype=np.float32)
        sh['ada_b%d' % l] = np.ascontiguousarray(inp['ada_b_l%d' % l].reshape(72, 128).T, dtype=np.float32)
        sh['norm%d' % l] = np.ascontiguousarray(inp['norm_l%d' % l].reshape(3, 8, 128).transpose(2, 0, 1), dtype=np.float32)
        for f in (1, 2):
            sh['f%din%d' % (f, l)] = np.ascontiguousarray(inp['ffn%d_in_l%d' % (f, l)], dtype=np.float32)
            sh['f%dout%d' % (f, l)] = np.ascontiguousarray(inp['ffn%d_out_l%d' % (f, l)], dtype=np.float32)
        sh['mixout%d' % l] = np.ascontiguousarray(inp['mix_out_l%d' % l], dtype=np.float32)
    w = np.asarray(inp['mix_in_l0'], dtype=np.float32)
    qa, ka, va = w[:, 0:512], w[:, 512:640], w[:, 640:768]
    qb, kb, vb = w[:, 768:1280], w[:, 1280:1792], w[:, 1792:2304]
    gt, ob = w[:, 2304:2336], w[:, 2336:2848]
    z = np.zeros((D, 24), np.float32)
    g1 = np.concatenate([gt[:, 0:8], z, gt[:, 16:24], z], 1)
    g2 = np.concatenate([gt[:, 8:16], z, gt[:, 24:32], z], 1)
    kadup = np.concatenate([ka[:, 0:64], ka[:, 0:64], ka[:, 64:128], ka[:, 64:128]], 1)
    m0_ = np.concatenate([qa, kadup, g1, g2, va, qb, kb, vb, ob, ka, np.zeros((D, L0_COLS - 3200), np.float32)], 1)
    assert m0_.shape[1] == L0_COLS
    sh['mixin0'] = np.ascontiguousarray(m0_)
    w = np.asarray(inp['mix_in_l1'], dtype=np.float32)
    qc, kc, vc, qd, kd, vd = w[:, 0:512], w[:, 512:1024], w[:, 1024:1536], w[:, 1536:2048], w[:, 2048:2176], w[:, 2176:2304]
    kddup = np.concatenate([kd[:, 0:64], kd[:, 0:64], kd[:, 64:128], kd[:, 64:128]], 1)
    m1_ = np.concatenate([qc, kc, vc, qd, kddup, kd, vd], 1)
    assert m1_.shape[1] == L1_COLS
    sh['mixin1'] = np.ascontiguousarray(m1_)
    sh['gfin'] = np.ascontiguousarray(np.asarray(inp['norm_final'], np.float32).reshape(8, 128).T)
    qk = np.asarray(inp['qk_norm_l0'], np.float32)
    sh['qkg'] = np.ascontiguousarray(np.stack([np.tile(qk[0], 2), np.tile(qk[1], 2)], 1))
    sh['gkbc'] = np.ascontiguousarray(qk[1])
    gb = np.asarray(inp['gate_bias_l0'], np.float32)
    gbt = np.zeros((64, 2), np.float32)
    gbt[0:8, 0] = gb[0:8]
    gbt[32:40, 0] = gb[16:24]
    gbt[0:8, 1] = gb[8:16]
    gbt[32:40, 1] = gb[24:32]
    sh['gbias'] = gbt
    sh['hgn'] = np.ascontiguousarray(inp['head_norm_l0'], dtype=np.float32)
    sh['sink'] = np.ascontiguousarray(inp['sink_l1'], dtype=np.float32)
    rpb = np.asarray(inp['rpb_l1'], np.float32)
    sc = np.arange(64)[:, None]
    qc_ = np.arange(64)[None, :]
    dx = np.clip(sc - qc_ + 15, 0, 30)
    tb = np.zeros((8, 128, 15, 64), np.float32)
    for dyi in range(15):
        blk = rpb[:, 14 - dyi, :][:, dx]
        tb[:, 0:64, dyi, :] = blk
        tb[:, 64:128, dyi, :] = blk
    sh['natb'] = np.ascontiguousarray(tb.reshape(8, 128, 15 * 64))
    for k, v in _consts().items():
        sh['c_' + k] = v
    return sh


def kernel(**inp):
    inp = {k: np.asarray(v) for k, v in inp.items()}
    dbg = inp.pop('_dbg', None)
    key = 'main'
    if key not in _PROG:
        _PROG[key] = build_program(None)
    nc = _PROG[key]
    sh = _prep_weights(inp)
    in_maps = []
    for i in range(8):
        b = i // 4
        m = dict(sh)
        xp = inp['x_prompt'][4 * i:4 * i + 4].reshape(1024, D)
        xs = inp['x_sample'][b]
        m['xin'] = np.ascontiguousarray(np.concatenate([xp, xs], 0), dtype=np.float32)
        cond = np.stack([inp['c_ctx'], inp['c'][b]], 0).astype(np.float32)
        m['condT'] = np.ascontiguousarray(cond.reshape(2, 8, 128).transpose(2, 1, 0))
        m['kctx0'] = np.ascontiguousarray(inp['cache_l0_attn_k'][b].reshape(256, 128), dtype=np.float32)
        m['vctx0'] = np.ascontiguousarray(inp['cache_l0_attn_v'][b].reshape(256, 128), dtype=np.float32)
        m['C0'] = np.ascontiguousarray(inp['state_l0_mlstm_C'][b], dtype=np.float32)
        m['n0'] = np.ascontiguousarray(inp['state_l0_mlstm_n'][b], dtype=np.float32)
        m['m0'] = np.ascontiguousarray(inp['state_l0_mlstm_m'][b].reshape(16), dtype=np.float32)
        m['kcctx'] = np.ascontiguousarray(inp['cache_l1_na_k'][b].reshape(256, 512), dtype=np.float32)
        m['vcctx'] = np.ascontiguousarray(inp['cache_l1_na_v'][b].reshape(256, 512), dtype=np.float32)
        m['kdctx'] = np.ascontiguousarray(inp['cache_l1_swa_k'][b].reshape(256, 128), dtype=np.float32)
        m['vdctx'] = np.ascontiguousarray(inp['cache_l1_swa_v'][b].reshape(256, 128), dtype=np.float32)
        in_maps.append(m)
    import os
    ncore = int(os.environ.get('KCORES', '8'))
    res = run_bass_kernel_spmd(nc, in_maps[:ncore], core_ids=list(range(ncore)))
    R = list(res.results)
    while len(R) < 8:
        R.append(R[0])
    cat = lambda k: np.concatenate([np.asarray(R[i][k]) for i in range(8)], 0)
    y_prompt = cat('o_yp').reshape(32, 256, D)
    y_sample = np.stack([np.asarray(R[0]['o_ys']), np.asarray(R[4]['o_ys'])], 0)
    k0 = cat('o_k0').reshape(32, 256, 2, 64)
    v0 = cat('o_v0').reshape(32, 256, 2, 64)
    Cst = cat('o_C')
    nst = cat('o_n')
    mst = cat('o_m')
    kc1 = cat('o_kc').reshape(32, 256, 8, 64)
    vc1 = cat('o_vc').reshape(32, 256, 8, 64)
    kd1 = cat('o_kd').reshape(32, 256, 2, 64)
    vd1 = cat('o_vd').reshape(32, 256, 2, 64)
    outs = (y_prompt, y_sample, k0, v0, Cst, nst, mst, kc1, vc1, kd1, vd1)
    return tuple(np.ascontiguousarray(o, dtype=np.float32) for o in outs)
```

```python
import contextlib
import numpy as np
import concourse.bass as bass
import concourse.mybir as mybir
from concourse.bass_utils import run_bass_kernel_spmd

F32 = mybir.dt.float32
BF16 = mybir.dt.bfloat16
AF = mybir.ActivationFunctionType
ALU = mybir.AluOpType
AX = mybir.AxisListType

ENGS = ('pe', 'act', 'dve', 'pool', 'sp')
NEG = -1.0e30
EPS = 1e-6


class Res:
    __slots__ = ('name', 'w', 'r', 'sem', 'dcount', 'excl')

    def __init__(self, name=''):
        self.name = name
        self.excl = False
        self.w = {}
        self.r = {}
        self.sem = None
        self.dcount = 0


class _Rec:
    def __init__(self):
        self.call = None

    def __getattr__(self, name):
        def f(*a, **k):
            self.call = (name, a, k)
            return self
        return f


class Ins:
    __slots__ = ('fn', 'waits', 'milestone', 'dma_res')

    def __init__(self, fn):
        rec = _Rec()
        fn(rec)
        name, a, k = rec.call
        self.fn = lambda eh: getattr(eh, name)(*a, **k)
        self.waits = []
        self.milestone = False
        self.dma_res = None


class Sched:
    def __init__(self, nc, stack):
        self.nc = nc
        self.stack = stack
        self.streams = {e: [] for e in ENGS}
        self.known = {e: {} for e in ENGS}
        self.dma_res = []
        self.semof = {}
        self.esem = {}
        for e in ('pe', 'act', 'dve', 'pool'):
            self.esem[e] = stack.enter_context(nc.semaphore('es_' + e))

    def _waits(self, ins, eng, deps):
        for key, val in deps.items():
            if self.known[eng].get(key, -1) >= val:
                continue
            self.known[eng][key] = val
            ins.waits.append((key, val))
            if key[0] == 'e':
                self.streams[key[1]][val].milestone = True

    def _deps(self, eng, reads, writes):
        deps = {}

        def add(d, raw):
            for k, v in d.items():
                if k[0] == 'e' and k[1] == eng and (eng == 'pe' or not raw):
                    continue
                if deps.get(k, -1) < v:
                    deps[k] = v
        for r in reads:
            add(r.w, True)
            if r.excl:
                add({k: v for k, v in r.r.items() if not (k[0] == 'e' and k[1] == eng)}, False)
        for r in writes:
            add(r.w, False)
            add(r.r, False)
        return deps

    def op(self, eng, fn, reads=(), writes=()):
        ins = Ins(fn)
        idx = len(self.streams[eng])
        self._waits(ins, eng, self._deps(eng, reads, writes))
        key = ('e', eng)
        for r in writes:
            r.w = {key: idx}
            r.r = {}
        for r in reads:
            if r not in writes:
                r.r[key] = idx
        self.streams[eng].append(ins)
        return ins

    def dma(self, eng, fn, sres, reads=(), writes=()):
        ins = Ins(fn)
        ins.dma_res = sres
        if sres.sem is None:
            sres.sem = self.stack.enter_context(self.nc.semaphore('ds_%d' % len(self.dma_res)))
            self.dma_res.append(sres)
            self.semof[id(sres)] = sres
        self._waits(ins, eng, self._deps(eng, reads, writes))
        sres.dcount += 1
        key = ('d', id(sres))
        val = 16 * sres.dcount
        for r in writes:
            r.w = {key: val}
            r.r = {}
        for r in reads:
            if r not in writes:
                r.r[key] = val
        self.streams[eng].append(ins)
        return ins

    def emit(self):
        nc = self.nc
        ordinal = {}
        for e in ('pe', 'act', 'dve', 'pool'):
            c = 0
            for i, ins in enumerate(self.streams[e]):
                if ins.milestone:
                    c += 1
                    ordinal[(e, i)] = c

        def run(eng, eh):
            for i, ins in enumerate(self.streams[eng]):
                for key, val in ins.waits:
                    if key[0] == 'e':
                        eh.wait_ge(self.esem[key[1]], ordinal[(key[1], val)])
                    else:
                        eh.wait_ge(self.semof[key[1]].sem, val)
                bi = ins.fn(eh)
                if ins.dma_res is not None:
                    bi.then_inc(ins.dma_res.sem, 16)
                elif ins.milestone:
                    bi.then_inc(self.esem[eng], 1)
            if eng == 'sp':
                for r in self.dma_res:
                    eh.wait_ge(r.sem, 16 * r.dcount)

        with nc.Block() as block:
            @block.tensor
            def _(e):
                run('pe', e)

            @block.scalar
            def _(e):
                run('act', e)

            @block.vector
            def _(e):
                run('dve', e)

            @block.gpsimd
            def _(e):
                run('pool', e)

            @block.sync
            def _(e):
                run('sp', e)


class Tl:
    __slots__ = ('a', 'r')

    def __init__(self, a, r):
        self.a = a
        self.r = r


D = 1024
DFF = 2816
NTOK = 2048
L0_COLS = 3328
L1_COLS = 2560


def _rope_tables():
    t = np.arange(1024)
    pos = np.stack([t // 64, t % 64], -1).astype(np.float32)
    freqs = (10000.0 ** (-np.arange(16, dtype=np.float32) / 16)).astype(np.float32)
    ang = (pos[:, :, None] * freqs).reshape(1024, 32).astype(np.float32)
    cos = np.cos(ang).astype(np.float32).T
    sin = np.sin(ang).astype(np.float32).T
    C = np.concatenate([cos, cos, cos, cos], 0)
    Sg = np.concatenate([-sin, sin, -sin, sin], 0)
    return np.ascontiguousarray(C), np.ascontiguousarray(Sg)


def _consts():
    c = {}
    C, Sg = _rope_tables()
    c['ropeC'] = C
    c['ropeS'] = Sg
    s = np.arange(128)[:, None]
    t = np.arange(128)[None, :]
    c['maskF'] = np.where(t >= s, 0.0, NEG).astype(np.float32)
    c['maskB'] = np.where(t <= s, 0.0, NEG).astype(np.float32)
    c['mask01F'] = (t >= s).astype(np.float32)
    c['mask01B'] = (t <= s).astype(np.float32)
    psw = np.zeros((128, 128), np.float32)
    for m in range(128):
        d = m % 64
        k = (m - d) + ((d + 32) % 64)
        psw[k, m] = 1.0
    c['psw'] = psw
    oh = np.zeros((64, 16), np.float32)
    for h in range(8):
        oh[h, h] = 1.0
        oh[32 + h, 8 + h] = 1.0
    c['oh'] = oh
    sc = np.arange(64)[:, None]
    qc = np.arange(64)[None, :]
    ws = np.clip(qc - 8, 0, 48)
    cm = np.where((sc >= ws) & (sc < ws + 16), 0.0, NEG).astype(np.float32)
    c['cmask'] = np.concatenate([cm, cm], 0)
    return c


def _na_rows():
    start = [min(max(r - 4, 0), 8) for r in range(16)]
    valid = [[start[r] <= s < start[r] + 8 for r in range(16)] for s in range(16)]
    return valid


def build_program(dbg=None):
    nc = bass.Bass("TRN2", target_bir_lowering=False)
    st = contextlib.ExitStack()
    with st:
        _build(nc, st, dbg)
    return nc


def _build(nc, st, dbg):
    S = Sched(nc, st)

    def din(name, shape):
        return nc.dram_tensor(name, list(shape), F32, kind="ExternalInput")

    def dout(name, shape):
        return nc.dram_tensor(name, list(shape), F32, kind="ExternalOutput")

    xin = din('xin', [NTOK, D])
    condT = din('condT', [128, 8, 2])
    gfin = din('gfin', [128, 8])
    Wd = []
    for l in range(2):
        w = {}
        w['ada_w'] = din('ada_w%d' % l, [D, 9 * D])
        w['ada_b'] = din('ada_b%d' % l, [128, 72])
        w['norm'] = din('norm%d' % l, [128, 3, 8])
        for f in (1, 2):
            w['f%din' % f] = din('f%din%d' % (f, l), [D, 2 * DFF])
            w['f%dout' % f] = din('f%dout%d' % (f, l), [DFF, D])
        w['mix_in'] = din('mixin%d' % l, [D, L0_COLS if l == 0 else L1_COLS])
        w['mix_out'] = din('mixout%d' % l, [D, D])
        Wd.append(w)
    qkg = din('qkg', [128, 2])
    gkbc = din('gkbc', [64])
    gbias = din('gbias', [64, 2])
    hgn = din('hgn', [512])
    kctx0 = din('kctx0', [256, 128])
    vctx0 = din('vctx0', [256, 128])
    C0 = din('C0', [2, 8, 64, 64])
    n0 = din('n0', [2, 8, 64])
    m0 = din('m0', [16])
    kcctx = din('kcctx', [256, 512])
    vcctx = din('vcctx', [256, 512])
    kdctx = din('kdctx', [256, 128])
    vdctx = din('vdctx', [256, 128])
    natb = din('natb', [8, 128, 15 * 64])
    sink = din('sink', [8])
    cd = {k: din('c_' + k, v.shape) for k, v in _consts().items()}

    o_yp = dout('o_yp', [1024, D])
    o_ys = dout('o_ys', [1024, D])
    o_k0 = dout('o_k0', [4, 256, 128])
    o_v0 = dout('o_v0', [4, 256, 128])
    o_C = dout('o_C', [4, 2, 8, 64, 64])
    o_n = dout('o_n', [4, 2, 8, 64])
    o_m = dout('o_m', [4, 2, 8])
    o_kc = dout('o_kc', [4, 256, 512])
    o_vc = dout('o_vc', [4, 256, 512])
    o_kd = dout('o_kd', [4, 256, 128])
    o_vd = dout('o_vd', [4, 256, 128])
    dbg_out = {}
    if dbg:
        for name, shape in dbg.items():
            dbg_out[name] = dout('dbg_' + name, shape)

    cnt = [0]

    def sb(shape, dt, name=None):
        cnt[0] += 1
        t = st.enter_context(nc.sbuf_tensor(name or ('t%d' % cnt[0]), list(shape), dt))
        return t

    def mk(shape, dt, name=None):
        t = sb(shape, dt, name)
        return Tl(t[:], Res(name or ''))

    banks = []
    for i in range(8):
        t = st.enter_context(nc.psum_tensor('bank%d' % i, [128, 512], F32))
        banks.append(Tl(t[:], Res('bank%d' % i)))
        banks[-1].r.excl = True
    rot = {'a': [0, 1], 'b': [2, 3], 'c': [4, 5], 'x': [6, 7], 's': [0, 1, 2, 3]}
    rotc = {k: 0 for k in rot}

    def ps(tag):
        i = rot[tag][rotc[tag] % len(rot[tag])]
        rotc[tag] += 1
        return banks[i]

    AR_BYTES = 94 * 1024
    arena = sb([128, AR_BYTES // 2], BF16, 'arena')
    ar = {'off': 0, 'live': [], 'inherit': {}}

    def ar_reset():
        toks = dict(ar['inherit'])
        for r in ar['live']:
            for d in (r.w, r.r):
                for k, v in d.items():
                    if toks.get(k, -1) < v:
                        toks[k] = v
        ar['inherit'] = toks
        ar['live'] = []
        ar['off'] = 0

    def ar_res(name=''):
        r = Res(name)
        r.w = dict(ar['inherit'])
        ar['live'].append(r)
        return r

    def carve(shape, dt, name=''):
        n = int(np.prod(shape[1:]))
        nb = n * (4 if dt == F32 else 2)
        nb = (nb + 31) // 32 * 32
        off = ar['off']
        assert off + nb <= AR_BYTES, ('arena overflow', name, off, nb)
        ar['off'] = off + nb
        a = arena[:, off // 2: off // 2 + (n * (2 if dt == F32 else 1))]
        if dt == F32:
            a = a.bitcast(F32)
        if shape[0] != 128:
            a = a[0:shape[0]]
        if len(shape) > 2:
            names = ' '.join('d%d' % i for i in range(1, len(shape)))
            kw = {'d%d' % i: shape[i] for i in range(1, len(shape) - 1)}
            a = a.rearrange('p (%s) -> p %s' % (names, names), **kw)
        return Tl(a, ar_res(name))

    xT = sb([128, 8, NTOK], F32, 'xT')
    xres = [[Res('x%d_%d' % (m, tg)) for tg in range(4)] for m in range(8)]

    def xap(m, tg):
        return xT[:, m, tg * 512:(tg + 1) * 512]

    NRING = 4
    ring_all = [mk([128, 4096], BF16, 'ring%d' % i) for i in range(NRING)]
    ring = list(ring_all)
    ringc = [0]

    def ring_next():
        s_ = ring[ringc[0] % len(ring)]
        ringc[0] += 1
        return s_

    def wslab_in(wd, c0, n):
        s = ring_next()
        dst = s.a[:, 0:8 * n].rearrange('p (k n) -> p k n', k=8)
        src = wd.ap()[:, c0:c0 + n].rearrange('(k p) n -> p k n', p=128)
        S.dma('pool', lambda e: e.dma_start(out=dst, in_=src), s.r, writes=[s.r])
        return Tl(dst, s.r)

    def wslab_out(wd, r0, nchunk):
        s = ring_next()
        dst = s.a[:, 0:nchunk * 1024].rearrange('p (c n) -> p c n', c=nchunk)
        src = wd.ap()[r0:r0 + 128 * nchunk, :].rearrange('(c p) n -> p c n', p=128)
        S.dma('pool', lambda e: e.dma_start(out=dst, in_=src), s.r, writes=[s.r])
        return Tl(dst, s.r)

    CR = Res('consts')

    def cload(dram, shape, dt=F32, src=None, name=None):
        t = sb(shape, dt, name)
        a = src if src is not None else dram.ap()
        S.dma('sp', lambda e: e.dma_start(out=t[:], in_=a), CR, writes=[CR])
        return t

    ident_f = sb([128, 128], F32, 'ident_f')
    ident_b = sb([128, 128], BF16, 'ident_b')
    ones_b = sb([128, 128], BF16, 'ones_b')
    bd_b = sb([128, 128], BF16, 'bd_b')
    ones_f = sb([64, 128], F32, 'ones_f')
    KR = Res('kconst')
    S.op('pool', lambda e: e.memset(ident_f[:], 0.0), writes=[KR])
    S.op('pool', lambda e: e.affine_select(out=ident_f[:], in_=ident_f[:], pattern=[[-1, 128]], compare_op=ALU.not_equal,
                                           fill=1.0, base=0, channel_multiplier=1), reads=[KR], writes=[KR])
    S.op('pool', lambda e: e.tensor_copy(ident_b[:], ident_f[:]), reads=[KR], writes=[KR])
    S.op('pool', lambda e: e.memset(ones_b[:], 1.0), writes=[KR])
    S.op('pool', lambda e: e.memset(ones_f[:], 1.0), writes=[KR])
    S.op('pool', lambda e: e.memset(bd_b[:], 0.0), writes=[KR])
    S.op('pool', lambda e: e.memset(bd_b[0:64, 0:64], 1.0), reads=[KR], writes=[KR])
    S.op('pool', lambda e: e.memset(bd_b[64:128, 64:128], 1.0), reads=[KR], writes=[KR])

    cond_t = cload(condT, [128, 8, 2])
    gfin_t = cload(gfin, [128, 8])
    adab_t = [cload(Wd[l]['ada_b'], [128, 72]) for l in range(2)]
    norm_t = [cload(Wd[l]['norm'], [128, 3, 8]) for l in range(2)]
    qkg_t = cload(qkg, [128, 2])
    gkbc_t = cload(gkbc, [128, 64], src=gkbc.ap().partition_broadcast(128))
    gbias_t = cload(gbias, [64, 2])
    hgn_t = cload(hgn, [128, 512], src=hgn.ap().partition_broadcast(128))
    m0bc_t = cload(m0, [128, 16], src=m0.ap().partition_broadcast(128))
    m0col_t = sb([64, 1], F32, 'm0col')
    S.dma('sp', lambda e: e.dma_start(out=m0col_t[0:8, :], in_=m0.ap()[0:8].rearrange('(p o) -> p o', o=1)), CR, writes=[CR])
    S.dma('sp', lambda e: e.dma_start(out=m0col_t[32:40, :], in_=m0.ap()[8:16].rearrange('(p o) -> p o', o=1)), CR, writes=[CR])
    sink_t = cload(sink, [128, 8], src=sink.ap().partition_broadcast(128))
    ropeC_t = sb([128, 1024], BF16, 'ropeC')
    ropeS_t = sb([128, 1024], BF16, 'ropeS')
    S.dma('pool', lambda e: e.dma_start(out=ropeC_t[:], in_=cd['ropeC'].ap()), CR, writes=[CR])
    S.dma('pool', lambda e: e.dma_start(out=ropeS_t[:], in_=cd['ropeS'].ap()), CR, writes=[CR])
    maskF_t = cload(cd['maskF'], [128, 128])
    maskB_t = cload(cd['maskB'], [128, 128])
    mask01F_t = cload(cd['mask01F'], [128, 128])
    mask01B_t = cload(cd['mask01B'], [128, 128])
    psw_f = cload(cd['psw'], [128, 128])
    oh_t = cload(cd['oh'], [64, 16])
    cmask_t = cload(cd['cmask'], [128, 64])
    psw_b = sb([128, 128], BF16, 'psw_b')
    S.op('pool', lambda e: e.tensor_copy(psw_b[:], psw_f[:]), reads=[CR], writes=[KR])
    esink_t = sb([128, 8], F32, 'esink')
    S.op('act', lambda e: e.activation(esink_t[:], sink_t[:], AF.Exp), reads=[CR], writes=[KR])
    ngbias_t = sb([64, 1], F32, 'ngbias')
    S.op('pool', lambda e: e.tensor_scalar(ngbias_t[:], gbias_t[:, 1:2], -1.0, None, ALU.mult), reads=[CR], writes=[KR])
    CK = [CR, KR]

    scond = sb([128, 8, 2], BF16, 'scond')
    S.op('act', lambda e: e.activation(scond[:], cond_t[:], AF.Silu), reads=[CR], writes=[KR])
    mod_t = [sb([128, 72, 2], F32, 'mod%d' % l) for l in range(2)]
    modT_R = [[Res('modT') for n in range(3)] for l in range(2)]
    modD_R = [[Res('modD') for n in range(3)] for l in range(2)]
    modA = [sb([128, 3, 8, 2], F32, 'modA%d' % l) for l in range(2)]
    modG = [sb([128, 3, 8, 2], F32, 'modG%d' % l) for l in range(2)]

    def adaln_slabs(l, s0, s1):
        for s in range(s0, s1):
            W = wslab_in(Wd[l]['ada_w'], 512 * s, 512)
            pb = ps('x')
            for c in range(4):
                for k in range(8):
                    S.op('pe', lambda e, c=c, k=k, W=W, pb=pb: e.matmul(pb.a[:, 2 * c:2 * c + 2], W.a[:, k, c * 128:(c + 1) * 128],
                                                                      scond[:, k, :], start=(k == 0), stop=(k == 7)),
                         reads=[W.r, KR], writes=[pb.r])
            dstv = mod_t[l][:, 4 * s:4 * s + 4, :]
            bv = adab_t[l][:, 4 * s:4 * s + 4].unsqueeze(2).to_broadcast([128, 4, 2])
            S.op('dve', lambda e, pb=pb, dstv=dstv, bv=bv: e.tensor_tensor(dstv, pb.a[:, 0:8].rearrange('p (c j) -> p c j', c=4), bv, ALU.add),
                 reads=[pb.r, CR], writes=[modT_R[l][s // 6]])

    def adaln_finish(l, ns=(0, 1, 2)):
        for n in ns:
            sc = mod_t[l][:, (3 * n + 1) * 8:(3 * n + 2) * 8, :]
            g = mod_t[l][:, (3 * n + 2) * 8:(3 * n + 3) * 8, :]
            gn = norm_t[l][:, n, :].unsqueeze(2).to_broadcast([128, 8, 2])
            S.op('dve', lambda e, n=n, sc=sc, gn=gn: e.scalar_tensor_tensor(modA[l][:, n, :, :], sc, 1.0, gn, ALU.add, ALU.mult),
                 reads=[modT_R[l][n], CR], writes=[modD_R[l][n]])
            fac = 1.0 if n == 1 else 0.5
            S.op('dve', lambda e, n=n, g=g, fac=fac: e.tensor_scalar(modG[l][:, n, :, :], g, fac, None, ALU.mult),
                 reads=[modT_R[l][n], modD_R[l][n]], writes=[modD_R[l][n]])

    def modB(l, n, m, cnd):
        return mod_t[l][:, (3 * n) * 8 + m, cnd:cnd + 1]

    def tg_cond(tg):
        return 0 if tg < 2 else 1

    def rms_stats(tg, tmp_sq, rs):
        pb = ps('x')
        for m in range(8):
            sq = tmp_sq[m % 2]
            S.op('act', lambda e, m=m, sq=sq: e.activation(sq.a, xap(m, tg), AF.Square), reads=[xres[m][tg]], writes=[sq.r])
            S.op('pe', lambda e, m=m, sq=sq, pb=pb: e.matmul(pb.a, ones_b[:], sq.a, start=(m == 0), stop=(m == 7)),
                 reads=[sq.r, KR], writes=[pb.r])
        S.op('act', lambda e, pb=pb: e.activation(rs.a, pb.a, AF.Sqrt, bias=EPS, scale=1.0 / D), reads=[pb.r], writes=[rs.r])
        S.op('dve', lambda e: e.reciprocal(rs.a, rs.a), reads=[rs.r], writes=[rs.r])

    def norm_mod(l, n, tg, hdst, hcol0, tmp_sq, rs, tmpf):
        cnd = tg_cond(tg)
        rms_stats(tg, tmp_sq, rs)
        for m in range(8):
            tf = tmpf[m % 2]
            S.op('dve', lambda e, m=m, tf=tf: e.scalar_tensor_tensor(tf.a, xap(m, tg), modA[l][:, n, m, cnd:cnd + 1], rs.a, ALU.mult, ALU.mult),
                 reads=[xres[m][tg], rs.r, modD_R[l][n]], writes=[tf.r])
            S.op('act', lambda e, m=m, tf=tf: e.activation(hdst.a[:, m, hcol0:hcol0 + 512], tf.a, AF.Identity, bias=modB(l, n, m, cnd), scale=1.0),
                 reads=[tf.r, modT_R[l][n]], writes=[hdst.r])

    def load_x():
        ar_reset()
        stg = [carve([128, 1024], F32, 'xstg%d' % i) for i in range(4)]
        for tg in range(4):
            for j in range(4):
                t = tg * 4 + j
                S.dma('sp', lambda e, j=j, t=t: e.dma_start(out=stg[j].a, in_=xin.ap()[t * 128:(t + 1) * 128, :]), stg[j].r, writes=[stg[j].r])
            for m in range(8):
                pb = ps('a' if m % 2 == 0 else 'b')
                for j in range(4):
                    S.op('pe', lambda e, j=j, m=m, pb=pb: e.transpose(pb.a[:, j * 128:(j + 1) * 128], stg[j].a[:, m * 128:(m + 1) * 128], ident_f[:]),
                         reads=[stg[j].r, KR], writes=[pb.r])
                if m % 2 == 0:
                    S.op('dve', lambda e, m=m, pb=pb: e.tensor_copy(xap(m, tg), pb.a), reads=[pb.r], writes=[xres[m][tg]])
                else:
                    S.op('act', lambda e, m=m, pb=pb: e.copy(xap(m, tg), pb.a), reads=[pb.r], writes=[xres[m][tg]])

    def ffn(l, which, between=None, tail=False):
        n = 0 if which == 1 else 2
        ar_reset()
        h = carve([128, 8, NTOK], BF16, 'h')
        hres = [ar_res('h%d' % tg) for tg in range(4)]
        hid = carve([128, 4, NTOK], BF16, 'hid')
        hidres = [[ar_res('hid') for tg in range(4)] for c in range(4)]
        tmp_sq = [carve([128, 512], BF16, 'sq') for _ in range(2)]
        rs = carve([128, 512], F32, 'rs')
        tmpf = [carve([128, 512], F32, 'tmpf') for _ in range(2)]
        sgt = [carve([128, 512], F32, 'sg') for _ in range(2)]
        fbufs = final_alloc() if tail else None
        for tg in range(4):
            norm_mod(l, n, tg, Tl(h.a, hres[tg]), tg * 512, tmp_sq, rs, tmpf)
        win = Wd[l]['f%din' % which]
        wout = Wd[l]['f%dout' % which]
        for i in range(6):
            nch = 4 if i < 5 else 2
            G = wslab_in(win, 512 * i, 128 * nch)
            U = wslab_in(win, DFF + 512 * i, 128 * nch)
            O = wslab_out(wout, 512 * i, nch)
            for tg in range(4):
                for c in range(nch):
                    pg = ps('a')
                    pu = ps('b')
                    for k in range(8):
                        S.op('pe', lambda e, k=k, c=c, tg=tg, pg=pg, G=G: e.matmul(pg.a, G.a[:, k, c * 128:(c + 1) * 128], h.a[:, k, tg * 512:(tg + 1) * 512],
                                                                                   start=(k == 0), stop=(k == 7)),
                             reads=[G.r, hres[tg]], writes=[pg.r])
                    for k in range(8):
                        S.op('pe', lambda e, k=k, c=c, tg=tg, pu=pu, U=U: e.matmul(pu.a, U.a[:, k, c * 128:(c + 1) * 128], h.a[:, k, tg * 512:(tg + 1) * 512],
                                                                                   start=(k == 0), stop=(k == 7)),
                             reads=[U.r, hres[tg]], writes=[pu.r])
                    sg = sgt[c % 2]
                    S.op('act', lambda e, pg=pg, sg=sg: e.activation(sg.a, pg.a, AF.Silu), reads=[pg.r], writes=[sg.r])
                    S.op('dve', lambda e, c=c, tg=tg, pu=pu, sg=sg: e.tensor_tensor(hid.a[:, c, tg * 512:(tg + 1) * 512], sg.a, pu.a, ALU.mult),
                         reads=[pu.r, sg.r], writes=[hidres[c][tg]])
            for tg in range(4):
                cnd = tg_cond(tg)
                for m in range(8):
                    py = ps('c')
                    for c in range(nch):
                        S.op('pe', lambda e, c=c, m=m, tg=tg, py=py, O=O, nch=nch: e.matmul(py.a, O.a[:, c, m * 128:(m + 1) * 128], hid.a[:, c, tg * 512:(tg + 1) * 512],
                                                                                            start=(c == 0), stop=(c == nch - 1)),
                             reads=[O.r, hidres[c][tg]], writes=[py.r])
                    S.op('dve', lambda e, m=m, tg=tg, py=py, cnd=cnd: e.scalar_tensor_tensor(xap(m, tg), py.a, modG[l][:, n, m, cnd:cnd + 1], xap(m, tg), ALU.mult, ALU.add),
                         reads=[py.r, modD_R[l][n], xres[m][tg]], writes=[xres[m][tg]])
                if tail and i == 5:
                    final_tg(fbufs, tg)
            if between is not None:
                between(i)

    def final_alloc():
        tmp_sq = [carve([128, 512], BF16, 'fsq') for _ in range(2)]
        rs = carve([128, 512], F32, 'frs')
        yT = carve([128, 8, 512], F32, 'yT')
        ystg = [carve([128, 1024], F32, 'ystg') for _ in range(2)]
        return tmp_sq, rs, yT, ystg

    def final_out():
        ar_reset()
        bufs = final_alloc()
        for tg in range(4):
            final_tg(bufs, tg)

    def final_tg(bufs, tg):
        tmp_sq, rs, yT, ystg = bufs
        if True:
            rms_stats(tg, tmp_sq, rs)
            for m in range(8):
                S.op('dve', lambda e, m=m: e.scalar_tensor_tensor(yT.a[:, m, :], xap(m, tg), gfin_t[:, m:m + 1], rs.a, ALU.mult, ALU.mult),
                     reads=[xres[m][tg], rs.r, CR], writes=[yT.r])
            for j in range(4):
                t = tg * 4 + j
                ys = ystg[j % 2]
                for half in range(2):
                    pb = ps('a' if half == 0 else 'b')
                    for mm in range(4):
                        m = half * 4 + mm
                        S.op('pe', lambda e, mm=mm, m=m, j=j, pb=pb: e.transpose(pb.a[:, mm * 128:(mm + 1) * 128], yT.a[:, m, j * 128:(j + 1) * 128], ident_f[:]),
                             reads=[yT.r, KR], writes=[pb.r])
                    if half == 0:
                        S.op('act', lambda e, pb=pb, ys=ys: e.copy(ys.a[:, 0:512], pb.a), reads=[pb.r], writes=[ys.r])
                    else:
                        S.op('dve', lambda e, pb=pb, ys=ys: e.tensor_copy(ys.a[:, 512:1024], pb.a), reads=[pb.r], writes=[ys.r])
                dst = (o_yp if t < 8 else o_ys).ap()[(t % 8) * 128:(t % 8 + 1) * 128, :]
                S.dma('sp', lambda e, ys=ys, dst=dst: e.dma_start(out=dst, in_=ys.a), ys.r, reads=[ys.r])

    def proj_fm(W, c0, M, hg, ncols, post):
        for b in range(ncols // 512):
            pb = ps('a')
            for k in range(8):
                S.op('pe', lambda e, k=k, b=b, pb=pb: e.matmul(pb.a[0:M, :], W.a[:, k, c0:c0 + M], hg.a[:, k, b * 512:(b + 1) * 512],
                                                               start=(k == 0), stop=(k == 7)),
                     reads=[W.r, hg.r], writes=[pb.r])
            post(b, pb)

    def proj_tm(W, c0, n, hg, ntiles, post):
        for t in range(ntiles):
            pb = ps('b')
            for k in range(8):
                S.op('pe', lambda e, k=k, t=t, pb=pb: e.matmul(pb.a[:, 0:n], hg.a[:, k, t * 128:(t + 1) * 128], W.a[:, k, c0:c0 + n],
                                                               start=(k == 0), stop=(k == 7)),
                     reads=[W.r, hg.r], writes=[pb.r])
            post(t, pb)

    class Work:
        pass

    LA = 3

    def _emit_pv(wk, blk, kd, vq, c0, PT, b0, last, qts, finalize):
        if blk['po'] is None:
            blk['po'] = ps('c')
        po = blk['po']
        ns = kd['ns']
        pb = kd.get('pbase', 0)
        for i in vq:
            oc = (i - b0) * 128
            first = blk['first']
            S.op('pe', lambda e: e.matmul(po.a[:, oc:oc + 65], PT.a[pb:pb + ns, i * 128 - c0:i * 128 - c0 + 128], kd['vaug'],
                                          start=first, stop=True, skip_group_check=True),
                 reads=[PT.r] + list(kd['res']), writes=[po.r])
            blk['first'] = False
        if last:
            if hasattr(finalize, 'blk'):
                finalize.blk(qts, po.a.rearrange('p (q c) -> p q c', q=4), po.r)
            else:
                for i in qts:
                    oc = (i - b0) * 128
                    finalize(i, po.a[:, oc:oc + 65], po.r)

    def attn_flush(wk):
        while wk.pipe:
            wk.pipe.pop(0)()

    def attn_job(wk, nq, qT, qres, ktiles, valid, prob, finalize, after_block=None):
        import functools
        nqt = nq // 128
        for b0 in range(0, nqt, 4):
            qts = list(range(b0, min(b0 + 4, nqt)))
            blk = {'po': None, 'first': True}
            steps = []
            for kt, kd in enumerate(ktiles):
                vq = [i for i in qts if valid(kt, i)]
                if vq:
                    steps.append((kt, kd, vq))
            for si, (kt, kd, vq) in enumerate(steps):
                lo, hi = min(vq), max(vq) + 1
                c0, n = lo * 128, (hi - lo) * 128
                ns = kd['ns']
                PT = wk.PT[wk.ptc % len(wk.PT)]
                wk.ptc += 1
                if kd.get('noscore'):
                    prob(kt, kd, c0, n, None, PT)
                else:
                    pscore = ps('s')
                    S.op('pe', lambda e: e.matmul(pscore.a[0:ns, 0:n], kd['kT'], qT(c0, n), start=True, stop=True),
                         reads=list(kd['res']) + list(qres), writes=[pscore.r])
                    prob(kt, kd, c0, n, pscore, PT)
                wk.pipe.append(functools.partial(_emit_pv, wk, blk, kd, vq, c0, PT, b0, si == len(steps) - 1, qts, finalize))
                while len(wk.pipe) > wk.LA:
                    wk.pipe.pop(0)()
            if after_block is not None:
                after_block(b0 * 128)

    def prob_exp(kt, kd, c0, n, pscore, PT):
        ns = kd['ns']
        S.op('act', lambda e: e.activation(PT.a[0:ns, 0:n], pscore.a[0:ns, 0:n], AF.Exp, scale=0.125), reads=[pscore.r], writes=[PT.r])

    def mixer(l, grp):
        ar_reset()
        sample = grp == 2
        NT = 1024 if sample else 512
        tgs = [2, 3] if sample else [grp]
        ntile = NT // 128
        nseq = 1 if sample else 2
        L = 1024 if sample else 256
        tps = L // 128
        cnd = 1 if sample else 0
        tok0 = 1024 if sample else grp * 512
        hg = carve([128, 8, NT], BF16, 'hg')
        tmp_sq = [carve([128, 512], BF16, 'sq') for _ in range(2)]
        rs = carve([128, 512], F32, 'rs')
        tmpf = [carve([128, 512], F32, 'tmpf') for _ in range(2)]
        for i, tg in enumerate(tgs):
            norm_mod(l, 1, tg, hg, i * 512, tmp_sq, rs, tmpf)
        oT = Tl(hg.a, hg.r)
        wk = Work()
        wk.LA = 3 if sample else 6
        wk.PT = [carve([128, 512], BF16, 'PT') for _ in range(4 if sample else 8)]
        wk.ptc = 0
        wk.pipe = []
        opair = carve([128, ntile, 128], BF16, 'opair')
        sm = [carve([128, 8], F32, 'sm') for _ in range(8)]
        smc = [0]

        def smt():
            smc[0] += 1
            return sm[smc[0] % 8]
        stg_f = [carve([128, 528], F32, 'stgf') for _ in range(2)]
        stc = [0]

        def stg():
            stc[0] += 1
            return stg_f[stc[0] % 2]

        def seq_tile(s, j):
            return s * tps + j

        def fin_softmax(e_h, extra_den=None):
            ecol = e_h * 64

            class F:
                tile0 = 0

                def bind(self, tile0):
                    self.tile0 = tile0
                    return self

                def blk(self, qts, pv, por):
                    i0, nq = qts[0], len(qts)
                    t = smt()
                    den = pv[:, 0:nq, 64]
                    if extra_den is None:
                        S.op('dve', lambda e: e.reciprocal(t.a[:, 0:nq], den), reads=[por], writes=[t.r])
                    else:
                        S.op('dve', lambda e: e.tensor_scalar(t.a[:, 0:nq], den, extra_den, None, ALU.add), reads=[por, KR], writes=[t.r])
                        S.op('dve', lambda e: e.reciprocal(t.a[:, 0:nq], t.a[:, 0:nq]), reads=[t.r], writes=[t.r])
                    S.op('dve', lambda e: e.tensor_tensor(opair.a[:, self.tile0 + i0:self.tile0 + i0 + nq, ecol:ecol + 64], pv[:, 0:nq, 0:64],
                                                          t.a[:, 0:nq].unsqueeze(2).to_broadcast([128, nq, 64]), ALU.mult),
                         reads=[por, t.r], writes=[opair.r])
            return F()

        def flush_pair(chunk):
            attn_flush(wk)
            for b in range(0, ntile, 4):
                pb = ps('x')
                pbv = pb.a.bitcast(BF16)
                nn = min(4, ntile - b)
                for j in range(nn):
                    S.op('pe', lambda e, j=j, b=b, pbv=pbv: e.transpose(pbv[:, j * 128:(j + 1) * 128], opair.a[:, b + j, :], ident_b[:]),
                         reads=[opair.r, KR], writes=[pb.r])
                S.op('act', lambda e, b=b, nn=nn, pbv=pbv: e.copy(oT.a[:, chunk, b * 128:(b + nn) * 128], pbv[:, 0:nn * 128]), reads=[pb.r], writes=[oT.r])

        def qknorm_fm(pb, gcol, dst_ap, dst_res, rope_cols=None):
            sq = tmp_sq[0]
            S.op('act', lambda e: e.activation(sq.a, pb.a, AF.Square), reads=[pb.r], writes=[sq.r])
            p2 = ps('x')
            S.op('pe', lambda e: e.matmul(p2.a, bd_b[:], sq.a, start=True, stop=True), reads=[sq.r, KR], writes=[p2.r])
            S.op('act', lambda e: e.activation(rs.a, p2.a, AF.Ln, bias=EPS, scale=1.0 / 64), reads=[p2.r], writes=[rs.r])
            S.op('act', lambda e: e.activation(rs.a, rs.a, AF.Exp, scale=-0.5), reads=[rs.r], writes=[rs.r])
            if rope_cols is None:
                S.op('dve', lambda e: e.scalar_tensor_tensor(dst_ap, pb.a, qkg_t[:, gcol:gcol + 1], rs.a, ALU.mult, ALU.mult),
                     reads=[pb.r, rs.r, CR], writes=[dst_res])
            else:
                qn = tmp_sq[1]
                S.op('dve', lambda e: e.scalar_tensor_tensor(qn.a, pb.a, qkg_t[:, gcol:gcol + 1], rs.a, ALU.mult, ALU.mult),
                     reads=[pb.r, rs.r, CR], writes=[qn.r])
                rope_fm(qn.a, qn.r, dst_ap, dst_res, rope_cols)

        def rope_fm(src_ap, src_res, dst_ap, dst_res, c0):
            p3 = ps('x')
            S.op('pe', lambda e: e.matmul(p3.a, psw_b[:], src_ap, start=True, stop=True), reads=[src_res, KR], writes=[p3.r])
            t1, t2 = tmpf[0], tmpf[1]
            S.op('dve', lambda e: e.tensor_tensor(t1.a, p3.a, ropeS_t[:, c0:c0 + 512], ALU.mult), reads=[p3.r, CR], writes=[t1.r])
            S.op('pool', lambda e: e.tensor_tensor(t2.a, src_ap, ropeC_t[:, c0:c0 + 512], ALU.mult), reads=[src_res, CR], writes=[t2.r])
            S.op('dve', lambda e: e.tensor_tensor(dst_ap, t1.a, t2.a, ALU.add), reads=[t1.r, t2.r], writes=[dst_res])

        def load_ctx_kT(dram, ncol_pairs, dst, dup):
            for j in range(2):
                s_ = stg()
                if dup:
                    srcv = bass.AP(dram, j * 128 * 128, [[128, 128], [64, 2], [0, 2], [1, 64]])
                    S.dma('sp', lambda e, s_=s_, srcv=srcv: e.dma_start(out=s_.a[:, 0:256].rearrange('p (a b c) -> p a b c', a=2, b=2), in_=srcv), s_.r, writes=[s_.r])
                else:
                    for q_ in range(ncol_pairs):
                        S.dma('sp', lambda e, s_=s_, j=j, q_=q_: e.dma_start(out=s_.a[:, q_ * 128:(q_ + 1) * 128], in_=dram.ap()[j * 128:(j + 1) * 128, q_ * 128:(q_ + 1) * 128]), s_.r, writes=[s_.r])
                pb = ps('x')
                for c in range(ncol_pairs):
                    S.op('pe', lambda e, c=c, s_=s_, pb=pb: e.transpose(pb.a[:, c * 128:(c + 1) * 128], s_.a[:, c * 128:(c + 1) * 128], ident_f[:]),
                         reads=[s_.r, KR], writes=[pb.r])
                for c in range(ncol_pairs):
                    S.op('act', lambda e, c=c, j=j, pb=pb: e.copy(dst.a[:, c, j * 128:(j + 1) * 128], pb.a[:, c * 128:(c + 1) * 128]), reads=[pb.r], writes=[dst.r])

        def load_ctx_v(dram, nh, dst, t0):
            for j in range(2):
                s_ = stg()
                for q_ in range(0, 64 * nh, 128):
                    S.dma('sp', lambda e, s_=s_, j=j, q_=q_: e.dma_start(out=s_.a[:, q_:q_ + 128], in_=dram.ap()[j * 128:(j + 1) * 128, q_:q_ + 128]), s_.r, writes=[s_.r])
                S.op('act', lambda e, s_=s_, j=j: e.copy(dst.a[:, t0 + j, :, 0:64], s_.a[:, 0:64 * nh].rearrange('p (h d) -> p h d', h=nh)), reads=[s_.r], writes=[dst.r])

        def out_tm(pb, n, dram, seq, j, cast_dst=None):
            import os as _os2
            if ('nodma%d' % n) in _os2.environ.get('KSUB', '') and l == 1:
                return
            s_ = stg()
            if True:
                S.op('dve', lambda e: e.tensor_copy(s_.a[:, 0:n], pb.a[:, 0:n]), reads=[pb.r], writes=[s_.r])
            else:
                S.op('act', lambda e: e.copy(s_.a[:, 0:n], pb.a[:, 0:n]), reads=[pb.r], writes=[s_.r])
            import os as _os3
            oq = _os3.environ.get('KOUTQ', 'pool')
            if ('nostore%d' % n) in _os2.environ.get('KSUB', '') and l == 1:
                return
            S.dma(oq, lambda e: e.dma_start(out=dram.ap()[seq, j * 128:(j + 1) * 128, :], in_=s_.a[:, 0:n]), s_.r, reads=[s_.r])

        mi = Wd[l]['mix_in']
        if l == 0:
            nkt = (2 if sample else 0) + ntile
            QA = carve([128, 4, NT], BF16, 'QA')
            KA = carve([128, 2, 256 + NT if sample else NT], BF16, 'KA')
            VA = carve([128, nkt, 2, 65], BF16, 'VA')
            if sample:
                rsv = [ring.pop(), ring.pop()]
                QB = Tl(rsv[0].a.rearrange('p (c n) -> p c n', c=4), rsv[0].r)
                KB = Tl(rsv[1].a.rearrange('p (c n) -> p c n', c=4), rsv[1].r)
            else:
                QB = carve([128, 4, NT], BF16, 'QB')
                KB = carve([128, 4, NT], BF16, 'KB')
            VB = carve([128, ntile, 8, 65], BF16, 'VB')
            OB = carve([128, ntile, 512], BF16, 'OB')
            KBT = None if sample else carve([128, ntile, 512], BF16, 'KBT')
            LI = carve([64, NT], F32, 'LI')
            LF = carve([64, NT], F32, 'LF')
            BT = carve([64, NT], F32, 'BT')
            TOK = carve([128, ntile, 3, 16], F32, 'TOK')
            HF = carve([128, tps, 64], F32, 'HF')
            NBC = [carve([128, 512], F32, 'NBC') for _ in range(2)]
            EE01 = carve([128, 1024], F32, 'EE01')
            EE = [Tl(EE01.a[:, i * 512:(i + 1) * 512], ar_res('EE%d' % i)) for i in range(2)]
            ONES = Tl(EE01.a[0:64, 0:NT], EE01.r)
            EEall = [EE01.r, EE[0].r, EE[1].r]
            RM = Tl(tmpf[1].a[0:64], tmpf[1].r)
            koff = 256 if sample else 0
            S.op('pool', lambda e: e.memset(VA.a, 1.0), writes=[VA.r])
            S.op('pool', lambda e: e.memset(VB.a, 1.0), writes=[VB.r])
            S.op('pool', lambda e: e.memset(LI.a, 0.0), writes=[LI.r])
            S.op('pool', lambda e: e.memset(LF.a, 0.0), writes=[LF.r])
            S.op('pool', lambda e: e.memset(BT.a, 0.0), writes=[BT.r])
            S.op('pool', lambda e: e.memset(ONES.a, 1.0), writes=EEall)
            if sample:
                load_ctx_kT(kctx0, 2, KA, True)
                load_ctx_v(vctx0, 2, VA, 0)
                VV = carve([128, 2, 4, 65], BF16, 'VV')
                C0v = C0.ap().rearrange('a (h t) k v -> a t k h v', t=2)
                n0v = n0.ap().rearrange('a (h t) k -> a t k h', t=2)
                for half in range(2):
                    s_ = stg()
                    for dr in range(2):
                        S.dma('sp', lambda e, s_=s_, dr=dr, half=half: e.dma_start(
                            out=s_.a[half * 64:half * 64 + 64, dr * 256: dr * 256 + 256].rearrange('p (h v) -> p h v', h=4),
                            in_=C0v[dr, half]), s_.r, writes=[s_.r])
                        S.dma('sp', lambda e, s_=s_, dr=dr, half=half: e.dma_start(
                            out=s_.a[half * 64:half * 64 + 64, 512 + dr * 4:512 + dr * 4 + 4],
                            in_=n0v[dr, half], allow_slow_non_contiguous=True), s_.r, writes=[s_.r])
                    for dr in range(2):
                        S.op('act', lambda e, s_=s_, dr=dr, half=half: e.copy(
                            VV.a[half * 64:half * 64 + 64, dr, :, 0:64],
                            s_.a[half * 64:half * 64 + 64, dr * 256:dr * 256 + 256].rearrange('p (h v) -> p h v', h=4)),
                            reads=[s_.r], writes=[VV.r])
                        S.op('act', lambda e, s_=s_, dr=dr, half=half: e.copy(
                            VV.a[half * 64:half * 64 + 64, dr, :, 64:65],
                            s_.a[half * 64:half * 64 + 64, 512 + dr * 4:512 + dr * 4 + 4].unsqueeze(2)),
                            reads=[s_.r], writes=[VV.r])
            W = wslab_in(mi, 0, 512)
            for c in range(4):
                def post(b, pb, c=c):
                    qknorm_fm(pb, 0, QA.a[:, c, b * 512:(b + 1) * 512], QA.r, rope_cols=(b * 512 if sample else None))
                proj_fm(W, c * 128, 128, hg, NT, post)
            W = wslab_in(mi, 512, 512)
            for c in range(2):
                def post(b, pb, c=c):
                    qknorm_fm(pb, 1, KA.a[:, c, koff + b * 512:koff + (b + 1) * 512], KA.r, rope_cols=(b * 512 if sample else None))
                proj_fm(W, c * 128, 128, hg, NT, post)

            def post_gi(b, pb):
                S.op('act', lambda e: e.activation(LI.a[0:40, b * 512:(b + 1) * 512], pb.a[0:40, :], AF.Identity, bias=gbias_t[0:40, 0:1], scale=1.0),
                     reads=[pb.r, CR], writes=[LI.r])
            proj_fm(W, 256, 64, hg, NT, post_gi)

            def post_gf(b, pb):
                t1 = tmpf[0]
                S.op('act', lambda e: e.activation(t1.a[0:40, :], pb.a[0:40, :], AF.Exp, bias=ngbias_t[0:40, 0:1], scale=-1.0), reads=[pb.r, KR], writes=[t1.r])
                S.op('act', lambda e: e.activation(t1.a[0:40, :], t1.a[0:40, :], AF.Ln, bias=1.0, scale=1.0), reads=[t1.r], writes=[t1.r])
                S.op('dve', lambda e: e.tensor_scalar(LF.a[0:40, b * 512:(b + 1) * 512], t1.a[0:40, :], -1.0, None, ALU.mult), reads=[t1.r], writes=[LF.r])
            proj_fm(W, 320, 64, hg, NT, post_gf)

            def post_va(t, pb):
                kt = (2 if sample else 0) + t
                S.op('act', lambda e: e.copy(VA.a[:, kt, :, 0:64], pb.a[:, 0:128].rearrange('p (h d) -> p h d', h=2)), reads=[pb.r], writes=[VA.r])
                if not sample:
                    out_tm(pb, 128, o_v0, grp * 2 + t // tps, t % tps)
            proj_tm(W, 384, 128, hg, ntile, post_va)
            W = wslab_in(mi, 1024, 512)
            for c in range(4):
                def post(b, pb, c=c):
                    S.op('act', lambda e: e.copy(QB.a[:, c, b * 512:(b + 1) * 512], pb.a), reads=[pb.r], writes=[QB.r])
                proj_fm(W, c * 128, 128, hg, NT, post)
            W = wslab_in(mi, 1536, 512)
            for c in range(4):
                def post(b, pb, c=c):
                    S.op('dve', lambda e: e.tensor_copy(KB.a[:, c, b * 512:(b + 1) * 512], pb.a), reads=[pb.r], writes=[KB.r])
                proj_fm(W, c * 128, 128, hg, NT, post)
            if not sample:
                def post(t, pb):
                    S.op('act', lambda e: e.copy(KBT.a[:, t, :], pb.a), reads=[pb.r], writes=[KBT.r])
                proj_tm(W, 0, 512, hg, ntile, post)
            W = wslab_in(mi, 2048, 512)
            def post(t, pb):
                S.op('dve', lambda e: e.tensor_copy(VB.a[:, t, :, 0:64], pb.a.rearrange('p (h d) -> p h d', h=8)), reads=[pb.r], writes=[VB.r])
            proj_tm(W, 0, 512, hg, ntile, post)
            W = wslab_in(mi, 2560, 512)
            def post(t, pb):
                S.op('act', lambda e: e.activation(OB.a[:, t, :], pb.a, AF.Sigmoid), reads=[pb.r], writes=[OB.r])
            proj_tm(W, 0, 512, hg, ntile, post)
            if not sample:
                W = wslab_in(mi, 3072, 128)
                def post(t, pb):
                    t_ = smt()
                    s_ = stg()
                    for hh in range(2):
                        S.op('act', lambda e, hh=hh: e.activation(s_.a[:, 256 + hh * 64:256 + hh * 64 + 64], pb.a[:, hh * 64:(hh + 1) * 64], AF.Square,
                                                                  accum_out=t_.a[:, hh:hh + 1]), reads=[pb.r], writes=[s_.r, t_.r])
                    S.op('act', lambda e: e.activation(t_.a[:, 0:2], t_.a[:, 0:2], AF.Sqrt, bias=EPS, scale=1.0 / 64), reads=[t_.r], writes=[t_.r])
                    S.op('dve', lambda e: e.reciprocal(t_.a[:, 0:2], t_.a[:, 0:2]), reads=[t_.r], writes=[t_.r])
                    for hh in range(2):
                        S.op('dve', lambda e, hh=hh: e.scalar_tensor_tensor(s_.a[:, hh * 64:(hh + 1) * 64], pb.a[:, hh * 64:(hh + 1) * 64], t_.a[:, hh:hh + 1], gkbc_t[:],
                                                                            ALU.mult, ALU.mult), reads=[pb.r, t_.r, CR], writes=[s_.r])
                    S.dma('sp', lambda e: e.dma_start(out=o_k0.ap()[grp * 2 + t // tps, (t % tps) * 128:(t % tps + 1) * 128, :], in_=s_.a[:, 0:128]), s_.r, reads=[s_.r])
                proj_tm(W, 0, 128, hg, ntile, post)

            for c in range(4):
                for s in range(nseq):
                    q0 = s * L
                    kv = c // 2
                    for e_ in range(2):
                        pbs = 64 * e_
                        kts = []
                        if sample:
                            for j in range(2):
                                kts.append(dict(kT=KA.a[pbs:pbs + 64, kv, j * 128:(j + 1) * 128], ns=128, vaug=VA.a[:, j, kv, :], res=[KA.r, VA.r]))
                        for j in range(tps):
                            kts.append(dict(kT=KA.a[pbs:pbs + 64, kv, koff + q0 + j * 128:koff + q0 + (j + 1) * 128], ns=128,
                                            vaug=VA.a[:, (2 if sample else 0) + seq_tile(s, j), kv, :], res=[KA.r, VA.r]))
                        fs = fin_softmax(e_)
                        attn_job(wk, L, lambda c0, n, c=c, pbs=pbs, q0=q0: QA.a[pbs:pbs + 64, c, q0 + c0:q0 + c0 + n], [QA.r], kts,
                                 lambda kt, i: True, prob_exp,
                                 fs.bind(s * tps))
                    if s == nseq - 1:
                        flush_pair(c)

            def revap(a, c0, n):
                return bass.AP(a.tensor, a[:, c0 + n - 1:c0 + n].offset, [list(a.ap[0]), [-1, n]])
            for s in range(nseq):
                c0 = s * L
                S.op('dve', lambda e, c0=c0: e.tensor_tensor_scan(BT.a[0:8, c0:c0 + L], ONES.a[0:8, c0:c0 + L], LF.a[0:8, c0:c0 + L], 0.0, ALU.mult, ALU.add),
                     reads=[LF.r] + EEall, writes=[BT.r])
                S.op('dve', lambda e, c0=c0: e.tensor_tensor_scan(revap(BT.a[32:40], c0, L), ONES.a[32:40, c0:c0 + L], revap(LF.a[32:40], c0, L), 0.0, ALU.mult, ALU.add),
                     reads=[LF.r] + EEall, writes=[BT.r])
            S.op('dve', lambda e: e.tensor_tensor(LI.a[0:40, :], LI.a[0:40, :], BT.a[0:40, :], ALU.subtract), reads=[LI.r, BT.r], writes=[LI.r])
            for s in range(nseq):
                c0 = s * L
                ini_f = m0col_t[0:8, :] if sample else 0.0
                ini_b = m0col_t[32:40, :] if sample else 0.0
                S.op('dve', lambda e, c0=c0, ini_f=ini_f: e.tensor_tensor_scan(LF.a[0:8, c0:c0 + L], ONES.a[0:8, c0:c0 + L], LI.a[0:8, c0:c0 + L], ini_f, ALU.mult, ALU.max),
                     reads=[LI.r, CR] + EEall, writes=[LF.r])
                S.op('dve', lambda e, c0=c0, ini_b=ini_b: e.tensor_tensor_scan(revap(LF.a[32:40], c0, L), ONES.a[32:40, c0:c0 + L], revap(LI.a[32:40], c0, L), ini_b, ALU.mult, ALU.max),
                     reads=[LI.r, CR] + EEall, writes=[LF.r])
            S.op('dve', lambda e: e.tensor_scalar(LF.a[0:40, :], LF.a[0:40, :], -1.0, None, ALU.mult), reads=[LF.r], writes=[LF.r])
            S.op('dve', lambda e: e.tensor_tensor(BT.a[0:40, :], LF.a[0:40, :], BT.a[0:40, :], ALU.subtract), reads=[LF.r, BT.r], writes=[BT.r])
            if not sample:
                for s in range(nseq):
                    c0 = s * L
                    sg_ = grp * 2 + s
                    t_ = smt()
                    S.op('dve', lambda e, c0=c0, t_=t_: e.tensor_scalar(t_.a[0:8, 0:1], BT.a[0:8, c0 + L - 1:c0 + L], -1.0, None, ALU.mult), reads=[BT.r], writes=[t_.r])
                    S.op('dve', lambda e, c0=c0, t_=t_: e.tensor_scalar(t_.a[32:40, 0:1], BT.a[32:40, c0:c0 + 1], -1.0, None, ALU.mult), reads=[BT.r], writes=[t_.r])
                    S.dma('sp', lambda e, t_=t_, sg_=sg_: e.dma_start(out=o_m.ap()[sg_, 0, :].rearrange('(p o) -> p o', o=1), in_=t_.a[0:8, 0:1], allow_slow_non_contiguous=True), t_.r, reads=[t_.r])
                    S.dma('sp', lambda e, t_=t_, sg_=sg_: e.dma_start(out=o_m.ap()[sg_, 1, :].rearrange('(p o) -> p o', o=1), in_=t_.a[32:40, 0:1], allow_slow_non_contiguous=True), t_.r, reads=[t_.r])
            S.op('act', lambda e: e.activation(BT.a[0:40, :], BT.a[0:40, :], AF.Exp), reads=[BT.r], writes=[BT.r])
            WF = None
            if not sample:
                WF = carve([64, NT], F32, 'WF')
                S.op('pool', lambda e: e.memset(WF.a, 0.0), writes=[WF.r])
                for s in range(nseq):
                    c0 = s * L
                    S.op('act', lambda e, c0=c0: e.activation(WF.a[0:8, c0:c0 + L], LI.a[0:8, c0:c0 + L], AF.Exp, bias=LF.a[0:8, c0 + L - 1:c0 + L], scale=1.0),
                         reads=[LI.r, LF.r], writes=[WF.r])
                    S.op('act', lambda e, c0=c0: e.activation(WF.a[32:40, c0:c0 + L], LI.a[32:40, c0:c0 + L], AF.Exp, bias=LF.a[32:40, c0:c0 + 1], scale=1.0),
                         reads=[LI.r, LF.r], writes=[WF.r])
            for t in range(ntile):
                pb = ps('x')
                srcs = [LI, BT] + ([WF] if WF is not None else [])
                for qi, src in enumerate(srcs):
                    S.op('pe', lambda e, qi=qi, src=src, t=t, pb=pb: e.transpose(pb.a[:, qi * 64:qi * 64 + 40], src.a[0:40, t * 128:(t + 1) * 128], ident_f[0:40, 0:40]),
                         reads=[src.r, KR], writes=[pb.r])
                nq_ = len(srcs)
                S.op('dve', lambda e, t=t, pb=pb, nq_=nq_: e.tensor_copy(TOK.a[:, t, 0:nq_, :].rearrange('p q (a h) -> p q a h', a=2),
                                                                        pb.a[:, 0:nq_ * 64].rearrange('p (q a h) -> p q a h', q=nq_, a=2)[:, :, :, 0:8]),
                     reads=[pb.r], writes=[TOK.r])

            S.op('dve', lambda e: e.tensor_scalar(TOK.a[:, :, 0, :], TOK.a[:, :, 0, :], float(np.log(0.125)), None, ALU.add), reads=[TOK.r], writes=[TOK.r])
            def _mk_mjob(c, s, e_, dr, jidx):
                    q0 = s * L
                    hd_ = 2 * c + e_
                    pbs = 64 * e_
                    hd = dr * 8 + hd_
                    row = dr * 32 + hd_
                    mask_t = maskF_t if dr == 0 else maskB_t
                    nbcs = {}

                    def pro(b0):
                        nb = min(512, L - b0)
                        S.op('act', lambda e, b0=b0, nb=nb, hd=hd: e.activation(RM.a[0:40, 0:nb], LF.a[0:40, q0 + b0:q0 + b0 + nb], AF.Copy, scale=oh_t[0:40, hd:hd + 1]),
                             reads=[LF.r, CR], writes=[RM.r])
                        pbc = ps('x')
                        S.op('pe', lambda e, nb=nb, pbc=pbc: e.matmul(pbc.a[:, 0:nb], ones_f[0:40, :], RM.a[0:40, 0:nb], start=True, stop=True),
                             reads=[RM.r, KR], writes=[pbc.r])
                        nbt = NBC[(b0 // 512) % 2] if sample else NBC[jidx % 2]
                        S.op('act', lambda e, nb=nb, pbc=pbc, nbt=nbt: e.copy(nbt.a[:, 0:nb], pbc.a[:, 0:nb]), reads=[pbc.r], writes=[nbt.r])
                        nbcs[b0] = nbt
                    kts = []
                    if sample:
                        kts.append(dict(noscore=True, virt=True, ns=64, pbase=pbs, vaug=VV.a[pbs:pbs + 64, dr, hd_ // 2, :], res=[VV.r]))
                    for j in range(tps):
                        kts.append(dict(kT=KB.a[pbs:pbs + 64, c, q0 + j * 128:q0 + (j + 1) * 128], ns=128, j=j,
                                        vaug=VB.a[:, seq_tile(s, j), hd_, :], res=[KB.r, VB.r]))

                    def valid(kt, i, dr=dr):
                        if sample:
                            if kt == 0:
                                return True
                            kt -= 1
                        return i >= kt if dr == 0 else i <= kt

                    def prob(kt, kd, c0, n, pscore, PT, dr=dr, hd=hd, pbs=pbs, c=c, q0=q0, nbcs=nbcs, mask_t=mask_t, s=s):
                        b0 = (c0 // 512) * 512
                        nbt = nbcs[b0]
                        lc = c0 - b0
                        ee = EE[wk.ptc % 2]
                        if kd.get('virt'):
                            S.op('act', lambda e: e.activation(ee.a[pbs:pbs + 64, 0:n], nbt.a[pbs:pbs + 64, lc:lc + n], AF.Exp, bias=m0bc_t[pbs:pbs + 64, hd:hd + 1], scale=1.0),
                                 reads=[nbt.r, CR], writes=[ee.r])
                            S.op('dve', lambda e: e.tensor_tensor(PT.a[pbs:pbs + 64, 0:n], ee.a[pbs:pbs + 64, 0:n], QB.a[pbs:pbs + 64, c, q0 + c0:q0 + c0 + n], ALU.mult),
                                 reads=[ee.r, QB.r], writes=[PT.r])
                            return
                        j = kd['j']
                        tl = seq_tile(s, j)
                        abias = TOK.a[:, tl, 0, hd:hd + 1]
                        dc = j * 128 - c0
                        m01 = mask01F_t if dr == 0 else mask01B_t
                        S.op('act', lambda e: e.activation(ee.a[:, 0:n], nbt.a[:, lc:lc + n], AF.Exp, bias=abias, scale=1.0), reads=[nbt.r, TOK.r], writes=[ee.r])
                        S.op('dve', lambda e: e.scalar_tensor_tensor(PT.a[:, 0:n], ee.a[:, 0:n], 0.125, pscore.a[:, 0:n], ALU.min, ALU.mult),
                             reads=[pscore.r, ee.r], writes=[PT.r])
                        if 0 <= dc < n:
                            S.op('dve', lambda e: e.tensor_tensor(PT.a[:, dc:dc + 128], PT.a[:, dc:dc + 128], m01[:], ALU.mult), reads=[PT.r, CR], writes=[PT.r])

                    def fin(i, po, por):
                        raise AssertionError('block finalize only')

                    def fin_blk(qts, pv, por, dr=dr, hd=hd, s=s, e_=e_, hd_=hd_):
                        i0, nq = qts[0], len(qts)
                        tl0 = seq_tile(s, i0)
                        t_ = smt()
                        den = pv[:, 0:nq, 64]
                        num = pv[:, 0:nq, 0:64]
                        S.op('dve', lambda e: e.tensor_tensor(t_.a[:, 0:nq], den, TOK.a[:, tl0:tl0 + nq, 1, hd], ALU.max), reads=[por, TOK.r], writes=[t_.r])
                        S.op('dve', lambda e: e.scalar_tensor_tensor(t_.a[:, 0:nq], den, -1.0, t_.a[:, 0:nq], ALU.mult, ALU.max), reads=[por, t_.r], writes=[t_.r])
                        S.op('dve', lambda e: e.reciprocal(t_.a[:, 0:nq], t_.a[:, 0:nq]), reads=[t_.r], writes=[t_.r])
                        rb = t_.a[:, 0:nq].unsqueeze(2).to_broadcast([128, nq, 64])
                        HFv = HF.a[:, i0:i0 + nq, :]
                        if dr == 0:
                            S.op('dve', lambda e: e.tensor_tensor(HFv, num, rb, ALU.mult), reads=[por, t_.r], writes=[HF.r])
                            return
                        sb_ = stg()
                        tv = sb_.a[:, 0:nq * 64].rearrange('p (t d) -> p t d', t=nq)
                        S.op('dve', lambda e: e.tensor_tensor(tv, num, rb, ALU.mult), reads=[por, t_.r], writes=[sb_.r])
                        S.op('dve', lambda e: e.tensor_tensor(HFv, HFv, tv, ALU.add), reads=[sb_.r, HF.r], writes=[HF.r])
                        if qts[-1] != tps - 1:
                            return
                        s_ = stg()
                        t8 = smt()
                        sv = s_.a[:, 0:tps * 64].rearrange('p (t d) -> p t d', t=tps)
                        S.op('dve', lambda e: e.tensor_tensor(sv, HF.a, HF.a, ALU.mult), reads=[HF.r], writes=[s_.r])
                        S.op('dve', lambda e: e.tensor_reduce(t8.a[:, 0:tps], sv, AX.X, ALU.add), reads=[s_.r], writes=[t8.r])
                        S.op('act', lambda e: e.activation(t8.a[:, 0:tps], t8.a[:, 0:tps], AF.Ln, bias=EPS, scale=1.0 / 64), reads=[t8.r], writes=[t8.r])
                        S.op('act', lambda e: e.activation(t8.a[:, 0:tps], t8.a[:, 0:tps], AF.Exp, scale=-0.5), reads=[t8.r], writes=[t8.r])
                        S.op('dve', lambda e: e.tensor_tensor(sv, HF.a, t8.a[:, 0:tps].unsqueeze(2).to_broadcast([128, tps, 64]), ALU.mult),
                             reads=[HF.r, t8.r], writes=[s_.r])
                        S.op('pool', lambda e: e.tensor_tensor(sv, sv, hgn_t[:, hd_ * 64:(hd_ + 1) * 64].unsqueeze(1).to_broadcast([128, tps, 64]), ALU.mult),
                             reads=[s_.r, CR], writes=[s_.r])
                        S.op('pool', lambda e: e.tensor_tensor(opair.a[:, s * tps:(s + 1) * tps, e_ * 64:(e_ + 1) * 64], sv,
                                                               OB.a[:, s * tps:(s + 1) * tps, hd_ * 64:(hd_ + 1) * 64], ALU.mult),
                             reads=[s_.r, OB.r], writes=[opair.r])


                    fin.blk = fin_blk

                    def run(after_block):
                        attn_job(wk, L, lambda c0, n: QB.a[pbs:pbs + 64, c, q0 + c0:q0 + c0 + n], [QB.r], kts, valid, prob, fin, after_block=after_block)
                    return dict(pro=pro, run=run)

            mjobs = []
            for c in range(4):
                for s in range(nseq):
                    for e_ in range(2):
                        for dr in range(2):
                            jb = _mk_mjob(c, s, e_, dr, len(mjobs))
                            jb['flush'] = (4 + c) if (s == nseq - 1 and e_ == 1 and dr == 1) else None
                            mjobs.append(jb)
            blocks0 = list(range(0, L, 512))
            for b0 in blocks0:
                mjobs[0]['pro'](b0)
            for k, jb in enumerate(mjobs):
                nxt = mjobs[k + 1] if k + 1 < len(mjobs) else None
                if sample:
                    jb['run'](lambda b0, nxt=nxt: nxt['pro'](b0) if nxt is not None else None)
                else:
                    if nxt is not None:
                        nxt['pro'](0)
                    jb['run'](None)
                if jb['flush'] is not None:
                    flush_pair(jb['flush'])

            if not sample:
                WVt = [carve([128, 8, 65], BF16, 'WV%d' % j) for j in range(tps)]
                for s in range(nseq):
                    sg_ = grp * 2 + s
                    for dr in range(2):
                        pcs = [ps('a'), ps('b')]
                        WVs = []
                        for j in range(tps):
                            tl = seq_tile(s, j)
                            wv = WVt[j]
                            wf = TOK.a[:, tl, 2, dr * 8:dr * 8 + 8].unsqueeze(2).to_broadcast([128, 8, 65])
                            S.op('dve', lambda e, wv=wv, tl=tl, wf=wf: e.tensor_tensor(wv.a, VB.a[:, tl, :, :], wf, ALU.mult), reads=[VB.r, TOK.r], writes=[wv.r])
                            WVs.append(wv)
                        for hh in range(8):
                            pc = pcs[hh // 4]
                            oc = (hh % 4) * 128
                            for j in range(tps):
                                tl = seq_tile(s, j)
                                S.op('pe', lambda e, hh=hh, j=j, tl=tl, pc=pc, oc=oc: e.matmul(pc.a[0:64, oc:oc + 65], KBT.a[:, tl, hh * 64:(hh + 1) * 64], WVs[j].a[:, hh, :],
                                                                                               start=(j == 0), stop=(j == tps - 1), skip_group_check=True),
                                     reads=[KBT.r, WVs[j].r], writes=[pc.r])
                        s_ = stg()
                        for half in range(2):
                            S.op('act', lambda e, half=half, s_=s_: e.activation(s_.a[0:64, half * 260:half * 260 + 260].rearrange('p (h v) -> p h v', h=4),
                                                                                 pcs[half].a[0:64, :].rearrange('p (h v) -> p h v', h=4)[:, :, 0:65], AF.Copy, scale=0.125),
                                 reads=[pcs[half].r], writes=[s_.r])
                        sv = s_.a[0:64, 0:520].rearrange('p (h v) -> p h v', h=8)
                        S.dma('sp', lambda e, sv=sv, sg_=sg_, dr=dr, s_=s_: e.dma_start(out=o_C.ap()[sg_, dr].rearrange('h k v -> k h v'), in_=sv[:, :, 0:64]), s_.r, reads=[s_.r])
                        S.dma('sp', lambda e, sv=sv, sg_=sg_, dr=dr, s_=s_: e.dma_start(out=o_n.ap()[sg_, dr].rearrange('h k -> k h'), in_=sv[:, :, 64], allow_slow_non_contiguous=True), s_.r, reads=[s_.r])
            if sample:
                ring.extend(rsv)
        else:
            nkt = (2 if sample else 0) + ntile
            koff = 256 if sample else 0
            QC = carve([128, 4, NT], BF16, 'QC')
            KC = carve([128, 4, koff + NT], BF16, 'KC')
            VC = carve([128, nkt, 8, 65], BF16, 'VC')
            QD = carve([128, 4, NT], BF16, 'QD')
            KD = carve([128, 2, koff + NT], BF16, 'KD')
            VD = carve([128, nkt, 2, 65], BF16, 'VD')
            S.op('pool', lambda e: e.memset(VC.a, 1.0), writes=[VC.r])
            S.op('pool', lambda e: e.memset(VD.a, 1.0), writes=[VD.r])
            if sample:
                load_ctx_kT(kcctx, 4, KC, False)
                load_ctx_v(vcctx, 8, VC, 0)
                load_ctx_kT(kdctx, 2, KD, True)
                load_ctx_v(vdctx, 2, VD, 0)
            W = wslab_in(mi, 0, 512)
            for c in range(4):
                def post(b, pb, c=c):
                    S.op('act', lambda e: e.copy(QC.a[:, c, b * 512:(b + 1) * 512], pb.a), reads=[pb.r], writes=[QC.r])
                proj_fm(W, c * 128, 128, hg, NT, post)
            W = wslab_in(mi, 512, 512)
            for c in range(4):
                def post(b, pb, c=c):
                    S.op('dve', lambda e: e.tensor_copy(KC.a[:, c, koff + b * 512:koff + (b + 1) * 512], pb.a), reads=[pb.r], writes=[KC.r])
                proj_fm(W, c * 128, 128, hg, NT, post)
            if not sample:
                def post(t, pb):
                    out_tm(pb, 512, o_kc, grp * 2 + t // tps, t % tps)
                proj_tm(W, 0, 512, hg, ntile, post)
            W = wslab_in(mi, 1024, 512)
            def post(t, pb):
                kt = (2 if sample else 0) + t
                S.op('dve', lambda e: e.tensor_copy(VC.a[:, kt, :, 0:64], pb.a.rearrange('p (h d) -> p h d', h=8)), reads=[pb.r], writes=[VC.r])
                if not sample:
                    out_tm(pb, 512, o_vc, grp * 2 + t // tps, t % tps)
            proj_tm(W, 0, 512, hg, ntile, post)
            W = wslab_in(mi, 1536, 512)
            for c in range(4):
                def post(b, pb, c=c):
                    if sample:
                        qn = tmp_sq[1]
                        S.op('act', lambda e: e.copy(qn.a, pb.a), reads=[pb.r], writes=[qn.r])
                        rope_fm(qn.a, qn.r, QD.a[:, c, b * 512:(b + 1) * 512], QD.r, b * 512)
                    else:
                        S.op('act', lambda e: e.copy(QD.a[:, c, b * 512:(b + 1) * 512], pb.a), reads=[pb.r], writes=[QD.r])
                proj_fm(W, c * 128, 128, hg, NT, post)
            W = wslab_in(mi, 2048, 512)
            for c in range(2):
                def post(b, pb, c=c):
                    if sample:
                        qn = tmp_sq[1]
                        S.op('act', lambda e: e.copy(qn.a, pb.a), reads=[pb.r], writes=[qn.r])
                        rope_fm(qn.a, qn.r, KD.a[:, c, koff + b * 512:koff + (b + 1) * 512], KD.r, b * 512)
                    else:
                        S.op('act', lambda e: e.copy(KD.a[:, c, b * 512:(b + 1) * 512], pb.a), reads=[pb.r], writes=[KD.r])
                proj_fm(W, c * 128, 128, hg, NT, post)
            if not sample:
                def post(t, pb):
                    out_tm(pb, 128, o_kd, grp * 2 + t // tps, t % tps)
                proj_tm(W, 256, 128, hg, ntile, post)
            def post(t, pb):
                kt = (2 if sample else 0) + t
                S.op('dve', lambda e: e.tensor_copy(VD.a[:, kt, :, 0:64], pb.a[:, 0:128].rearrange('p (h d) -> p h d', h=2)), reads=[pb.r], writes=[VD.r])
                if not sample:
                    out_tm(pb, 128, o_vd, grp * 2 + t // tps, t % tps)
            proj_tm(W, 384, 128, hg, ntile, post)

            import os as _os
            ksub = _os.environ.get('KSUB', '')
            if not sample:
                for c in range(4 if 'nomha' not in ksub else 0):
                    for s in range(nseq):
                        q0 = s * L
                        for e_ in range(2):
                            pbs = 64 * e_
                            hh = 2 * c + e_
                            kts = [dict(kT=KC.a[pbs:pbs + 64, c, q0 + j * 128:q0 + (j + 1) * 128], ns=128, vaug=VC.a[:, seq_tile(s, j), hh, :], res=[KC.r, VC.r])
                                   for j in range(tps)]
                            fs = fin_softmax(e_)
                            attn_job(wk, L, lambda c0, n, c=c, pbs=pbs, q0=q0: QC.a[pbs:pbs + 64, c, q0 + c0:q0 + c0 + n], [QC.r], kts,
                                     lambda kt, i: True, prob_exp, fs.bind(s * tps))
                        if s == nseq - 1:
                            flush_pair(c)
                for c in range(4 if 'nogqa' not in ksub else 0):
                    for s in range(nseq):
                        q0 = s * L
                        kv = c // 2
                        for e_ in range(2):
                            pbs = 64 * e_
                            hh = 2 * c + e_
                            kts = [dict(kT=KD.a[pbs:pbs + 64, kv, q0 + j * 128:q0 + (j + 1) * 128], ns=128, vaug=VD.a[:, seq_tile(s, j), kv, :], res=[KD.r, VD.r])
                                   for j in range(tps)]
                            fs = fin_softmax(e_, extra_den=esink_t[:, hh:hh + 1])
                            attn_job(wk, L, lambda c0, n, c=c, pbs=pbs, q0=q0: QD.a[pbs:pbs + 64, c, q0 + c0:q0 + c0 + n], [QD.r], kts,
                                     lambda kt, i: True, prob_exp, fs.bind(s * tps))
                        if s == nseq - 1:
                            flush_pair(4 + c)
            else:
                navalid = _na_rows()
                TB = [carve([128, 15, 64], F32, 'TB') for _ in range(2)]
                ARG = [carve([128, 512], F32, 'ARG') for _ in range(2)]
                for c in range(4 if 'nona' not in ksub else 0):
                    for e_ in range(2):
                        pbs = 64 * e_
                        hh = 2 * c + e_
                        tb = TB[hh % 2]
                        S.dma('sp', lambda e, tb=tb, hh=hh: e.dma_start(out=tb.a, in_=natb.ap()[hh].rearrange('p (a b) -> p a b', a=15)), tb.r, writes=[tb.r])
                        S.op('pool', lambda e, tb=tb: e.tensor_tensor(tb.a, tb.a, cmask_t[:].unsqueeze(1).to_broadcast([128, 15, 64]), ALU.add), reads=[tb.r, CR], writes=[tb.r])
                        kts = []
                        for j in range(2):
                            kts.append(dict(kT=KC.a[pbs:pbs + 64, c, j * 128:(j + 1) * 128], ns=128, vaug=VC.a[:, j, hh, :], res=[KC.r, VC.r], ctx=True))
                        for j in range(8):
                            kts.append(dict(kT=KC.a[pbs:pbs + 64, c, 256 + j * 128:256 + (j + 1) * 128], ns=128, vaug=VC.a[:, 2 + j, hh, :], res=[KC.r, VC.r], j=j))

                        def valid(kt, i):
                            if kt < 2:
                                return True
                            j = kt - 2
                            return any(navalid[2 * j + a][2 * i + b] for a in range(2) for b in range(2))

                        def prob(kt, kd, c0, n, pscore, PT, tb=tb):
                            if kd.get('ctx'):
                                return prob_exp(kt, kd, c0, n, pscore, PT)
                            j = kd['j']
                            S.op('pool', lambda e: e.memset(PT.a[:, 0:n], 0.0), writes=[PT.r])
                            arg = ARG[wk.ptc % 2]
                            r0 = c0 // 64
                            nr = n // 64
                            for a in range(2):
                                srow = 2 * j + a
                                rows = [r for r in range(r0, r0 + nr) if navalid[srow][r]]
                                if not rows:
                                    continue
                                rl, rh = min(rows), max(rows) + 1
                                cl, cn = (rl - r0) * 64, (rh - rl) * 64
                                dy0 = rl - srow + 7
                                pa = a * 64
                                S.op('dve', lambda e, pa=pa, cl=cl, cn=cn, dy0=dy0, rl=rl, rh=rh: e.scalar_tensor_tensor(
                                    arg.a[pa:pa + 64, cl:cl + cn], pscore.a[pa:pa + 64, cl:cl + cn], 0.125,
                                    tb.a[pa:pa + 64, dy0:dy0 + (rh - rl), :].rearrange('p a b -> p (a b)'), ALU.mult, ALU.add),
                                    reads=[pscore.r, tb.r], writes=[arg.r])
                                S.op('act', lambda e, pa=pa, cl=cl, cn=cn: e.activation(PT.a[pa:pa + 64, cl:cl + cn], arg.a[pa:pa + 64, cl:cl + cn], AF.Exp),
                                     reads=[arg.r], writes=[PT.r])
                        fs = fin_softmax(e_)
                        attn_job(wk, L, lambda c0, n, c=c, pbs=pbs: QC.a[pbs:pbs + 64, c, c0:c0 + n], [QC.r], kts, valid, prob,
                                 fs.bind(0))
                    flush_pair(c)
                for c in range(4 if 'noswa' not in ksub else 0):
                    kv = c // 2
                    for e_ in range(2):
                        pbs = 64 * e_
                        hh = 2 * c + e_
                        kts = []
                        for j in range(2):
                            kts.append(dict(kT=KD.a[pbs:pbs + 64, kv, j * 128:(j + 1) * 128], ns=128, vaug=VD.a[:, j, kv, :], res=[KD.r, VD.r], ctx=True))
                        for j in range(8):
                            kts.append(dict(kT=KD.a[pbs:pbs + 64, kv, 256 + j * 128:256 + (j + 1) * 128], ns=128, vaug=VD.a[:, 2 + j, kv, :], res=[KD.r, VD.r], j=j))

                        def valid(kt, i):
                            return True if kt < 2 else abs(i - (kt - 2)) <= 1

                        def prob(kt, kd, c0, n, pscore, PT):
                            if kd.get('ctx'):
                                return prob_exp(kt, kd, c0, n, pscore, PT)
                            j = kd['j']
                            arg = ARG[wk.ptc % 2]
                            for i in range(c0 // 128, (c0 + n) // 128):
                                lc = i * 128 - c0
                                if i == j:
                                    S.op('act', lambda e, lc=lc: e.activation(PT.a[:, lc:lc + 128], pscore.a[:, lc:lc + 128], AF.Exp, scale=0.125), reads=[pscore.r], writes=[PT.r])
                                else:
                                    mk_ = maskF_t if i == j - 1 else maskB_t
                                    S.op('dve', lambda e, lc=lc, mk_=mk_: e.scalar_tensor_tensor(arg.a[:, lc:lc + 128], pscore.a[:, lc:lc + 128], 0.125, mk_[:], ALU.mult, ALU.add),
                                         reads=[pscore.r, CR], writes=[arg.r])
                                    S.op('act', lambda e, lc=lc: e.activation(PT.a[:, lc:lc + 128], arg.a[:, lc:lc + 128], AF.Exp), reads=[arg.r], writes=[PT.r])
                        fs = fin_softmax(e_, extra_den=esink_t[:, hh:hh + 1])
                        attn_job(wk, L, lambda c0, n, c=c, pbs=pbs: QD.a[pbs:pbs + 64, c, c0:c0 + n], [QD.r], kts, valid, prob,
                                 fs.bind(0))
                    flush_pair(4 + c)

        mo = Wd[l]['mix_out']
        O1 = wslab_out(mo, 0, 4)
        O2 = wslab_out(mo, 512, 4)
        for i, tg in enumerate(tgs):
            for m in range(8):
                py = ps('a')
                for cc in range(8):
                    Ow = O1 if cc < 4 else O2
                    S.op('pe', lambda e, cc=cc, m=m, i=i, py=py, Ow=Ow: e.matmul(py.a, Ow.a[:, cc % 4, m * 128:(m + 1) * 128], oT.a[:, cc, i * 512:(i + 1) * 512],
                                                                                start=(cc == 0), stop=(cc == 7)),
                         reads=[Ow.r, oT.r], writes=[py.r])
                S.op('dve', lambda e, m=m, tg=tg, py=py: e.scalar_tensor_tensor(xap(m, tg), py.a, modG[l][:, 1, m, cnd:cnd + 1], xap(m, tg), ALU.mult, ALU.add),
                     reads=[py.r, modD_R[l][1], xres[m][tg]], writes=[xres[m][tg]])

    import os
    parts = os.environ.get('KPARTS', 'all')

    def on(p):
        return parts == 'all' or p in parts.split(',')
    load_x()
    adaln_slabs(0, 0, 6)
    adaln_finish(0, (0,))
    for l in range(2):
        if l == 0:
            if on('f01'):
                ffn(0, 1, between=lambda i: adaln_slabs(0, 6 + 2 * i, 8 + 2 * i))
            else:
                adaln_slabs(0, 6, 18)
            adaln_finish(0, (1, 2))
        else:
            if on('f11'):
                ffn(1, 1)
        for grp in range(3):
            if on('m%d%d' % (l, grp)):
                mixer(l, grp)
        if l == 0:
            if on('f02'):
                ffn(0, 2, between=lambda i: adaln_slabs(1, 3 * i, 3 * i + 3))
            else:
                adaln_slabs(1, 0, 18)
            adaln_finish(1)
        else:
            if on('f12'):
                ffn(1, 2, tail=True)
    if not on('f12'):
        final_out()
    S.emit()


_PROG = {}


def _prep_weights(inp):
    sh = {}
    for l in range(2):
        sh['ada_w%d' % l] = np.ascontiguousarray(inp['ada_w_l%d' % l], dtype=np.float32)
        sh['ada_b%d' % l] = np.ascontiguousarray(inp['ada_b_l%d' % l].reshape(72, 128).T, dtype=np.float32)
        sh['norm%d' % l] = np.ascontiguousarray(inp['norm_l%d' % l].reshape(3, 8, 128).transpose(2, 0, 1), dtype=np.float32)
        for f in (1, 2):
            sh['f%din%d' % (f, l)] = np.ascontiguousarray(inp['ffn%d_in_l%d' % (f, l)], dtype=np.float32)
            sh['f%dout%d' % (f, l)] = np.ascontiguousarray(inp['ffn%d_out_l%d' % (f, l)], dtype=np.float32)
        sh['mixout%d' % l] = np.ascontiguousarray(inp['mix_out_l%d' % l], dtype=np.float32)
    w = np.asarray(inp['mix_in_l0'], dtype=np.float32)
    qa, ka, va = w[:, 0:512], w[:, 512:640], w[:, 640:768]
    qb, kb, vb = w[:, 768:1280], w[:, 1280:1792], w[:, 1792:2304]
    gt, ob = w[:, 2304:2336], w[:, 2336:2848]
    z = np.zeros((D, 24), np.float32)
    g1 = np.concatenate([gt[:, 0:8], z, gt[:, 16:24], z], 1)
    g2 = np.concatenate([gt[:, 8:16], z, gt[:, 24:32], z], 1)
    kadup = np.concatenate([ka[:, 0:64], ka[:, 0:64], ka[:, 64:128], ka[:, 64:128]], 1)
    m0_ = np.concatenate([qa, kadup, g1, g2, va, qb, kb, vb, ob, ka, np.zeros((D, L0_COLS - 3200), np.float32)], 1)
    assert m0_.shape[1] == L0_COLS
    sh['mixin0'] = np.ascontiguousarray(m0_)
    w = np.asarray(inp['mix_in_l1'], dtype=np.float32)
    qc, kc, vc, qd, kd, vd = w[:, 0:512], w[:, 512:1024], w[:, 1024:1536], w[:, 1536:2048], w[:, 2048:2176], w[:, 2176:2304]
    kddup = np.concatenate([kd[:, 0:64], kd[:, 0:64], kd[:, 64:128], kd[:, 64:128]], 1)
    m1_ = np.concatenate([qc, kc, vc, qd, kddup, kd, vd], 1)
    assert m1_.shape[1] == L1_COLS
    sh['mixin1'] = np.ascontiguousarray(m1_)
    sh['gfin'] = np.ascontiguousarray(np.asarray(inp['norm_final'], np.float32).reshape(8, 128).T)
    qk = np.asarray(inp['qk_norm_l0'], np.float32)
    sh['qkg'] = np.ascontiguousarray(np.stack([np.tile(qk[0], 2), np.tile(qk[1], 2)], 1))
    sh['gkbc'] = np.ascontiguousarray(qk[1])
    gb = np.asarray(inp['gate_bias_l0'], np.float32)
    gbt = np.zeros((64, 2), np.float32)
    gbt[0:8, 0] = gb[0:8]
    gbt[32:40, 0] = gb[16:24]
    gbt[0:8, 1] = gb[8:16]
    gbt[32:40, 1] = gb[24:32]
    sh['gbias'] = gbt
    sh['hgn'] = np.ascontiguousarray(inp['head_norm_l0'], dtype=np.float32)
    sh['sink'] = np.ascontiguousarray(inp['sink_l1'], dtype=np.float32)
    rpb = np.asarray(inp['rpb_l1'], np.float32)
    sc = np.arange(64)[:, None]
    qc_ = np.arange(64)[None, :]
    dx = np.clip(sc - qc_ + 15, 0, 30)
    tb = np.zeros((8, 128, 15, 64), np.float32)
    for dyi in range(15):
        blk = rpb[:, 14 - dyi, :][:, dx]
        tb[:, 0:64, dyi, :] = blk
        tb[:, 64:128, dyi, :] = blk
    sh['natb'] = np.ascontiguousarray(tb.reshape(8, 128, 15 * 64))
    for k, v in _consts().items():
        sh['c_' + k] = v
    return sh


def kernel(**inp):
    inp = {k: np.asarray(v) for k, v in inp.items()}
    dbg = inp.pop('_dbg', None)
    key = 'main'
    if key not in _PROG:
        _PROG[key] = build_program(None)
    nc = _PROG[key]
    sh = _prep_weights(inp)
    in_maps = []
    for i in range(8):
        b = i // 4
        m = dict(sh)
        xp = inp['x_prompt'][4 * i:4 * i + 4].reshape(1024, D)
        xs = inp['x_sample'][b]
        m['xin'] = np.ascontiguousarray(np.concatenate([xp, xs], 0), dtype=np.float32)
        cond = np.stack([inp['c_ctx'], inp['c'][b]], 0).astype(np.float32)
        m['condT'] = np.ascontiguousarray(cond.reshape(2, 8, 128).transpose(2, 1, 0))
        m['kctx0'] = np.ascontiguousarray(inp['cache_l0_attn_k'][b].reshape(256, 128), dtype=np.float32)
        m['vctx0'] = np.ascontiguousarray(inp['cache_l0_attn_v'][b].reshape(256, 128), dtype=np.float32)
        m['C0'] = np.ascontiguousarray(inp['state_l0_mlstm_C'][b], dtype=np.float32)
        m['n0'] = np.ascontiguousarray(inp['state_l0_mlstm_n'][b], dtype=np.float32)
        m['m0'] = np.ascontiguousarray(inp['state_l0_mlstm_m'][b].reshape(16), dtype=np.float32)
        m['kcctx'] = np.ascontiguousarray(inp['cache_l1_na_k'][b].reshape(256, 512), dtype=np.float32)
        m['vcctx'] = np.ascontiguousarray(inp['cache_l1_na_v'][b].reshape(256, 512), dtype=np.float32)
        m['kdctx'] = np.ascontiguousarray(inp['cache_l1_swa_k'][b].reshape(256, 128), dtype=np.float32)
        m['vdctx'] = np.ascontiguousarray(inp['cache_l1_swa_v'][b].reshape(256, 128), dtype=np.float32)
        in_maps.append(m)
    import os
    ncore = int(os.environ.get('KCORES', '8'))
    res = run_bass_kernel_spmd(nc, in_maps[:ncore], core_ids=list(range(ncore)))
    R = list(res.results)
    while len(R) < 8:
        R.append(R[0])
    cat = lambda k: np.concatenate([np.asarray(R[i][k]) for i in range(8)], 0)
    y_prompt = cat('o_yp').reshape(32, 256, D)
    y_sample = np.stack([np.asarray(R[0]['o_ys']), np.asarray(R[4]['o_ys'])], 0)
    k0 = cat('o_k0').reshape(32, 256, 2, 64)
    v0 = cat('o_v0').reshape(32, 256, 2, 64)
    Cst = cat('o_C')
    nst = cat('o_n')
    mst = cat('o_m')
    kc1 = cat('o_kc').reshape(32, 256, 8, 64)
    vc1 = cat('o_vc').reshape(32, 256, 8, 64)
    kd1 = cat('o_kd').reshape(32, 256, 2, 64)
    vd1 = cat('o_vd').reshape(32, 256, 2, 64)
    outs = (y_prompt, y_sample, k0, v0, Cst, nst, mst, kc1, vc1, kd1, vd1)
    return tuple(np.ascontiguousarray(o, dtype=np.float32) for o in outs)
```

```python
import contextlib
import numpy as np
import concourse.bass as bass
import concourse.mybir as mybir
from concourse.bass_utils import run_bass_kernel_spmd

F32 = mybir.dt.float32
BF16 = mybir.dt.bfloat16
AF = mybir.ActivationFunctionType
ALU = mybir.AluOpType
AX = mybir.AxisListType

ENGS = ('pe', 'act', 'dve', 'pool', 'sp')
NEG = -1.0e30
EPS = 1e-6


class Res:
    __slots__ = ('name', 'w', 'r', 'sem', 'dcount', 'excl')

    def __init__(self, name=''):
        self.name = name
        self.excl = False
        self.w = {}
        self.r = {}
        self.sem = None
        self.dcount = 0


class _Rec:
    def __init__(self):
        self.call = None

    def __getattr__(self, name):
        def f(*a, **k):
            self.call = (name, a, k)
            return self
        return f


class Ins:
    __slots__ = ('fn', 'waits', 'milestone', 'dma_res')

    def __init__(self, fn):
        rec = _Rec()
        fn(rec)
        name, a, k = rec.call
        self.fn = lambda eh: getattr(eh, name)(*a, **k)
        self.waits = []
        self.milestone = False
        self.dma_res = None


class Sched:
    def __init__(self, nc, stack):
        self.nc = nc
        self.stack = stack
        self.streams = {e: [] for e in ENGS}
        self.known = {e: {} for e in ENGS}
        self.dma_res = []
        self.semof = {}
        self.esem = {}
        for e in ('pe', 'act', 'dve', 'pool'):
            self.esem[e] = stack.enter_context(nc.semaphore('es_' + e))

    def _waits(self, ins, eng, deps):
        for key, val in deps.items():
            if self.known[eng].get(key, -1) >= val:
                continue
            self.known[eng][key] = val
            ins.waits.append((key, val))
            if key[0] == 'e':
                self.streams[key[1]][val].milestone = True

    def _deps(self, eng, reads, writes):
        deps = {}

        def add(d, raw):
            for k, v in d.items():
                if k[0] == 'e' and k[1] == eng and (eng == 'pe' or not raw):
                    continue
                if deps.get(k, -1) < v:
                    deps[k] = v
        for r in reads:
            add(r.w, True)
            if r.excl:
                add({k: v for k, v in r.r.items() if not (k[0] == 'e' and k[1] == eng)}, False)
        for r in writes:
            add(r.w, False)
            add(r.r, False)
        return deps

    def op(self, eng, fn, reads=(), writes=()):
        ins = Ins(fn)
        idx = len(self.streams[eng])
        self._waits(ins, eng, self._deps(eng, reads, writes))
        key = ('e', eng)
        for r in writes:
            r.w = {key: idx}
            r.r = {}
        for r in reads:
            if r not in writes:
                r.r[key] = idx
        self.streams[eng].append(ins)
        return ins

    def dma(self, eng, fn, sres, reads=(), writes=()):
        ins = Ins(fn)
        ins.dma_res = sres
        if sres.sem is None:
            sres.sem = self.stack.enter_context(self.nc.semaphore('ds_%d' % len(self.dma_res)))
            self.dma_res.append(sres)
            self.semof[id(sres)] = sres
        self._waits(ins, eng, self._deps(eng, reads, writes))
        sres.dcount += 1
        key = ('d', id(sres))
        val = 16 * sres.dcount
        for r in writes:
            r.w = {key: val}
            r.r = {}
        for r in reads:
            if r not in writes:
                r.r[key] = val
        self.streams[eng].append(ins)
        return ins

    def emit(self):
        nc = self.nc
        ordinal = {}
        for e in ('pe', 'act', 'dve', 'pool'):
            c = 0
            for i, ins in enumerate(self.streams[e]):
                if ins.milestone:
                    c += 1
                    ordinal[(e, i)] = c

        def run(eng, eh):
            for i, ins in enumerate(self.streams[eng]):
                for key, val in ins.waits:
                    if key[0] == 'e':
                        eh.wait_ge(self.esem[key[1]], ordinal[(key[1], val)])
                    else:
                        eh.wait_ge(self.semof[key[1]].sem, val)
                bi = ins.fn(eh)
                if ins.dma_res is not None:
                    bi.then_inc(ins.dma_res.sem, 16)
                elif ins.milestone:
                    bi.then_inc(self.esem[eng], 1)
            if eng == 'sp':
                for r in self.dma_res:
                    eh.wait_ge(r.sem, 16 * r.dcount)

        with nc.Block() as block:
            @block.tensor
            def _(e):
                run('pe', e)

            @block.scalar
            def _(e):
                run('act', e)

            @block.vector
            def _(e):
                run('dve', e)

            @block.gpsimd
            def _(e):
                run('pool', e)

            @block.sync
            def _(e):
                run('sp', e)


class Tl:
    __slots__ = ('a', 'r')

    def __init__(self, a, r):
        self.a = a
        self.r = r


D = 1024
DFF = 2816
NTOK = 2048
L0_COLS = 3328
L1_COLS = 2560


def _rope_tables():
    t = np.arange(1024)
    pos = np.stack([t // 64, t % 64], -1).astype(np.float32)
    freqs = (10000.0 ** (-np.arange(16, dtype=np.float32) / 16)).astype(np.float32)
    ang = (pos[:, :, None] * freqs).reshape(1024, 32).astype(np.float32)
    cos = np.cos(ang).astype(np.float32).T
    sin = np.sin(ang).astype(np.float32).T
    C = np.concatenate([cos, cos, cos, cos], 0)
    Sg = np.concatenate([-sin, sin, -sin, sin], 0)
    return np.ascontiguousarray(C), np.ascontiguousarray(Sg)


def _consts():
    c = {}
    C, Sg = _rope_tables()
    c['ropeC'] = C
    c['ropeS'] = Sg
    s = np.arange(128)[:, None]
    t = np.arange(128)[None, :]
    c['maskF'] = np.where(t >= s, 0.0, NEG).astype(np.float32)
    c['maskB'] = np.where(t <= s, 0.0, NEG).astype(np.float32)
    c['mask01F'] = (t >= s).astype(np.float32)
    c['mask01B'] = (t <= s).astype(np.float32)
    psw = np.zeros((128, 128), np.float32)
    for m in range(128):
        d = m % 64
        k = (m - d) + ((d + 32) % 64)
        psw[k, m] = 1.0
    c['psw'] = psw
    oh = np.zeros((64, 16), np.float32)
    for h in range(8):
        oh[h, h] = 1.0
        oh[32 + h, 8 + h] = 1.0
    c['oh'] = oh
    sc = np.arange(64)[:, None]
    qc = np.arange(64)[None, :]
    ws = np.clip(qc - 8, 0, 48)
    cm = np.where((sc >= ws) & (sc < ws + 16), 0.0, NEG).astype(np.float32)
    c['cmask'] = np.concatenate([cm, cm], 0)
    return c


def _na_rows():
    start = [min(max(r - 4, 0), 8) for r in range(16)]
    valid = [[start[r] <= s < start[r] + 8 for r in range(16)] for s in range(16)]
    return valid


def build_program(dbg=None):
    nc = bass.Bass("TRN2", target_bir_lowering=False)
    st = contextlib.ExitStack()
    with st:
        _build(nc, st, dbg)
    return nc


def _build(nc, st, dbg):
    S = Sched(nc, st)

    def din(name, shape):
        return nc.dram_tensor(name, list(shape), F32, kind="ExternalInput")

    def dout(name, shape):
        return nc.dram_tensor(name, list(shape), F32, kind="ExternalOutput")

    xin = din('xin', [NTOK, D])
    condT = din('condT', [128, 8, 2])
    gfin = din('gfin', [128, 8])
    Wd = []
    for l in range(2):
        w = {}
        w['ada_w'] = din('ada_w%d' % l, [D, 9 * D])
        w['ada_b'] = din('ada_b%d' % l, [128, 72])
        w['norm'] = din('norm%d' % l, [128, 3, 8])
        for f in (1, 2):
            w['f%din' % f] = din('f%din%d' % (f, l), [D, 2 * DFF])
            w['f%dout' % f] = din('f%dout%d' % (f, l), [DFF, D])
        w['mix_in'] = din('mixin%d' % l, [D, L0_COLS if l == 0 else L1_COLS])
        w['mix_out'] = din('mixout%d' % l, [D, D])
        Wd.append(w)
    qkg = din('qkg', [128, 2])
    gkbc = din('gkbc', [64])
    gbias = din('gbias', [64, 2])
    hgn = din('hgn', [512])
    kctx0 = din('kctx0', [256, 128])
    vctx0 = din('vctx0', [256, 128])
    C0 = din('C0', [2, 8, 64, 64])
    n0 = din('n0', [2, 8, 64])
    m0 = din('m0', [16])
    kcctx = din('kcctx', [256, 512])
    vcctx = din('vcctx', [256, 512])
    kdctx = din('kdctx', [256, 128])
    vdctx = din('vdctx', [256, 128])
    natb = din('natb', [8, 128, 15 * 64])
    sink = din('sink', [8])
    cd = {k: din('c_' + k, v.shape) for k, v in _consts().items()}

    o_yp = dout('o_yp', [1024, D])
    o_ys = dout('o_ys', [1024, D])
    o_k0 = dout('o_k0', [4, 256, 128])
    o_v0 = dout('o_v0', [4, 256, 128])
    o_C = dout('o_C', [4, 2, 8, 64, 64])
    o_n = dout('o_n', [4, 2, 8, 64])
    o_m = dout('o_m', [4, 2, 8])
    o_kc = dout('o_kc', [4, 256, 512])
    o_vc = dout('o_vc', [4, 256, 512])
    o_kd = dout('o_kd', [4, 256, 128])
    o_vd = dout('o_vd', [4, 256, 128])
    dbg_out = {}
    if dbg:
        for name, shape in dbg.items():
            dbg_out[name] = dout('dbg_' + name, shape)

    cnt = [0]

    def sb(shape, dt, name=None):
        cnt[0] += 1
        t = st.enter_context(nc.sbuf_tensor(name or ('t%d' % cnt[0]), list(shape), dt))
        return t

    def mk(shape, dt, name=None):
        t = sb(shape, dt, name)
        return Tl(t[:], Res(name or ''))

    banks = []
    for i in range(8):
        t = st.enter_context(nc.psum_tensor('bank%d' % i, [128, 512], F32))
        banks.append(Tl(t[:], Res('bank%d' % i)))
        banks[-1].r.excl = True
    rot = {'a': [0, 1], 'b': [2, 3], 'c': [4, 5], 'x': [6, 7], 's': [0, 1, 2, 3]}
    rotc = {k: 0 for k in rot}

    def ps(tag):
        i = rot[tag][rotc[tag] % len(rot[tag])]
        rotc[tag] += 1
        return banks[i]

    AR_BYTES = 94 * 1024
    arena = sb([128, AR_BYTES // 2], BF16, 'arena')
    ar = {'off': 0, 'live': [], 'inherit': {}}

    def ar_reset():
        toks = dict(ar['inherit'])
        for r in ar['live']:
            for d in (r.w, r.r):
                for k, v in d.items():
                    if toks.get(k, -1) < v:
                        toks[k] = v
        ar['inherit'] = toks
        ar['live'] = []
        ar['off'] = 0

    def ar_res(name=''):
        r = Res(name)
        r.w = dict(ar['inherit'])
        ar['live'].append(r)
        return r

    def carve(shape, dt, name=''):
        n = int(np.prod(shape[1:]))
        nb = n * (4 if dt == F32 else 2)
        nb = (nb + 31) // 32 * 32
        off = ar['off']
        assert off + nb <= AR_BYTES, ('arena overflow', name, off, nb)
        ar['off'] = off + nb
        a = arena[:, off // 2: off // 2 + (n * (2 if dt == F32 else 1))]
        if dt == F32:
            a = a.bitcast(F32)
        if shape[0] != 128:
            a = a[0:shape[0]]
        if len(shape) > 2:
            names = ' '.join('d%d' % i for i in range(1, len(shape)))
            kw = {'d%d' % i: shape[i] for i in range(1, len(shape) - 1)}
            a = a.rearrange('p (%s) -> p %s' % (names, names), **kw)
        return Tl(a, ar_res(name))

    xT = sb([128, 8, NTOK], F32, 'xT')
    xres = [[Res('x%d_%d' % (m, tg)) for tg in range(4)] for m in range(8)]

    def xap(m, tg):
        return xT[:, m, tg * 512:(tg + 1) * 512]

    NRING = 4
    ring_all = [mk([128, 4096], BF16, 'ring%d' % i) for i in range(NRING)]
    ring = list(ring_all)
    ringc = [0]

    def ring_next():
        s_ = ring[ringc[0] % len(ring)]
        ringc[0] += 1
        return s_

    def wslab_in(wd, c0, n):
        s = ring_next()
        dst = s.a[:, 0:8 * n].rearrange('p (k n) -> p k n', k=8)
        src = wd.ap()[:, c0:c0 + n].rearrange('(k p) n -> p k n', p=128)
        S.dma('pool', lambda e: e.dma_start(out=dst, in_=src), s.r, writes=[s.r])
        return Tl(dst, s.r)

    def wslab_out(wd, r0, nchunk):
        s = ring_next()
        dst = s.a[:, 0:nchunk * 1024].rearrange('p (c n) -> p c n', c=nchunk)
        src = wd.ap()[r0:r0 + 128 * nchunk, :].rearrange('(c p) n -> p c n', p=128)
        S.dma('pool', lambda e: e.dma_start(out=dst, in_=src), s.r, writes=[s.r])
        return Tl(dst, s.r)

    CR = Res('consts')

    def cload(dram, shape, dt=F32, src=None, name=None):
        t = sb(shape, dt, name)
        a = src if src is not None else dram.ap()
        S.dma('sp', lambda e: e.dma_start(out=t[:], in_=a), CR, writes=[CR])
        return t

    ident_f = sb([128, 128], F32, 'ident_f')
    ident_b = sb([128, 128], BF16, 'ident_b')
    ones_b = sb([128, 128], BF16, 'ones_b')
    bd_b = sb([128, 128], BF16, 'bd_b')
    ones_f = sb([64, 128], F32, 'ones_f')
    KR = Res('kconst')
    S.op('pool', lambda e: e.memset(ident_f[:], 0.0), writes=[KR])
    S.op('pool', lambda e: e.affine_select(out=ident_f[:], in_=ident_f[:], pattern=[[-1, 128]], compare_op=ALU.not_equal,
                                           fill=1.0, base=0, channel_multiplier=1), reads=[KR], writes=[KR])
    S.op('pool', lambda e: e.tensor_copy(ident_b[:], ident_f[:]), reads=[KR], writes=[KR])
    S.op('pool', lambda e: e.memset(ones_b[:], 1.0), writes=[KR])
    S.op('pool', lambda e: e.memset(ones_f[:], 1.0), writes=[KR])
    S.op('pool', lambda e: e.memset(bd_b[:], 0.0), writes=[KR])
    S.op('pool', lambda e: e.memset(bd_b[0:64, 0:64], 1.0), reads=[KR], writes=[KR])
    S.op('pool', lambda e: e.memset(bd_b[64:128, 64:128], 1.0), reads=[KR], writes=[KR])

    cond_t = cload(condT, [128, 8, 2])
    gfin_t = cload(gfin, [128, 8])
    adab_t = [cload(Wd[l]['ada_b'], [128, 72]) for l in range(2)]
    norm_t = [cload(Wd[l]['norm'], [128, 3, 8]) for l in range(2)]
    qkg_t = cload(qkg, [128, 2])
    gkbc_t = cload(gkbc, [128, 64], src=gkbc.ap().partition_broadcast(128))
    gbias_t = cload(gbias, [64, 2])
    hgn_t = cload(hgn, [128, 512], src=hgn.ap().partition_broadcast(128))
    m0bc_t = cload(m0, [128, 16], src=m0.ap().partition_broadcast(128))
    m0col_t = sb([64, 1], F32, 'm0col')
    S.dma('sp', lambda e: e.dma_start(out=m0col_t[0:8, :], in_=m0.ap()[0:8].rearrange('(p o) -> p o', o=1)), CR, writes=[CR])
    S.dma('sp', lambda e: e.dma_start(out=m0col_t[32:40, :], in_=m0.ap()[8:16].rearrange('(p o) -> p o', o=1)), CR, writes=[CR])
    sink_t = cload(sink, [128, 8], src=sink.ap().partition_broadcast(128))
    ropeC_t = sb([128, 1024], BF16, 'ropeC')
    ropeS_t = sb([128, 1024], BF16, 'ropeS')
    S.dma('pool', lambda e: e.dma_start(out=ropeC_t[:], in_=cd['ropeC'].ap()), CR, writes=[CR])
    S.dma('pool', lambda e: e.dma_start(out=ropeS_t[:], in_=cd['ropeS'].ap()), CR, writes=[CR])
    maskF_t = cload(cd['maskF'], [128, 128])
    maskB_t = cload(cd['maskB'], [128, 128])
    mask01F_t = cload(cd['mask01F'], [128, 128])
    mask01B_t = cload(cd['mask01B'], [128, 128])
    psw_f = cload(cd['psw'], [128, 128])
    oh_t = cload(cd['oh'], [64, 16])
    cmask_t = cload(cd['cmask'], [128, 64])
    psw_b = sb([128, 128], BF16, 'psw_b')
    S.op('pool', lambda e: e.tensor_copy(psw_b[:], psw_f[:]), reads=[CR], writes=[KR])
    esink_t = sb([128, 8], F32, 'esink')
    S.op('act', lambda e: e.activation(esink_t[:], sink_t[:], AF.Exp), reads=[CR], writes=[KR])
    ngbias_t = sb([64, 1], F32, 'ngbias')
    S.op('pool', lambda e: e.tensor_scalar(ngbias_t[:], gbias_t[:, 1:2], -1.0, None, ALU.mult), reads=[CR], writes=[KR])
    CK = [CR, KR]

    scond = sb([128, 8, 2], BF16, 'scond')
    S.op('act', lambda e: e.activation(scond[:], cond_t[:], AF.Silu), reads=[CR], writes=[KR])
    mod_t = [sb([128, 72, 2], F32, 'mod%d' % l) for l in range(2)]
    modT_R = [[Res('modT') for n in range(3)] for l in range(2)]
    modD_R = [[Res('modD') for n in range(3)] for l in range(2)]
    modA = [sb([128, 3, 8, 2], F32, 'modA%d' % l) for l in range(2)]
    modG = [sb([128, 3, 8, 2], F32, 'modG%d' % l) for l in range(2)]

    def adaln_slabs(l, s0, s1):
        for s in range(s0, s1):
            W = wslab_in(Wd[l]['ada_w'], 512 * s, 512)
            pb = ps('x')
            for c in range(4):
                for k in range(8):
                    S.op('pe', lambda e, c=c, k=k, W=W, pb=pb: e.matmul(pb.a[:, 2 * c:2 * c + 2], W.a[:, k, c * 128:(c + 1) * 128],
                                                                      scond[:, k, :], start=(k == 0), stop=(k == 7)),
                         reads=[W.r, KR], writes=[pb.r])
            dstv = mod_t[l][:, 4 * s:4 * s + 4, :]
            bv = adab_t[l][:, 4 * s:4 * s + 4].unsqueeze(2).to_broadcast([128, 4, 2])
            S.op('dve', lambda e, pb=pb, dstv=dstv, bv=bv: e.tensor_tensor(dstv, pb.a[:, 0:8].rearrange('p (c j) -> p c j', c=4), bv, ALU.add),
                 reads=[pb.r, CR], writes=[modT_R[l][s // 6]])

    def adaln_finish(l, ns=(0, 1, 2)):
        for n in ns:
            sc = mod_t[l][:, (3 * n + 1) * 8:(3 * n + 2) * 8, :]
            g = mod_t[l][:, (3 * n + 2) * 8:(3 * n + 3) * 8, :]
            gn = norm_t[l][:, n, :].unsqueeze(2).to_broadcast([128, 8, 2])
            S.op('dve', lambda e, n=n, sc=sc, gn=gn: e.scalar_tensor_tensor(modA[l][:, n, :, :], sc, 1.0, gn, ALU.add, ALU.mult),
                 reads=[modT_R[l][n], CR], writes=[modD_R[l][n]])
            fac = 1.0 if n == 1 else 0.5
            S.op('dve', lambda e, n=n, g=g, fac=fac: e.tensor_scalar(modG[l][:, n, :, :], g, fac, None, ALU.mult),
                 reads=[modT_R[l][n], modD_R[l][n]], writes=[modD_R[l][n]])

    def modB(l, n, m, cnd):
        return mod_t[l][:, (3 * n) * 8 + m, cnd:cnd + 1]

    def tg_cond(tg):
        return 0 if tg < 2 else 1

    def rms_stats(tg, tmp_sq, rs):
        pb = ps('x')
        for m in range(8):
            sq = tmp_sq[m % 2]
            S.op('act', lambda e, m=m, sq=sq: e.activation(sq.a, xap(m, tg), AF.Square), reads=[xres[m][tg]], writes=[sq.r])
            S.op('pe', lambda e, m=m, sq=sq, pb=pb: e.matmul(pb.a, ones_b[:], sq.a, start=(m == 0), stop=(m == 7)),
                 reads=[sq.r, KR], writes=[pb.r])
        S.op('act', lambda e, pb=pb: e.activation(rs.a, pb.a, AF.Sqrt, bias=EPS, scale=1.0 / D), reads=[pb.r], writes=[rs.r])
        S.op('dve', lambda e: e.reciprocal(rs.a, rs.a), reads=[rs.r], writes=[rs.r])

    def norm_mod(l, n, tg, hdst, hcol0, tmp_sq, rs, tmpf):
        cnd = tg_cond(tg)
        rms_stats(tg, tmp_sq, rs)
        for m in range(8):
            tf = tmpf[m % 2]
            S.op('dve', lambda e, m=m, tf=tf: e.scalar_tensor_tensor(tf.a, xap(m, tg), modA[l][:, n, m, cnd:cnd + 1], rs.a, ALU.mult, ALU.mult),
                 reads=[xres[m][tg], rs.r, modD_R[l][n]], writes=[tf.r])
            S.op('act', lambda e, m=m, tf=tf: e.activation(hdst.a[:, m, hcol0:hcol0 + 512], tf.a, AF.Identity, bias=modB(l, n, m, cnd), scale=1.0),
                 reads=[tf.r, modT_R[l][n]], writes=[hdst.r])

    def load_x():
        ar_reset()
        stg = [carve([128, 1024], F32, 'xstg%d' % i) for i in range(4)]
        for tg in range(4):
            for j in range(4):
                t = tg * 4 + j
                S.dma('sp', lambda e, j=j, t=t: e.dma_start(out=stg[j].a, in_=xin.ap()[t * 128:(t + 1) * 128, :]), stg[j].r, writes=[stg[j].r])
            for m in range(8):
                pb = ps('a' if m % 2 == 0 else 'b')
                for j in range(4):
                    S.op('pe', lambda e, j=j, m=m, pb=pb: e.transpose(pb.a[:, j * 128:(j + 1) * 128], stg[j].a[:, m * 128:(m + 1) * 128], ident_f[:]),
                         reads=[stg[j].r, KR], writes=[pb.r])
                if m % 2 == 0:
                    S.op('dve', lambda e, m=m, pb=pb: e.tensor_copy(xap(m, tg), pb.a), reads=[pb.r], writes=[xres[m][tg]])
                else:
                    S.op('act', lambda e, m=m, pb=pb: e.copy(xap(m, tg), pb.a), reads=[pb.r], writes=[xres[m][tg]])

    def ffn(l, which, between=None, tail=False):
        n = 0 if which == 1 else 2
        ar_reset()
        h = carve([128, 8, NTOK], BF16, 'h')
        hres = [ar_res('h%d' % tg) for tg in range(4)]
        hid = carve([128, 4, NTOK], BF16, 'hid')
        hidres = [[ar_res('hid') for tg in range(4)] for c in range(4)]
        tmp_sq = [carve([128, 512], BF16, 'sq') for _ in range(2)]
        rs = carve([128, 512], F32, 'rs')
        tmpf = [carve([128, 512], F32, 'tmpf') for _ in range(2)]
        sgt = [carve([128, 512], F32, 'sg') for _ in range(2)]
        fbufs = final_alloc() if tail else None
        for tg in range(4):
            norm_mod(l, n, tg, Tl(h.a, hres[tg]), tg * 512, tmp_sq, rs, tmpf)
        win = Wd[l]['f%din' % which]
        wout = Wd[l]['f%dout' % which]
        for i in range(6):
            nch = 4 if i < 5 else 2
            G = wslab_in(win, 512 * i, 128 * nch)
            U = wslab_in(win, DFF + 512 * i, 128 * nch)
            O = wslab_out(wout, 512 * i, nch)
            for tg in range(4):
                for c in range(nch):
                    pg = ps('a')
                    pu = ps('b')
                    for k in range(8):
                        S.op('pe', lambda e, k=k, c=c, tg=tg, pg=pg, G=G: e.matmul(pg.a, G.a[:, k, c * 128:(c + 1) * 128], h.a[:, k, tg * 512:(tg + 1) * 512],
                                                                                   start=(k == 0), stop=(k == 7)),
                             reads=[G.r, hres[tg]], writes=[pg.r])
                    for k in range(8):
                        S.op('pe', lambda e, k=k, c=c, tg=tg, pu=pu, U=U: e.matmul(pu.a, U.a[:, k, c * 128:(c + 1) * 128], h.a[:, k, tg * 512:(tg + 1) * 512],
                                                                                   start=(k == 0), stop=(k == 7)),
                             reads=[U.r, hres[tg]], writes=[pu.r])
                    sg = sgt[c % 2]
                    S.op('act', lambda e, pg=pg, sg=sg: e.activation(sg.a, pg.a, AF.Silu), reads=[pg.r], writes=[sg.r])
                    S.op('dve', lambda e, c=c, tg=tg, pu=pu, sg=sg: e.tensor_tensor(hid.a[:, c, tg * 512:(tg + 1) * 512], sg.a, pu.a, ALU.mult),
                         reads=[pu.r, sg.r], writes=[hidres[c][tg]])
            for tg in range(4):
                cnd = tg_cond(tg)
                for m in range(8):
                    py = ps('c')
                    for c in range(nch):
                        S.op('pe', lambda e, c=c, m=m, tg=tg, py=py, O=O, nch=nch: e.matmul(py.a, O.a[:, c, m * 128:(m + 1) * 128], hid.a[:, c, tg * 512:(tg + 1) * 512],
                                                                                            start=(c == 0), stop=(c == nch - 1)),
                             reads=[O.r, hidres[c][tg]], writes=[py.r])
                    S.op('dve', lambda e, m=m, tg=tg, py=py, cnd=cnd: e.scalar_tensor_tensor(xap(m, tg), py.a, modG[l][:, n, m, cnd:cnd + 1], xap(m, tg), ALU.mult, ALU.add),
                         reads=[py.r, modD_R[l][n], xres[m][tg]], writes=[xres[m][tg]])
                if tail and i == 5:
                    final_tg(fbufs, tg)
            if between is not None:
                between(i)

    def final_alloc():
        tmp_sq = [carve([128, 512], BF16, 'fsq') for _ in range(2)]
        rs = carve([128, 512], F32, 'frs')
        yT = carve([128, 8, 512], F32, 'yT')
        ystg = [carve([128, 1024], F32, 'ystg') for _ in range(2)]
        return tmp_sq, rs, yT, ystg

    def final_out():
        ar_reset()
        bufs = final_alloc()
        for tg in range(4):
            final_tg(bufs, tg)

    def final_tg(bufs, tg):
        tmp_sq, rs, yT, ystg = bufs
        if True:
            rms_stats(tg, tmp_sq, rs)
            for m in range(8):
                S.op('dve', lambda e, m=m: e.scalar_tensor_tensor(yT.a[:, m, :], xap(m, tg), gfin_t[:, m:m + 1], rs.a, ALU.mult, ALU.mult),
                     reads=[xres[m][tg], rs.r, CR], writes=[yT.r])
            for j in range(4):
                t = tg * 4 + j
                ys = ystg[j % 2]
                for half in range(2):
                    pb = ps('a' if half == 0 else 'b')
                    for mm in range(4):
                        m = half * 4 + mm
                        S.op('pe', lambda e, mm=mm, m=m, j=j, pb=pb: e.transpose(pb.a[:, mm * 128:(mm + 1) * 128], yT.a[:, m, j * 128:(j + 1) * 128], ident_f[:]),
                             reads=[yT.r, KR], writes=[pb.r])
                    if half == 0:
                        S.op('act', lambda e, pb=pb, ys=ys: e.copy(ys.a[:, 0:512], pb.a), reads=[pb.r], writes=[ys.r])
                    else:
                        S.op('dve', lambda e, pb=pb, ys=ys: e.tensor_copy(ys.a[:, 512:1024], pb.a), reads=[pb.r], writes=[ys.r])
                dst = (o_yp if t < 8 else o_ys).ap()[(t % 8) * 128:(t % 8 + 1) * 128, :]
                S.dma('sp', lambda e, ys=ys, dst=dst: e.dma_start(out=dst, in_=ys.a), ys.r, reads=[ys.r])

    def proj_fm(W, c0, M, hg, ncols, post):
        for b in range(ncols // 512):
            pb = ps('a')
            for k in range(8):
                S.op('pe', lambda e, k=k, b=b, pb=pb: e.matmul(pb.a[0:M, :], W.a[:, k, c0:c0 + M], hg.a[:, k, b * 512:(b + 1) * 512],
                                                               start=(k == 0), stop=(k == 7)),
                     reads=[W.r, hg.r], writes=[pb.r])
            post(b, pb)

    def proj_tm(W, c0, n, hg, ntiles, post):
        for t in range(ntiles):
            pb = ps('b')
            for k in range(8):
                S.op('pe', lambda e, k=k, t=t, pb=pb: e.matmul(pb.a[:, 0:n], hg.a[:, k, t * 128:(t + 1) * 128], W.a[:, k, c0:c0 + n],
                                                               start=(k == 0), stop=(k == 7)),
                     reads=[W.r, hg.r], writes=[pb.r])
            post(t, pb)

    class Work:
        pass

    LA = 3

    def _emit_pv(wk, blk, kd, vq, c0, PT, b0, last, qts, finalize):
        if blk['po'] is None:
            blk['po'] = ps('c')
        po = blk['po']
        ns = kd['ns']
        pb = kd.get('pbase', 0)
        for i in vq:
            oc = (i - b0) * 128
            first = blk['first']
            S.op('pe', lambda e: e.matmul(po.a[:, oc:oc + 65], PT.a[pb:pb + ns, i * 128 - c0:i * 128 - c0 + 128], kd['vaug'],
                                          start=first, stop=True, skip_group_check=True),
                 reads=[PT.r] + list(kd['res']), writes=[po.r])
            blk['first'] = False
        if last:
            if hasattr(finalize, 'blk'):
                finalize.blk(qts, po.a.rearrange('p (q c) -> p q c', q=4), po.r)
            else:
                for i in qts:
                    oc = (i - b0) * 128
                    finalize(i, po.a[:, oc:oc + 65], po.r)

    def attn_flush(wk):
        while wk.pipe:
            wk.pipe.pop(0)()

    def attn_job(wk, nq, qT, qres, ktiles, valid, prob, finalize, after_block=None):
        import functools
        nqt = nq // 128
        for b0 in range(0, nqt, 4):
            qts = list(range(b0, min(b0 + 4, nqt)))
            blk = {'po': None, 'first': True}
            steps = []
            for kt, kd in enumerate(ktiles):
                vq = [i for i in qts if valid(kt, i)]
                if vq:
                    steps.append((kt, kd, vq))
            for si, (kt, kd, vq) in enumerate(steps):
                lo, hi = min(vq), max(vq) + 1
                c0, n = lo * 128, (hi - lo) * 128
                ns = kd['ns']
                PT = wk.PT[wk.ptc % len(wk.PT)]
                wk.ptc += 1
                if kd.get('noscore'):
                    prob(kt, kd, c0, n, None, PT)
                else:
                    pscore = ps('s')
                    S.op('pe', lambda e: e.matmul(pscore.a[0:ns, 0:n], kd['kT'], qT(c0, n), start=True, stop=True),
                         reads=list(kd['res']) + list(qres), writes=[pscore.r])
                    prob(kt, kd, c0, n, pscore, PT)
                wk.pipe.append(functools.partial(_emit_pv, wk, blk, kd, vq, c0, PT, b0, si == len(steps) - 1, qts, finalize))
                while len(wk.pipe) > wk.LA:
                    wk.pipe.pop(0)()
            if after_block is not None:
                after_block(b0 * 128)

    def prob_exp(kt, kd, c0, n, pscore, PT):
        ns = kd['ns']
        S.op('act', lambda e: e.activation(PT.a[0:ns, 0:n], pscore.a[0:ns, 0:n], AF.Exp, scale=0.125), reads=[pscore.r], writes=[PT.r])

    def mixer(l, grp):
        ar_reset()
        sample = grp == 2
        NT = 1024 if sample else 512
        tgs = [2, 3] if sample else [grp]
        ntile = NT // 128
        nseq = 1 if sample else 2
        L = 1024 if sample else 256
        tps = L // 128
        cnd = 1 if sample else 0
        tok0 = 1024 if sample else grp * 512
        hg = carve([128, 8, NT], BF16, 'hg')
        tmp_sq = [carve([128, 512], BF16, 'sq') for _ in range(2)]
        rs = carve([128, 512], F32, 'rs')
        tmpf = [carve([128, 512], F32, 'tmpf') for _ in range(2)]
        for i, tg in enumerate(tgs):
            norm_mod(l, 1, tg, hg, i * 512, tmp_sq, rs, tmpf)
        oT = Tl(hg.a, hg.r)
        wk = Work()
        wk.LA = 3 if sample else 6
        wk.PT = [carve([128, 512], BF16, 'PT') for _ in range(4 if sample else 8)]
        wk.ptc = 0
        wk.pipe = []
        opair = carve([128, ntile, 128], BF16, 'opair')
        sm = [carve([128, 8], F32, 'sm') for _ in range(8)]
        smc = [0]

        def smt():
            smc[0] += 1
            return sm[smc[0] % 8]
        stg_f = [carve([128, 528], F32, 'stgf') for _ in range(2)]
        stc = [0]

        def stg():
            stc[0] += 1
            return stg_f[stc[0] % 2]

        def seq_tile(s, j):
            return s * tps + j

        def fin_softmax(e_h, extra_den=None):
            ecol = e_h * 64

            class F:
                tile0 = 0

                def bind(self, tile0):
                    self.tile0 = tile0
                    return self

                def blk(self, qts, pv, por):
                    i0, nq = qts[0], len(qts)
                    t = smt()
                    den = pv[:, 0:nq, 64]
                    if extra_den is None:
                        S.op('dve', lambda e: e.reciprocal(t.a[:, 0:nq], den), reads=[por], writes=[t.r])
                    else:
                        S.op('dve', lambda e: e.tensor_scalar(t.a[:, 0:nq], den, extra_den, None, ALU.add), reads=[por, KR], writes=[t.r])
                        S.op('dve', lambda e: e.reciprocal(t.a[:, 0:nq], t.a[:, 0:nq]), reads=[t.r], writes=[t.r])
                    S.op('dve', lambda e: e.tensor_tensor(opair.a[:, self.tile0 + i0:self.tile0 + i0 + nq, ecol:ecol + 64], pv[:, 0:nq, 0:64],
                                                          t.a[:, 0:nq].unsqueeze(2).to_broadcast([128, nq, 64]), ALU.mult),
                         reads=[por, t.r], writes=[opair.r])
            return F()

        def flush_pair(chunk):
            attn_flush(wk)
            for b in range(0, ntile, 4):
                pb = ps('x')
                pbv = pb.a.bitcast(BF16)
                nn = min(4, ntile - b)
                for j in range(nn):
                    S.op('pe', lambda e, j=j, b=b, pbv=pbv: e.transpose(pbv[:, j * 128:(j + 1) * 128], opair.a[:, b + j, :], ident_b[:]),
                         reads=[opair.r, KR], writes=[pb.r])
                S.op('act', lambda e, b=b, nn=nn, pbv=pbv: e.copy(oT.a[:, chunk, b * 128:(b + nn) * 128], pbv[:, 0:nn * 128]), reads=[pb.r], writes=[oT.r])

        def qknorm_fm(pb, gcol, dst_ap, dst_res, rope_cols=None):
            sq = tmp_sq[0]
            S.op('act', lambda e: e.activation(sq.a, pb.a, AF.Square), reads=[pb.r], writes=[sq.r])
            p2 = ps('x')
            S.op('pe', lambda e: e.matmul(p2.a, bd_b[:], sq.a, start=True, stop=True), reads=[sq.r, KR], writes=[p2.r])
            S.op('act', lambda e: e.activation(rs.a, p2.a, AF.Ln, bias=EPS, scale=1.0 / 64), reads=[p2.r], writes=[rs.r])
            S.op('act', lambda e: e.activation(rs.a, rs.a, AF.Exp, scale=-0.5), reads=[rs.r], writes=[rs.r])
            if rope_cols is None:
                S.op('dve', lambda e: e.scalar_tensor_tensor(dst_ap, pb.a, qkg_t[:, gcol:gcol + 1], rs.a, ALU.mult, ALU.mult),
                     reads=[pb.r, rs.r, CR], writes=[dst_res])
            else:
                qn = tmp_sq[1]
                S.op('dve', lambda e: e.scalar_tensor_tensor(qn.a, pb.a, qkg_t[:, gcol:gcol + 1], rs.a, ALU.mult, ALU.mult),
                     reads=[pb.r, rs.r, CR], writes=[qn.r])
                rope_fm(qn.a, qn.r, dst_ap, dst_res, rope_cols)

        def rope_fm(src_ap, src_res, dst_ap, dst_res, c0):
            p3 = ps('x')
            S.op('pe', lambda e: e.matmul(p3.a, psw_b[:], src_ap, start=True, stop=True), reads=[src_res, KR], writes=[p3.r])
            t1, t2 = tmpf[0], tmpf[1]
            S.op('dve', lambda e: e.tensor_tensor(t1.a, p3.a, ropeS_t[:, c0:c0 + 512], ALU.mult), reads=[p3.r, CR], writes=[t1.r])
            S.op('pool', lambda e: e.tensor_tensor(t2.a, src_ap, ropeC_t[:, c0:c0 + 512], ALU.mult), reads=[src_res, CR], writes=[t2.r])
            S.op('dve', lambda e: e.tensor_tensor(dst_ap, t1.a, t2.a, ALU.add), reads=[t1.r, t2.r], writes=[dst_res])

        def load_ctx_kT(dram, ncol_pairs, dst, dup):
            for j in range(2):
                s_ = stg()
                if dup:
                    srcv = bass.AP(dram, j * 128 * 128, [[128, 128], [64, 2], [0, 2], [1, 64]])
                    S.dma('sp', lambda e, s_=s_, srcv=srcv: e.dma_start(out=s_.a[:, 0:256].rearrange('p (a b c) -> p a b c', a=2, b=2), in_=srcv), s_.r, writes=[s_.r])
                else:
                    for q_ in range(ncol_pairs):
                        S.dma('sp', lambda e, s_=s_, j=j, q_=q_: e.dma_start(out=s_.a[:, q_ * 128:(q_ + 1) * 128], in_=dram.ap()[j * 128:(j + 1) * 128, q_ * 128:(q_ + 1) * 128]), s_.r, writes=[s_.r])
                pb = ps('x')
                for c in range(ncol_pairs):
                    S.op('pe', lambda e, c=c, s_=s_, pb=pb: e.transpose(pb.a[:, c * 128:(c + 1) * 128], s_.a[:, c * 128:(c + 1) * 128], ident_f[:]),
                         reads=[s_.r, KR], writes=[pb.r])
                for c in range(ncol_pairs):
                    S.op('act', lambda e, c=c, j=j, pb=pb: e.copy(dst.a[:, c, j * 128:(j + 1) * 128], pb.a[:, c * 128:(c + 1) * 128]), reads=[pb.r], writes=[dst.r])

        def load_ctx_v(dram, nh, dst, t0):
            for j in range(2):
                s_ = stg()
                for q_ in range(0, 64 * nh, 128):
                    S.dma('sp', lambda e, s_=s_, j=j, q_=q_: e.dma_start(out=s_.a[:, q_:q_ + 128], in_=dram.ap()[j * 128:(j + 1) * 128, q_:q_ + 128]), s_.r, writes=[s_.r])
                S.op('act', lambda e, s_=s_, j=j: e.copy(dst.a[:, t0 + j, :, 0:64], s_.a[:, 0:64 * nh].rearrange('p (h d) -> p h d', h=nh)), reads=[s_.r], writes=[dst.r])

        def out_tm(pb, n, dram, seq, j, cast_dst=None):
            import os as _os2
            if ('nodma%d' % n) in _os2.environ.get('KSUB', '') and l == 1:
                return
            s_ = stg()
            if True:
                S.op('dve', lambda e: e.tensor_copy(s_.a[:, 0:n], pb.a[:, 0:n]), reads=[pb.r], writes=[s_.r])
            else:
                S.op('act', lambda e: e.copy(s_.a[:, 0:n], pb.a[:, 0:n]), reads=[pb.r], writes=[s_.r])
            import os as _os3
            oq = _os3.environ.get('KOUTQ', 'sp')
            if ('nostore%d' % n) in _os2.environ.get('KSUB', '') and l == 1:
                return
            S.dma(oq, lambda e: e.dma_start(out=dram.ap()[seq, j * 128:(j + 1) * 128, :], in_=s_.a[:, 0:n]), s_.r, reads=[s_.r])

        mi = Wd[l]['mix_in']
        if l == 0:
            nkt = (2 if sample else 0) + ntile
            QA = carve([128, 4, NT], BF16, 'QA')
            KA = carve([128, 2, 256 + NT if sample else NT], BF16, 'KA')
            VA = carve([128, nkt, 2, 65], BF16, 'VA')
            if sample:
                rsv = [ring.pop(), ring.pop()]
                QB = Tl(rsv[0].a.rearrange('p (c n) -> p c n', c=4), rsv[0].r)
                KB = Tl(rsv[1].a.rearrange('p (c n) -> p c n', c=4), rsv[1].r)
            else:
                QB = carve([128, 4, NT], BF16, 'QB')
                KB = carve([128, 4, NT], BF16, 'KB')
            VB = carve([128, ntile, 8, 65], BF16, 'VB')
            OB = carve([128, ntile, 512], BF16, 'OB')
            KBT = None if sample else carve([128, ntile, 512], BF16, 'KBT')
            LI = carve([64, NT], F32, 'LI')
            LF = carve([64, NT], F32, 'LF')
            BT = carve([64, NT], F32, 'BT')
            TOK = carve([128, ntile, 3, 16], F32, 'TOK')
            HF = carve([128, tps, 64], F32, 'HF')
            NBC = [carve([128, 512], F32, 'NBC') for _ in range(2)]
            EE01 = carve([128, 1024], F32, 'EE01')
            EE = [Tl(EE01.a[:, i * 512:(i + 1) * 512], ar_res('EE%d' % i)) for i in range(2)]
            ONES = Tl(EE01.a[0:64, 0:NT], EE01.r)
            EEall = [EE01.r, EE[0].r, EE[1].r]
            RM = Tl(tmpf[1].a[0:64], tmpf[1].r)
            koff = 256 if sample else 0
            S.op('pool', lambda e: e.memset(VA.a, 1.0), writes=[VA.r])
            S.op('pool', lambda e: e.memset(VB.a, 1.0), writes=[VB.r])
            S.op('pool', lambda e: e.memset(LI.a, 0.0), writes=[LI.r])
            S.op('pool', lambda e: e.memset(LF.a, 0.0), writes=[LF.r])
            S.op('pool', lambda e: e.memset(BT.a, 0.0), writes=[BT.r])
            S.op('pool', lambda e: e.memset(ONES.a, 1.0), writes=EEall)
            if sample:
                load_ctx_kT(kctx0, 2, KA, True)
                load_ctx_v(vctx0, 2, VA, 0)
                VV = carve([128, 2, 4, 65], BF16, 'VV')
                C0v = C0.ap().rearrange('a (h t) k v -> a t k h v', t=2)
                n0v = n0.ap().rearrange('a (h t) k -> a t k h', t=2)
                for half in range(2):
                    s_ = stg()
                    for dr in range(2):
                        S.dma('sp', lambda e, s_=s_, dr=dr, half=half: e.dma_start(
                            out=s_.a[half * 64:half * 64 + 64, dr * 256: dr * 256 + 256].rearrange('p (h v) -> p h v', h=4),
                            in_=C0v[dr, half]), s_.r, writes=[s_.r])
                        S.dma('sp', lambda e, s_=s_, dr=dr, half=half: e.dma_start(
                            out=s_.a[half * 64:half * 64 + 64, 512 + dr * 4:512 + dr * 4 + 4],
                            in_=n0v[dr, half], allow_slow_non_contiguous=True), s_.r, writes=[s_.r])
                    for dr in range(2):
                        S.op('act', lambda e, s_=s_, dr=dr, half=half: e.copy(
                            VV.a[half * 64:half * 64 + 64, dr, :, 0:64],
                            s_.a[half * 64:half * 64 + 64, dr * 256:dr * 256 + 256].rearrange('p (h v) -> p h v', h=4)),
                            reads=[s_.r], writes=[VV.r])
                        S.op('act', lambda e, s_=s_, dr=dr, half=half: e.copy(
                            VV.a[half * 64:half * 64 + 64, dr, :, 64:65],
                            s_.a[half * 64:half * 64 + 64, 512 + dr * 4:512 + dr * 4 + 4].unsqueeze(2)),
                            reads=[s_.r], writes=[VV.r])
            W = wslab_in(mi, 0, 512)
            for c in range(4):
                def post(b, pb, c=c):
                    qknorm_fm(pb, 0, QA.a[:, c, b * 512:(b + 1) * 512], QA.r, rope_cols=(b * 512 if sample else None))
                proj_fm(W, c * 128, 128, hg, NT, post)
            W = wslab_in(mi, 512, 512)
            for c in range(2):
                def post(b, pb, c=c):
                    qknorm_fm(pb, 1, KA.a[:, c, koff + b * 512:koff + (b + 1) * 512], KA.r, rope_cols=(b * 512 if sample else None))
                proj_fm(W, c * 128, 128, hg, NT, post)

            def post_gi(b, pb):
                S.op('act', lambda e: e.activation(LI.a[0:40, b * 512:(b + 1) * 512], pb.a[0:40, :], AF.Identity, bias=gbias_t[0:40, 0:1], scale=1.0),
                     reads=[pb.r, CR], writes=[LI.r])
            proj_fm(W, 256, 64, hg, NT, post_gi)

            def post_gf(b, pb):
                t1 = tmpf[0]
                S.op('act', lambda e: e.activation(t1.a[0:40, :], pb.a[0:40, :], AF.Exp, bias=ngbias_t[0:40, 0:1], scale=-1.0), reads=[pb.r, KR], writes=[t1.r])
                S.op('act', lambda e: e.activation(t1.a[0:40, :], t1.a[0:40, :], AF.Ln, bias=1.0, scale=1.0), reads=[t1.r], writes=[t1.r])
                S.op('dve', lambda e: e.tensor_scalar(LF.a[0:40, b * 512:(b + 1) * 512], t1.a[0:40, :], -1.0, None, ALU.mult), reads=[t1.r], writes=[LF.r])
            proj_fm(W, 320, 64, hg, NT, post_gf)

            def post_va(t, pb):
                kt = (2 if sample else 0) + t
                S.op('act', lambda e: e.copy(VA.a[:, kt, :, 0:64], pb.a[:, 0:128].rearrange('p (h d) -> p h d', h=2)), reads=[pb.r], writes=[VA.r])
                if not sample:
                    out_tm(pb, 128, o_v0, grp * 2 + t // tps, t % tps)
            proj_tm(W, 384, 128, hg, ntile, post_va)
            W = wslab_in(mi, 1024, 512)
            for c in range(4):
                def post(b, pb, c=c):
                    S.op('act', lambda e: e.copy(QB.a[:, c, b * 512:(b + 1) * 512], pb.a), reads=[pb.r], writes=[QB.r])
                proj_fm(W, c * 128, 128, hg, NT, post)
            W = wslab_in(mi, 1536, 512)
            for c in range(4):
                def post(b, pb, c=c):
                    S.op('dve', lambda e: e.tensor_copy(KB.a[:, c, b * 512:(b + 1) * 512], pb.a), reads=[pb.r], writes=[KB.r])
                proj_fm(W, c * 128, 128, hg, NT, post)
            if not sample:
                def post(t, pb):
                    S.op('act', lambda e: e.copy(KBT.a[:, t, :], pb.a), reads=[pb.r], writes=[KBT.r])
                proj_tm(W, 0, 512, hg, ntile, post)
            W = wslab_in(mi, 2048, 512)
            def post(t, pb):
                S.op('dve', lambda e: e.tensor_copy(VB.a[:, t, :, 0:64], pb.a.rearrange('p (h d) -> p h d', h=8)), reads=[pb.r], writes=[VB.r])
            proj_tm(W, 0, 512, hg, ntile, post)
            W = wslab_in(mi, 2560, 512)
            def post(t, pb):
                S.op('act', lambda e: e.activation(OB.a[:, t, :], pb.a, AF.Sigmoid), reads=[pb.r], writes=[OB.r])
            proj_tm(W, 0, 512, hg, ntile, post)
            if not sample:
                W = wslab_in(mi, 3072, 128)
                def post(t, pb):
                    t_ = smt()
                    s_ = stg()
                    for hh in range(2):
                        S.op('act', lambda e, hh=hh: e.activation(s_.a[:, 256 + hh * 64:256 + hh * 64 + 64], pb.a[:, hh * 64:(hh + 1) * 64], AF.Square,
                                                                  accum_out=t_.a[:, hh:hh + 1]), reads=[pb.r], writes=[s_.r, t_.r])
                    S.op('act', lambda e: e.activation(t_.a[:, 0:2], t_.a[:, 0:2], AF.Sqrt, bias=EPS, scale=1.0 / 64), reads=[t_.r], writes=[t_.r])
                    S.op('dve', lambda e: e.reciprocal(t_.a[:, 0:2], t_.a[:, 0:2]), reads=[t_.r], writes=[t_.r])
                    for hh in range(2):
                        S.op('dve', lambda e, hh=hh: e.scalar_tensor_tensor(s_.a[:, hh * 64:(hh + 1) * 64], pb.a[:, hh * 64:(hh + 1) * 64], t_.a[:, hh:hh + 1], gkbc_t[:],
                                                                            ALU.mult, ALU.mult), reads=[pb.r, t_.r, CR], writes=[s_.r])
                    S.dma('sp', lambda e: e.dma_start(out=o_k0.ap()[grp * 2 + t // tps, (t % tps) * 128:(t % tps + 1) * 128, :], in_=s_.a[:, 0:128]), s_.r, reads=[s_.r])
                proj_tm(W, 0, 128, hg, ntile, post)

            for c in range(4):
                for s in range(nseq):
                    q0 = s * L
                    kv = c // 2
                    for e_ in range(2):
                        pbs = 64 * e_
                        kts = []
                        if sample:
                            for j in range(2):
                                kts.append(dict(kT=KA.a[pbs:pbs + 64, kv, j * 128:(j + 1) * 128], ns=128, vaug=VA.a[:, j, kv, :], res=[KA.r, VA.r]))
                        for j in range(tps):
                            kts.append(dict(kT=KA.a[pbs:pbs + 64, kv, koff + q0 + j * 128:koff + q0 + (j + 1) * 128], ns=128,
                                            vaug=VA.a[:, (2 if sample else 0) + seq_tile(s, j), kv, :], res=[KA.r, VA.r]))
                        fs = fin_softmax(e_)
                        attn_job(wk, L, lambda c0, n, c=c, pbs=pbs, q0=q0: QA.a[pbs:pbs + 64, c, q0 + c0:q0 + c0 + n], [QA.r], kts,
                                 lambda kt, i: True, prob_exp,
                                 fs.bind(s * tps))
                    if s == nseq - 1:
                        flush_pair(c)

            def revap(a, c0, n):
                return bass.AP(a.tensor, a[:, c0 + n - 1:c0 + n].offset, [list(a.ap[0]), [-1, n]])
            for s in range(nseq):
                c0 = s * L
                S.op('dve', lambda e, c0=c0: e.tensor_tensor_scan(BT.a[0:8, c0:c0 + L], ONES.a[0:8, c0:c0 + L], LF.a[0:8, c0:c0 + L], 0.0, ALU.mult, ALU.add),
                     reads=[LF.r] + EEall, writes=[BT.r])
                S.op('dve', lambda e, c0=c0: e.tensor_tensor_scan(revap(BT.a[32:40], c0, L), ONES.a[32:40, c0:c0 + L], revap(LF.a[32:40], c0, L), 0.0, ALU.mult, ALU.add),
                     reads=[LF.r] + EEall, writes=[BT.r])
            S.op('dve', lambda e: e.tensor_tensor(LI.a[0:40, :], LI.a[0:40, :], BT.a[0:40, :], ALU.subtract), reads=[LI.r, BT.r], writes=[LI.r])
            for s in range(nseq):
                c0 = s * L
                ini_f = m0col_t[0:8, :] if sample else 0.0
                ini_b = m0col_t[32:40, :] if sample else 0.0
                S.op('dve', lambda e, c0=c0, ini_f=ini_f: e.tensor_tensor_scan(LF.a[0:8, c0:c0 + L], ONES.a[0:8, c0:c0 + L], LI.a[0:8, c0:c0 + L], ini_f, ALU.mult, ALU.max),
                     reads=[LI.r, CR] + EEall, writes=[LF.r])
                S.op('dve', lambda e, c0=c0, ini_b=ini_b: e.tensor_tensor_scan(revap(LF.a[32:40], c0, L), ONES.a[32:40, c0:c0 + L], revap(LI.a[32:40], c0, L), ini_b, ALU.mult, ALU.max),
                     reads=[LI.r, CR] + EEall, writes=[LF.r])
            S.op('dve', lambda e: e.tensor_scalar(LF.a[0:40, :], LF.a[0:40, :], -1.0, None, ALU.mult), reads=[LF.r], writes=[LF.r])
            S.op('dve', lambda e: e.tensor_tensor(BT.a[0:40, :], LF.a[0:40, :], BT.a[0:40, :], ALU.subtract), reads=[LF.r, BT.r], writes=[BT.r])
            if not sample:
                for s in range(nseq):
                    c0 = s * L
                    sg_ = grp * 2 + s
                    t_ = smt()
                    S.op('dve', lambda e, c0=c0, t_=t_: e.tensor_scalar(t_.a[0:8, 0:1], BT.a[0:8, c0 + L - 1:c0 + L], -1.0, None, ALU.mult), reads=[BT.r], writes=[t_.r])
                    S.op('dve', lambda e, c0=c0, t_=t_: e.tensor_scalar(t_.a[32:40, 0:1], BT.a[32:40, c0:c0 + 1], -1.0, None, ALU.mult), reads=[BT.r], writes=[t_.r])
                    S.dma('sp', lambda e, t_=t_, sg_=sg_: e.dma_start(out=o_m.ap()[sg_, 0, :].rearrange('(p o) -> p o', o=1), in_=t_.a[0:8, 0:1], allow_slow_non_contiguous=True), t_.r, reads=[t_.r])
                    S.dma('sp', lambda e, t_=t_, sg_=sg_: e.dma_start(out=o_m.ap()[sg_, 1, :].rearrange('(p o) -> p o', o=1), in_=t_.a[32:40, 0:1], allow_slow_non_contiguous=True), t_.r, reads=[t_.r])
            S.op('act', lambda e: e.activation(BT.a[0:40, :], BT.a[0:40, :], AF.Exp), reads=[BT.r], writes=[BT.r])
            WF = None
            if not sample:
                WF = carve([64, NT], F32, 'WF')
                S.op('pool', lambda e: e.memset(WF.a, 0.0), writes=[WF.r])
                for s in range(nseq):
                    c0 = s * L
                    S.op('act', lambda e, c0=c0: e.activation(WF.a[0:8, c0:c0 + L], LI.a[0:8, c0:c0 + L], AF.Exp, bias=LF.a[0:8, c0 + L - 1:c0 + L], scale=1.0),
                         reads=[LI.r, LF.r], writes=[WF.r])
                    S.op('act', lambda e, c0=c0: e.activation(WF.a[32:40, c0:c0 + L], LI.a[32:40, c0:c0 + L], AF.Exp, bias=LF.a[32:40, c0:c0 + 1], scale=1.0),
                         reads=[LI.r, LF.r], writes=[WF.r])
            for t in range(ntile):
                pb = ps('x')
                srcs = [LI, BT] + ([WF] if WF is not None else [])
                for qi, src in enumerate(srcs):
                    S.op('pe', lambda e, qi=qi, src=src, t=t, pb=pb: e.transpose(pb.a[:, qi * 64:qi * 64 + 40], src.a[0:40, t * 128:(t + 1) * 128], ident_f[0:40, 0:40]),
                         reads=[src.r, KR], writes=[pb.r])
                nq_ = len(srcs)
                S.op('dve', lambda e, t=t, pb=pb, nq_=nq_: e.tensor_copy(TOK.a[:, t, 0:nq_, :].rearrange('p q (a h) -> p q a h', a=2),
                                                                        pb.a[:, 0:nq_ * 64].rearrange('p (q a h) -> p q a h', q=nq_, a=2)[:, :, :, 0:8]),
                     reads=[pb.r], writes=[TOK.r])

            S.op('dve', lambda e: e.tensor_scalar(TOK.a[:, :, 0, :], TOK.a[:, :, 0, :], float(np.log(0.125)), None, ALU.add), reads=[TOK.r], writes=[TOK.r])
            def _mk_mjob(c, s, e_, dr, jidx):
                    q0 = s * L
                    hd_ = 2 * c + e_
                    pbs = 64 * e_
                    hd = dr * 8 + hd_
                    row = dr * 32 + hd_
                    mask_t = maskF_t if dr == 0 else maskB_t
                    nbcs = {}

                    def pro(b0):
                        nb = min(512, L - b0)
                        S.op('act', lambda e, b0=b0, nb=nb, hd=hd: e.activation(RM.a[0:40, 0:nb], LF.a[0:40, q0 + b0:q0 + b0 + nb], AF.Copy, scale=oh_t[0:40, hd:hd + 1]),
                             reads=[LF.r, CR], writes=[RM.r])
                        pbc = ps('x')
                        S.op('pe', lambda e, nb=nb, pbc=pbc: e.matmul(pbc.a[:, 0:nb], ones_f[0:40, :], RM.a[0:40, 0:nb], start=True, stop=True),
                             reads=[RM.r, KR], writes=[pbc.r])
                        nbt = NBC[(b0 // 512) % 2] if sample else NBC[jidx % 2]
                        S.op('act', lambda e, nb=nb, pbc=pbc, nbt=nbt: e.copy(nbt.a[:, 0:nb], pbc.a[:, 0:nb]), reads=[pbc.r], writes=[nbt.r])
                        nbcs[b0] = nbt
                    kts = []
                    if sample:
                        kts.append(dict(noscore=True, virt=True, ns=64, pbase=pbs, vaug=VV.a[pbs:pbs + 64, dr, hd_ // 2, :], res=[VV.r]))
                    for j in range(tps):
                        kts.append(dict(kT=KB.a[pbs:pbs + 64, c, q0 + j * 128:q0 + (j + 1) * 128], ns=128, j=j,
                                        vaug=VB.a[:, seq_tile(s, j), hd_, :], res=[KB.r, VB.r]))

                    def valid(kt, i, dr=dr):
                        if sample:
                            if kt == 0:
                                return True
                            kt -= 1
                        return i >= kt if dr == 0 else i <= kt

                    def prob(kt, kd, c0, n, pscore, PT, dr=dr, hd=hd, pbs=pbs, c=c, q0=q0, nbcs=nbcs, mask_t=mask_t, s=s):
                        b0 = (c0 // 512) * 512
                        nbt = nbcs[b0]
                        lc = c0 - b0
                        ee = EE[wk.ptc % 2]
                        if kd.get('virt'):
                            S.op('act', lambda e: e.activation(ee.a[pbs:pbs + 64, 0:n], nbt.a[pbs:pbs + 64, lc:lc + n], AF.Exp, bias=m0bc_t[pbs:pbs + 64, hd:hd + 1], scale=1.0),
                                 reads=[nbt.r, CR], writes=[ee.r])
                            S.op('dve', lambda e: e.tensor_tensor(PT.a[pbs:pbs + 64, 0:n], ee.a[pbs:pbs + 64, 0:n], QB.a[pbs:pbs + 64, c, q0 + c0:q0 + c0 + n], ALU.mult),
                                 reads=[ee.r, QB.r], writes=[PT.r])
                            return
                        j = kd['j']
                        tl = seq_tile(s, j)
                        abias = TOK.a[:, tl, 0, hd:hd + 1]
                        dc = j * 128 - c0
                        m01 = mask01F_t if dr == 0 else mask01B_t
                        S.op('act', lambda e: e.activation(ee.a[:, 0:n], nbt.a[:, lc:lc + n], AF.Exp, bias=abias, scale=1.0), reads=[nbt.r, TOK.r], writes=[ee.r])
                        S.op('dve', lambda e: e.scalar_tensor_tensor(PT.a[:, 0:n], ee.a[:, 0:n], 0.125, pscore.a[:, 0:n], ALU.min, ALU.mult),
                             reads=[pscore.r, ee.r], writes=[PT.r])
                        if 0 <= dc < n:
                            S.op('dve', lambda e: e.tensor_tensor(PT.a[:, dc:dc + 128], PT.a[:, dc:dc + 128], m01[:], ALU.mult), reads=[PT.r, CR], writes=[PT.r])

                    def fin(i, po, por):
                        raise AssertionError('block finalize only')

                    def fin_blk(qts, pv, por, dr=dr, hd=hd, s=s, e_=e_, hd_=hd_):
                        i0, nq = qts[0], len(qts)
                        tl0 = seq_tile(s, i0)
                        t_ = smt()
                        den = pv[:, 0:nq, 64]
                        num = pv[:, 0:nq, 0:64]
                        S.op('dve', lambda e: e.tensor_tensor(t_.a[:, 0:nq], den, TOK.a[:, tl0:tl0 + nq, 1, hd], ALU.max), reads=[por, TOK.r], writes=[t_.r])
                        S.op('dve', lambda e: e.scalar_tensor_tensor(t_.a[:, 0:nq], den, -1.0, t_.a[:, 0:nq], ALU.mult, ALU.max), reads=[por, t_.r], writes=[t_.r])
                        S.op('dve', lambda e: e.reciprocal(t_.a[:, 0:nq], t_.a[:, 0:nq]), reads=[t_.r], writes=[t_.r])
                        rb = t_.a[:, 0:nq].unsqueeze(2).to_broadcast([128, nq, 64])
                        HFv = HF.a[:, i0:i0 + nq, :]
                        if dr == 0:
                            S.op('dve', lambda e: e.tensor_tensor(HFv, num, rb, ALU.mult), reads=[por, t_.r], writes=[HF.r])
                            return
                        sb_ = stg()
                        tv = sb_.a[:, 0:nq * 64].rearrange('p (t d) -> p t d', t=nq)
                        S.op('dve', lambda e: e.tensor_tensor(tv, num, rb, ALU.mult), reads=[por, t_.r], writes=[sb_.r])
                        S.op('dve', lambda e: e.tensor_tensor(HFv, HFv, tv, ALU.add), reads=[sb_.r, HF.r], writes=[HF.r])
                        if qts[-1] != tps - 1:
                            return
                        s_ = stg()
                        t8 = smt()
                        sv = s_.a[:, 0:tps * 64].rearrange('p (t d) -> p t d', t=tps)
                        S.op('dve', lambda e: e.tensor_tensor(sv, HF.a, HF.a, ALU.mult), reads=[HF.r], writes=[s_.r])
                        S.op('dve', lambda e: e.tensor_reduce(t8.a[:, 0:tps], sv, AX.X, ALU.add), reads=[s_.r], writes=[t8.r])
                        S.op('act', lambda e: e.activation(t8.a[:, 0:tps], t8.a[:, 0:tps], AF.Ln, bias=EPS, scale=1.0 / 64), reads=[t8.r], writes=[t8.r])
                        S.op('act', lambda e: e.activation(t8.a[:, 0:tps], t8.a[:, 0:tps], AF.Exp, scale=-0.5), reads=[t8.r], writes=[t8.r])
                        S.op('dve', lambda e: e.tensor_tensor(sv, HF.a, t8.a[:, 0:tps].unsqueeze(2).to_broadcast([128, tps, 64]), ALU.mult),
                             reads=[HF.r, t8.r], writes=[s_.r])
                        S.op('pool', lambda e: e.tensor_tensor(sv, sv, hgn_t[:, hd_ * 64:(hd_ + 1) * 64].unsqueeze(1).to_broadcast([128, tps, 64]), ALU.mult),
                             reads=[s_.r, CR], writes=[s_.r])
                        S.op('pool', lambda e: e.tensor_tensor(opair.a[:, s * tps:(s + 1) * tps, e_ * 64:(e_ + 1) * 64], sv,
                                                               OB.a[:, s * tps:(s + 1) * tps, hd_ * 64:(hd_ + 1) * 64], ALU.mult),
                             reads=[s_.r, OB.r], writes=[opair.r])


                    fin.blk = fin_blk

                    def run(after_block):
                        attn_job(wk, L, lambda c0, n: QB.a[pbs:pbs + 64, c, q0 + c0:q0 + c0 + n], [QB.r], kts, valid, prob, fin, after_block=after_block)
                    return dict(pro=pro, run=run)

            mjobs = []
            for c in range(4):
                for s in range(nseq):
                    for e_ in range(2):
                        for dr in range(2):
                            jb = _mk_mjob(c, s, e_, dr, len(mjobs))
                            jb['flush'] = (4 + c) if (s == nseq - 1 and e_ == 1 and dr == 1) else None
                            mjobs.append(jb)
            blocks0 = list(range(0, L, 512))
            for b0 in blocks0:
                mjobs[0]['pro'](b0)
            for k, jb in enumerate(mjobs):
                nxt = mjobs[k + 1] if k + 1 < len(mjobs) else None
                if sample:
                    jb['run'](lambda b0, nxt=nxt: nxt['pro'](b0) if nxt is not None else None)
                else:
                    if nxt is not None:
                        nxt['pro'](0)
                    jb['run'](None)
                if jb['flush'] is not None:
                    flush_pair(jb['flush'])

            if not sample:
                WVt = [carve([128, 8, 65], BF16, 'WV%d' % j) for j in range(tps)]
                for s in range(nseq):
                    sg_ = grp * 2 + s
                    for dr in range(2):
                        pcs = [ps('a'), ps('b')]
                        WVs = []
                        for j in range(tps):
                            tl = seq_tile(s, j)
                            wv = WVt[j]
                            wf = TOK.a[:, tl, 2, dr * 8:dr * 8 + 8].unsqueeze(2).to_broadcast([128, 8, 65])
                            S.op('dve', lambda e, wv=wv, tl=tl, wf=wf: e.tensor_tensor(wv.a, VB.a[:, tl, :, :], wf, ALU.mult), reads=[VB.r, TOK.r], writes=[wv.r])
                            WVs.append(wv)
                        for hh in range(8):
                            pc = pcs[hh // 4]
                            oc = (hh % 4) * 128
                            for j in range(tps):
                                tl = seq_tile(s, j)
                                S.op('pe', lambda e, hh=hh, j=j, tl=tl, pc=pc, oc=oc: e.matmul(pc.a[0:64, oc:oc + 65], KBT.a[:, tl, hh * 64:(hh + 1) * 64], WVs[j].a[:, hh, :],
                                                                                               start=(j == 0), stop=(j == tps - 1), skip_group_check=True),
                                     reads=[KBT.r, WVs[j].r], writes=[pc.r])
                        s_ = stg()
                        for half in range(2):
                            S.op('act', lambda e, half=half, s_=s_: e.activation(s_.a[0:64, half * 260:half * 260 + 260].rearrange('p (h v) -> p h v', h=4),
                                                                                 pcs[half].a[0:64, :].rearrange('p (h v) -> p h v', h=4)[:, :, 0:65], AF.Copy, scale=0.125),
                                 reads=[pcs[half].r], writes=[s_.r])
                        sv = s_.a[0:64, 0:520].rearrange('p (h v) -> p h v', h=8)
                        S.dma('sp', lambda e, sv=sv, sg_=sg_, dr=dr, s_=s_: e.dma_start(out=o_C.ap()[sg_, dr].rearrange('h k v -> k h v'), in_=sv[:, :, 0:64]), s_.r, reads=[s_.r])
                        S.dma('sp', lambda e, sv=sv, sg_=sg_, dr=dr, s_=s_: e.dma_start(out=o_n.ap()[sg_, dr].rearrange('h k -> k h'), in_=sv[:, :, 64], allow_slow_non_contiguous=True), s_.r, reads=[s_.r])
            if sample:
                ring.extend(rsv)
        else:
            nkt = (2 if sample else 0) + ntile
            koff = 256 if sample else 0
            QC = carve([128, 4, NT], BF16, 'QC')
            KC = carve([128, 4, koff + NT], BF16, 'KC')
            VC = carve([128, nkt, 8, 65], BF16, 'VC')
            QD = carve([128, 4, NT], BF16, 'QD')
            KD = carve([128, 2, koff + NT], BF16, 'KD')
            VD = carve([128, nkt, 2, 65], BF16, 'VD')
            S.op('pool', lambda e: e.memset(VC.a, 1.0), writes=[VC.r])
            S.op('pool', lambda e: e.memset(VD.a, 1.0), writes=[VD.r])
            if sample:
                load_ctx_kT(kcctx, 4, KC, False)
                load_ctx_v(vcctx, 8, VC, 0)
                load_ctx_kT(kdctx, 2, KD, True)
                load_ctx_v(vdctx, 2, VD, 0)
            W = wslab_in(mi, 0, 512)
            for c in range(4):
                def post(b, pb, c=c):
                    S.op('act', lambda e: e.copy(QC.a[:, c, b * 512:(b + 1) * 512], pb.a), reads=[pb.r], writes=[QC.r])
                proj_fm(W, c * 128, 128, hg, NT, post)
            W = wslab_in(mi, 512, 512)
            for c in range(4):
                def post(b, pb, c=c):
                    S.op('dve', lambda e: e.tensor_copy(KC.a[:, c, koff + b * 512:koff + (b + 1) * 512], pb.a), reads=[pb.r], writes=[KC.r])
                proj_fm(W, c * 128, 128, hg, NT, post)
            if not sample:
                def post(t, pb):
                    out_tm(pb, 512, o_kc, grp * 2 + t // tps, t % tps)
                proj_tm(W, 0, 512, hg, ntile, post)
            W = wslab_in(mi, 1024, 512)
            def post(t, pb):
                kt = (2 if sample else 0) + t
                S.op('dve', lambda e: e.tensor_copy(VC.a[:, kt, :, 0:64], pb.a.rearrange('p (h d) -> p h d', h=8)), reads=[pb.r], writes=[VC.r])
                if not sample:
                    out_tm(pb, 512, o_vc, grp * 2 + t // tps, t % tps)
            proj_tm(W, 0, 512, hg, ntile, post)
            W = wslab_in(mi, 1536, 512)
            for c in range(4):
                def post(b, pb, c=c):
                    if sample:
                        qn = tmp_sq[1]
                        S.op('act', lambda e: e.copy(qn.a, pb.a), reads=[pb.r], writes=[qn.r])
                        rope_fm(qn.a, qn.r, QD.a[:, c, b * 512:(b + 1) * 512], QD.r, b * 512)
                    else:
                        S.op('act', lambda e: e.copy(QD.a[:, c, b * 512:(b + 1) * 512], pb.a), reads=[pb.r], writes=[QD.r])
                proj_fm(W, c * 128, 128, hg, NT, post)
            W = wslab_in(mi, 2048, 512)
            for c in range(2):
                def post(b, pb, c=c):
                    if sample:
                        qn = tmp_sq[1]
                        S.op('act', lambda e: e.copy(qn.a, pb.a), reads=[pb.r], writes=[qn.r])
                        rope_fm(qn.a, qn.r, KD.a[:, c, koff + b * 512:koff + (b + 1) * 512], KD.r, b * 512)
                    else:
                        S.op('act', lambda e: e.copy(KD.a[:, c, b * 512:(b + 1) * 512], pb.a), reads=[pb.r], writes=[KD.r])
                proj_fm(W, c * 128, 128, hg, NT, post)
            if not sample:
                def post(t, pb):
                    out_tm(pb, 128, o_kd, grp * 2 + t // tps, t % tps)
                proj_tm(W, 256, 128, hg, ntile, post)
            def post(t, pb):
                kt = (2 if sample else 0) + t
                S.op('dve', lambda e: e.tensor_copy(VD.a[:, kt, :, 0:64], pb.a[:, 0:128].rearrange('p (h d) -> p h d', h=2)), reads=[pb.r], writes=[VD.r])
                if not sample:
                    out_tm(pb, 128, o_vd, grp * 2 + t // tps, t % tps)
            proj_tm(W, 384, 128, hg, ntile, post)

            import os as _os
            ksub = _os.environ.get('KSUB', '')
            if not sample:
                for c in range(4 if 'nomha' not in ksub else 0):
                    for s in range(nseq):
                        q0 = s * L
                        for e_ in range(2):
                            pbs = 64 * e_
                            hh = 2 * c + e_
                            kts = [dict(kT=KC.a[pbs:pbs + 64, c, q0 + j * 128:q0 + (j + 1) * 128], ns=128, vaug=VC.a[:, seq_tile(s, j), hh, :], res=[KC.r, VC.r])
                                   for j in range(tps)]
                            fs = fin_softmax(e_)
                            attn_job(wk, L, lambda c0, n, c=c, pbs=pbs, q0=q0: QC.a[pbs:pbs + 64, c, q0 + c0:q0 + c0 + n], [QC.r], kts,
                                     lambda kt, i: True, prob_exp, fs.bind(s * tps))
                        if s == nseq - 1:
                            flush_pair(c)
                for c in range(4 if 'nogqa' not in ksub else 0):
                    for s in range(nseq):
                        q0 = s * L
                        kv = c // 2
                        for e_ in range(2):
                            pbs = 64 * e_
                            hh = 2 * c + e_
                            kts = [dict(kT=KD.a[pbs:pbs + 64, kv, q0 + j * 128:q0 + (j + 1) * 128], ns=128, vaug=VD.a[:, seq_tile(s, j), kv, :], res=[KD.r, VD.r])
                                   for j in range(tps)]
                            fs = fin_softmax(e_, extra_den=esink_t[:, hh:hh + 1])
                            attn_job(wk, L, lambda c0, n, c=c, pbs=pbs, q0=q0: QD.a[pbs:pbs + 64, c, q0 + c0:q0 + c0 + n], [QD.r], kts,
                                     lambda kt, i: True, prob_exp, fs.bind(s * tps))
                        if s == nseq - 1:
                            flush_pair(4 + c)
            else:
                navalid = _na_rows()
                TB = [carve([128, 15, 64], F32, 'TB') for _ in range(2)]
                ARG = [carve([128, 512], F32, 'ARG') for _ in range(2)]
                for c in range(4 if 'nona' not in ksub else 0):
                    for e_ in range(2):
                        pbs = 64 * e_
                        hh = 2 * c + e_
                        tb = TB[hh % 2]
                        S.dma('sp', lambda e, tb=tb, hh=hh: e.dma_start(out=tb.a, in_=natb.ap()[hh].rearrange('p (a b) -> p a b', a=15)), tb.r, writes=[tb.r])
                        S.op('pool', lambda e, tb=tb: e.tensor_tensor(tb.a, tb.a, cmask_t[:].unsqueeze(1).to_broadcast([128, 15, 64]), ALU.add), reads=[tb.r, CR], writes=[tb.r])
                        kts = []
                        for j in range(2):
                            kts.append(dict(kT=KC.a[pbs:pbs + 64, c, j * 128:(j + 1) * 128], ns=128, vaug=VC.a[:, j, hh, :], res=[KC.r, VC.r], ctx=True))
                        for j in range(8):
                            kts.append(dict(kT=KC.a[pbs:pbs + 64, c, 256 + j * 128:256 + (j + 1) * 128], ns=128, vaug=VC.a[:, 2 + j, hh, :], res=[KC.r, VC.r], j=j))

                        def valid(kt, i):
                            if kt < 2:
                                return True
                            j = kt - 2
                            return any(navalid[2 * j + a][2 * i + b] for a in range(2) for b in range(2))

                        def prob(kt, kd, c0, n, pscore, PT, tb=tb):
                            if kd.get('ctx'):
                                return prob_exp(kt, kd, c0, n, pscore, PT)
                            j = kd['j']
                            S.op('pool', lambda e: e.memset(PT.a[:, 0:n], 0.0), writes=[PT.r])
                            arg = ARG[wk.ptc % 2]
                            r0 = c0 // 64
                            nr = n // 64
                            for a in range(2):
                                srow = 2 * j + a
                                rows = [r for r in range(r0, r0 + nr) if navalid[srow][r]]
                                if not rows:
                                    continue
                                rl, rh = min(rows), max(rows) + 1
                                cl, cn = (rl - r0) * 64, (rh - rl) * 64
                                dy0 = rl - srow + 7
                                pa = a * 64
                                S.op('dve', lambda e, pa=pa, cl=cl, cn=cn, dy0=dy0, rl=rl, rh=rh: e.scalar_tensor_tensor(
                                    arg.a[pa:pa + 64, cl:cl + cn], pscore.a[pa:pa + 64, cl:cl + cn], 0.125,
                                    tb.a[pa:pa + 64, dy0:dy0 + (rh - rl), :].rearrange('p a b -> p (a b)'), ALU.mult, ALU.add),
                                    reads=[pscore.r, tb.r], writes=[arg.r])
                                S.op('act', lambda e, pa=pa, cl=cl, cn=cn: e.activation(PT.a[pa:pa + 64, cl:cl + cn], arg.a[pa:pa + 64, cl:cl + cn], AF.Exp),
                                     reads=[arg.r], writes=[PT.r])
                        fs = fin_softmax(e_)
                        attn_job(wk, L, lambda c0, n, c=c, pbs=pbs: QC.a[pbs:pbs + 64, c, c0:c0 + n], [QC.r], kts, valid, prob,
                                 fs.bind(0))
                    flush_pair(c)
                for c in range(4 if 'noswa' not in ksub else 0):
                    kv = c // 2
                    for e_ in range(2):
                        pbs = 64 * e_
                        hh = 2 * c + e_
                        kts = []
                        for j in range(2):
                            kts.append(dict(kT=KD.a[pbs:pbs + 64, kv, j * 128:(j + 1) * 128], ns=128, vaug=VD.a[:, j, kv, :], res=[KD.r, VD.r], ctx=True))
                        for j in range(8):
                            kts.append(dict(kT=KD.a[pbs:pbs + 64, kv, 256 + j * 128:256 + (j + 1) * 128], ns=128, vaug=VD.a[:, 2 + j, kv, :], res=[KD.r, VD.r], j=j))

                        def valid(kt, i):
                            return True if kt < 2 else abs(i - (kt - 2)) <= 1

                        def prob(kt, kd, c0, n, pscore, PT):
                            if kd.get('ctx'):
                                return prob_exp(kt, kd, c0, n, pscore, PT)
                            j = kd['j']
                            arg = ARG[wk.ptc % 2]
                            for i in range(c0 // 128, (c0 + n) // 128):
                                lc = i * 128 - c0
                                if i == j:
                                    S.op('act', lambda e, lc=lc: e.activation(PT.a[:, lc:lc + 128], pscore.a[:, lc:lc + 128], AF.Exp, scale=0.125), reads=[pscore.r], writes=[PT.r])
                                else:
                                    mk_ = maskF_t if i == j - 1 else maskB_t
                                    S.op('dve', lambda e, lc=lc, mk_=mk_: e.scalar_tensor_tensor(arg.a[:, lc:lc + 128], pscore.a[:, lc:lc + 128], 0.125, mk_[:], ALU.mult, ALU.add),
                                         reads=[pscore.r, CR], writes=[arg.r])
                                    S.op('act', lambda e, lc=lc: e.activation(PT.a[:, lc:lc + 128], arg.a[:, lc:lc + 128], AF.Exp), reads=[arg.r], writes=[PT.r])
                        fs = fin_softmax(e_, extra_den=esink_t[:, hh:hh + 1])
                        attn_job(wk, L, lambda c0, n, c=c, pbs=pbs: QD.a[pbs:pbs + 64, c, c0:c0 + n], [QD.r], kts, valid, prob,
                                 fs.bind(0))
                    flush_pair(4 + c)

        mo = Wd[l]['mix_out']
        O1 = wslab_out(mo, 0, 4)
        O2 = wslab_out(mo, 512, 4)
        for i, tg in enumerate(tgs):
            for m in range(8):
                py = ps('a')
                for cc in range(8):
                    Ow = O1 if cc < 4 else O2
                    S.op('pe', lambda e, cc=cc, m=m, i=i, py=py, Ow=Ow: e.matmul(py.a, Ow.a[:, cc % 4, m * 128:(m + 1) * 128], oT.a[:, cc, i * 512:(i + 1) * 512],
                                                                                start=(cc == 0), stop=(cc == 7)),
                         reads=[Ow.r, oT.r], writes=[py.r])
                S.op('dve', lambda e, m=m, tg=tg, py=py: e.scalar_tensor_tensor(xap(m, tg), py.a, modG[l][:, 1, m, cnd:cnd + 1], xap(m, tg), ALU.mult, ALU.add),
                     reads=[py.r, modD_R[l][1], xres[m][tg]], writes=[xres[m][tg]])

    import os
    parts = os.environ.get('KPARTS', 'all')

    def on(p):
        return parts == 'all' or p in parts.split(',')
    load_x()
    adaln_slabs(0, 0, 6)
    adaln_finish(0, (0,))
    for l in range(2):
        if l == 0:
            if on('f01'):
                ffn(0, 1, between=lambda i: adaln_slabs(0, 6 + 2 * i, 8 + 2 * i))
            else:
                adaln_slabs(0, 6, 18)
            adaln_finish(0, (1, 2))
        else:
            if on('f11'):
                ffn(1, 1)
        for grp in range(3):
            if on('m%d%d' % (l, grp)):
                mixer(l, grp)
        if l == 0:
            if on('f02'):
                ffn(0, 2, between=lambda i: adaln_slabs(1, 3 * i, 3 * i + 3))
            else:
                adaln_slabs(1, 0, 18)
            adaln_finish(1)
        else:
            if on('f12'):
                ffn(1, 2, tail=True)
    if not on('f12'):
        final_out()
    S.emit()


_PROG = {}


def _prep_weights(inp):
    sh = {}
    for l in range(2):
        sh['ada_w%d' % l] = np.ascontiguousarray(inp['ada_w_l%d' % l], dtype=np.float32)
        sh['ada_b%d' % l] = np.ascontiguousarray(inp['ada_b_l%d' % l].reshape(72, 128).T, dtype=np.float32)
        sh['norm%d' % l] = np.ascontiguousarray(inp['norm_l%d' % l].reshape(3, 8, 128).transpose(2, 0, 1), dtype=np.float32)
        for f in (1, 2):
            sh['f%din%d' % (f, l)] = np.ascontiguousarray(inp['ffn%d_in_l%d' % (f, l)], dtype=np.float32)
            sh['f%dout%d' % (f, l)] = np.ascontiguousarray(inp['ffn%d_out_l%d' % (f, l)], dtype=np.float32)
        sh['mixout%d' % l] = np.ascontiguousarray(inp['mix_out_l%d' % l], dtype=np.float32)
    w = np.asarray(inp['mix_in_l0'], dtype=np.float32)
    qa, ka, va = w[:, 0:512], w[:, 512:640], w[:, 640:768]
    qb, kb, vb = w[:, 768:1280], w[:, 1280:1792], w[:, 1792:2304]
    gt, ob = w[:, 2304:2336], w[:, 2336:2848]
    z = np.zeros((D, 24), np.float32)
    g1 = np.concatenate([gt[:, 0:8], z, gt[:, 16:24], z], 1)
    g2 = np.concatenate([gt[:, 8:16], z, gt[:, 24:32], z], 1)
    kadup = np.concatenate([ka[:, 0:64], ka[:, 0:64], ka[:, 64:128], ka[:, 64:128]], 1)
    m0_ = np.concatenate([qa, kadup, g1, g2, va, qb, kb, vb, ob, ka, np.zeros((D, L0_COLS - 3200), np.float32)], 1)
    assert m0_.shape[1] == L0_COLS
    sh['mixin0'] = np.ascontiguousarray(m0_)
    w = np.asarray(inp['mix_in_l1'], dtype=np.float32)
    qc, kc, vc, qd, kd, vd = w[:, 0:512], w[:, 512:1024], w[:, 1024:1536], w[:, 1536:2048], w[:, 2048:2176], w[:, 2176:2304]
    kddup = np.concatenate([kd[:, 0:64], kd[:, 0:64], kd[:, 64:128], kd[:, 64:128]], 1)
    m1_ = np.concatenate([qc, kc, vc, qd, kddup, kd, vd], 1)
    assert m1_.shape[1] == L1_COLS
    sh['mixin1'] = np.ascontiguousarray(m1_)
    sh['gfin'] = np.ascontiguousarray(np.asarray(inp['norm_final'], np.float32).reshape(8, 128).T)
    qk = np.asarray(inp['qk_norm_l0'], np.float32)
    sh['qkg'] = np.ascontiguousarray(np.stack([np.tile(qk[0], 2), np.tile(qk[1], 2)], 1))
    sh['gkbc'] = np.ascontiguousarray(qk[1])
    gb = np.asarray(inp['gate_bias_l0'], np.float32)
    gbt = np.zeros((64, 2), np.float32)
    gbt[0:8, 0] = gb[0:8]
    gbt[32:40, 0] = gb[16:24]
    gbt[0:8, 1] = gb[8:16]
    gbt[32:40, 1] = gb[24:32]
    sh['gbias'] = gbt
    sh['hgn'] = np.ascontiguousarray(inp['head_norm_l0'], dtype=np.float32)
    sh['sink'] = np.ascontiguousarray(inp['sink_l1'], dtype=np.float32)
    rpb = np.asarray(inp['rpb_l1'], np.float32)
    sc = np.arange(64)[:, None]
    qc_ = np.arange(64)[None, :]
    dx = np.clip(sc - qc_ + 15, 0, 30)
    tb = np.zeros((8, 128, 15, 64), np.float32)
    for dyi in range(15):
        blk = rpb[:, 14 - dyi, :][:, dx]
        tb[:, 0:64, dyi, :] = blk
        tb[:, 64:128, dyi, :] = blk
    sh['natb'] = np.ascontiguousarray(tb.reshape(8, 128, 15 * 64))
    for k, v in _consts().items():
        sh['c_' + k] = v
    return sh


def kernel(**inp):
    inp = {k: np.asarray(v) for k, v in inp.items()}
    dbg = inp.pop('_dbg', None)
    key = 'main'
    if key not in _PROG:
        _PROG[key] = build_program(None)
    nc = _PROG[key]
    sh = _prep_weights(inp)
    in_maps = []
    for i in range(8):
        b = i // 4
        m = dict(sh)
        xp = inp['x_prompt'][4 * i:4 * i + 4].reshape(1024, D)
        xs = inp['x_sample'][b]
        m['xin'] = np.ascontiguousarray(np.concatenate([xp, xs], 0), dtype=np.float32)
        cond = np.stack([inp['c_ctx'], inp['c'][b]], 0).astype(np.float32)
        m['condT'] = np.ascontiguousarray(cond.reshape(2, 8, 128).transpose(2, 1, 0))
        m['kctx0'] = np.ascontiguousarray(inp['cache_l0_attn_k'][b].reshape(256, 128), dtype=np.float32)
        m['vctx0'] = np.ascontiguousarray(inp['cache_l0_attn_v'][b].reshape(256, 128), dtype=np.float32)
        m['C0'] = np.ascontiguousarray(inp['state_l0_mlstm_C'][b], dtype=np.float32)
        m['n0'] = np.ascontiguousarray(inp['state_l0_mlstm_n'][b], dtype=np.float32)
        m['m0'] = np.ascontiguousarray(inp['state_l0_mlstm_m'][b].reshape(16), dtype=np.float32)
        m['kcctx'] = np.ascontiguousarray(inp['cache_l1_na_k'][b].reshape(256, 512), dtype=np.float32)
        m['vcctx'] = np.ascontiguousarray(inp['cache_l1_na_v'][b].reshape(256, 512), dtype=np.float32)
        m['kdctx'] = np.ascontiguousarray(inp['cache_l1_swa_k'][b].reshape(256, 128), dtype=np.float32)
        m['vdctx'] = np.ascontiguousarray(inp['cache_l1_swa_v'][b].reshape(256, 128), dtype=np.float32)
        in_maps.append(m)
    import os
    ncore = int(os.environ.get('KCORES', '8'))
    res = run_bass_kernel_spmd(nc, in_maps[:ncore], core_ids=list(range(ncore)))
    R = list(res.results)
    while len(R) < 8:
        R.append(R[0])
    cat = lambda k: np.concatenate([np.asarray(R[i][k]) for i in range(8)], 0)
    y_prompt = cat('o_yp').reshape(32, 256, D)
    y_sample = np.stack([np.asarray(R[0]['o_ys']), np.asarray(R[4]['o_ys'])], 0)
    k0 = cat('o_k0').reshape(32, 256, 2, 64)
    v0 = cat('o_v0').reshape(32, 256, 2, 64)
    Cst = cat('o_C')
    nst = cat('o_n')
    mst = cat('o_m')
    kc1 = cat('o_kc').reshape(32, 256, 8, 64)
    vc1 = cat('o_vc').reshape(32, 256, 8, 64)
    kd1 = cat('o_kd').reshape(32, 256, 2, 64)
    vd1 = cat('o_vd').reshape(32, 256, 2, 64)
    outs = (y_prompt, y_sample, k0, v0, Cst, nst, mst, kc1, vc1, kd1, vd1)
    return tuple(np.ascontiguousarray(o, dtype=np.float32) for o in outs)
```

```python
import contextlib
import numpy as np
import concourse.bass as bass
import concourse.mybir as mybir
from concourse.bass_utils import run_bass_kernel_spmd

F32 = mybir.dt.float32
BF16 = mybir.dt.bfloat16
AF = mybir.ActivationFunctionType
ALU = mybir.AluOpType
AX = mybir.AxisListType

ENGS = ('pe', 'act', 'dve', 'pool', 'sp')
NEG = -1.0e30
EPS = 1e-6


class Res:
    __slots__ = ('name', 'w', 'r', 'sem', 'dcount', 'excl')

    def __init__(self, name=''):
        self.name = name
        self.excl = False
        self.w = {}
        self.r = {}
        self.sem = None
        self.dcount = 0


class _Rec:
    def __init__(self):
        self.call = None

    def __getattr__(self, name):
        def f(*a, **k):
            self.call = (name, a, k)
            return self
        return f


class Ins:
    __slots__ = ('fn', 'waits', 'milestone', 'dma_res')

    def __init__(self, fn):
        rec = _Rec()
        fn(rec)
        name, a, k = rec.call
        self.fn = lambda eh: getattr(eh, name)(*a, **k)
        self.waits = []
        self.milestone = False
        self.dma_res = None


class Sched:
    def __init__(self, nc, stack):
        self.nc = nc
        self.stack = stack
        self.streams = {e: [] for e in ENGS}
        self.known = {e: {} for e in ENGS}
        self.dma_res = []
        self.semof = {}
        self.esem = {}
        for e in ('pe', 'act', 'dve', 'pool'):
            self.esem[e] = stack.enter_context(nc.semaphore('es_' + e))

    def _waits(self, ins, eng, deps):
        for key, val in deps.items():
            if self.known[eng].get(key, -1) >= val:
                continue
            self.known[eng][key] = val
            ins.waits.append((key, val))
            if key[0] == 'e':
                self.streams[key[1]][val].milestone = True

    def _deps(self, eng, reads, writes):
        deps = {}

        def add(d, raw):
            for k, v in d.items():
                if k[0] == 'e' and k[1] == eng and (eng == 'pe' or not raw):
                    continue
                if deps.get(k, -1) < v:
                    deps[k] = v
        for r in reads:
            add(r.w, True)
            if r.excl:
                add({k: v for k, v in r.r.items() if not (k[0] == 'e' and k[1] == eng)}, False)
        for r in writes:
            add(r.w, False)
            add(r.r, False)
        return deps

    def op(self, eng, fn, reads=(), writes=()):
        ins = Ins(fn)
        idx = len(self.streams[eng])
        self._waits(ins, eng, self._deps(eng, reads, writes))
        key = ('e', eng)
        for r in writes:
            r.w = {key: idx}
            r.r = {}
        for r in reads:
            if r not in writes:
                r.r[key] = idx
        self.streams[eng].append(ins)
        return ins

    def dma(self, eng, fn, sres, reads=(), writes=()):
        ins = Ins(fn)
        ins.dma_res = sres
        if sres.sem is None:
            sres.sem = self.stack.enter_context(self.nc.semaphore('ds_%d' % len(self.dma_res)))
            self.dma_res.append(sres)
            self.semof[id(sres)] = sres
        self._waits(ins, eng, self._deps(eng, reads, writes))
        sres.dcount += 1
        key = ('d', id(sres))
        val = 16 * sres.dcount
        for r in writes:
            r.w = {key: val}
            r.r = {}
        for r in reads:
            if r not in writes:
                r.r[key] = val
        self.streams[eng].append(ins)
        return ins

    def emit(self):
        nc = self.nc
        ordinal = {}
        for e in ('pe', 'act', 'dve', 'pool'):
            c = 0
            for i, ins in enumerate(self.streams[e]):
                if ins.milestone:
                    c += 1
                    ordinal[(e, i)] = c

        def run(eng, eh):
            for i, ins in enumerate(self.streams[eng]):
                for key, val in ins.waits:
                    if key[0] == 'e':
                        eh.wait_ge(self.esem[key[1]], ordinal[(key[1], val)])
                    else:
                        eh.wait_ge(self.semof[key[1]].sem, val)
                bi = ins.fn(eh)
                if ins.dma_res is not None:
                    bi.then_inc(ins.dma_res.sem, 16)
                elif ins.milestone:
                    bi.then_inc(self.esem[eng], 1)
            if eng == 'sp':
                for r in self.dma_res:
                    eh.wait_ge(r.sem, 16 * r.dcount)

        with nc.Block() as block:
            @block.tensor
            def _(e):
                run('pe', e)

            @block.scalar
            def _(e):
                run('act', e)

            @block.vector
            def _(e):
                run('dve', e)

            @block.gpsimd
            def _(e):
                run('pool', e)

            @block.sync
            def _(e):
                run('sp', e)


class Tl:
    __slots__ = ('a', 'r')

    def __init__(self, a, r):
        self.a = a
        self.r = r


D = 1024
DFF = 2816
NTOK = 2048
L0_COLS = 3328
L1_COLS = 2560


def _rope_tables():
    t = np.arange(1024)
    pos = np.stack([t // 64, t % 64], -1).astype(np.float32)
    freqs = (10000.0 ** (-np.arange(16, dtype=np.float32) / 16)).astype(np.float32)
    ang = (pos[:, :, None] * freqs).reshape(1024, 32).astype(np.float32)
    cos = np.cos(ang).astype(np.float32).T
    sin = np.sin(ang).astype(np.float32).T
    C = np.concatenate([cos, cos, cos, cos], 0)
    Sg = np.concatenate([-sin, sin, -sin, sin], 0)
    return np.ascontiguousarray(C), np.ascontiguousarray(Sg)


def _consts():
    c = {}
    C, Sg = _rope_tables()
    c['ropeC'] = C
    c['ropeS'] = Sg
    s = np.arange(128)[:, None]
    t = np.arange(128)[None, :]
    c['maskF'] = np.where(t >= s, 0.0, NEG).astype(np.float32)
    c['maskB'] = np.where(t <= s, 0.0, NEG).astype(np.float32)
    c['mask01F'] = (t >= s).astype(np.float32)
    c['mask01B'] = (t <= s).astype(np.float32)
    psw = np.zeros((128, 128), np.float32)
    for m in range(128):
        d = m % 64
        k = (m - d) + ((d + 32) % 64)
        psw[k, m] = 1.0
    c['psw'] = psw
    oh = np.zeros((64, 16), np.float32)
    for h in range(8):
        oh[h, h] = 1.0
        oh[32 + h, 8 + h] = 1.0
    c['oh'] = oh
    sc = np.arange(64)[:, None]
    qc = np.arange(64)[None, :]
    ws = np.clip(qc - 8, 0, 48)
    cm = np.where((sc >= ws) & (sc < ws + 16), 0.0, NEG).astype(np.float32)
    c['cmask'] = np.concatenate([cm, cm], 0)
    return c


def _na_rows():
    start = [min(max(r - 4, 0), 8) for r in range(16)]
    valid = [[start[r] <= s < start[r] + 8 for r in range(16)] for s in range(16)]
    return valid


def build_program(dbg=None):
    nc = bass.Bass("TRN2", target_bir_lowering=False)
    st = contextlib.ExitStack()
    with st:
        _build(nc, st, dbg)
    return nc


def _build(nc, st, dbg):
    S = Sched(nc, st)

    def din(name, shape):
        return nc.dram_tensor(name, list(shape), F32, kind="ExternalInput")

    def dout(name, shape):
        return nc.dram_tensor(name, list(shape), F32, kind="ExternalOutput")

    xin = din('xin', [NTOK, D])
    condT = din('condT', [128, 8, 2])
    gfin = din('gfin', [128, 8])
    Wd = []
    for l in range(2):
        w = {}
        w['ada_w'] = din('ada_w%d' % l, [D, 9 * D])
        w['ada_b'] = din('ada_b%d' % l, [128, 72])
        w['norm'] = din('norm%d' % l, [128, 3, 8])
        for f in (1, 2):
            w['f%din' % f] = din('f%din%d' % (f, l), [D, 2 * DFF])
            w['f%dout' % f] = din('f%dout%d' % (f, l), [DFF, D])
        w['mix_in'] = din('mixin%d' % l, [D, L0_COLS if l == 0 else L1_COLS])
        w['mix_out'] = din('mixout%d' % l, [D, D])
        Wd.append(w)
    qkg = din('qkg', [128, 2])
    gkbc = din('gkbc', [64])
    gbias = din('gbias', [64, 2])
    hgn = din('hgn', [512])
    kctx0 = din('kctx0', [256, 128])
    vctx0 = din('vctx0', [256, 128])
    C0 = din('C0', [2, 8, 64, 64])
    n0 = din('n0', [2, 8, 64])
    m0 = din('m0', [16])
    kcctx = din('kcctx', [256, 512])
    vcctx = din('vcctx', [256, 512])
    kdctx = din('kdctx', [256, 128])
    vdctx = din('vdctx', [256, 128])
    natb = din('natb', [8, 128, 15 * 64])
    sink = din('sink', [8])
    cd = {k: din('c_' + k, v.shape) for k, v in _consts().items()}

    o_yp = dout('o_yp', [1024, D])
    o_ys = dout('o_ys', [1024, D])
    o_k0 = dout('o_k0', [4, 256, 128])
    o_v0 = dout('o_v0', [4, 256, 128])
    o_C = dout('o_C', [4, 2, 8, 64, 64])
    o_n = dout('o_n', [4, 2, 8, 64])
    o_m = dout('o_m', [4, 2, 8])
    o_kc = dout('o_kc', [4, 256, 512])
    o_vc = dout('o_vc', [4, 256, 512])
    o_kd = dout('o_kd', [4, 256, 128])
    o_vd = dout('o_vd', [4, 256, 128])
    dbg_out = {}
    if dbg:
        for name, shape in dbg.items():
            dbg_out[name] = dout('dbg_' + name, shape)

    cnt = [0]

    def sb(shape, dt, name=None):
        cnt[0] += 1
        t = st.enter_context(nc.sbuf_tensor(name or ('t%d' % cnt[0]), list(shape), dt))
        return t

    def mk(shape, dt, name=None):
        t = sb(shape, dt, name)
        return Tl(t[:], Res(name or ''))

    banks = []
    for i in range(8):
        t = st.enter_context(nc.psum_tensor('bank%d' % i, [128, 512], F32))
        banks.append(Tl(t[:], Res('bank%d' % i)))
        banks[-1].r.excl = True
    rot = {'a': [0, 1], 'b': [2, 3], 'c': [4, 5], 'x': [6, 7], 's': [0, 1, 2, 3]}
    rotc = {k: 0 for k in rot}

    def ps(tag):
        i = rot[tag][rotc[tag] % len(rot[tag])]
        rotc[tag] += 1
        return banks[i]

    AR_BYTES = 94 * 1024
    arena = sb([128, AR_BYTES // 2], BF16, 'arena')
    ar = {'off': 0, 'live': [], 'inherit': {}}

    def ar_reset():
        toks = dict(ar['inherit'])
        for r in ar['live']:
            for d in (r.w, r.r):
                for k, v in d.items():
                    if toks.get(k, -1) < v:
                        toks[k] = v
        ar['inherit'] = toks
        ar['live'] = []
        ar['off'] = 0

    def ar_res(name=''):
        r = Res(name)
        r.w = dict(ar['inherit'])
        ar['live'].append(r)
        return r

    def carve(shape, dt, name=''):
        n = int(np.prod(shape[1:]))
        nb = n * (4 if dt == F32 else 2)
        nb = (nb + 31) // 32 * 32
        off = ar['off']
        assert off + nb <= AR_BYTES, ('arena overflow', name, off, nb)
        ar['off'] = off + nb
        a = arena[:, off // 2: off // 2 + (n * (2 if dt == F32 else 1))]
        if dt == F32:
            a = a.bitcast(F32)
        if shape[0] != 128:
            a = a[0:shape[0]]
        if len(shape) > 2:
            names = ' '.join('d%d' % i for i in range(1, len(shape)))
            kw = {'d%d' % i: shape[i] for i in range(1, len(shape) - 1)}
            a = a.rearrange('p (%s) -> p %s' % (names, names), **kw)
        return Tl(a, ar_res(name))

    xT = sb([128, 8, NTOK], F32, 'xT')
    xres = [[Res('x%d_%d' % (m, tg)) for tg in range(4)] for m in range(8)]

    def xap(m, tg):
        return xT[:, m, tg * 512:(tg + 1) * 512]

    NRING = 4
    ring_all = [mk([128, 4096], BF16, 'ring%d' % i) for i in range(NRING)]
    ring = list(ring_all)
    ringc = [0]

    def ring_next():
        s_ = ring[ringc[0] % len(ring)]
        ringc[0] += 1
        return s_

    def wslab_in(wd, c0, n):
        s = ring_next()
        dst = s.a[:, 0:8 * n].rearrange('p (k n) -> p k n', k=8)
        src = wd.ap()[:, c0:c0 + n].rearrange('(k p) n -> p k n', p=128)
        S.dma('pool', lambda e: e.dma_start(out=dst, in_=src), s.r, writes=[s.r])
        return Tl(dst, s.r)

    def wslab_out(wd, r0, nchunk):
        s = ring_next()
        dst = s.a[:, 0:nchunk * 1024].rearrange('p (c n) -> p c n', c=nchunk)
        src = wd.ap()[r0:r0 + 128 * nchunk, :].rearrange('(c p) n -> p c n', p=128)
        S.dma('pool', lambda e: e.dma_start(out=dst, in_=src), s.r, writes=[s.r])
        return Tl(dst, s.r)

    CR = Res('consts')

    def cload(dram, shape, dt=F32, src=None, name=None):
        t = sb(shape, dt, name)
        a = src if src is not None else dram.ap()
        S.dma('sp', lambda e: e.dma_start(out=t[:], in_=a), CR, writes=[CR])
        return t

    ident_f = sb([128, 128], F32, 'ident_f')
    ident_b = sb([128, 128], BF16, 'ident_b')
    ones_b = sb([128, 128], BF16, 'ones_b')
    bd_b = sb([128, 128], BF16, 'bd_b')
    ones_f = sb([64, 128], F32, 'ones_f')
    KR = Res('kconst')
    S.op('pool', lambda e: e.memset(ident_f[:], 0.0), writes=[KR])
    S.op('pool', lambda e: e.affine_select(out=ident_f[:], in_=ident_f[:], pattern=[[-1, 128]], compare_op=ALU.not_equal,
                                           fill=1.0, base=0, channel_multiplier=1), reads=[KR], writes=[KR])
    S.op('pool', lambda e: e.tensor_copy(ident_b[:], ident_f[:]), reads=[KR], writes=[KR])
    S.op('pool', lambda e: e.memset(ones_b[:], 1.0), writes=[KR])
    S.op('pool', lambda e: e.memset(ones_f[:], 1.0), writes=[KR])
    S.op('pool', lambda e: e.memset(bd_b[:], 0.0), writes=[KR])
    S.op('pool', lambda e: e.memset(bd_b[0:64, 0:64], 1.0), reads=[KR], writes=[KR])
    S.op('pool', lambda e: e.memset(bd_b[64:128, 64:128], 1.0), reads=[KR], writes=[KR])

    cond_t = cload(condT, [128, 8, 2])
    gfin_t = cload(gfin, [128, 8])
    adab_t = [cload(Wd[l]['ada_b'], [128, 72]) for l in range(2)]
    norm_t = [cload(Wd[l]['norm'], [128, 3, 8]) for l in range(2)]
    qkg_t = cload(qkg, [128, 2])
    gkbc_t = cload(gkbc, [128, 64], src=gkbc.ap().partition_broadcast(128))
    gbias_t = cload(gbias, [64, 2])
    hgn_t = cload(hgn, [128, 512], src=hgn.ap().partition_broadcast(128))
    m0bc_t = cload(m0, [128, 16], src=m0.ap().partition_broadcast(128))
    m0col_t = sb([64, 1], F32, 'm0col')
    S.dma('sp', lambda e: e.dma_start(out=m0col_t[0:8, :], in_=m0.ap()[0:8].rearrange('(p o) -> p o', o=1)), CR, writes=[CR])
    S.dma('sp', lambda e: e.dma_start(out=m0col_t[32:40, :], in_=m0.ap()[8:16].rearrange('(p o) -> p o', o=1)), CR, writes=[CR])
    sink_t = cload(sink, [128, 8], src=sink.ap().partition_broadcast(128))
    ropeC_t = sb([128, 1024], BF16, 'ropeC')
    ropeS_t = sb([128, 1024], BF16, 'ropeS')
    CRP = Res('consts_pool')
    S.dma('pool', lambda e: e.dma_start(out=ropeC_t[:], in_=cd['ropeC'].ap()), CRP, writes=[CRP])
    S.dma('pool', lambda e: e.dma_start(out=ropeS_t[:], in_=cd['ropeS'].ap()), CRP, writes=[CRP])
    maskF_t = cload(cd['maskF'], [128, 128])
    maskB_t = cload(cd['maskB'], [128, 128])
    mask01F_t = cload(cd['mask01F'], [128, 128])
    mask01B_t = cload(cd['mask01B'], [128, 128])
    psw_f = cload(cd['psw'], [128, 128])
    oh_t = cload(cd['oh'], [64, 16])
    cmask_t = cload(cd['cmask'], [128, 64])
    psw_b = sb([128, 128], BF16, 'psw_b')
    S.op('pool', lambda e: e.tensor_copy(psw_b[:], psw_f[:]), reads=[CR], writes=[KR])
    esink_t = sb([128, 8], F32, 'esink')
    S.op('act', lambda e: e.activation(esink_t[:], sink_t[:], AF.Exp), reads=[CR], writes=[KR])
    ngbias_t = sb([64, 1], F32, 'ngbias')
    S.op('pool', lambda e: e.tensor_scalar(ngbias_t[:], gbias_t[:, 1:2], -1.0, None, ALU.mult), reads=[CR], writes=[KR])
    CK = [CR, KR]

    scond = sb([128, 8, 2], BF16, 'scond')
    S.op('act', lambda e: e.activation(scond[:], cond_t[:], AF.Silu), reads=[CR], writes=[KR])
    mod_t = [sb([128, 72, 2], F32, 'mod%d' % l) for l in range(2)]
    modT_R = [[Res('modT') for n in range(3)] for l in range(2)]
    modD_R = [[Res('modD') for n in range(3)] for l in range(2)]
    modA = [sb([128, 3, 8, 2], F32, 'modA%d' % l) for l in range(2)]
    modG = [sb([128, 3, 8, 2], F32, 'modG%d' % l) for l in range(2)]

    def adaln_slabs(l, s0, s1):
        for s in range(s0, s1):
            W = wslab_in(Wd[l]['ada_w'], 512 * s, 512)
            pb = ps('x')
            for c in range(4):
                for k in range(8):
                    S.op('pe', lambda e, c=c, k=k, W=W, pb=pb: e.matmul(pb.a[:, 2 * c:2 * c + 2], W.a[:, k, c * 128:(c + 1) * 128],
                                                                      scond[:, k, :], start=(k == 0), stop=(k == 7)),
                         reads=[W.r, KR], writes=[pb.r])
            dstv = mod_t[l][:, 4 * s:4 * s + 4, :]
            bv = adab_t[l][:, 4 * s:4 * s + 4].unsqueeze(2).to_broadcast([128, 4, 2])
            S.op('dve', lambda e, pb=pb, dstv=dstv, bv=bv: e.tensor_tensor(dstv, pb.a[:, 0:8].rearrange('p (c j) -> p c j', c=4), bv, ALU.add),
                 reads=[pb.r, CR], writes=[modT_R[l][s // 6]])

    def adaln_finish(l, ns=(0, 1, 2)):
        for n in ns:
            sc = mod_t[l][:, (3 * n + 1) * 8:(3 * n + 2) * 8, :]
            g = mod_t[l][:, (3 * n + 2) * 8:(3 * n + 3) * 8, :]
            gn = norm_t[l][:, n, :].unsqueeze(2).to_broadcast([128, 8, 2])
            S.op('dve', lambda e, n=n, sc=sc, gn=gn: e.scalar_tensor_tensor(modA[l][:, n, :, :], sc, 1.0, gn, ALU.add, ALU.mult),
                 reads=[modT_R[l][n], CR], writes=[modD_R[l][n]])
            fac = 1.0 if n == 1 else 0.5
            S.op('dve', lambda e, n=n, g=g, fac=fac: e.tensor_scalar(modG[l][:, n, :, :], g, fac, None, ALU.mult),
                 reads=[modT_R[l][n], modD_R[l][n]], writes=[modD_R[l][n]])

    def modB(l, n, m, cnd):
        return mod_t[l][:, (3 * n) * 8 + m, cnd:cnd + 1]

    def tg_cond(tg):
        return 0 if tg < 2 else 1

    def rms_stats(tg, tmp_sq, rs):
        pb = ps('x')
        for m in range(8):
            sq = tmp_sq[m % 2]
            S.op('act', lambda e, m=m, sq=sq: e.activation(sq.a, xap(m, tg), AF.Square), reads=[xres[m][tg]], writes=[sq.r])
            S.op('pe', lambda e, m=m, sq=sq, pb=pb: e.matmul(pb.a, ones_b[:], sq.a, start=(m == 0), stop=(m == 7)),
                 reads=[sq.r, KR], writes=[pb.r])
        S.op('act', lambda e, pb=pb: e.activation(rs.a, pb.a, AF.Sqrt, bias=EPS, scale=1.0 / D), reads=[pb.r], writes=[rs.r])
        S.op('dve', lambda e: e.reciprocal(rs.a, rs.a), reads=[rs.r], writes=[rs.r])

    def norm_mod(l, n, tg, hdst, hcol0, tmp_sq, rs, tmpf):
        cnd = tg_cond(tg)
        rms_stats(tg, tmp_sq, rs)
        for m in range(8):
            tf = tmpf[m % 2]
            S.op('dve', lambda e, m=m, tf=tf: e.scalar_tensor_tensor(tf.a, xap(m, tg), modA[l][:, n, m, cnd:cnd + 1], rs.a, ALU.mult, ALU.mult),
                 reads=[xres[m][tg], rs.r, modD_R[l][n]], writes=[tf.r])
            S.op('act', lambda e, m=m, tf=tf: e.activation(hdst.a[:, m, hcol0:hcol0 + 512], tf.a, AF.Identity, bias=modB(l, n, m, cnd), scale=1.0),
                 reads=[tf.r, modT_R[l][n]], writes=[hdst.r])

    def load_x():
        ar_reset()
        stg = [carve([128, 1024], F32, 'xstg%d' % i) for i in range(4)]
        for tg in range(4):
            for j in range(4):
                t = tg * 4 + j
                S.dma('sp', lambda e, j=j, t=t: e.dma_start(out=stg[j].a, in_=xin.ap()[t * 128:(t + 1) * 128, :]), stg[j].r, writes=[stg[j].r])
            for m in range(8):
                pb = ps('a' if m % 2 == 0 else 'b')
                for j in range(4):
                    S.op('pe', lambda e, j=j, m=m, pb=pb: e.transpose(pb.a[:, j * 128:(j + 1) * 128], stg[j].a[:, m * 128:(m + 1) * 128], ident_f[:]),
                         reads=[stg[j].r, KR], writes=[pb.r])
                if m % 2 == 0:
                    S.op('dve', lambda e, m=m, pb=pb: e.tensor_copy(xap(m, tg), pb.a), reads=[pb.r], writes=[xres[m][tg]])
                else:
                    S.op('act', lambda e, m=m, pb=pb: e.copy(xap(m, tg), pb.a), reads=[pb.r], writes=[xres[m][tg]])

    def ffn(l, which, between=None, tail=False):
        n = 0 if which == 1 else 2
        ar_reset()
        h = carve([128, 8, NTOK], BF16, 'h')
        hres = [ar_res('h%d' % tg) for tg in range(4)]
        hid = carve([128, 4, NTOK], BF16, 'hid')
        hidres = [[ar_res('hid') for tg in range(4)] for c in range(4)]
        tmp_sq = [carve([128, 512], BF16, 'sq') for _ in range(2)]
        rs = carve([128, 512], F32, 'rs')
        tmpf = [carve([128, 512], F32, 'tmpf') for _ in range(2)]
        sgt = [carve([128, 512], F32, 'sg') for _ in range(2)]
        fbufs = final_alloc() if tail else None
        for tg in range(4):
            norm_mod(l, n, tg, Tl(h.a, hres[tg]), tg * 512, tmp_sq, rs, tmpf)
        win = Wd[l]['f%din' % which]
        wout = Wd[l]['f%dout' % which]
        for i in range(6):
            nch = 4 if i < 5 else 2
            G = wslab_in(win, 512 * i, 128 * nch)
            U = wslab_in(win, DFF + 512 * i, 128 * nch)
            O = wslab_out(wout, 512 * i, nch)
            for tg in range(4):
                for c in range(nch):
                    pg = ps('a')
                    pu = ps('b')
                    for k in range(8):
                        S.op('pe', lambda e, k=k, c=c, tg=tg, pg=pg, G=G: e.matmul(pg.a, G.a[:, k, c * 128:(c + 1) * 128], h.a[:, k, tg * 512:(tg + 1) * 512],
                                                                                   start=(k == 0), stop=(k == 7)),
                             reads=[G.r, hres[tg]], writes=[pg.r])
                    for k in range(8):
                        S.op('pe', lambda e, k=k, c=c, tg=tg, pu=pu, U=U: e.matmul(pu.a, U.a[:, k, c * 128:(c + 1) * 128], h.a[:, k, tg * 512:(tg + 1) * 512],
                                                                                   start=(k == 0), stop=(k == 7)),
                             reads=[U.r, hres[tg]], writes=[pu.r])
                    sg = sgt[c % 2]
                    S.op('act', lambda e, pg=pg, sg=sg: e.activation(sg.a, pg.a, AF.Silu), reads=[pg.r], writes=[sg.r])
                    S.op('dve', lambda e, c=c, tg=tg, pu=pu, sg=sg: e.tensor_tensor(hid.a[:, c, tg * 512:(tg + 1) * 512], sg.a, pu.a, ALU.mult),
                         reads=[pu.r, sg.r], writes=[hidres[c][tg]])
            for tg in range(4):
                cnd = tg_cond(tg)
                for m in range(8):
                    py = ps('c')
                    for c in range(nch):
                        S.op('pe', lambda e, c=c, m=m, tg=tg, py=py, O=O, nch=nch: e.matmul(py.a, O.a[:, c, m * 128:(m + 1) * 128], hid.a[:, c, tg * 512:(tg + 1) * 512],
                                                                                            start=(c == 0), stop=(c == nch - 1)),
                             reads=[O.r, hidres[c][tg]], writes=[py.r])
                    S.op('dve', lambda e, m=m, tg=tg, py=py, cnd=cnd: e.scalar_tensor_tensor(xap(m, tg), py.a, modG[l][:, n, m, cnd:cnd + 1], xap(m, tg), ALU.mult, ALU.add),
                         reads=[py.r, modD_R[l][n], xres[m][tg]], writes=[xres[m][tg]])
                if tail and i == 5:
                    final_tg(fbufs, tg)
            if between is not None:
                between(i)

    def final_alloc():
        tmp_sq = [carve([128, 512], BF16, 'fsq') for _ in range(2)]
        rs = carve([128, 512], F32, 'frs')
        yT = carve([128, 8, 512], F32, 'yT')
        ystg = [carve([128, 1024], F32, 'ystg') for _ in range(2)]
        return tmp_sq, rs, yT, ystg

    def final_out():
        ar_reset()
        bufs = final_alloc()
        for tg in range(4):
            final_tg(bufs, tg)

    def final_tg(bufs, tg):
        tmp_sq, rs, yT, ystg = bufs
        if True:
            rms_stats(tg, tmp_sq, rs)
            for m in range(8):
                S.op('dve', lambda e, m=m: e.scalar_tensor_tensor(yT.a[:, m, :], xap(m, tg), gfin_t[:, m:m + 1], rs.a, ALU.mult, ALU.mult),
                     reads=[xres[m][tg], rs.r, CR], writes=[yT.r])
            for j in range(4):
                t = tg * 4 + j
                ys = ystg[j % 2]
                for half in range(2):
                    pb = ps('a' if half == 0 else 'b')
                    for mm in range(4):
                        m = half * 4 + mm
                        S.op('pe', lambda e, mm=mm, m=m, j=j, pb=pb: e.transpose(pb.a[:, mm * 128:(mm + 1) * 128], yT.a[:, m, j * 128:(j + 1) * 128], ident_f[:]),
                             reads=[yT.r, KR], writes=[pb.r])
                    if half == 0:
                        S.op('act', lambda e, pb=pb, ys=ys: e.copy(ys.a[:, 0:512], pb.a), reads=[pb.r], writes=[ys.r])
                    else:
                        S.op('dve', lambda e, pb=pb, ys=ys: e.tensor_copy(ys.a[:, 512:1024], pb.a), reads=[pb.r], writes=[ys.r])
                dst = (o_yp if t < 8 else o_ys).ap()[(t % 8) * 128:(t % 8 + 1) * 128, :]
                S.dma('sp', lambda e, ys=ys, dst=dst: e.dma_start(out=dst, in_=ys.a), ys.r, reads=[ys.r])

    def proj_fm(W, c0, M, hg, ncols, post):
        for b in range(ncols // 512):
            pb = ps('s')
            for k in range(8):
                S.op('pe', lambda e, k=k, b=b, pb=pb: e.matmul(pb.a[0:M, :], W.a[:, k, c0:c0 + M], hg.a[:, k, b * 512:(b + 1) * 512],
                                                               start=(k == 0), stop=(k == 7)),
                     reads=[W.r, hg.r], writes=[pb.r])
            post(b, pb)

    def proj_tm(W, c0, n, hg, ntiles, post):
        for t in range(ntiles):
            pb = ps('s')
            for k in range(8):
                S.op('pe', lambda e, k=k, t=t, pb=pb: e.matmul(pb.a[:, 0:n], hg.a[:, k, t * 128:(t + 1) * 128], W.a[:, k, c0:c0 + n],
                                                               start=(k == 0), stop=(k == 7)),
                     reads=[W.r, hg.r], writes=[pb.r])
            post(t, pb)

    class Work:
        pass

    LA = 3

    def _emit_pv(wk, blk, kd, vq, c0, PT, b0, last, qts, finalize):
        if blk['po'] is None:
            blk['po'] = ps('c')
        po = blk['po']
        ns = kd['ns']
        pb = kd.get('pbase', 0)
        for i in vq:
            oc = (i - b0) * 128
            first = blk['first']
            S.op('pe', lambda e: e.matmul(po.a[:, oc:oc + 65], PT.a[pb:pb + ns, i * 128 - c0:i * 128 - c0 + 128], kd['vaug'],
                                          start=first, stop=True, skip_group_check=True),
                 reads=[PT.r] + list(kd['res']), writes=[po.r])
            blk['first'] = False
        if last:
            if hasattr(finalize, 'blk'):
                finalize.blk(qts, po.a.rearrange('p (q c) -> p q c', q=4), po.r)
            else:
                for i in qts:
                    oc = (i - b0) * 128
                    finalize(i, po.a[:, oc:oc + 65], po.r)

    def attn_flush(wk):
        while wk.pipe:
            wk.pipe.pop(0)()

    def attn_job(wk, nq, qT, qres, ktiles, valid, prob, finalize, after_block=None):
        import functools
        nqt = nq // 128
        for b0 in range(0, nqt, 4):
            qts = list(range(b0, min(b0 + 4, nqt)))
            blk = {'po': None, 'first': True}
            steps = []
            for kt, kd in enumerate(ktiles):
                vq = [i for i in qts if valid(kt, i)]
                if vq:
                    steps.append((kt, kd, vq))
            for si, (kt, kd, vq) in enumerate(steps):
                lo, hi = min(vq), max(vq) + 1
                c0, n = lo * 128, (hi - lo) * 128
                ns = kd['ns']
                PT = wk.PT[wk.ptc % len(wk.PT)]
                wk.ptc += 1
                if kd.get('noscore'):
                    prob(kt, kd, c0, n, None, PT)
                else:
                    pscore = ps('s')
                    S.op('pe', lambda e: e.matmul(pscore.a[0:ns, 0:n], kd['kT'], qT(c0, n), start=True, stop=True),
                         reads=list(kd['res']) + list(qres), writes=[pscore.r])
                    prob(kt, kd, c0, n, pscore, PT)
                wk.pipe.append(functools.partial(_emit_pv, wk, blk, kd, vq, c0, PT, b0, si == len(steps) - 1, qts, finalize))
                while len(wk.pipe) > wk.LA:
                    wk.pipe.pop(0)()
            if after_block is not None:
                after_block(b0 * 128)

    def prob_exp(kt, kd, c0, n, pscore, PT):
        ns = kd['ns']
        S.op('act', lambda e: e.activation(PT.a[0:ns, 0:n], pscore.a[0:ns, 0:n], AF.Exp, scale=0.125), reads=[pscore.r], writes=[PT.r])

    def mixer(l, grp):
        ar_reset()
        sample = grp == 2
        NT = 1024 if sample else 512
        tgs = [2, 3] if sample else [grp]
        ntile = NT // 128
        nseq = 1 if sample else 2
        L = 1024 if sample else 256
        tps = L // 128
        cnd = 1 if sample else 0
        tok0 = 1024 if sample else grp * 512
        hg = carve([128, 8, NT], BF16, 'hg')
        tmp_sq = [carve([128, 512], BF16, 'sq') for _ in range(2)]
        rs = carve([128, 512], F32, 'rs')
        tmpf = [carve([128, 512], F32, 'tmpf') for _ in range(2)]
        for i, tg in enumerate(tgs):
            norm_mod(l, 1, tg, hg, i * 512, tmp_sq, rs, tmpf)
        oT = Tl(hg.a, hg.r)
        wk = Work()
        wk.LA = 3 if sample else 6
        wk.PT = [carve([128, 512], BF16, 'PT') for _ in range(4 if sample else 8)]
        wk.ptc = 0
        wk.pipe = []
        opair = carve([128, ntile, 128], BF16, 'opair')
        sm = [carve([128, 8], F32, 'sm') for _ in range(8)]
        smc = [0]

        def smt():
            smc[0] += 1
            return sm[smc[0] % 8]
        stg_f = [carve([128, 528], F32, 'stgf') for _ in range(2)]
        stc = [0]

        def stg():
            stc[0] += 1
            return stg_f[stc[0] % 2]

        def seq_tile(s, j):
            return s * tps + j

        def fin_softmax(e_h, extra_den=None):
            ecol = e_h * 64

            class F:
                tile0 = 0

                def bind(self, tile0):
                    self.tile0 = tile0
                    return self

                def blk(self, qts, pv, por):
                    i0, nq = qts[0], len(qts)
                    t = smt()
                    den = pv[:, 0:nq, 64]
                    if extra_den is None:
                        S.op('dve', lambda e: e.reciprocal(t.a[:, 0:nq], den), reads=[por], writes=[t.r])
                    else:
                        S.op('dve', lambda e: e.tensor_scalar(t.a[:, 0:nq], den, extra_den, None, ALU.add), reads=[por, KR], writes=[t.r])
                        S.op('dve', lambda e: e.reciprocal(t.a[:, 0:nq], t.a[:, 0:nq]), reads=[t.r], writes=[t.r])
                    S.op('dve', lambda e: e.tensor_tensor(opair.a[:, self.tile0 + i0:self.tile0 + i0 + nq, ecol:ecol + 64], pv[:, 0:nq, 0:64],
                                                          t.a[:, 0:nq].unsqueeze(2).to_broadcast([128, nq, 64]), ALU.mult),
                         reads=[por, t.r], writes=[opair.r])
            return F()

        def flush_pair(chunk):
            attn_flush(wk)
            for b in range(0, ntile, 4):
                pb = ps('x')
                pbv = pb.a.bitcast(BF16)
                nn = min(4, ntile - b)
                for j in range(nn):
                    S.op('pe', lambda e, j=j, b=b, pbv=pbv: e.transpose(pbv[:, j * 128:(j + 1) * 128], opair.a[:, b + j, :], ident_b[:]),
                         reads=[opair.r, KR], writes=[pb.r])
                S.op('act', lambda e, b=b, nn=nn, pbv=pbv: e.copy(oT.a[:, chunk, b * 128:(b + nn) * 128], pbv[:, 0:nn * 128]), reads=[pb.r], writes=[oT.r])

        def qknorm_fm(pb, gcol, dst_ap, dst_res, rope_cols=None):
            sq = tmp_sq[0]
            S.op('act', lambda e: e.activation(sq.a, pb.a, AF.Square), reads=[pb.r], writes=[sq.r])
            p2 = ps('x')
            S.op('pe', lambda e: e.matmul(p2.a, bd_b[:], sq.a, start=True, stop=True), reads=[sq.r, KR], writes=[p2.r])
            S.op('act', lambda e: e.activation(rs.a, p2.a, AF.Ln, bias=EPS, scale=1.0 / 64), reads=[p2.r], writes=[rs.r])
            S.op('act', lambda e: e.activation(rs.a, rs.a, AF.Exp, scale=-0.5), reads=[rs.r], writes=[rs.r])
            if rope_cols is None:
                S.op('dve', lambda e: e.scalar_tensor_tensor(dst_ap, pb.a, qkg_t[:, gcol:gcol + 1], rs.a, ALU.mult, ALU.mult),
                     reads=[pb.r, rs.r, CR], writes=[dst_res])
            else:
                qn = tmp_sq[1]
                S.op('dve', lambda e: e.scalar_tensor_tensor(qn.a, pb.a, qkg_t[:, gcol:gcol + 1], rs.a, ALU.mult, ALU.mult),
                     reads=[pb.r, rs.r, CR], writes=[qn.r])
                rope_fm(qn.a, qn.r, dst_ap, dst_res, rope_cols)

        def rope_fm(src_ap, src_res, dst_ap, dst_res, c0):
            p3 = ps('x')
            S.op('pe', lambda e: e.matmul(p3.a, psw_b[:], src_ap, start=True, stop=True), reads=[src_res, KR], writes=[p3.r])
            t1, t2 = tmpf[0], tmpf[1]
            S.op('dve', lambda e: e.tensor_tensor(t1.a, p3.a, ropeS_t[:, c0:c0 + 512], ALU.mult), reads=[p3.r, CRP], writes=[t1.r])
            S.op('pool', lambda e: e.tensor_tensor(t2.a, src_ap, ropeC_t[:, c0:c0 + 512], ALU.mult), reads=[src_res, CRP], writes=[t2.r])
            S.op('dve', lambda e: e.tensor_tensor(dst_ap, t1.a, t2.a, ALU.add), reads=[t1.r, t2.r], writes=[dst_res])

        def load_ctx_kT(dram, ncol_pairs, dst, dup):
            for j in range(2):
                s_ = stg()
                if dup:
                    srcv = bass.AP(dram, j * 128 * 128, [[128, 128], [64, 2], [0, 2], [1, 64]])
                    S.dma('sp', lambda e, s_=s_, srcv=srcv: e.dma_start(out=s_.a[:, 0:256].rearrange('p (a b c) -> p a b c', a=2, b=2), in_=srcv), s_.r, writes=[s_.r])
                else:
                    for q_ in range(ncol_pairs):
                        S.dma('sp', lambda e, s_=s_, j=j, q_=q_: e.dma_start(out=s_.a[:, q_ * 128:(q_ + 1) * 128], in_=dram.ap()[j * 128:(j + 1) * 128, q_ * 128:(q_ + 1) * 128]), s_.r, writes=[s_.r])
                pb = ps('x')
                for c in range(ncol_pairs):
                    S.op('pe', lambda e, c=c, s_=s_, pb=pb: e.transpose(pb.a[:, c * 128:(c + 1) * 128], s_.a[:, c * 128:(c + 1) * 128], ident_f[:]),
                         reads=[s_.r, KR], writes=[pb.r])
                for c in range(ncol_pairs):
                    S.op('act', lambda e, c=c, j=j, pb=pb: e.copy(dst.a[:, c, j * 128:(j + 1) * 128], pb.a[:, c * 128:(c + 1) * 128]), reads=[pb.r], writes=[dst.r])

        def load_ctx_v(dram, nh, dst, t0):
            for j in range(2):
                s_ = stg()
                for q_ in range(0, 64 * nh, 128):
                    S.dma('sp', lambda e, s_=s_, j=j, q_=q_: e.dma_start(out=s_.a[:, q_:q_ + 128], in_=dram.ap()[j * 128:(j + 1) * 128, q_:q_ + 128]), s_.r, writes=[s_.r])
                S.op('act', lambda e, s_=s_, j=j: e.copy(dst.a[:, t0 + j, :, 0:64], s_.a[:, 0:64 * nh].rearrange('p (h d) -> p h d', h=nh)), reads=[s_.r], writes=[dst.r])

        def out_tm(pb, n, dram, seq, j, cast_dst=None):
            import os as _os2
            if ('nodma%d' % n) in _os2.environ.get('KSUB', '') and l == 1:
                return
            s_ = stg()
            if True:
                S.op('dve', lambda e: e.tensor_copy(s_.a[:, 0:n], pb.a[:, 0:n]), reads=[pb.r], writes=[s_.r])
            else:
                S.op('act', lambda e: e.copy(s_.a[:, 0:n], pb.a[:, 0:n]), reads=[pb.r], writes=[s_.r])
            import os as _os3
            oq = _os3.environ.get('KOUTQ', 'sp')
            if ('nostore%d' % n) in _os2.environ.get('KSUB', '') and l == 1:
                return
            S.dma(oq, lambda e: e.dma_start(out=dram.ap()[seq, j * 128:(j + 1) * 128, :], in_=s_.a[:, 0:n]), s_.r, reads=[s_.r])

        mi = Wd[l]['mix_in']
        if l == 0:
            nkt = (2 if sample else 0) + ntile
            QA = carve([128, 4, NT], BF16, 'QA')
            KA = carve([128, 2, 256 + NT if sample else NT], BF16, 'KA')
            VA = carve([128, nkt, 2, 65], BF16, 'VA')
            if sample:
                rsv = [ring.pop(), ring.pop()]
                QB = Tl(rsv[0].a.rearrange('p (c n) -> p c n', c=4), rsv[0].r)
                KB = Tl(rsv[1].a.rearrange('p (c n) -> p c n', c=4), rsv[1].r)
            else:
                QB = carve([128, 4, NT], BF16, 'QB')
                KB = carve([128, 4, NT], BF16, 'KB')
            VB = carve([128, ntile, 8, 65], BF16, 'VB')
            OB = carve([128, ntile, 512], BF16, 'OB')
            KBT = None if sample else carve([128, ntile, 512], BF16, 'KBT')
            LI = carve([64, NT], F32, 'LI')
            LF = carve([64, NT], F32, 'LF')
            BT = carve([64, NT], F32, 'BT')
            TOK = carve([128, ntile, 3, 16], F32, 'TOK')
            HF = carve([128, tps, 64], F32, 'HF')
            NBC = [carve([128, 512], F32, 'NBC') for _ in range(2)]
            EE01 = carve([128, 1024], F32, 'EE01')
            EE = [Tl(EE01.a[:, i * 512:(i + 1) * 512], ar_res('EE%d' % i)) for i in range(2)]
            ONES = Tl(EE01.a[0:64, 0:NT], EE01.r)
            EEall = [EE01.r, EE[0].r, EE[1].r]
            RM = Tl(tmpf[1].a[0:64], tmpf[1].r)
            koff = 256 if sample else 0
            S.op('pool', lambda e: e.memset(VA.a, 1.0), writes=[VA.r])
            S.op('pool', lambda e: e.memset(VB.a, 1.0), writes=[VB.r])
            S.op('pool', lambda e: e.memset(LI.a, 0.0), writes=[LI.r])
            S.op('pool', lambda e: e.memset(LF.a, 0.0), writes=[LF.r])
            S.op('pool', lambda e: e.memset(BT.a, 0.0), writes=[BT.r])
            S.op('pool', lambda e: e.memset(ONES.a, 1.0), writes=EEall)
            if sample:
                load_ctx_kT(kctx0, 2, KA, True)
                load_ctx_v(vctx0, 2, VA, 0)
                VV = carve([128, 2, 4, 65], BF16, 'VV')
                C0v = C0.ap().rearrange('a (h t) k v -> a t k h v', t=2)
                n0v = n0.ap().rearrange('a (h t) k -> a t k h', t=2)
                for half in range(2):
                    s_ = stg()
                    for dr in range(2):
                        S.dma('sp', lambda e, s_=s_, dr=dr, half=half: e.dma_start(
                            out=s_.a[half * 64:half * 64 + 64, dr * 256: dr * 256 + 256].rearrange('p (h v) -> p h v', h=4),
                            in_=C0v[dr, half]), s_.r, writes=[s_.r])
                        S.dma('sp', lambda e, s_=s_, dr=dr, half=half: e.dma_start(
                            out=s_.a[half * 64:half * 64 + 64, 512 + dr * 4:512 + dr * 4 + 4],
                            in_=n0v[dr, half], allow_slow_non_contiguous=True), s_.r, writes=[s_.r])
                    for dr in range(2):
                        S.op('act', lambda e, s_=s_, dr=dr, half=half: e.copy(
                            VV.a[half * 64:half * 64 + 64, dr, :, 0:64],
                            s_.a[half * 64:half * 64 + 64, dr * 256:dr * 256 + 256].rearrange('p (h v) -> p h v', h=4)),
                            reads=[s_.r], writes=[VV.r])
                        S.op('act', lambda e, s_=s_, dr=dr, half=half: e.copy(
                            VV.a[half * 64:half * 64 + 64, dr, :, 64:65],
                            s_.a[half * 64:half * 64 + 64, 512 + dr * 4:512 + dr * 4 + 4].unsqueeze(2)),
                            reads=[s_.r], writes=[VV.r])
            W = wslab_in(mi, 0, 512)
            for c in range(4):
                def post(b, pb, c=c):
                    qknorm_fm(pb, 0, QA.a[:, c, b * 512:(b + 1) * 512], QA.r, rope_cols=(b * 512 if sample else None))
                proj_fm(W, c * 128, 128, hg, NT, post)
            W = wslab_in(mi, 512, 512)
            for c in range(2):
                def post(b, pb, c=c):
                    qknorm_fm(pb, 1, KA.a[:, c, koff + b * 512:koff + (b + 1) * 512], KA.r, rope_cols=(b * 512 if sample else None))
                proj_fm(W, c * 128, 128, hg, NT, post)

            def post_gi(b, pb):
                S.op('act', lambda e: e.activation(LI.a[0:40, b * 512:(b + 1) * 512], pb.a[0:40, :], AF.Identity, bias=gbias_t[0:40, 0:1], scale=1.0),
                     reads=[pb.r, CR], writes=[LI.r])
            proj_fm(W, 256, 64, hg, NT, post_gi)

            def post_gf(b, pb):
                t1 = tmpf[0]
                S.op('act', lambda e: e.activation(t1.a[0:40, :], pb.a[0:40, :], AF.Exp, bias=ngbias_t[0:40, 0:1], scale=-1.0), reads=[pb.r, KR], writes=[t1.r])
                S.op('act', lambda e: e.activation(t1.a[0:40, :], t1.a[0:40, :], AF.Ln, bias=1.0, scale=1.0), reads=[t1.r], writes=[t1.r])
                S.op('dve', lambda e: e.tensor_scalar(LF.a[0:40, b * 512:(b + 1) * 512], t1.a[0:40, :], -1.0, None, ALU.mult), reads=[t1.r], writes=[LF.r])
            proj_fm(W, 320, 64, hg, NT, post_gf)

            def post_va(t, pb):
                kt = (2 if sample else 0) + t
                S.op('act', lambda e: e.copy(VA.a[:, kt, :, 0:64], pb.a[:, 0:128].rearrange('p (h d) -> p h d', h=2)), reads=[pb.r], writes=[VA.r])
                if not sample:
                    out_tm(pb, 128, o_v0, grp * 2 + t // tps, t % tps)
            proj_tm(W, 384, 128, hg, ntile, post_va)
            W = wslab_in(mi, 1024, 512)
            for c in range(4):
                def post(b, pb, c=c):
                    S.op('act', lambda e: e.copy(QB.a[:, c, b * 512:(b + 1) * 512], pb.a), reads=[pb.r], writes=[QB.r])
                proj_fm(W, c * 128, 128, hg, NT, post)
            W = wslab_in(mi, 1536, 512)
            for c in range(4):
                def post(b, pb, c=c):
                    S.op('dve', lambda e: e.tensor_copy(KB.a[:, c, b * 512:(b + 1) * 512], pb.a), reads=[pb.r], writes=[KB.r])
                proj_fm(W, c * 128, 128, hg, NT, post)
            if not sample:
                def post(t, pb):
                    S.op('act', lambda e: e.copy(KBT.a[:, t, :], pb.a), reads=[pb.r], writes=[KBT.r])
                proj_tm(W, 0, 512, hg, ntile, post)
            W = wslab_in(mi, 2048, 512)
            def post(t, pb):
                S.op('dve', lambda e: e.tensor_copy(VB.a[:, t, :, 0:64], pb.a.rearrange('p (h d) -> p h d', h=8)), reads=[pb.r], writes=[VB.r])
            proj_tm(W, 0, 512, hg, ntile, post)
            W = wslab_in(mi, 2560, 512)
            def post(t, pb):
                S.op('act', lambda e: e.activation(OB.a[:, t, :], pb.a, AF.Sigmoid), reads=[pb.r], writes=[OB.r])
            proj_tm(W, 0, 512, hg, ntile, post)
            if not sample:
                W = wslab_in(mi, 3072, 128)
                def post(t, pb):
                    t_ = smt()
                    s_ = stg()
                    for hh in range(2):
                        S.op('act', lambda e, hh=hh: e.activation(s_.a[:, 256 + hh * 64:256 + hh * 64 + 64], pb.a[:, hh * 64:(hh + 1) * 64], AF.Square,
                                                                  accum_out=t_.a[:, hh:hh + 1]), reads=[pb.r], writes=[s_.r, t_.r])
                    S.op('act', lambda e: e.activation(t_.a[:, 0:2], t_.a[:, 0:2], AF.Sqrt, bias=EPS, scale=1.0 / 64), reads=[t_.r], writes=[t_.r])
                    S.op('dve', lambda e: e.reciprocal(t_.a[:, 0:2], t_.a[:, 0:2]), reads=[t_.r], writes=[t_.r])
                    for hh in range(2):
                        S.op('dve', lambda e, hh=hh: e.scalar_tensor_tensor(s_.a[:, hh * 64:(hh + 1) * 64], pb.a[:, hh * 64:(hh + 1) * 64], t_.a[:, hh:hh + 1], gkbc_t[:],
                                                                            ALU.mult, ALU.mult), reads=[pb.r, t_.r, CR], writes=[s_.r])
                    S.dma('sp', lambda e: e.dma_start(out=o_k0.ap()[grp * 2 + t // tps, (t % tps) * 128:(t % tps + 1) * 128, :], in_=s_.a[:, 0:128]), s_.r, reads=[s_.r])
                proj_tm(W, 0, 128, hg, ntile, post)

            for c in range(4):
                for s in range(nseq):
                    q0 = s * L
                    kv = c // 2
                    for e_ in range(2):
                        pbs = 64 * e_
                        kts = []
                        if sample:
                            for j in range(2):
                                kts.append(dict(kT=KA.a[pbs:pbs + 64, kv, j * 128:(j + 1) * 128], ns=128, vaug=VA.a[:, j, kv, :], res=[KA.r, VA.r]))
                        for j in range(tps):
                            kts.append(dict(kT=KA.a[pbs:pbs + 64, kv, koff + q0 + j * 128:koff + q0 + (j + 1) * 128], ns=128,
                                            vaug=VA.a[:, (2 if sample else 0) + seq_tile(s, j), kv, :], res=[KA.r, VA.r]))
                        fs = fin_softmax(e_)
                        attn_job(wk, L, lambda c0, n, c=c, pbs=pbs, q0=q0: QA.a[pbs:pbs + 64, c, q0 + c0:q0 + c0 + n], [QA.r], kts,
                                 lambda kt, i: True, prob_exp,
                                 fs.bind(s * tps))
                    if s == nseq - 1:
                        flush_pair(c)

            def revap(a, c0, n):
                return bass.AP(a.tensor, a[:, c0 + n - 1:c0 + n].offset, [list(a.ap[0]), [-1, n]])
            for s in range(nseq):
                c0 = s * L
                S.op('dve', lambda e, c0=c0: e.tensor_tensor_scan(BT.a[0:8, c0:c0 + L], ONES.a[0:8, c0:c0 + L], LF.a[0:8, c0:c0 + L], 0.0, ALU.mult, ALU.add),
                     reads=[LF.r] + EEall, writes=[BT.r])
                S.op('dve', lambda e, c0=c0: e.tensor_tensor_scan(revap(BT.a[32:40], c0, L), ONES.a[32:40, c0:c0 + L], revap(LF.a[32:40], c0, L), 0.0, ALU.mult, ALU.add),
                     reads=[LF.r] + EEall, writes=[BT.r])
            S.op('dve', lambda e: e.tensor_tensor(LI.a[0:40, :], LI.a[0:40, :], BT.a[0:40, :], ALU.subtract), reads=[LI.r, BT.r], writes=[LI.r])
            for s in range(nseq):
                c0 = s * L
                ini_f = m0col_t[0:8, :] if sample else 0.0
                ini_b = m0col_t[32:40, :] if sample else 0.0
                S.op('dve', lambda e, c0=c0, ini_f=ini_f: e.tensor_tensor_scan(LF.a[0:8, c0:c0 + L], ONES.a[0:8, c0:c0 + L], LI.a[0:8, c0:c0 + L], ini_f, ALU.mult, ALU.max),
                     reads=[LI.r, CR] + EEall, writes=[LF.r])
                S.op('dve', lambda e, c0=c0, ini_b=ini_b: e.tensor_tensor_scan(revap(LF.a[32:40], c0, L), ONES.a[32:40, c0:c0 + L], revap(LI.a[32:40], c0, L), ini_b, ALU.mult, ALU.max),
                     reads=[LI.r, CR] + EEall, writes=[LF.r])
            S.op('dve', lambda e: e.tensor_scalar(LF.a[0:40, :], LF.a[0:40, :], -1.0, None, ALU.mult), reads=[LF.r], writes=[LF.r])
            S.op('dve', lambda e: e.tensor_tensor(BT.a[0:40, :], LF.a[0:40, :], BT.a[0:40, :], ALU.subtract), reads=[LF.r, BT.r], writes=[BT.r])
            if not sample:
                for s in range(nseq):
                    c0 = s * L
                    sg_ = grp * 2 + s
                    t_ = smt()
                    S.op('dve', lambda e, c0=c0, t_=t_: e.tensor_scalar(t_.a[0:8, 0:1], BT.a[0:8, c0 + L - 1:c0 + L], -1.0, None, ALU.mult), reads=[BT.r], writes=[t_.r])
                    S.op('dve', lambda e, c0=c0, t_=t_: e.tensor_scalar(t_.a[32:40, 0:1], BT.a[32:40, c0:c0 + 1], -1.0, None, ALU.mult), reads=[BT.r], writes=[t_.r])
                    S.dma('sp', lambda e, t_=t_, sg_=sg_: e.dma_start(out=o_m.ap()[sg_, 0, :].rearrange('(p o) -> p o', o=1), in_=t_.a[0:8, 0:1], allow_slow_non_contiguous=True), t_.r, reads=[t_.r])
                    S.dma('sp', lambda e, t_=t_, sg_=sg_: e.dma_start(out=o_m.ap()[sg_, 1, :].rearrange('(p o) -> p o', o=1), in_=t_.a[32:40, 0:1], allow_slow_non_contiguous=True), t_.r, reads=[t_.r])
            S.op('act', lambda e: e.activation(BT.a[0:40, :], BT.a[0:40, :], AF.Exp), reads=[BT.r], writes=[BT.r])
            WF = None
            if not sample:
                WF = carve([64, NT], F32, 'WF')
                S.op('pool', lambda e: e.memset(WF.a, 0.0), writes=[WF.r])
                for s in range(nseq):
                    c0 = s * L
                    S.op('act', lambda e, c0=c0: e.activation(WF.a[0:8, c0:c0 + L], LI.a[0:8, c0:c0 + L], AF.Exp, bias=LF.a[0:8, c0 + L - 1:c0 + L], scale=1.0),
                         reads=[LI.r, LF.r], writes=[WF.r])
                    S.op('act', lambda e, c0=c0: e.activation(WF.a[32:40, c0:c0 + L], LI.a[32:40, c0:c0 + L], AF.Exp, bias=LF.a[32:40, c0:c0 + 1], scale=1.0),
                         reads=[LI.r, LF.r], writes=[WF.r])
            for t in range(ntile):
                pb = ps('x')
                srcs = [LI, BT] + ([WF] if WF is not None else [])
                for qi, src in enumerate(srcs):
                    S.op('pe', lambda e, qi=qi, src=src, t=t, pb=pb: e.transpose(pb.a[:, qi * 64:qi * 64 + 40], src.a[0:40, t * 128:(t + 1) * 128], ident_f[0:40, 0:40]),
                         reads=[src.r, KR], writes=[pb.r])
                nq_ = len(srcs)
                S.op('dve', lambda e, t=t, pb=pb, nq_=nq_: e.tensor_copy(TOK.a[:, t, 0:nq_, :].rearrange('p q (a h) -> p q a h', a=2),
                                                                        pb.a[:, 0:nq_ * 64].rearrange('p (q a h) -> p q a h', q=nq_, a=2)[:, :, :, 0:8]),
                     reads=[pb.r], writes=[TOK.r])

            S.op('dve', lambda e: e.tensor_scalar(TOK.a[:, :, 0, :], TOK.a[:, :, 0, :], float(np.log(0.125)), None, ALU.add), reads=[TOK.r], writes=[TOK.r])
            def _mk_mjob(c, s, e_, dr, jidx):
                    q0 = s * L
                    hd_ = 2 * c + e_
                    pbs = 64 * e_
                    hd = dr * 8 + hd_
                    row = dr * 32 + hd_
                    mask_t = maskF_t if dr == 0 else maskB_t
                    nbcs = {}

                    def pro(b0):
                        nb = min(512, L - b0)
                        S.op('act', lambda e, b0=b0, nb=nb, hd=hd: e.activation(RM.a[0:40, 0:nb], LF.a[0:40, q0 + b0:q0 + b0 + nb], AF.Copy, scale=oh_t[0:40, hd:hd + 1]),
                             reads=[LF.r, CR], writes=[RM.r])
                        pbc = ps('x')
                        S.op('pe', lambda e, nb=nb, pbc=pbc: e.matmul(pbc.a[:, 0:nb], ones_f[0:40, :], RM.a[0:40, 0:nb], start=True, stop=True),
                             reads=[RM.r, KR], writes=[pbc.r])
                        nbt = NBC[(b0 // 512) % 2] if sample else NBC[jidx % 2]
                        S.op('act', lambda e, nb=nb, pbc=pbc, nbt=nbt: e.copy(nbt.a[:, 0:nb], pbc.a[:, 0:nb]), reads=[pbc.r], writes=[nbt.r])
                        nbcs[b0] = nbt
                    kts = []
                    if sample:
                        kts.append(dict(noscore=True, virt=True, ns=64, pbase=pbs, vaug=VV.a[pbs:pbs + 64, dr, hd_ // 2, :], res=[VV.r]))
                    for j in range(tps):
                        kts.append(dict(kT=KB.a[pbs:pbs + 64, c, q0 + j * 128:q0 + (j + 1) * 128], ns=128, j=j,
                                        vaug=VB.a[:, seq_tile(s, j), hd_, :], res=[KB.r, VB.r]))

                    def valid(kt, i, dr=dr):
                        if sample:
                            if kt == 0:
                                return True
                            kt -= 1
                        return i >= kt if dr == 0 else i <= kt

                    def prob(kt, kd, c0, n, pscore, PT, dr=dr, hd=hd, pbs=pbs, c=c, q0=q0, nbcs=nbcs, mask_t=mask_t, s=s):
                        b0 = (c0 // 512) * 512
                        nbt = nbcs[b0]
                        lc = c0 - b0
                        ee = EE[wk.ptc % 2]
                        if kd.get('virt'):
                            S.op('act', lambda e: e.activation(ee.a[pbs:pbs + 64, 0:n], nbt.a[pbs:pbs + 64, lc:lc + n], AF.Exp, bias=m0bc_t[pbs:pbs + 64, hd:hd + 1], scale=1.0),
                                 reads=[nbt.r, CR], writes=[ee.r])
                            S.op('dve', lambda e: e.tensor_tensor(PT.a[pbs:pbs + 64, 0:n], ee.a[pbs:pbs + 64, 0:n], QB.a[pbs:pbs + 64, c, q0 + c0:q0 + c0 + n], ALU.mult),
                                 reads=[ee.r, QB.r], writes=[PT.r])
                            return
                        j = kd['j']
                        tl = seq_tile(s, j)
                        abias = TOK.a[:, tl, 0, hd:hd + 1]
                        dc = j * 128 - c0
                        m01 = mask01F_t if dr == 0 else mask01B_t
                        S.op('act', lambda e: e.activation(ee.a[:, 0:n], nbt.a[:, lc:lc + n], AF.Exp, bias=abias, scale=1.0), reads=[nbt.r, TOK.r], writes=[ee.r])
                        S.op('dve', lambda e: e.scalar_tensor_tensor(PT.a[:, 0:n], ee.a[:, 0:n], 0.125, pscore.a[:, 0:n], ALU.min, ALU.mult),
                             reads=[pscore.r, ee.r], writes=[PT.r])
                        if 0 <= dc < n:
                            S.op('dve', lambda e: e.tensor_tensor(PT.a[:, dc:dc + 128], PT.a[:, dc:dc + 128], m01[:], ALU.mult), reads=[PT.r, CR], writes=[PT.r])

                    def fin(i, po, por):
                        raise AssertionError('block finalize only')

                    def fin_blk(qts, pv, por, dr=dr, hd=hd, s=s, e_=e_, hd_=hd_):
                        i0, nq = qts[0], len(qts)
                        tl0 = seq_tile(s, i0)
                        t_ = smt()
                        den = pv[:, 0:nq, 64]
                        num = pv[:, 0:nq, 0:64]
                        S.op('dve', lambda e: e.tensor_tensor(t_.a[:, 0:nq], den, TOK.a[:, tl0:tl0 + nq, 1, hd], ALU.max), reads=[por, TOK.r], writes=[t_.r])
                        S.op('dve', lambda e: e.scalar_tensor_tensor(t_.a[:, 0:nq], den, -1.0, t_.a[:, 0:nq], ALU.mult, ALU.max), reads=[por, t_.r], writes=[t_.r])
                        S.op('dve', lambda e: e.reciprocal(t_.a[:, 0:nq], t_.a[:, 0:nq]), reads=[t_.r], writes=[t_.r])
                        rb = t_.a[:, 0:nq].unsqueeze(2).to_broadcast([128, nq, 64])
                        HFv = HF.a[:, i0:i0 + nq, :]
                        if dr == 0:
                            S.op('dve', lambda e: e.tensor_tensor(HFv, num, rb, ALU.mult), reads=[por, t_.r], writes=[HF.r])
                            return
                        sb_ = stg()
                        tv = sb_.a[:, 0:nq * 64].rearrange('p (t d) -> p t d', t=nq)
                        S.op('dve', lambda e: e.tensor_tensor(tv, num, rb, ALU.mult), reads=[por, t_.r], writes=[sb_.r])
                        S.op('dve', lambda e: e.tensor_tensor(HFv, HFv, tv, ALU.add), reads=[sb_.r, HF.r], writes=[HF.r])
                        if qts[-1] != tps - 1:
                            return
                        s_ = stg()
                        t8 = smt()
                        sv = s_.a[:, 0:tps * 64].rearrange('p (t d) -> p t d', t=tps)
                        S.op('dve', lambda e: e.tensor_tensor(sv, HF.a, HF.a, ALU.mult), reads=[HF.r], writes=[s_.r])
                        S.op('dve', lambda e: e.tensor_reduce(t8.a[:, 0:tps], sv, AX.X, ALU.add), reads=[s_.r], writes=[t8.r])
                        S.op('act', lambda e: e.activation(t8.a[:, 0:tps], t8.a[:, 0:tps], AF.Ln, bias=EPS, scale=1.0 / 64), reads=[t8.r], writes=[t8.r])
                        S.op('act', lambda e: e.activation(t8.a[:, 0:tps], t8.a[:, 0:tps], AF.Exp, scale=-0.5), reads=[t8.r], writes=[t8.r])
                        S.op('dve', lambda e: e.tensor_tensor(sv, HF.a, t8.a[:, 0:tps].unsqueeze(2).to_broadcast([128, tps, 64]), ALU.mult),
                             reads=[HF.r, t8.r], writes=[s_.r])
                        S.op('pool', lambda e: e.tensor_tensor(sv, sv, hgn_t[:, hd_ * 64:(hd_ + 1) * 64].unsqueeze(1).to_broadcast([128, tps, 64]), ALU.mult),
                             reads=[s_.r, CR], writes=[s_.r])
                        S.op('pool', lambda e: e.tensor_tensor(opair.a[:, s * tps:(s + 1) * tps, e_ * 64:(e_ + 1) * 64], sv,
                                                               OB.a[:, s * tps:(s + 1) * tps, hd_ * 64:(hd_ + 1) * 64], ALU.mult),
                             reads=[s_.r, OB.r], writes=[opair.r])


                    fin.blk = fin_blk

                    def run(after_block):
                        attn_job(wk, L, lambda c0, n: QB.a[pbs:pbs + 64, c, q0 + c0:q0 + c0 + n], [QB.r], kts, valid, prob, fin, after_block=after_block)
                    return dict(pro=pro, run=run)

            mjobs = []
            for c in range(4):
                for s in range(nseq):
                    for e_ in range(2):
                        for dr in range(2):
                            jb = _mk_mjob(c, s, e_, dr, len(mjobs))
                            jb['flush'] = (4 + c) if (s == nseq - 1 and e_ == 1 and dr == 1) else None
                            mjobs.append(jb)
            blocks0 = list(range(0, L, 512))
            for b0 in blocks0:
                mjobs[0]['pro'](b0)
            for k, jb in enumerate(mjobs):
                nxt = mjobs[k + 1] if k + 1 < len(mjobs) else None
                if sample:
                    jb['run'](lambda b0, nxt=nxt: nxt['pro'](b0) if nxt is not None else None)
                else:
                    if nxt is not None:
                        nxt['pro'](0)
                    jb['run'](None)
                if jb['flush'] is not None:
                    flush_pair(jb['flush'])

            if not sample:
                WVt = [carve([128, 8, 65], BF16, 'WV%d' % j) for j in range(tps)]
                for s in range(nseq):
                    sg_ = grp * 2 + s
                    for dr in range(2):
                        pcs = [ps('a'), ps('b')]
                        WVs = []
                        for j in range(tps):
                            tl = seq_tile(s, j)
                            wv = WVt[j]
                            wf = TOK.a[:, tl, 2, dr * 8:dr * 8 + 8].unsqueeze(2).to_broadcast([128, 8, 65])
                            S.op('dve', lambda e, wv=wv, tl=tl, wf=wf: e.tensor_tensor(wv.a, VB.a[:, tl, :, :], wf, ALU.mult), reads=[VB.r, TOK.r], writes=[wv.r])
                            WVs.append(wv)
                        for hh in range(8):
                            pc = pcs[hh // 4]
                            oc = (hh % 4) * 128
                            for j in range(tps):
                                tl = seq_tile(s, j)
                                S.op('pe', lambda e, hh=hh, j=j, tl=tl, pc=pc, oc=oc: e.matmul(pc.a[0:64, oc:oc + 65], KBT.a[:, tl, hh * 64:(hh + 1) * 64], WVs[j].a[:, hh, :],
                                                                                               start=(j == 0), stop=(j == tps - 1), skip_group_check=True),
                                     reads=[KBT.r, WVs[j].r], writes=[pc.r])
                        s_ = stg()
                        for half in range(2):
                            S.op('act', lambda e, half=half, s_=s_: e.activation(s_.a[0:64, half * 260:half * 260 + 260].rearrange('p (h v) -> p h v', h=4),
                                                                                 pcs[half].a[0:64, :].rearrange('p (h v) -> p h v', h=4)[:, :, 0:65], AF.Copy, scale=0.125),
                                 reads=[pcs[half].r], writes=[s_.r])
                        sv = s_.a[0:64, 0:520].rearrange('p (h v) -> p h v', h=8)
                        S.dma('sp', lambda e, sv=sv, sg_=sg_, dr=dr, s_=s_: e.dma_start(out=o_C.ap()[sg_, dr].rearrange('h k v -> k h v'), in_=sv[:, :, 0:64]), s_.r, reads=[s_.r])
                        S.dma('sp', lambda e, sv=sv, sg_=sg_, dr=dr, s_=s_: e.dma_start(out=o_n.ap()[sg_, dr].rearrange('h k -> k h'), in_=sv[:, :, 64], allow_slow_non_contiguous=True), s_.r, reads=[s_.r])
            if sample:
                ring.extend(rsv)
        else:
            nkt = (2 if sample else 0) + ntile
            koff = 256 if sample else 0
            QC = carve([128, 4, NT], BF16, 'QC')
            KC = carve([128, 4, koff + NT], BF16, 'KC')
            VC = carve([128, nkt, 8, 65], BF16, 'VC')
            QD = carve([128, 4, NT], BF16, 'QD')
            KD = carve([128, 2, koff + NT], BF16, 'KD')
            VD = carve([128, nkt, 2, 65], BF16, 'VD')
            S.op('pool', lambda e: e.memset(VC.a, 1.0), writes=[VC.r])
            S.op('pool', lambda e: e.memset(VD.a, 1.0), writes=[VD.r])
            if sample:
                load_ctx_kT(kcctx, 4, KC, False)
                load_ctx_v(vcctx, 8, VC, 0)
                load_ctx_kT(kdctx, 2, KD, True)
                load_ctx_v(vdctx, 2, VD, 0)
            W = wslab_in(mi, 0, 512)
            for c in range(4):
                def post(b, pb, c=c):
                    S.op('act', lambda e: e.copy(QC.a[:, c, b * 512:(b + 1) * 512], pb.a), reads=[pb.r], writes=[QC.r])
                proj_fm(W, c * 128, 128, hg, NT, post)
            W = wslab_in(mi, 512, 512)
            for c in range(4):
                def post(b, pb, c=c):
                    S.op('dve', lambda e: e.tensor_copy(KC.a[:, c, koff + b * 512:koff + (b + 1) * 512], pb.a), reads=[pb.r], writes=[KC.r])
                proj_fm(W, c * 128, 128, hg, NT, post)
            if not sample:
                def post(t, pb):
                    out_tm(pb, 512, o_kc, grp * 2 + t // tps, t % tps)
                proj_tm(W, 0, 512, hg, ntile, post)
            W = wslab_in(mi, 1024, 512)
            def post(t, pb):
                kt = (2 if sample else 0) + t
                S.op('dve', lambda e: e.tensor_copy(VC.a[:, kt, :, 0:64], pb.a.rearrange('p (h d) -> p h d', h=8)), reads=[pb.r], writes=[VC.r])
                if not sample:
                    out_tm(pb, 512, o_vc, grp * 2 + t // tps, t % tps)
            proj_tm(W, 0, 512, hg, ntile, post)
            W = wslab_in(mi, 1536, 512)
            for c in range(4):
                def post(b, pb, c=c):
                    if sample:
                        qn = tmp_sq[1]
                        S.op('act', lambda e: e.copy(qn.a, pb.a), reads=[pb.r], writes=[qn.r])
                        rope_fm(qn.a, qn.r, QD.a[:, c, b * 512:(b + 1) * 512], QD.r, b * 512)
                    else:
                        S.op('act', lambda e: e.copy(QD.a[:, c, b * 512:(b + 1) * 512], pb.a), reads=[pb.r], writes=[QD.r])
                proj_fm(W, c * 128, 128, hg, NT, post)
            W = wslab_in(mi, 2048, 512)
            for c in range(2):
                def post(b, pb, c=c):
                    if sample:
                        qn = tmp_sq[1]
                        S.op('act', lambda e: e.copy(qn.a, pb.a), reads=[pb.r], writes=[qn.r])
                        rope_fm(qn.a, qn.r, KD.a[:, c, koff + b * 512:koff + (b + 1) * 512], KD.r, b * 512)
                    else:
                        S.op('act', lambda e: e.copy(KD.a[:, c, b * 512:(b + 1) * 512], pb.a), reads=[pb.r], writes=[KD.r])
                proj_fm(W, c * 128, 128, hg, NT, post)
            if not sample:
                def post(t, pb):
                    out_tm(pb, 128, o_kd, grp * 2 + t // tps, t % tps)
                proj_tm(W, 256, 128, hg, ntile, post)
            def post(t, pb):
                kt = (2 if sample else 0) + t
                S.op('dve', lambda e: e.tensor_copy(VD.a[:, kt, :, 0:64], pb.a[:, 0:128].rearrange('p (h d) -> p h d', h=2)), reads=[pb.r], writes=[VD.r])
                if not sample:
                    out_tm(pb, 128, o_vd, grp * 2 + t // tps, t % tps)
            proj_tm(W, 384, 128, hg, ntile, post)

            import os as _os
            ksub = _os.environ.get('KSUB', '')
            if not sample:
                for c in range(4 if 'nomha' not in ksub else 0):
                    for s in range(nseq):
                        q0 = s * L
                        for e_ in range(2):
                            pbs = 64 * e_
                            hh = 2 * c + e_
                            kts = [dict(kT=KC.a[pbs:pbs + 64, c, q0 + j * 128:q0 + (j + 1) * 128], ns=128, vaug=VC.a[:, seq_tile(s, j), hh, :], res=[KC.r, VC.r])
                                   for j in range(tps)]
                            fs = fin_softmax(e_)
                            attn_job(wk, L, lambda c0, n, c=c, pbs=pbs, q0=q0: QC.a[pbs:pbs + 64, c, q0 + c0:q0 + c0 + n], [QC.r], kts,
                                     lambda kt, i: True, prob_exp, fs.bind(s * tps))
                        if s == nseq - 1:
                            flush_pair(c)
                for c in range(4 if 'nogqa' not in ksub else 0):
                    for s in range(nseq):
                        q0 = s * L
                        kv = c // 2
                        for e_ in range(2):
                            pbs = 64 * e_
                            hh = 2 * c + e_
                            kts = [dict(kT=KD.a[pbs:pbs + 64, kv, q0 + j * 128:q0 + (j + 1) * 128], ns=128, vaug=VD.a[:, seq_tile(s, j), kv, :], res=[KD.r, VD.r])
                                   for j in range(tps)]
                            fs = fin_softmax(e_, extra_den=esink_t[:, hh:hh + 1])
                            attn_job(wk, L, lambda c0, n, c=c, pbs=pbs, q0=q0: QD.a[pbs:pbs + 64, c, q0 + c0:q0 + c0 + n], [QD.r], kts,
                                     lambda kt, i: True, prob_exp, fs.bind(s * tps))
                        if s == nseq - 1:
                            flush_pair(4 + c)
            else:
                navalid = _na_rows()
                TB = [carve([128, 15, 64], F32, 'TB') for _ in range(2)]
                ARG = [carve([128, 512], F32, 'ARG') for _ in range(2)]
                for c in range(4 if 'nona' not in ksub else 0):
                    for e_ in range(2):
                        pbs = 64 * e_
                        hh = 2 * c + e_
                        tb = TB[hh % 2]
                        S.dma('sp', lambda e, tb=tb, hh=hh: e.dma_start(out=tb.a, in_=natb.ap()[hh].rearrange('p (a b) -> p a b', a=15)), tb.r, writes=[tb.r])
                        S.op('pool', lambda e, tb=tb: e.tensor_tensor(tb.a, tb.a, cmask_t[:].unsqueeze(1).to_broadcast([128, 15, 64]), ALU.add), reads=[tb.r, CR], writes=[tb.r])
                        kts = []
                        for j in range(2):
                            kts.append(dict(kT=KC.a[pbs:pbs + 64, c, j * 128:(j + 1) * 128], ns=128, vaug=VC.a[:, j, hh, :], res=[KC.r, VC.r], ctx=True))
                        for j in range(8):
                            kts.append(dict(kT=KC.a[pbs:pbs + 64, c, 256 + j * 128:256 + (j + 1) * 128], ns=128, vaug=VC.a[:, 2 + j, hh, :], res=[KC.r, VC.r], j=j))

                        def valid(kt, i):
                            if kt < 2:
                                return True
                            j = kt - 2
                            return any(navalid[2 * j + a][2 * i + b] for a in range(2) for b in range(2))

                        def prob(kt, kd, c0, n, pscore, PT, tb=tb):
                            if kd.get('ctx'):
                                return prob_exp(kt, kd, c0, n, pscore, PT)
                            j = kd['j']
                            S.op('pool', lambda e: e.memset(PT.a[:, 0:n], 0.0), writes=[PT.r])
                            arg = ARG[wk.ptc % 2]
                            r0 = c0 // 64
                            nr = n // 64
                            for a in range(2):
                                srow = 2 * j + a
                                rows = [r for r in range(r0, r0 + nr) if navalid[srow][r]]
                                if not rows:
                                    continue
                                rl, rh = min(rows), max(rows) + 1
                                cl, cn = (rl - r0) * 64, (rh - rl) * 64
                                dy0 = rl - srow + 7
                                pa = a * 64
                                S.op('dve', lambda e, pa=pa, cl=cl, cn=cn, dy0=dy0, rl=rl, rh=rh: e.scalar_tensor_tensor(
                                    arg.a[pa:pa + 64, cl:cl + cn], pscore.a[pa:pa + 64, cl:cl + cn], 0.125,
                                    tb.a[pa:pa + 64, dy0:dy0 + (rh - rl), :].rearrange('p a b -> p (a b)'), ALU.mult, ALU.add),
                                    reads=[pscore.r, tb.r], writes=[arg.r])
                                S.op('act', lambda e, pa=pa, cl=cl, cn=cn: e.activation(PT.a[pa:pa + 64, cl:cl + cn], arg.a[pa:pa + 64, cl:cl + cn], AF.Exp),
                                     reads=[arg.r], writes=[PT.r])
                        fs = fin_softmax(e_)
                        attn_job(wk, L, lambda c0, n, c=c, pbs=pbs: QC.a[pbs:pbs + 64, c, c0:c0 + n], [QC.r], kts, valid, prob,
                                 fs.bind(0))
                    flush_pair(c)
                for c in range(4 if 'noswa' not in ksub else 0):
                    kv = c // 2
                    for e_ in range(2):
                        pbs = 64 * e_
                        hh = 2 * c + e_
                        kts = []
                        for j in range(2):
                            kts.append(dict(kT=KD.a[pbs:pbs + 64, kv, j * 128:(j + 1) * 128], ns=128, vaug=VD.a[:, j, kv, :], res=[KD.r, VD.r], ctx=True))
                        for j in range(8):
                            kts.append(dict(kT=KD.a[pbs:pbs + 64, kv, 256 + j * 128:256 + (j + 1) * 128], ns=128, vaug=VD.a[:, 2 + j, kv, :], res=[KD.r, VD.r], j=j))

                        def valid(kt, i):
                            return True if kt < 2 else abs(i - (kt - 2)) <= 1

                        def prob(kt, kd, c0, n, pscore, PT):
                            if kd.get('ctx'):
                                return prob_exp(kt, kd, c0, n, pscore, PT)
                            j = kd['j']
                            arg = ARG[wk.ptc % 2]
                            for i in range(c0 // 128, (c0 + n) // 128):
                                lc = i * 128 - c0
                                if i == j:
                                    S.op('act', lambda e, lc=lc: e.activation(PT.a[:, lc:lc + 128], pscore.a[:, lc:lc + 128], AF.Exp, scale=0.125), reads=[pscore.r], writes=[PT.r])
                                else:
                                    mk_ = maskF_t if i == j - 1 else maskB_t
                                    S.op('dve', lambda e, lc=lc, mk_=mk_: e.scalar_tensor_tensor(arg.a[:, lc:lc + 128], pscore.a[:, lc:lc + 128], 0.125, mk_[:], ALU.mult, ALU.add),
                                         reads=[pscore.r, CR], writes=[arg.r])
                                    S.op('act', lambda e, lc=lc: e.activation(PT.a[:, lc:lc + 128], arg.a[:, lc:lc + 128], AF.Exp), reads=[arg.r], writes=[PT.r])
                        fs = fin_softmax(e_, extra_den=esink_t[:, hh:hh + 1])
                        attn_job(wk, L, lambda c0, n, c=c, pbs=pbs: QD.a[pbs:pbs + 64, c, c0:c0 + n], [QD.r], kts, valid, prob,
                                 fs.bind(0))
                    flush_pair(4 + c)

        mo = Wd[l]['mix_out']
        O1 = wslab_out(mo, 0, 4)
        O2 = wslab_out(mo, 512, 4)
        for i, tg in enumerate(tgs):
            for m in range(8):
                py = ps('a')
                for cc in range(8):
                    Ow = O1 if cc < 4 else O2
                    S.op('pe', lambda e, cc=cc, m=m, i=i, py=py, Ow=Ow: e.matmul(py.a, Ow.a[:, cc % 4, m * 128:(m + 1) * 128], oT.a[:, cc, i * 512:(i + 1) * 512],
                                                                                start=(cc == 0), stop=(cc == 7)),
                         reads=[Ow.r, oT.r], writes=[py.r])
                S.op('dve', lambda e, m=m, tg=tg, py=py: e.scalar_tensor_tensor(xap(m, tg), py.a, modG[l][:, 1, m, cnd:cnd + 1], xap(m, tg), ALU.mult, ALU.add),
                     reads=[py.r, modD_R[l][1], xres[m][tg]], writes=[xres[m][tg]])

    import os
    parts = os.environ.get('KPARTS', 'all')

    def on(p):
        return parts == 'all' or p in parts.split(',')
    load_x()
    adaln_slabs(0, 0, 6)
    adaln_finish(0, (0,))
    for l in range(2):
        if l == 0:
            if on('f01'):
                ffn(0, 1, between=lambda i: adaln_slabs(0, 6 + 2 * i, 8 + 2 * i))
            else:
                adaln_slabs(0, 6, 18)
            adaln_finish(0, (1, 2))
        else:
            if on('f11'):
                ffn(1, 1)
        for grp in range(3):
            if on('m%d%d' % (l, grp)):
                mixer(l, grp)
        if l == 0:
            if on('f02'):
                ffn(0, 2, between=lambda i: adaln_slabs(1, 3 * i, 3 * i + 3))
            else:
                adaln_slabs(1, 0, 18)
            adaln_finish(1)
        else:
            if on('f12'):
                ffn(1, 2, tail=True)
    if not on('f12'):
        final_out()
    S.emit()


_PROG = {}


def _prep_weights(inp):
    sh = {}
    for l in range(2):
        sh['ada_w%d' % l] = np.ascontiguousarray(inp['ada_w_l%d' % l], dtype=np.float32)
        sh['ada_b%d' % l] = np.ascontiguousarray(inp['ada_b_l%d' % l].reshape(72, 128).T, dtype=np.float32)
        sh['norm%d' % l] = np.ascontiguousarray(inp['norm_l%d' % l].reshape(3, 8, 128).transpose(2, 0, 1), dtype=np.float32)
        for f in (1, 2):
            sh['f%din%d' % (f, l)] = np.ascontiguousarray(inp['ffn%d_in_l%d' % (f, l)], dtype=np.float32)
            sh['f%dout%d' % (f, l)] = np.ascontiguousarray(inp['ffn%d_out_l%d' % (f, l)], dtype=np.float32)
        sh['mixout%d' % l] = np.ascontiguousarray(inp['mix_out_l%d' % l], dtype=np.float32)
    w = np.asarray(inp['mix_in_l0'], dtype=np.float32)
    qa, ka, va = w[:, 0:512], w[:, 512:640], w[:, 640:768]
    qb, kb, vb = w[:, 768:1280], w[:, 1280:1792], w[:, 1792:2304]
    gt, ob = w[:, 2304:2336], w[:, 2336:2848]
    z = np.zeros((D, 24), np.float32)
    g1 = np.concatenate([gt[:, 0:8], z, gt[:, 16:24], z], 1)
    g2 = np.concatenate([gt[:, 8:16], z, gt[:, 24:32], z], 1)
    kadup = np.concatenate([ka[:, 0:64], ka[:, 0:64], ka[:, 64:128], ka[:, 64:128]], 1)
    m0_ = np.concatenate([qa, kadup, g1, g2, va, qb, kb, vb, ob, ka, np.zeros((D, L0_COLS - 3200), np.float32)], 1)
    assert m0_.shape[1] == L0_COLS
    sh['mixin0'] = np.ascontiguousarray(m0_)
    w = np.asarray(inp['mix_in_l1'], dtype=np.float32)
    qc, kc, vc, qd, kd, vd = w[:, 0:512], w[:, 512:1024], w[:, 1024:1536], w[:, 1536:2048], w[:, 2048:2176], w[:, 2176:2304]
    kddup = np.concatenate([kd[:, 0:64], kd[:, 0:64], kd[:, 64:128], kd[:, 64:128]], 1)
    m1_ = np.concatenate([qc, kc, vc, qd, kddup, kd, vd], 1)
    assert m1_.shape[1] == L1_COLS
    sh['mixin1'] = np.ascontiguousarray(m1_)
    sh['gfin'] = np.ascontiguousarray(np.asarray(inp['norm_final'], np.float32).reshape(8, 128).T)
    qk = np.asarray(inp['qk_norm_l0'], np.float32)
    sh['qkg'] = np.ascontiguousarray(np.stack([np.tile(qk[0], 2), np.tile(qk[1], 2)], 1))
    sh['gkbc'] = np.ascontiguousarray(qk[1])
    gb = np.asarray(inp['gate_bias_l0'], np.float32)
    gbt = np.zeros((64, 2), np.float32)
    gbt[0:8, 0] = gb[0:8]
    gbt[32:40, 0] = gb[16:24]
    gbt[0:8, 1] = gb[8:16]
    gbt[32:40, 1] = gb[24:32]
    sh['gbias'] = gbt
    sh['hgn'] = np.ascontiguousarray(inp['head_norm_l0'], dtype=np.float32)
    sh['sink'] = np.ascontiguousarray(inp['sink_l1'], dtype=np.float32)
    rpb = np.asarray(inp['rpb_l1'], np.float32)
    sc = np.arange(64)[:, None]
    qc_ = np.arange(64)[None, :]
    dx = np.clip(sc - qc_ + 15, 0, 30)
    tb = np.zeros((8, 128, 15, 64), np.float32)
    for dyi in range(15):
        blk = rpb[:, 14 - dyi, :][:, dx]
        tb[:, 0:64, dyi, :] = blk
        tb[:, 64:128, dyi, :] = blk
    sh['natb'] = np.ascontiguousarray(tb.reshape(8, 128, 15 * 64))
    for k, v in _consts().items():
        sh['c_' + k] = v
    return sh


def kernel(**inp):
    inp = {k: np.asarray(v) for k, v in inp.items()}
    dbg = inp.pop('_dbg', None)
    key = 'main'
    if key not in _PROG:
        _PROG[key] = build_program(None)
    nc = _PROG[key]
    sh = _prep_weights(inp)
    in_maps = []
    for i in range(8):
        b = i // 4
        m = dict(sh)
        xp = inp['x_prompt'][4 * i:4 * i + 4].reshape(1024, D)
        xs = inp['x_sample'][b]
        m['xin'] = np.ascontiguousarray(np.concatenate([xp, xs], 0), dtype=np.float32)
        cond = np.stack([inp['c_ctx'], inp['c'][b]], 0).astype(np.float32)
        m['condT'] = np.ascontiguousarray(cond.reshape(2, 8, 128).transpose(2, 1, 0))
        m['kctx0'] = np.ascontiguousarray(inp['cache_l0_attn_k'][b].reshape(256, 128), dtype=np.float32)
        m['vctx0'] = np.ascontiguousarray(inp['cache_l0_attn_v'][b].reshape(256, 128), dtype=np.float32)
        m['C0'] = np.ascontiguousarray(inp['state_l0_mlstm_C'][b], dtype=np.float32)
        m['n0'] = np.ascontiguousarray(inp['state_l0_mlstm_n'][b], dtype=np.float32)
        m['m0'] = np.ascontiguousarray(inp['state_l0_mlstm_m'][b].reshape(16), dtype=np.float32)
        m['kcctx'] = np.ascontiguousarray(inp['cache_l1_na_k'][b].reshape(256, 512), dtype=np.float32)
        m['vcctx'] = np.ascontiguousarray(inp['cache_l1_na_v'][b].reshape(256, 512), dtype=np.float32)
        m['kdctx'] = np.ascontiguousarray(inp['cache_l1_swa_k'][b].reshape(256, 128), dtype=np.float32)
        m['vdctx'] = np.ascontiguousarray(inp['cache_l1_swa_v'][b].reshape(256, 128), dtype=np.float32)
        in_maps.append(m)
    import os
    ncore = int(os.environ.get('KCORES', '8'))
    res = run_bass_kernel_spmd(nc, in_maps[:ncore], core_ids=list(range(ncore)))
    R = list(res.results)
    while len(R) < 8:
        R.append(R[0])
    cat = lambda k: np.concatenate([np.asarray(R[i][k]) for i in range(8)], 0)
    y_prompt = cat('o_yp').reshape(32, 256, D)
    y_sample = np.stack([np.asarray(R[0]['o_ys']), np.asarray(R[4]['o_ys'])], 0)
    k0 = cat('o_k0').reshape(32, 256, 2, 64)
    v0 = cat('o_v0').reshape(32, 256, 2, 64)
    Cst = cat('o_C')
    nst = cat('o_n')
    mst = cat('o_m')
    kc1 = cat('o_kc').reshape(32, 256, 8, 64)
    vc1 = cat('o_vc').reshape(32, 256, 8, 64)
    kd1 = cat('o_kd').reshape(32, 256, 2, 64)
    vd1 = cat('o_vd').reshape(32, 256, 2, 64)
    outs = (y_prompt, y_sample, k0, v0, Cst, nst, mst, kc1, vc1, kd1, vd1)
    return tuple(np.ascontiguousarray(o, dtype=np.float32) for o in outs)
```
